# Optimizing a Trainium2 kernel written in Bass

```python
import jax, jax.numpy as jnp
from jax import lax
import numpy as np

D_MODEL = 1024
BATCH = 8
SEQ = 4096
DEPTH = 4

N_HEADS = 16
HEAD_DIM = D_MODEL // N_HEADS
ROPE_THETA = 10000.0
NORM_EPS = 1e-6
PLE_DIM = 256
N_MIXERS = 2
N_MOBA_LAYERS = (DEPTH + 1) // 2
N_NSA_LAYERS = DEPTH // 2
NEG_INF = -1e30
FORCE_SCORE = 1e30

MOBA_BLOCK = 256
MOBA_TOPK = 3
MOBA_Q_CHUNK = 16
MOBA_IN = 4 * D_MODEL

NSA_KV_GROUPS = 4
NSA_HEADS_PER_GROUP = N_HEADS // NSA_KV_GROUPS
NSA_KV_DIM = NSA_KV_GROUPS * HEAD_DIM
NSA_N_BRANCH = 3
CMP_LEN = 32
CMP_STRIDE = 16
CMP_HIDDEN = 4 * HEAD_DIM
SEL_BLOCK = 64
SEL_TOPN = 16
WINDOW = 512
NSA_Q_CHUNK = 64
NSA_IN = D_MODEL + 2 * NSA_N_BRANCH * NSA_KV_DIM + NSA_N_BRANCH * N_HEADS + D_MODEL

kernel_name = "hybrid_moba_nsa_gated_ple_trunk"


def rms_norm(x, gain):
    xf = x.astype(jnp.float32)
    y = xf * lax.rsqrt(jnp.mean(xf * xf, axis=-1, keepdims=True) + NORM_EPS)
    return (y * gain.astype(jnp.float32)).astype(x.dtype)


def rope(x, pos):
    half = HEAD_DIM // 2
    inv_freq = ROPE_THETA ** (-jnp.arange(half, dtype=jnp.float32) / half)
    ang = pos.astype(jnp.float32)[..., None] * inv_freq
    cos, sin = jnp.cos(ang), jnp.sin(ang)
    xf = x.astype(jnp.float32)
    x1, x2 = xf[..., :half], xf[..., half:]
    return jnp.concatenate([x1 * cos - x2 * sin, x2 * cos + x1 * sin], axis=-1).astype(x.dtype)


def masked_softmax(s, mask):
    s = jnp.where(mask, s.astype(jnp.float32), NEG_INF)
    return jnp.where(mask, jax.nn.softmax(s, axis=-1), 0.0)


def split_heads(x, n):
    B, T, _ = x.shape
    return x.reshape(B, T, n, HEAD_DIM).transpose(0, 2, 1, 3)


def merge_heads(x):
    B, H, T, d = x.shape
    return x.transpose(0, 2, 1, 3).reshape(B, T, H * d)


def moba_attention(q, k, v):
    B, H, T, _ = q.shape
    nb = -(-T // MOBA_BLOCK)
    Tp = nb * MOBA_BLOCK
    pad = ((0, 0), (0, 0), (0, Tp - T), (0, 0))
    q, k, v = jnp.pad(q, pad), jnp.pad(k, pad), jnp.pad(v, pad)
    kb = k.reshape(B, H, nb, MOBA_BLOCK, HEAD_DIM)
    vb = v.reshape(B, H, nb, MOBA_BLOCK, HEAD_DIM)
    k_mean = kb.astype(jnp.float32).mean(axis=3).astype(k.dtype)
    q_blk = jnp.arange(Tp) // MOBA_BLOCK
    gate = jnp.einsum('bhtd,bhnd->bhtn', q, k_mean).astype(jnp.float32)
    past = jnp.arange(nb)[None, :] < q_blk[:, None]
    gate = jnp.where(past, gate, NEG_INF)
    topk = min(MOBA_TOPK, nb)
    g_val, g_idx = lax.top_k(gate, topk)
    g_valid = g_val > 0.5 * NEG_INF
    scale = HEAD_DIM ** -0.5
    C = MOBA_Q_CHUNK
    b_ix = jnp.arange(B)[:, None, None, None]
    h_ix = jnp.arange(H)[None, :, None, None]

    def chunk(c):
        s0 = c * C
        qc = lax.dynamic_slice_in_dim(q, s0, C, axis=2)
        idx = lax.dynamic_slice_in_dim(g_idx, s0, C, axis=2)
        valid = lax.dynamic_slice_in_dim(g_valid, s0, C, axis=2)
        k_sel = kb[b_ix, h_ix, idx]
        v_sel = vb[b_ix, h_ix, idx]
        s_sel = jnp.einsum('bhcd,bhcnkd->bhcnk', qc, k_sel).reshape(B, H, C, topk * MOBA_BLOCK)
        m_sel = jnp.broadcast_to(valid[..., None], (B, H, C, topk, MOBA_BLOCK)).reshape(B, H, C, topk * MOBA_BLOCK)
        blk0 = (s0 // MOBA_BLOCK) * MOBA_BLOCK
        k_own = lax.dynamic_slice_in_dim(k, blk0, MOBA_BLOCK, axis=2)
        v_own = lax.dynamic_slice_in_dim(v, blk0, MOBA_BLOCK, axis=2)
        s_own = jnp.einsum('bhcd,bhkd->bhck', qc, k_own)
        q_pos = s0 + jnp.arange(C)
        k_pos = blk0 + jnp.arange(MOBA_BLOCK)
        m_own = jnp.broadcast_to(k_pos[None, :] <= q_pos[:, None], (B, H, C, MOBA_BLOCK))
        s = jnp.concatenate([s_sel, s_own], axis=-1) * scale
        m = jnp.concatenate([m_sel, m_own], axis=-1)
        pr = masked_softmax(s, m).astype(v.dtype)
        p_sel = pr[..., :topk * MOBA_BLOCK].reshape(B, H, C, topk, MOBA_BLOCK)
        p_own = pr[..., topk * MOBA_BLOCK:]
        return (jnp.einsum('bhcnk,bhcnkd->bhcd', p_sel, v_sel)
                + jnp.einsum('bhck,bhkd->bhcd', p_own, v_own))

    out = lax.map(chunk, jnp.arange(Tp // C))
    out = out.transpose(1, 2, 0, 3, 4).reshape(B, H, Tp, HEAD_DIM)
    return out[:, :, :T]


def moba_layer(h, w_in, q_gain, k_gain, w_out, pos):
    q, k, v, z = jnp.split(h @ w_in, 4, axis=-1)
    q = rope(rms_norm(split_heads(q, N_HEADS), q_gain), pos)
    k = rope(rms_norm(split_heads(k, N_HEADS), k_gain), pos)
    v = split_heads(v, N_HEADS)
    o = merge_heads(moba_attention(q, k, v)) * jax.nn.silu(z)
    return o @ w_out


def compress(x, pe, w1, w2):
    B, G, T, _ = x.shape
    nc = (T - CMP_LEN) // CMP_STRIDE + 1
    idx = np.arange(nc)[:, None] * CMP_STRIDE + np.arange(CMP_LEN)[None, :]
    blocks = x[:, :, idx] + pe
    flat = blocks.reshape(B, G, nc, CMP_LEN * HEAD_DIM)
    return jax.nn.gelu(flat @ w1) @ w2


def selection_overlap(nc, ns):
    c_start = np.arange(nc) * CMP_STRIDE
    s_start = np.arange(ns) * SEL_BLOCK
    ovl = (c_start[:, None] < s_start[None, :] + SEL_BLOCK) & (c_start[:, None] + CMP_LEN > s_start[None, :])
    return jnp.asarray(ovl.astype(np.float32))


def nsa_attention(q, kc, vc, ks, vs, kw, vw, g):
    B, H, T, _ = q.shape
    G, R, C = NSA_KV_GROUPS, NSA_HEADS_PER_GROUP, NSA_Q_CHUNK
    q = q.reshape(B, G, R, T, HEAD_DIM)
    g = g.reshape(B, G, R, T, NSA_N_BRANCH)
    nc = kc.shape[2]
    ns = T // SEL_BLOCK
    topn = min(SEL_TOPN, ns)
    cmp_end = jnp.arange(nc) * CMP_STRIDE + CMP_LEN - 1
    ovl = selection_overlap(nc, ns)
    ksb = ks.reshape(B, G, ns, SEL_BLOCK, HEAD_DIM)
    vsb = vs.reshape(B, G, ns, SEL_BLOCK, HEAD_DIM)
    wpad = ((0, 0), (0, 0), (WINDOW, 0), (0, 0))
    kw_pad, vw_pad = jnp.pad(kw, wpad), jnp.pad(vw, wpad)
    scale = HEAD_DIM ** -0.5
    b_ix = jnp.arange(B)[:, None, None, None]
    g_ix = jnp.arange(G)[None, :, None, None]
    blk = jnp.arange(ns)

    def chunk(c):
        s0 = c * C
        qc = lax.dynamic_slice_in_dim(q, s0, C, axis=3)
        gc = lax.dynamic_slice_in_dim(g, s0, C, axis=3)
        q_pos = s0 + jnp.arange(C)
        s_c = jnp.einsum('bgrcd,bgnd->bgrcn', qc, kc) * scale
        p_c = masked_softmax(s_c, cmp_end[None, :] <= q_pos[:, None])
        o_c = jnp.einsum('bgrcn,bgnd->bgrcd', p_c.astype(vc.dtype), vc)
        imp = jnp.einsum('bgrcn,ns->bgcs', p_c, ovl)
        own = q_pos // SEL_BLOCK
        forced = (blk[None, :] == 0) | (blk[None, :] == own[:, None]) | (blk[None, :] == own[:, None] - 1)
        causal = blk[None, :] <= own[:, None]
        imp = jnp.where(forced, FORCE_SCORE, jnp.where(causal, imp, NEG_INF))
        val, idx = lax.top_k(imp, topn)
        valid = val > 0.5 * NEG_INF
        k_sel = ksb[b_ix, g_ix, idx]
        v_sel = vsb[b_ix, g_ix, idx]
        s_s = jnp.einsum('bgrcd,bgcnkd->bgrcnk', qc, k_sel).reshape(B, G, R, C, topn * SEL_BLOCK) * scale
        k_pos = idx[..., None] * SEL_BLOCK + jnp.arange(SEL_BLOCK)
        m_s = valid[..., None] & (k_pos <= q_pos[None, None, :, None, None])
        p_s = masked_softmax(s_s, m_s.reshape(B, G, 1, C, topn * SEL_BLOCK))
        p_s = p_s.reshape(B, G, R, C, topn, SEL_BLOCK).astype(v_sel.dtype)
        o_s = jnp.einsum('bgrcnk,bgcnkd->bgrcd', p_s, v_sel)
        k_w = lax.dynamic_slice_in_dim(kw_pad, s0, C + WINDOW, axis=2)
        v_w = lax.dynamic_slice_in_dim(vw_pad, s0, C + WINDOW, axis=2)
        s_w = jnp.einsum('bgrcd,bgkd->bgrck', qc, k_w) * scale
        k_pos_w = s0 - WINDOW + jnp.arange(C + WINDOW)
        diff = q_pos[:, None] - k_pos_w[None, :]
        m_w = (diff >= 0) & (diff < WINDOW) & (k_pos_w[None, :] >= 0)
        o_w = jnp.einsum('bgrck,bgkd->bgrcd', masked_softmax(s_w, m_w).astype(v_w.dtype), v_w)
        return gc[..., 0:1] * o_c + gc[..., 1:2] * o_s + gc[..., 2:3] * o_w

    out = lax.map(chunk, jnp.arange(T // C))
    return out.transpose(1, 2, 3, 0, 4, 5).reshape(B, H, T, HEAD_DIM)


def nsa_layer(h, w_in, q_gain, k_gain, cmp_pe, cmp_w1, cmp_w2, w_out, pos):
    B, T, _ = h.shape
    offs = np.cumsum([D_MODEL] + [NSA_KV_DIM] * 6 + [NSA_N_BRANCH * N_HEADS]).tolist()
    q, kc, vc, ks, vs, kw, vw, gl, z = jnp.split(h @ w_in, offs, axis=-1)
    q = rope(rms_norm(split_heads(q, N_HEADS), q_gain), pos)
    kc = compress(split_heads(kc, NSA_KV_GROUPS), cmp_pe[0], cmp_w1[0], cmp_w2[0])
    vc = compress(split_heads(vc, NSA_KV_GROUPS), cmp_pe[1], cmp_w1[1], cmp_w2[1])
    cmp_pos = jnp.arange(kc.shape[2]) * CMP_STRIDE + CMP_LEN - 1
    kc = rope(rms_norm(kc, k_gain[0]), cmp_pos)
    ks = rope(rms_norm(split_heads(ks, NSA_KV_GROUPS), k_gain[1]), pos)
    kw = rope(rms_norm(split_heads(kw, NSA_KV_GROUPS), k_gain[2]), pos)
    vs = split_heads(vs, NSA_KV_GROUPS)
    vw = split_heads(vw, NSA_KV_GROUPS)
    g = jax.nn.sigmoid(gl).reshape(B, T, N_HEADS, NSA_N_BRANCH).transpose(0, 2, 1, 3)
    o = merge_heads(nsa_attention(q, kc, vc, ks, vs, kw, vw, g)) * jax.nn.silu(z)
    return o @ w_out


def setup_inputs(seed: int = 0) -> dict:
    key = jax.random.key(seed)
    ks = jax.random.split(key, 17)
    f32 = jnp.float32

    def nrm(k, shape, scale):
        return jax.random.normal(k, shape, f32) * scale

    def gain(k, shape):
        return 1.0 + 0.01 * jax.random.normal(k, shape, f32)

    return {
        "x": nrm(ks[0], (BATCH, SEQ, D_MODEL), 1.0),
        "p": nrm(ks[1], (DEPTH, BATCH, SEQ, PLE_DIM), 1.0),
        "norm_gain": gain(ks[2], (DEPTH, D_MODEL)),
        "moba_w_in": nrm(ks[3], (N_MOBA_LAYERS, D_MODEL, MOBA_IN), D_MODEL ** -0.5),
        "moba_q_gain": gain(ks[4], (N_MOBA_LAYERS, HEAD_DIM)),
        "moba_k_gain": gain(ks[5], (N_MOBA_LAYERS, HEAD_DIM)),
        "moba_w_out": nrm(ks[6], (N_MOBA_LAYERS, D_MODEL, D_MODEL), D_MODEL ** -0.5),
        "nsa_w_in": nrm(ks[7], (N_NSA_LAYERS, D_MODEL, NSA_IN), D_MODEL ** -0.5),
        "nsa_q_gain": gain(ks[8], (N_NSA_LAYERS, HEAD_DIM)),
        "nsa_k_gain": gain(ks[9], (N_NSA_LAYERS, NSA_N_BRANCH, HEAD_DIM)),
        "nsa_cmp_pe": nrm(ks[10], (N_NSA_LAYERS, 2, CMP_LEN, HEAD_DIM), 0.1),
        "nsa_cmp_w1": nrm(ks[11], (N_NSA_LAYERS, 2, CMP_LEN * HEAD_DIM, CMP_HIDDEN), (CMP_LEN * HEAD_DIM) ** -0.5),
        "nsa_cmp_w2": nrm(ks[12], (N_NSA_LAYERS, 2, CMP_HIDDEN, HEAD_DIM), CMP_HIDDEN ** -0.5),
        "nsa_w_out": nrm(ks[13], (N_NSA_LAYERS, D_MODEL, D_MODEL), D_MODEL ** -0.5),
        "ple_w_proj": nrm(ks[14], (DEPTH, PLE_DIM, D_MODEL), PLE_DIM ** -0.5),
        "ple_gate_gain": gain(ks[15], (DEPTH, D_MODEL)),
        "ple_w_gate": nrm(ks[16], (DEPTH, D_MODEL, D_MODEL), D_MODEL ** -0.5),
    }


def reference(x, p, norm_gain, moba_w_in, moba_q_gain, moba_k_gain, moba_w_out,
              nsa_w_in, nsa_q_gain, nsa_k_gain, nsa_cmp_pe, nsa_cmp_w1, nsa_cmp_w2, nsa_w_out,
              ple_w_proj, ple_gate_gain, ple_w_gate):
    pos = jnp.arange(x.shape[1])
    for i in range(DEPTH):
        h = rms_norm(x, norm_gain[i])
        j = i // N_MIXERS
        if i % N_MIXERS == 0:
            x = x + moba_layer(h, moba_w_in[j], moba_q_gain[j], moba_k_gain[j], moba_w_out[j], pos)
        else:
            x = x + nsa_layer(h, nsa_w_in[j], nsa_q_gain[j], nsa_k_gain[j], nsa_cmp_pe[j],
                              nsa_cmp_w1[j], nsa_cmp_w2[j], nsa_w_out[j], pos)
        gate = jax.nn.sigmoid(rms_norm(x, ple_gate_gain[i]) @ ple_w_gate[i])
        x = x + gate * (p[i] @ ple_w_proj[i])
    return x
```

```python
import math
from contextlib import ExitStack

import numpy as np
import concourse.bass as bass
import concourse.mybir as mybir
from concourse.bass_utils import run_bass_kernel_spmd

F32 = mybir.dt.float32
BF16 = mybir.dt.bfloat16
I32 = mybir.dt.int32
AF = mybir.ActivationFunctionType
ALU = mybir.AluOpType
AX = mybir.AxisListType

D = 1024
H = 16
HD = 64
PLE = 256
EPS = 1e-6
NSA_IN = 3632
NEGB = 30000.0


class Op:
    __slots__ = ("eng", "fn", "deps", "needed", "sigval", "stream", "is_dma", "idx")


class Sched:
    ENGS = ("pe", "act", "dve", "pool", "sp")

    def __init__(self):
        self.q = {e: [] for e in self.ENGS}
        self.tw = {}
        self.tr = {}
        self.dma_cnt = {}
        self.nops = 0
        self.last = {}
        self.pending = {}

    def barrier(self):
        for e in self.ENGS:
            self.pending[e] = list(self.last.values())

    def op(self, eng, fn, reads=(), writes=(), pwrites=(), dma=None):
        o = Op()
        o.eng = eng
        o.fn = fn
        o.is_dma = dma is not None
        o.stream = ("dma", dma) if dma is not None else ("eng", eng)
        o.needed = o.is_dma
        o.sigval = None
        o.idx = self.nops
        self.nops += 1
        deps = {}

        def add(d):
            if (not o.is_dma) and (not d.is_dma) and d.eng == "pe" and eng == "pe":
                return
            k = d.stream
            if k not in deps or deps[k].idx < d.idx:
                deps[k] = d

        for d in self.pending.pop(eng, ()):
            add(d)
        for t in reads:
            for d in self.tw.get(t, {}).values():
                add(d)
        for t in writes:
            for d in self.tw.get(t, {}).values():
                add(d)
            for d in self.tr.get(t, {}).values():
                add(d)
        for t in pwrites:
            for d in self.tr.get(t, {}).values():
                add(d)
        o.deps = list(deps.values())
        for d in o.deps:
            d.needed = True
        for t in reads:
            self.tr.setdefault(t, {})[o.stream] = o
        for t in writes:
            self.tw[t] = {o.stream: o}
            self.tr[t] = {}
        for t in pwrites:
            self.tw.setdefault(t, {})[o.stream] = o
        if o.is_dma:
            c = self.dma_cnt.get(dma, 0) + 16
            self.dma_cnt[dma] = c
            o.sigval = c
        self.last[o.stream] = o
        self.q[eng].append(o)
        return o

    def emit(self, nc, stack):
        for e in self.ENGS:
            c = 0
            for o in self.q[e]:
                if not o.is_dma and o.needed:
                    c += 1
                    o.sigval = c
        sems = {}
        for e in self.ENGS:
            sems[("eng", e)] = stack.enter_context(nc.semaphore("s_" + e))
        for d in self.dma_cnt:
            sems[("dma", d)] = stack.enter_context(nc.semaphore("d_" + str(d)))
        block = stack.enter_context(nc.Block())
        q = self.q

        def run(ename, eng):
            seen = {}
            for o in q[ename]:
                for d in o.deps:
                    v = d.sigval
                    if seen.get(d.stream, 0) >= v:
                        continue
                    seen[d.stream] = v
                    eng.wait_ge(sems[d.stream], v)
                ins = o.fn(eng)
                if o.needed:
                    ins.then_inc(sems[o.stream], 16 if o.is_dma else 1)

        @block.tensor
        def _(eng):
            run("pe", eng)

        @block.scalar
        def _(eng):
            run("act", eng)

        @block.vector
        def _(eng):
            run("dve", eng)

        @block.gpsimd
        def _(eng):
            run("pool", eng)

        @block.sync
        def _(eng):
            run("sp", eng)
            for d, c in self.dma_cnt.items():
                eng.wait_ge(sems[("dma", d)], c)


class Builder:
    def __init__(self, T, layers):
        self.T = T
        self.NT = T // 128
        self.NC = T // 16
        self.NCT = max(1, self.NC // 128)
        self.layers = layers
        self.nc = bass.Bass("TRN2", target_bir_lowering=False)
        self.S = Sched()
        self.uid = 0

    def E(self, eng, meth, *a, r=(), w=(), pw=(), **kw):
        self.S.op(eng, lambda e: getattr(e, meth)(*a, **kw), reads=r, writes=w, pwrites=pw)

    def dma(self, out, in_, sem, r=(), w=(), pw=(), q="sp", **kw):
        self.S.op(q, lambda e: e.dma_start(out=out, in_=in_, **kw), reads=r, writes=w, pwrites=pw, dma=sem)

    def mm(self, out, lhsT, rhs, start, stop, r=(), w=()):
        self.S.op("pe", lambda e: e.matmul(out, lhsT=lhsT, rhs=rhs, start=start, stop=stop), reads=r, writes=w)

    def tr(self, out, in_, r=(), w=()):
        ident = self.ident
        self.S.op("pe", lambda e: e.transpose(out=out, in_=in_, identity=ident[:]), reads=tuple(r) + ("ident",), writes=w)

    def act(self, out, in_, func, r=(), w=(), pw=(), **kw):
        self.S.op("act", lambda e: e.activation(out=out, in_=in_, func=func, **kw), reads=r, writes=w, pwrites=pw)

    def persist(self, name, shape, dt):
        return self.st.enter_context(self.nc.sbuf_tensor(name, shape, dt))

    def phase(self):
        self.S.barrier()
        self.aoff = 0
        self.pid = getattr(self, "pid", 0) + 1

    def ph(self, name, shape, dt):
        esz = 4 if dt in (F32, I32) else 2
        n = 1
        for s_ in shape[1:]:
            n *= s_
        nbytes = (n * esz + 63) // 64 * 64
        a = self.aoff
        self.aoff += nbytes
        assert self.aoff <= self.ARENA_BYTES, (name, self.aoff)
        v = self.arena[:, a // 2:(a + n * esz) // 2]
        if esz == 4:
            v = v.bitcast(dt)
        if len(shape) == 3:
            v = v.rearrange("p (a b) -> p a b", b=shape[2])
        elif len(shape) == 4:
            v = v.rearrange("p (a b c) -> p a b c", b=shape[2], c=shape[3])
        return v, ("ph", self.pid, name)

    def build(self):
        nc, T, NT = self.nc, self.T, self.NT
        dt = nc.dram_tensor
        self.x_in = dt("x", [T, D], F32, kind="ExternalInput").ap()
        self.p_in = dt("p", [4, T, PLE], F32, kind="ExternalInput").ap()
        self.norm_gain = dt("norm_gain", [4, D], F32, kind="ExternalInput").ap()
        self.moba_w_in = dt("moba_w_in", [2, D, 4 * D], F32, kind="ExternalInput").ap()
        self.moba_q_gain = dt("moba_q_gain", [2, HD], F32, kind="ExternalInput").ap()
        self.moba_k_gain = dt("moba_k_gain", [2, HD], F32, kind="ExternalInput").ap()
        self.moba_w_out = dt("moba_w_out", [2, D, D], F32, kind="ExternalInput").ap()
        self.nsa_w_in = dt("nsa_w_in", [2, D, NSA_IN], F32, kind="ExternalInput").ap()
        self.nsa_q_gain = dt("nsa_q_gain", [2, HD], F32, kind="ExternalInput").ap()
        self.nsa_k_gain = dt("nsa_k_gain", [2, 3, HD], F32, kind="ExternalInput").ap()
        self.nsa_cmp_pe = dt("nsa_cmp_pe", [2, 2, 32, HD], F32, kind="ExternalInput").ap()
        self.nsa_cmp_w1 = dt("nsa_cmp_w1", [2, 2, 2048, 256], F32, kind="ExternalInput").ap()
        self.nsa_cmp_w2 = dt("nsa_cmp_w2", [2, 2, 256, HD], F32, kind="ExternalInput").ap()
        self.nsa_w_out = dt("nsa_w_out", [2, D, D], F32, kind="ExternalInput").ap()
        self.ple_w_proj = dt("ple_w_proj", [4, PLE, D], F32, kind="ExternalInput").ap()
        self.ple_gate_gain = dt("ple_gate_gain", [4, D], F32, kind="ExternalInput").ap()
        self.ple_w_gate = dt("ple_w_gate", [4, D, D], F32, kind="ExternalInput").ap()
        self.rope_cs = dt("rope_cs", [2, 128, NT, 32], F32, kind="ExternalInput").ap()
        self.rope_cs_c = dt("rope_cs_c", [2, 128, self.NCT, 32], F32, kind="ExternalInput").ap()
        self.y = dt("y", [T, D], F32, kind="ExternalOutput").ap()
        self.qT_d = dt("qT_d", [D, T], BF16).ap()
        self.kT_d = dt("kT_d", [D, T], BF16).ap()
        self.v_d = dt("v_d", [T, D], BF16).ap()
        self.zs_d = dt("zs_d", [T, D], BF16).ap()

        with ExitStack() as st:
            self.st = st
            self.ARENA_BYTES = 100 * 1024
            self.arena = self.persist("arena", [128, self.ARENA_BYTES // 2], BF16)
            self.big = self.persist("big", [128, 32768], BF16)
            self.ident = self.persist("ident", [128, 128], BF16)
            self.tri = self.persist("tri", [128, 128], BF16)
            self.atri = self.persist("atri", [128, 128], BF16)
            self.cmpm = self.persist("cmpm", [128, 17, 128], BF16)
            self.cs = self.persist("cs", [128, 2, NT, 32], F32)
            self.csc = self.persist("csc", [128, 2, self.NCT, 32], F32)
            self.g_sb = self.persist("g_sb", [128, NT, 48], F32)
            self.msel = self.persist("msel", [128, NT, 64], F32)
            self.kcTa = self.persist("kcTa", [128, 4, self.NC], BF16)
            self.vca = self.persist("vca", [128, 4, self.NCT, 128], BF16)
            self.gq = self.persist("gq", [128, HD], F32)
            self.gk = self.persist("gk", [128, 3, HD], F32)
            self.gcol = self.persist("gcol", [128, 8], F32)
            self.gcol2 = self.persist("gcol2", [128, 8], F32)
            self.pf = [st.enter_context(nc.psum_tensor(f"pf{i}", [128, 512], F32)) for i in range(6)]
            self.pb = [st.enter_context(nc.psum_tensor(f"pb{i}", [128, 1024], BF16)) for i in range(2)]
            self.consts()
            first = True
            for li in self.layers:
                src = self.x_in if first else self.y
                first = False
                if li % 2 == 0:
                    self.phase_P(li, src, moba=True)
                    self.phase_A_moba(li)
                else:
                    import os
                    stop = os.environ.get("NSA_STOP", "")
                    self.stop = stop
                    self.phase_P(li, src, moba=False)
                    if stop == "P":
                        break
                    self.phase_C(li)
                    if stop == "C":
                        break
                    self.phase_A_nsa(li)
                    if stop in ("cmp", "sel", "A"):
                        break
                self.phase_O(li, src)
            self.S.emit(nc, st)
        return nc

    def consts(self):
        self.phase()
        NT = self.NT
        tf, tft = self.ph("c_tf", [128, 128], F32)
        self.E("pool", "memset", tf, 1.0, w=[tft])
        self.E("pool", "affine_select", out=tf, in_=tf, pattern=[[1, 128]], compare_op=ALU.is_equal, fill=0.0,
               base=0, channel_multiplier=-1, r=[tft], w=[tft])
        self.E("dve", "tensor_copy", out=self.ident[:], in_=tf, r=[tft], w=["ident"])
        tf2, tf2t = self.ph("c_tf2", [128, 128], F32)
        self.E("pool", "memset", tf2, 0.0, w=[tf2t])
        self.E("pool", "affine_select", out=tf2, in_=tf2, pattern=[[1, 128]], compare_op=ALU.is_ge, fill=-NEGB,
               base=0, channel_multiplier=-1, r=[tf2t], w=[tf2t])
        self.E("dve", "tensor_copy", out=self.tri[:], in_=tf2, r=[tf2t], w=["tri"])
        tf3, tf3t = self.ph("c_tf3", [128, 128], F32)
        self.E("pool", "memset", tf3, 0.0, w=[tf3t])
        self.E("pool", "affine_select", out=tf3, in_=tf3, pattern=[[-1, 128]], compare_op=ALU.is_ge, fill=-NEGB,
               base=-1, channel_multiplier=1, r=[tf3t], w=[tf3t])
        self.E("dve", "tensor_copy", out=self.atri[:], in_=tf3, r=[tf3t], w=["atri"])
        tf4, tf4t = self.ph("c_tf4", [128, 17, 128], F32)
        self.E("pool", "memset", tf4, 0.0, w=[tf4t])
        self.E("pool", "affine_select", out=tf4, in_=tf4, pattern=[[128, 17], [1, 128]], compare_op=ALU.is_ge,
               fill=-NEGB, base=-31, channel_multiplier=-16, r=[tf4t], w=[tf4t])
        self.E("dve", "tensor_copy", out=self.cmpm[:], in_=tf4, r=[tf4t], w=["cmpm"])
        self.dma(self.cs[:, 0], self.rope_cs[0], "c_cs", w=["cs"])
        self.dma(self.cs[:, 1], self.rope_cs[1], "c_cs2", pw=["cs"])
        self.dma(self.csc[:, 0], self.rope_cs_c[0], "c_csc", w=["csc"])
        self.dma(self.csc[:, 1], self.rope_cs_c[1], "c_csc2", pw=["csc"])
        m = self.msel
        self.E("pool", "memset", m[:], 0.0, w=["msel"])
        for half in range(2):
            mh = m[half * 64:(half + 1) * 64]
            self.E("pool", "affine_select", out=mh, in_=mh, pattern=[[2, NT], [-1, 64]], compare_op=ALU.is_ge,
                   fill=1.0e4, base=half - 2, channel_multiplier=0, r=["msel"], w=["msel"])
            self.E("pool", "affine_select", out=mh, in_=mh, pattern=[[2, NT], [-1, 64]], compare_op=ALU.is_ge,
                   fill=-1.0e4, base=half, channel_multiplier=0, r=["msel"], w=["msel"])
        self.E("pool", "memset", m[:, :, 0:1], 1.0e4, r=["msel"], w=["msel"])
        ov, ovt = self.ph("c_ov", [128, self.NCT, 64], F32)
        self.E("pool", "memset", ov, 1.0, w=[ovt])
        for ct in range(self.NCT):
            o1 = ov[:, ct, :]
            self.E("pool", "affine_select", out=o1, in_=o1, pattern=[[4, 64]], compare_op=ALU.is_ge, fill=0.0,
                   base=3 - 128 * ct, channel_multiplier=-1, r=[ovt], w=[ovt])
            self.E("pool", "affine_select", out=o1, in_=o1, pattern=[[-4, 64]], compare_op=ALU.is_ge, fill=0.0,
                   base=1 + 128 * ct, channel_multiplier=1, r=[ovt], w=[ovt])
        self.E("dve", "memset", self.vca[:], 1.0, w=["vca"])
        for g in range(4):
            for ct in range(self.NCT):
                self.E("dve", "tensor_copy", out=self.vca[:, g, ct, 65:128], in_=ov[:, ct, 0:63], r=[ovt, "vca"], w=["vca"])
        self.E("dve", "memset", self.kcTa[:], 0.0, w=["kcTa"])

    def load_gcol(self, dst, tok, src_row):
        self.dma(dst[:], src_row.rearrange("(k p) -> p k", p=128), "gcol_" + tok, w=[tok],
                 allow_slow_non_contiguous=True)

    def load_w(self, dst3, dtok, w_ap, ncols, gcol, gtok, nk=8):
        if getattr(self, "_stg_pid", None) != self.pid:
            self._stg = [self.ph(f"wstg{i}", [128, 1024], F32) for i in range(2)]
            self._stg_pid = self.pid
        stg = self._stg
        i = 0
        first = True
        for k in range(nk):
            for c0 in range(0, ncols, 1024):
                cw = min(1024, ncols - c0)
                s_ap, s_tok = stg[i % 2]
                self.dma(s_ap[:, 0:cw], w_ap[k * 128:(k + 1) * 128, c0:c0 + cw], f"wst{i % 2}", w=[s_tok])
                kw = dict(r=[s_tok] + ([gtok] if gcol is not None else []))
                if first:
                    kw["w"] = [dtok]
                    first = False
                else:
                    kw["pw"] = [dtok]
                if gcol is not None:
                    self.E("dve", "tensor_scalar", out=dst3[:, k, c0:c0 + cw], in0=s_ap[:, 0:cw], scalar1=gcol[:, k:k + 1],
                           scalar2=None, op0=ALU.mult, **kw)
                else:
                    self.E("dve", "tensor_copy", out=dst3[:, k, c0:c0 + cw], in_=s_ap[:, 0:cw], **kw)
                i += 1

    def rstd_of(self, x_ap, xtok, n, scratch, stok, out_col, otok):
        self.act(scratch, x_ap, AF.Square, r=[xtok], w=[stok, otok], accum_out=out_col)
        self.act(out_col, out_col, AF.Sqrt, r=[otok], w=[otok], scale=1.0 / n, bias=EPS)
        self.E("dve", "reciprocal", out=out_col, in_=out_col, r=[otok], w=[otok])

    def transpose8(self, src_bf, stok, dstT, dtok, pbi, nblk=8):
        pbt = self.pb[pbi]
        ptok = ("pb", pbi)
        for k in range(nblk):
            kw = dict(w=[ptok]) if k == 0 else dict(w=())
            if k == 0:
                self.tr(pbt[:, k * 128:(k + 1) * 128], src_bf[:, k * 128:(k + 1) * 128], r=[stok], w=[ptok])
            else:
                self.S.op("pe", (lambda e, k=k: e.transpose(out=pbt[:, k * 128:(k + 1) * 128], in_=src_bf[:, k * 128:(k + 1) * 128],
                                                            identity=self.ident[:])), reads=[stok, "ident"], pwrites=[ptok])
        self.act(dstT.rearrange("p a b -> p (a b)") if len(dstT.shape) == 3 else dstT, pbt[:, 0:nblk * 128], AF.Copy, r=[ptok], w=[dtok])

    def norm_rope(self, src, stok, nh, gain_ap, gtok, cos_ap, sin_ap, cstok, out_bf, otok, tmp):
        (sq, sqt), (ss, sst), (A, At), (Bt_, Btt), (t13, t13t), (tsw, tswt) = tmp
        W = nh * 64
        v3 = lambda ap: ap[:, 0:W].rearrange("p (h d) -> p h d", d=64)
        v4 = lambda ap: ap[:, 0:W].rearrange("p (h a d) -> p h a d", a=2, d=32)
        self.act(sq[:, 0:W], src, AF.Square, r=[stok], w=[sqt])
        self.E("dve", "tensor_reduce", out=ss[:, 0:nh], in_=v3(sq), axis=AX.X, op=ALU.add, r=[sqt], w=[sst])
        self.act(ss[:, 0:nh], ss[:, 0:nh], AF.Sqrt, r=[sst], w=[sst], scale=1.0 / 64, bias=EPS)
        self.E("dve", "reciprocal", out=ss[:, 0:nh], in_=ss[:, 0:nh], r=[sst], w=[sst])
        self.E("dve", "tensor_tensor", out=v3(A), in0=src.rearrange("p (h d) -> p h d", d=64),
               in1=ss[:, 0:nh].unsqueeze(2).to_broadcast([128, nh, 64]), op=ALU.mult, r=[stok, sst], w=[At])
        self.E("dve", "tensor_tensor", out=v3(Bt_), in0=v3(A), in1=gain_ap.unsqueeze(1).to_broadcast([128, nh, 64]),
               op=ALU.mult, r=[At, gtok], w=[Btt])
        cosb = cos_ap.unsqueeze(1).unsqueeze(1).to_broadcast([128, nh, 2, 32])
        sinb = sin_ap.unsqueeze(1).to_broadcast([128, nh, 32])
        self.E("dve", "tensor_tensor", out=v4(t13), in0=v4(Bt_), in1=cosb, op=ALU.mult, r=[Btt, cstok], w=[t13t])
        self.E("dve", "tensor_tensor", out=v4(tsw)[:, :, 0, :], in0=v4(Bt_)[:, :, 1, :], in1=sinb, op=ALU.mult,
               r=[Btt, cstok], w=[tswt])
        self.E("dve", "tensor_tensor", out=v4(tsw)[:, :, 1, :], in0=v4(Bt_)[:, :, 0, :], in1=sinb, op=ALU.mult,
               r=[Btt, cstok, tswt], w=[tswt])
        self.E("dve", "tensor_tensor", out=v4(out_bf)[:, :, 0, :], in0=v4(t13)[:, :, 0, :], in1=v4(tsw)[:, :, 0, :],
               op=ALU.subtract, r=[t13t, tswt], w=[otok])
        self.E("dve", "tensor_tensor", out=v4(out_bf)[:, :, 1, :], in0=v4(t13)[:, :, 1, :], in1=v4(tsw)[:, :, 1, :],
               op=ALU.add, r=[t13t, tswt, otok], w=[otok])

    def phase_P(self, li, src, moba):
        self.phase()
        NT = self.NT
        j = li // 2
        ncols = 4 * D if moba else NSA_IN
        w_ap = (self.moba_w_in if moba else self.nsa_w_in)[j]
        W3 = self.big[:, 0:8 * 4096].rearrange("p (k n) -> p k n", n=4096)
        self.load_gcol(self.gcol, "gcol", self.norm_gain[li])
        self.load_w(W3, "big", w_ap, ncols, self.gcol, "gcol")
        qg = (self.moba_q_gain if moba else self.nsa_q_gain)[j]
        self.dma(self.gq[:], qg.partition_broadcast(128), "gq", w=["gq"])
        if moba:
            self.dma(self.gk[:, 0, :], self.moba_k_gain[j].partition_broadcast(128), "gk", w=["gk"])
        else:
            for b_ in range(3):
                self.dma(self.gk[:, b_, :], self.nsa_k_gain[j, b_].partition_broadcast(128), f"gk{b_}",
                         **(dict(w=["gk"]) if b_ == 0 else dict(pw=["gk"])))
        xt = [self.ph(f"xt{i}", [128, D], F32) for i in range(2)]
        sqs = self.ph("sqs", [128, D], F32)
        rs = [self.ph(f"rs{i}", [128, 1], F32) for i in range(2)]
        hb = [self.ph(f"hb{i}", [128, D], BF16) for i in range(2)]
        hT = [self.ph(f"hT{i}", [128, 8, 128], BF16) for i in range(2)]
        b1, b1t = self.ph("nr_b1", [128, 2048], F32)
        b2, b2t = self.ph("nr_b2", [128, 2048], F32)
        b3, b3t = self.ph("nr_b3", [128, 2048], F32)
        ss, sst = self.ph("nr_ss", [128, 32], F32)
        qbf = [self.ph(f"qbf{i}", [128, 2048], BF16) for i in range(2)]
        qTt = [self.ph(f"qTt{i}", [128, 16, 128], BF16) for i in range(2)]
        vbf = [self.ph(f"vbf{i}", [128, D], BF16) for i in range(2)]
        zbf = [self.ph(f"zbf{i}", [128, D], BF16) for i in range(2)]
        rawbf = [self.ph(f"rawbf{i}", [128, 512], BF16) for i in range(2)]
        if moba:
            chunks = [(c * 512, 512) for c in range(8)]
            HT_ = 32
            srcs = [(0, 0, 8, self.gq[:], "gq"), (1, 0, 8, self.gq[:], "gq"), (2, 0, 8, self.gk[:, 0, :], "gk"),
                    (3, 0, 8, self.gk[:, 0, :], "gk")]
            qk_chunks = [(0, 0), (1, 1), (2, 2), (3, 3)]
            dests = [(self.qT_d, jb * 128) for jb in range(8)] + [(self.kT_d, jb * 128) for jb in range(8)]
        else:
            chunks = [(0, 512), (512, 512), (1024, 512), (1536, 512), (2048, 512), (2560, 48), (2608, 512), (3120, 512)]
            HT_ = 24
            srcs = [(0, 0, 8, self.gq[:], "gq"), (1, 0, 8, self.gq[:], "gq"), (2, 0, 4, self.gk[:, 1, :], "gk"),
                    (3, 0, 4, self.gk[:, 2, :], "gk")]
            qk_chunks = [(0, 0), (1, 1), (3, 2), (4, 3), (2, 4)]
            dests = ([(self.qT_d, jb * 128) for jb in range(8)] + [(self.kT_d, 0), (self.kT_d, 128), (self.kT_d, 256),
                     (self.kT_d, 384)] + [(self.kT_d, 512 + jb * 128) for jb in range(4)])
        WQ = HT_ * 64

        def pre_a(t):
            x_ap, xtok = xt[t % 2]
            self.dma(x_ap, src[t * 128:(t + 1) * 128, :], f"xld{t % 2}", r=[("y", t)], w=[xtok])
            r_ap, rtok = rs[t % 2]
            self.rstd_of(x_ap, xtok, D, sqs[0], sqs[1], r_ap, rtok)
            h_ap, htok = hb[t % 2]
            self.E("dve", "tensor_scalar", out=h_ap, in0=x_ap, scalar1=r_ap[:, 0:1], scalar2=None, op0=ALU.mult,
                   r=[xtok, rtok], w=[htok])

        def pre_b(t):
            h_ap, htok = hb[t % 2]
            hT_ap, hTtok = hT[t % 2]
            self.transpose8(h_ap, htok, hT_ap, hTtok, 0)

        def mm_chunk(t, ci, bank):
            c0, cw = chunks[ci]
            hT_ap, hTtok = hT[t % 2]
            ps = self.pf[bank]
            pstok = ("pf", bank)
            for k in range(8):
                self.mm(ps[:, 0:cw], hT_ap[:, k, :], W3[:, k, c0:c0 + cw], k == 0, k == 7, r=[hTtok, "big"],
                        w=[pstok] if k == 0 else ())
            last = self.S.q["pe"][-1]
            self.S.tw[pstok] = {last.stream: last}

        def chain(t):
            q_ap, qbt = qbf[t % 2]
            v3 = lambda ap, c0, w: ap[:, c0:c0 + w].rearrange("p (h d) -> p h d", d=64)
            v4 = lambda ap, c0, w: ap[:, c0:c0 + w].rearrange("p (h a d) -> p h a d", a=2, d=32)
            c = 0
            for n_, (bank, col0, nh, g_ap, g_tok) in enumerate(srcs):
                w_ = nh * 64
                self.act(b1[:, c:c + w_], self.pf[bank][:, col0:col0 + w_], AF.Square, r=[("pf", bank)],
                         **(dict(w=[b1t]) if n_ == 0 else dict(pw=[b1t])))
                c += w_
            self.E("dve", "tensor_reduce", out=ss[:, 0:HT_], in_=v3(b1, 0, WQ), axis=AX.X, op=ALU.add, r=[b1t], w=[sst])
            self.act(ss[:, 0:HT_], ss[:, 0:HT_], AF.Sqrt, r=[sst], w=[sst], scale=1.0 / 64, bias=EPS)
            self.E("dve", "reciprocal", out=ss[:, 0:HT_], in_=ss[:, 0:HT_], r=[sst], w=[sst])
            c = 0
            h0 = 0
            for n_, (bank, col0, nh, g_ap, g_tok) in enumerate(srcs):
                w_ = nh * 64
                self.E("dve", "tensor_tensor", out=v3(b2, c, w_),
                       in0=self.pf[bank][:, col0:col0 + w_].rearrange("p (h d) -> p h d", d=64),
                       in1=ss[:, h0:h0 + nh].unsqueeze(2).to_broadcast([128, nh, 64]), op=ALU.mult,
                       r=[("pf", bank), sst], **(dict(w=[b2t]) if n_ == 0 else dict(pw=[b2t])))
                c += w_
                h0 += nh
            c = 0
            first = True
            i_ = 0
            while i_ < len(srcs):
                g_ap, g_tok = srcs[i_][3], srcs[i_][4]
                nh = srcs[i_][2]
                k_ = i_ + 1
                while k_ < len(srcs) and srcs[k_][3] is g_ap:
                    nh += srcs[k_][2]
                    k_ += 1
                w_ = nh * 64
                self.E("pool", "tensor_tensor", out=v3(b1, c, w_), in0=v3(b2, c, w_),
                       in1=g_ap.unsqueeze(1).to_broadcast([128, nh, 64]), op=ALU.mult, r=[b2t, g_tok],
                       **(dict(w=[b1t]) if first else dict(pw=[b1t])))
                first = False
                c += w_
                i_ = k_
            import os
            if "norope" in os.environ.get("PDBG", ""):
                return
            cos_ap = self.cs[:, 0, t, :]
            sin_ap = self.cs[:, 1, t, :]
            cosb = cos_ap.unsqueeze(1).unsqueeze(1).to_broadcast([128, HT_, 2, 32])
            sinb = sin_ap.unsqueeze(1).to_broadcast([128, HT_, 32])
            self.E("dve", "tensor_tensor", out=v4(b2, 0, WQ), in0=v4(b1, 0, WQ), in1=cosb, op=ALU.mult, r=[b1t, "cs"], w=[b2t])
            self.E("pool", "tensor_tensor", out=v4(b3, 0, WQ)[:, :, 0, :], in0=v4(b1, 0, WQ)[:, :, 1, :], in1=sinb, op=ALU.mult,
                   r=[b1t, "cs"], w=[b3t])
            self.E("pool", "tensor_tensor", out=v4(b3, 0, WQ)[:, :, 1, :], in0=v4(b1, 0, WQ)[:, :, 0, :], in1=sinb, op=ALU.mult,
                   r=[b1t, "cs"], pw=[b3t])
            self.E("dve", "tensor_tensor", out=v4(q_ap, 0, WQ)[:, :, 0, :], in0=v4(b2, 0, WQ)[:, :, 0, :],
                   in1=v4(b3, 0, WQ)[:, :, 0, :], op=ALU.subtract, r=[b2t, b3t], w=[qbt])
            self.E("dve", "tensor_tensor", out=v4(q_ap, 0, WQ)[:, :, 1, :], in0=v4(b2, 0, WQ)[:, :, 1, :],
                   in1=v4(b3, 0, WQ)[:, :, 1, :], op=ALU.add, r=[b2t, b3t], pw=[qbt])
            import os
            if not moba and "noraw" not in os.environ.get("PDBG", ""):
                self.E("dve", "tensor_copy", out=rawbf[t % 2][0], in_=self.pf[4][:, 0:512], r=[("pf", 4)], w=[rawbf[t % 2][1]])

        def tq(t):
            q_ap, qbt = qbf[t % 2]
            qt_, qtt = qTt[t % 2]
            for half in range(2):
                pbt = self.pb[1]
                ptok = ("pb", 1)
                for k in range(8):
                    jb = half * 8 + k
                    if (not moba) and jb >= 12:
                        s_ap, s_tok = rawbf[t % 2]
                        s_in = s_ap[:, (jb - 12) * 128:(jb - 11) * 128]
                    else:
                        s_tok = qbt
                        s_in = q_ap[:, jb * 128:(jb + 1) * 128]
                    if k == 0:
                        self.tr(pbt[:, 0:128], s_in, r=[s_tok], w=[ptok])
                    else:
                        self.S.op("pe", (lambda e, k=k, s_in=s_in: e.transpose(out=pbt[:, k * 128:(k + 1) * 128], in_=s_in,
                                                                              identity=self.ident[:])),
                                  reads=[s_tok, "ident"], pwrites=[ptok])
                self.act(qt_[:, half * 8:(half + 1) * 8, :].rearrange("p a b -> p (a b)"), pbt[:, 0:1024], AF.Copy, r=[ptok],
                         **(dict(w=[qtt]) if half == 0 else dict(pw=[qtt])))
            jb = 0
            while jb < 16:
                dram, row0 = dests[jb]
                k_ = jb + 1
                while k_ < 16 and dests[k_][0] is dram and dests[k_][1] == row0 + (k_ - jb) * 128:
                    k_ += 1
                nb = k_ - jb
                self.dma(dram[row0:row0 + nb * 128, t * 128:(t + 1) * 128].rearrange("(j p) t -> p j t", p=128),
                         qt_[:, jb:k_, :], f"qst{t % 2}", r=[qtt], pw=[("kTq_d",)], q="pool")
                jb = k_

        def mm_vz(t):
            v_ap, vtok = vbf[t % 2]
            z_ap, ztok = zbf[t % 2]
            if moba:
                for ci in (4, 5):
                    bank = ci
                    mm_chunk(t, ci, bank)
                    self.act(v_ap[:, (ci - 4) * 512:(ci - 3) * 512], self.pf[bank][:, 0:512], AF.Copy, r=[("pf", bank)],
                             **(dict(w=[vtok]) if ci == 4 else dict(pw=[vtok])))
                self.dma(self.v_d[t * 128:(t + 1) * 128, :], v_ap, f"vst{t % 2}", r=[vtok], pw=[("v_d",)], q="pool")
            else:
                self.act(v_ap[:, 0:256], self.pf[2][:, 256:512], AF.Copy, r=[("pf", 2)], w=[vtok])
                self.act(v_ap[:, 256:512], self.pf[3][:, 256:512], AF.Copy, r=[("pf", 3)], pw=[vtok])
                self.dma(self.v_d[t * 128:(t + 1) * 128, 0:512], v_ap[:, 0:512], f"vst{t % 2}", r=[vtok], pw=[("v_d",)], q="pool")
                mm_chunk(t, 5, 5)
                self.act(self.g_sb[:, t, :], self.pf[5][:, 0:48], AF.Sigmoid, r=[("pf", 5)], pw=["g_sb"])
            for ci in (6, 7):
                bank = ci - 2
                mm_chunk(t, ci, bank)
                self.act(z_ap[:, (ci - 6) * 512:(ci - 5) * 512], self.pf[bank][:, 0:512], AF.Silu, r=[("pf", bank)],
                         **(dict(w=[ztok]) if ci == 6 else dict(pw=[ztok])))
            self.dma(self.zs_d[t * 128:(t + 1) * 128, :], z_ap, f"zst{t % 2}", r=[ztok], pw=[("zs_d",)], q="pool")

        import os
        dbg = os.environ.get("PDBG", "")
        pre_a(0)
        pre_b(0)
        for t in range(NT):
            for (ci, bank) in qk_chunks:
                mm_chunk(t, ci, bank)
            if t + 1 < NT:
                pre_a(t + 1)
            if "nochain" not in dbg:
                chain(t)
            if t + 1 < NT:
                pre_b(t + 1)
            if "novz" not in dbg:
                mm_vz(t)
            if t >= 1 and "notq" not in dbg:
                tq(t - 1)
        if "notq" not in dbg:
            tq(NT - 1)

    def attention(self, qTa, qtok, kTa, ktok, Va, vtok, vw, pairs_fn, fin, pts, tag, side=None, side_delay=0):
        NT = self.NT
        qtoks = list(qtok) if isinstance(qtok, list) else [qtok]
        units = []
        for qt in range(NT):
            pairs = pairs_fn(qt)
            ng = (len(pairs) + 3) // 4
            for gi in range(ng):
                units.append((qt, pairs[gi * 4:(gi + 1) * 4], gi == 0, gi == ng - 1))
        n = len(units)
        SK = 2
        u0 = getattr(self, "_u", 0)

        def qk(i):
            qt, grp, _, _ = units[i]
            si = (u0 + i) % 3
            ps = self.pf[si]
            pstok = ("pf", si)
            firstw = True
            for jx, (kt, bias) in enumerate(grp):
                o_ap = ps[:, jx * 128:(jx + 1) * 128]
                self.mm(o_ap, kTa[:, kt * 128:(kt + 1) * 128], qTa[:, qt * 128:(qt + 1) * 128], True, bias is None,
                        r=[ktok] + qtoks, w=[pstok] if firstw else ())
                firstw = False
                if bias is not None:
                    self.mm(o_ap, self.ident[:], bias, False, True, r=["ident", "tri", "atri", "cmpm"])
            last = self.S.q["pe"][-1]
            self.S.tw[pstok] = {last.stream: last}

        def pv(i):
            qt, grp, isf, isl = units[i]
            si = (u0 + i) % 3
            ps = self.pf[si]
            pstok = ("pf", si)
            pt_ap, pttok = pts[si % len(pts)]
            po = self.pf[4 + qt % 2]
            potok = ("pf", 4 + qt % 2)
            nn = len(grp) * 128
            self.act(pt_ap[:, 0:nn], ps[:, 0:nn], AF.Exp, r=[pstok], w=[pttok], scale=0.125)
            for jx, (kt, bias) in enumerate(grp):
                is_first = isf and jx == 0
                is_last = isl and jx == len(grp) - 1
                self.mm(po[:, 0:vw], pt_ap[:, jx * 128:(jx + 1) * 128], Va[:, kt, 0:vw], is_first, is_last,
                        r=[pttok, vtok], w=[potok] if is_first else ())
            last = self.S.q["pe"][-1]
            self.S.tw[potok] = {last.stream: last}
            if isl:
                fin(qt, po, potok)

        for i in range(n + SK):
            if i < n:
                qk(i)
            if i >= SK:
                pv(i - SK)
                if side is not None and i - SK >= side_delay:
                    next(side, None)
        self._u = u0 + n

    def phase_A_moba(self, li):
        self.phase()
        T, NT = self.T, self.NT
        NB = T // 256
        o_all = self.big[:, 0:NT * D].rearrange("p (t f) -> p t f", f=D)
        qTa = [self.ph(f"qTa{i}", [128, T], BF16) for i in range(2)]
        kTa = [self.ph(f"kTa{i}", [128, T], BF16) for i in range(2)]
        Va = [self.ph(f"Va{i}", [128, NT, 65], BF16) for i in range(2)]
        pts = [self.ph(f"pt{i}", [128, 512], BF16) for i in range(3)]
        kmf, kmft = self.ph("kmf", [128, 16], F32)
        kmT, kmTt = self.ph("kmT", [128, 16], BF16)
        gt_, gtt = self.ph("gate", [128, 16], F32)
        m8, m8t = self.ph("m8", [128, 8], F32)
        bq, bqt = self.ph("bq", [128, 128], BF16)
        rz, rzt = self.ph("rz", [128, 1], F32)
        self.E("dve", "memset", kmT, 0.0, w=[kmTt])
        self.E("dve", "memset", bq, 0.0, w=[bqt])
        for i in range(2):
            q_ap, qtok = qTa[i]
            k_ap, ktok = kTa[i]
            v_ap, vtok = Va[i]
            self.E("pool", "memset", q_ap, 0.0, w=[qtok])
            self.E("pool", "memset", k_ap[0:64, :], 0.0, w=[ktok])
            ke = k_ap[64:128, :]
            self.E("pool", "memset", ke, NEGB, r=[ktok], w=[ktok])
            self.E("pool", "affine_select", out=ke, in_=ke, pattern=[[1, T]], compare_op=ALU.is_ge, fill=0.0, base=0,
                   channel_multiplier=-256, r=[ktok], w=[ktok])
            self.E("pool", "affine_select", out=ke, in_=ke, pattern=[[-1, T]], compare_op=ALU.is_ge, fill=0.0, base=255,
                   channel_multiplier=256, r=[ktok], w=[ktok])
            self.E("dve", "memset", v_ap, 1.0, w=[vtok])

        def load_head(h):
            i = h % 2
            q_ap, qtok = qTa[i]
            k_ap, ktok = kTa[i]
            v_ap, vtok = Va[i]
            self.dma(q_ap[0:64, :], self.qT_d[h * 64:(h + 1) * 64, :], f"aq{i}", r=[("kTq_d",)], w=[qtok])
            self.dma(k_ap[0:64, :], self.kT_d[h * 64:(h + 1) * 64, :], f"ak{i}", r=[("kTq_d",)], w=[ktok])
            for c in range(0, NT, 8):
                n = min(8, NT - c)
                self.dma(v_ap[:, c:c + n, 0:64],
                         self.v_d[c * 128:(c + n) * 128, h * 64:(h + 1) * 64].rearrange("(t p) d -> p t d", p=128),
                         f"av{i}", r=[("v_d",)], **(dict(w=[vtok]) if c == 0 else dict(pw=[vtok])))

        def gate_steps(h):
            i = h % 2
            q_ap, qtok = qTa[i]
            k_ap, ktok = kTa[i]
            self.E("dve", "tensor_reduce", out=kmf[0:64, 0:NB], in_=k_ap[0:64, :].rearrange("p (n k) -> p n k", k=256),
                   axis=AX.X, op=ALU.add, r=[ktok], w=[kmft])
            self.E("dve", "tensor_scalar", out=kmT[0:64, 0:NB], in0=kmf[0:64, 0:NB], scalar1=1.0 / 256, scalar2=None,
                   op0=ALU.mult, r=[kmft], w=[kmTt])
            self.E("dve", "memset", gt_, -1.0e30, w=[gtt])
            yield
            for qt in range(NT):
                b = qt // 2
                if b <= 3:
                    continue
                pg = self.pf[3]
                self.mm(pg[:, 0:16], q_ap[:, qt * 128:(qt + 1) * 128], kmT, True, True, r=[qtok, kmTt], w=[("pf", 3)])
                self.E("dve", "tensor_copy", out=gt_[:, 0:b], in_=pg[:, 0:b], r=[("pf", 3), gtt], w=[gtt])
                self.E("dve", "max", out=m8, in_=gt_, r=[gtt], w=[m8t])
                self.E("dve", "tensor_scalar", out=bq[:, 64:80], in0=gt_, scalar1=m8[:, 2:3], scalar2=1.0, op0=ALU.is_ge,
                       op1=ALU.subtract, r=[gtt, m8t, bqt], w=[bqt])
                self.E("dve", "memset", bq[:, 64 + b:65 + b], 0.0, r=[bqt], w=[bqt])
                yield
                yield
                pbt = self.pb[1]
                self.tr(pbt[:, 0:128], bq, r=[bqt], w=[("pb", 1)])
                self.E("dve", "tensor_copy", out=q_ap[64:128, qt * 128:(qt + 1) * 128], in_=pbt[64:128, 0:128],
                       r=[("pb", 1), qtok], w=[qtok])
                yield

        load_head(0)
        for _ in gate_steps(0):
            pass
        for h in range(H):
            side = None
            if h + 1 < H:
                load_head(h + 1)
                side = gate_steps(h + 1)
            i = h % 2
            q_ap, qtok = qTa[i]
            k_ap, ktok = kTa[i]
            v_ap, vtok = Va[i]

            def pairs_fn(qt):
                return [(kt, (self.tri[:] if kt == qt else None)) for kt in range(qt + 1)]

            def fin(qt, po, potok, h=h):
                self.E("dve", "reciprocal", out=rz, in_=po[:, 64:65], r=[potok], w=[rzt])
                self.E("dve", "tensor_scalar", out=o_all[:, qt, h * 64:(h + 1) * 64], in0=po[:, 0:64], scalar1=rz[:, 0:1],
                       scalar2=None, op0=ALU.mult, r=[potok, rzt], pw=["big"])

            self.attention(q_ap, qtok, k_ap, ktok, v_ap, vtok, 65, pairs_fn, fin, pts, f"m{h}", side=side, side_delay=min(24, self.NT))
            if side is not None:
                for _ in side:
                    pass

    def phase_C(self, li):
        self.phase()
        T, NT = self.T, self.NT
        j = li // 2
        xkp, xkpt = self.ph("xkp", [128, T + 16], BF16)
        xk16, xk16t = self.ph("xk16", [128, 2, 16, T // 16], BF16)
        W1d, W1t = self.ph("W1d", [128, 32, 256], BF16)
        w1s = [self.ph(f"w1s{i}", [128, 8, 256], F32) for i in range(2)]
        W2b, W2t = self.ph("W2b", [128, 2, 64], BF16)
        w2s, w2st = self.ph("w2s", [128, 2, 64], F32)
        peT, peTt = self.ph("peT", [128, 32], BF16)
        pes, pest = self.ph("pes", [128, 32], F32)
        bh, bht = self.ph("bh", [128, 2], F32)
        NC, NCT = self.NC, self.NCT
        xx, xxt = self.ph("xx", [128, NC], F32)
        x2, x2t = self.ph("x2", [128, NC], F32)
        sg, sgt = self.ph("sg", [128, NC], F32)
        gl, glt = self.ph("gl", [128, 2, NC], BF16)
        kcb, kcbt = self.ph("kcb", [128, 128], BF16)
        tmp = [self.ph(n, [128, 64], F32) for n in ("c_sq", "c_ss", "c_A", "c_B", "c_t13", "c_tsw")]
        self.E("pool", "memset", xkp, 0.0, w=[xkpt])
        self.E("pool", "memset", W1d, 0.0, w=[W1t])
        self.E("pool", "memset", peT, 0.0, w=[peTt])
        self.E("pool", "memset", kcb, 0.0, w=[kcbt])
        for kv in range(2):
            w1 = self.nsa_cmp_w1[j, kv].rearrange("(l d) j -> d l j", d=64)
            for c in range(4):
                s_ap, s_tok = w1s[c % 2]
                self.dma(s_ap[0:64], w1[:, c * 8:(c + 1) * 8, :], f"w1s{c % 2}", w=[s_tok])
                self.E("dve", "tensor_copy", out=W1d[0:64, c * 8:(c + 1) * 8, :], in_=s_ap[0:64], r=[s_tok],
                       **(dict(w=[W1t]) if c == 0 else dict(pw=[W1t])))
            self.dma(w2s, self.nsa_cmp_w2[j, kv].rearrange("(c p) d -> p c d", p=128), "w2s", w=[w2st])
            self.E("dve", "tensor_copy", out=W2b, in_=w2s, r=[w2st], w=[W2t])
            for q4 in range(4):
                self.dma(pes[0:64, q4 * 8:(q4 + 1) * 8], self.nsa_cmp_pe[j, kv, q4 * 8:(q4 + 1) * 8, :].rearrange("l d -> d l"),
                         "pes", **(dict(w=[pest]) if q4 == 0 else dict(pw=[pest])), allow_slow_non_contiguous=True)
            self.E("dve", "tensor_copy", out=peT[0:64, :], in_=pes[0:64, :], r=[pest], w=[peTt])
            pbias = self.pf[3]
            for jc in range(2):
                for l in range(32):
                    self.mm(pbias[:, jc:jc + 1], W1d[:, l, jc * 128:(jc + 1) * 128], peT[:, l:l + 1], l == 0, l == 31,
                            r=[W1t, peTt], w=[("pf", 3)] if (l == 0 and jc == 0) else ())
            last = self.S.q["pe"][-1]
            self.S.tw[("pf", 3)] = {last.stream: last}
            self.E("dve", "tensor_copy", out=bh, in_=pbias[:, 0:2], r=[("pf", 3)], w=[bht])
            for g in range(4):
                row0 = 512 + kv * 256 + g * 64
                self.dma(xkp[0:64, 0:T], self.kT_d[row0:row0 + 64, :], "xkp", r=[("kTq_d",)], w=[xkpt])
                self.E("dve", "tensor_copy", out=xk16[:, 0], in_=xkp[:, 0:T].rearrange("p (m r) -> p r m", r=16), r=[xkpt], w=[xk16t])
                self.E("dve", "tensor_copy", out=xk16[:, 1], in_=xkp[:, 16:T + 16].rearrange("p (m r) -> p r m", r=16), r=[xkpt],
                       pw=[xk16t])
                for jc in range(2):
                    ph_ = self.pf[jc]
                    for l in range(32):
                        self.mm(ph_[:, 0:NC], W1d[:, l, jc * 128:(jc + 1) * 128], xk16[:, l // 16, l % 16, :], l == 0, l == 31,
                                r=[W1t, xk16t], w=[("pf", jc)] if l == 0 else ())
                    last = self.S.q["pe"][-1]
                    self.S.tw[("pf", jc)] = {last.stream: last}
                    self.act(xx, ph_[:, 0:NC], AF.Identity, r=[("pf", jc), bht], w=[xxt], bias=bh[:, jc:jc + 1])
                    self.E("dve", "tensor_tensor", out=x2, in0=xx, in1=xx, op=ALU.mult, r=[xxt], w=[x2t])
                    self.E("dve", "tensor_scalar", out=x2, in0=x2, scalar1=0.044715, scalar2=1.0, op0=ALU.mult, op1=ALU.add,
                           r=[x2t], w=[x2t])
                    self.E("dve", "tensor_tensor", out=x2, in0=x2, in1=xx, op=ALU.mult, r=[x2t, xxt], w=[x2t])
                    self.act(sg, x2, AF.Sigmoid, r=[x2t], w=[sgt], scale=1.5957691216057308)
                    self.E("dve", "tensor_tensor", out=gl[:, jc, :], in0=xx, in1=sg, op=ALU.mult, r=[xxt, sgt],
                           **(dict(w=[glt]) if jc == 0 else dict(pw=[glt])))
                for ct in range(NCT):
                    pk = self.pf[2]
                    for jc in range(2):
                        self.mm(pk[:, 0:64], gl[:, jc, ct * 128:(ct + 1) * 128], W2b[:, jc, :], jc == 0, jc == 1,
                                r=[glt, W2t], w=[("pf", 2)] if jc == 0 else ())
                    last = self.S.q["pe"][-1]
                    self.S.tw[("pf", 2)] = {last.stream: last}
                    if kv == 0:
                        self.norm_rope(pk[:, 0:64], ("pf", 2), 1, self.gk[:, 0, :], "gk", self.csc[:, 0, ct, :],
                                       self.csc[:, 1, ct, :], "csc", kcb, kcbt, tmp)
                        pbt = self.pb[1]
                        self.tr(pbt[:, 0:128], kcb, r=[kcbt], w=[("pb", 1)])
                        self.E("dve", "tensor_copy", out=self.kcTa[:, g, ct * 128:(ct + 1) * 128], in_=pbt[:, 0:128],
                               r=[("pb", 1)], pw=["kcTa"])
                    else:
                        self.act(self.vca[:, g, ct, 0:64], pk[:, 0:64], AF.Copy, r=[("pf", 2)], pw=["vca"])

    def phase_A_nsa(self, li):
        self.phase()
        T, NT = self.T, self.NT
        o_all = self.big[:, 0:NT * D].rearrange("p (t f) -> p t f", f=D)
        qTa = [self.ph(f"nqTa{i}", [128, T], BF16) for i in range(4)]
        ksTa, kst = self.ph("ksTa", [128, T], BF16)
        kwTa, kwt = self.ph("kwTa", [128, T], BF16)
        Vs, Vst = self.ph("Vs", [128, NT, 65], BF16)
        Vw, Vwt = self.ph("Vw", [128, NT, 65], BF16)
        pts = [self.ph(f"npt{i}", [128, 512], BF16) for i in range(3)]
        oacc, oacct = self.ph("oacc", [128, NT, 64], F32)
        imp, impt = self.ph("imp", [128, NT, 64], F32)
        sc, sct = self.ph("sc", [128, 64], F32)
        sc2, sc2t = self.ph("sc2", [128, 64], F32)
        m8a, m8at = self.ph("m8a", [128, 8], F32)
        m8b, m8bt = self.ph("m8b", [128, 8], F32)
        bs, bst = self.ph("bs", [128, 128], BF16)
        rz, rzt = self.ph("nrz", [128, 1], F32)
        cf, cft = self.ph("ncf", [128, 1], F32)
        for hl, (q_ap, qtok) in enumerate(qTa):
            self.E("pool", "memset", q_ap, 0.0, w=[qtok, ("qhi", hl)])
        self.E("pool", "memset", kwTa, 0.0, w=[kwt])
        self.E("pool", "memset", ksTa[0:64, :], 0.0, w=[kst])
        ke = ksTa[64:128, :]
        self.E("pool", "memset", ke, NEGB, r=[kst], w=[kst])
        self.E("pool", "affine_select", out=ke, in_=ke, pattern=[[1, T]], compare_op=ALU.is_ge, fill=0.0, base=0,
               channel_multiplier=-64, r=[kst], w=[kst])
        self.E("pool", "affine_select", out=ke, in_=ke, pattern=[[-1, T]], compare_op=ALU.is_ge, fill=0.0, base=63,
               channel_multiplier=64, r=[kst], w=[kst])
        self.E("dve", "memset", Vs, 1.0, w=[Vst])
        self.E("dve", "memset", Vw, 1.0, w=[Vwt])
        self.E("dve", "memset", bs, 0.0, w=[bst])
        self.E("dve", "memset", imp, 0.0, w=[impt])

        for g in range(4):
            for hl in range(4):
                h = g * 4 + hl
                q_ap, qtok = qTa[hl]
                self.dma(q_ap[0:64, :], self.qT_d[h * 64:(h + 1) * 64, :], f"nq{hl}", r=[("kTq_d",)], w=[qtok])
            self.dma(ksTa[0:64, :], self.kT_d[g * 64:(g + 1) * 64, :], "nks", r=[("kTq_d",)], w=[kst])
            self.dma(kwTa[0:64, :], self.kT_d[256 + g * 64:256 + (g + 1) * 64, :], "nkw", r=[("kTq_d",)], w=[kwt])
            for (V_, Vt_, c0, nm) in ((Vs, Vst, g * 64, "nvs"), (Vw, Vwt, 256 + g * 64, "nvw")):
                for c in range(0, NT, 8):
                    n = min(8, NT - c)
                    self.dma(V_[:, c:c + n, 0:64],
                             self.v_d[c * 128:(c + n) * 128, c0:c0 + 64].rearrange("(t p) d -> p t d", p=128),
                             nm, r=[("v_d",)], **(dict(w=[Vt_]) if c == 0 else dict(pw=[Vt_])))

            def cmp_pairs(qt):
                out = []
                for kt in range(self.NCT):
                    dl = qt - 16 * kt
                    if dl < 0:
                        continue
                    out.append((kt, self.cmpm[:, dl, :] if dl <= 16 else None))
                return out

            def sel_A(qt):
                self.E("dve", "tensor_tensor", out=sc, in0=imp[:, qt, :], in1=self.msel[:, qt, :], op=ALU.add,
                       r=[impt, "msel"], w=[sct])
                self.E("dve", "max", out=m8a, in_=sc, r=[sct], w=[m8at])
                self.E("dve", "tensor_scalar", out=sc2, in0=sc, scalar1=m8a[:, 7:8], scalar2=-6.0e4, op0=ALU.is_ge, op1=ALU.mult,
                       r=[sct, m8at], w=[sc2t])
                self.E("dve", "tensor_tensor", out=sc2, in0=sc2, in1=sc, op=ALU.add, r=[sc2t, sct], w=[sc2t])
                self.E("dve", "max", out=m8b, in_=sc2, r=[sc2t], w=[m8bt])
                self.E("dve", "tensor_scalar", out=bs[:, 64:128], in0=sc, scalar1=m8b[:, 7:8], scalar2=1.0, op0=ALU.is_ge,
                       op1=ALU.subtract, r=[sct, m8bt, bst], w=[bst])

            def sel_B(qt):
                pbt = self.pb[1]
                self.tr(pbt[:, 0:128], bs, r=[bst], w=[("pb", 1)])
                for hl2 in range(4):
                    q2, _ = qTa[hl2]
                    self.E("dve", "tensor_copy", out=q2[64:128, qt * 128:(qt + 1) * 128], in_=pbt[64:128, 0:128],
                           r=[("pb", 1)], pw=[("qhi", hl2)])

            for hl in range(4):
                h = g * 4 + hl
                q_ap, qtok = qTa[hl]

                def fin_c(qt, po, potok, h=h, hl=hl):
                    self.E("dve", "tensor_scalar", out=rz, in0=po[:, 64:65], scalar1=1.0e-30, scalar2=None, op0=ALU.max,
                           r=[potok], w=[rzt])
                    self.E("dve", "reciprocal", out=rz, in_=rz, r=[rzt], w=[rzt])
                    self.E("dve", "tensor_tensor", out=cf, in0=rz, in1=self.g_sb[:, qt, h * 3:h * 3 + 1], op=ALU.mult,
                           r=[rzt, "g_sb"], w=[cft])
                    self.E("dve", "tensor_scalar", out=o_all[:, qt, h * 64:(h + 1) * 64], in0=po[:, 0:64], scalar1=cf[:, 0:1],
                           scalar2=None, op0=ALU.mult, r=[potok, cft], pw=["big"])
                    if hl == 0:
                        self.E("dve", "tensor_scalar", out=imp[:, qt, 0:63], in0=po[:, 65:128], scalar1=rz[:, 0:1],
                               scalar2=None, op0=ALU.mult, r=[potok, rzt], pw=[impt])
                    else:
                        self.E("dve", "scalar_tensor_tensor", out=imp[:, qt, 0:63], in0=po[:, 65:128], scalar=rz[:, 0:1],
                               in1=imp[:, qt, 0:63], op0=ALU.mult, op1=ALU.add, r=[potok, rzt, impt], pw=[impt])
                    if hl == 3:
                        if qt > 0:
                            sel_B(qt - 1)
                        sel_A(qt)

                self.attention(q_ap, qtok, self.kcTa[:, g, :], "kcTa", self.vca[:, g], "vca", 128, cmp_pairs, fin_c, pts,
                               f"c{h}")
            sel_B(NT - 1)

            if getattr(self, "stop", "") == "sel":
                break
            def sel_pairs(qt):
                return [(kt, (self.tri[:] if kt == qt else None)) for kt in range(qt + 1)]

            def win_pairs(qt):
                out = []
                for kt in range(max(0, qt - 4), qt + 1):
                    b = self.tri[:] if kt == qt else (self.atri[:] if kt == qt - 4 else None)
                    out.append((kt, b))
                return out

            for hl in range(4):
                h = g * 4 + hl
                q_ap, qtok = qTa[hl]

                def fin_s(qt, po, potok, h=h):
                    self.E("dve", "reciprocal", out=rz, in_=po[:, 64:65], r=[potok], w=[rzt])
                    self.E("dve", "tensor_tensor", out=cf, in0=rz, in1=self.g_sb[:, qt, h * 3 + 1:h * 3 + 2], op=ALU.mult,
                           r=[rzt, "g_sb"], w=[cft])
                    self.E("dve", "scalar_tensor_tensor", out=oacc[:, qt, :], in0=po[:, 0:64], scalar=cf[:, 0:1],
                           in1=o_all[:, qt, h * 64:(h + 1) * 64], op0=ALU.mult, op1=ALU.add, r=[potok, cft, "big"],
                           pw=[oacct])

                def fin_w(qt, po, potok, h=h):
                    self.E("dve", "reciprocal", out=rz, in_=po[:, 64:65], r=[potok], w=[rzt])
                    self.E("dve", "tensor_tensor", out=cf, in0=rz, in1=self.g_sb[:, qt, h * 3 + 2:h * 3 + 3], op=ALU.mult,
                           r=[rzt, "g_sb"], w=[cft])
                    self.E("dve", "scalar_tensor_tensor", out=o_all[:, qt, h * 64:(h + 1) * 64], in0=po[:, 0:64],
                           scalar=cf[:, 0:1], in1=oacc[:, qt, :], op0=ALU.mult, op1=ALU.add, r=[potok, cft, oacct],
                           pw=["big"])

                self.attention(q_ap, [qtok, ("qhi", hl)], ksTa, kst, Vs, Vst, 65, sel_pairs, fin_s, pts, f"s{h}")
                self.attention(q_ap, qtok, kwTa, kwt, Vw, Vwt, 65, win_pairs, fin_w, pts, f"w{h}")

    def phase_O(self, li, src):
        self.phase()
        NT = self.NT
        j = li // 2
        moba = (li % 2 == 0)
        o_all = self.big[:, 0:NT * D].rearrange("p (t f) -> p t f", f=D)
        Wo, Wot = self.ph("Wo", [128, 8, D], BF16)
        Wg, Wgt = self.ph("Wg", [128, 8, D], BF16)
        Wp, Wpt = self.ph("Wp", [128, 2, D], BF16)
        self.load_w(Wo, Wot, (self.moba_w_out if moba else self.nsa_w_out)[j], D, None, None)
        self.load_gcol(self.gcol2, "gcol2", self.ple_gate_gain[li])
        self.load_w(Wg, Wgt, self.ple_w_gate[li], D, self.gcol2, "gcol2")
        self.load_w(Wp, Wpt, self.ple_w_proj[li], D, None, None, nk=2)
        xt = [self.ph(f"oxt{i}", [128, D], F32) for i in range(2)]
        zt = [self.ph(f"ozt{i}", [128, D], BF16) for i in range(2)]
        pt_ = [self.ph(f"opt{i}", [128, PLE], F32) for i in range(2)]
        og = [self.ph(f"og{i}", [128, D], BF16) for i in range(2)]
        ogT = [self.ph(f"ogT{i}", [128, 8, 128], BF16) for i in range(2)]
        x1 = [self.ph(f"x1_{i}", [128, D], F32) for i in range(2)]
        sqs, sqst = self.ph("osq", [128, D], F32)
        rs = [self.ph(f"ors{i}", [128, 1], F32) for i in range(2)]
        xn, xnt = self.ph("xn", [128, D], BF16)
        xnT, xnTt = self.ph("xnT", [128, 8, 128], BF16)
        gate, gatet = self.ph("gatef", [128, D], F32)
        pbf, pbft = self.ph("pbf", [128, PLE], BF16)
        pT, pTt = self.ph("pT", [128, 2, 128], BF16)
        x2 = [self.ph(f"x2_{i}", [128, D], F32) for i in range(2)]

        def S1(t):
            x_ap, xtok = xt[t % 2]
            z_ap, ztok = zt[t % 2]
            p_ap, ptok = pt_[t % 2]
            og_ap, ogt = og[t % 2]
            ogT_ap, ogTt = ogT[t % 2]
            x1_ap, x1t = x1[t % 2]
            self.dma(x_ap, src[t * 128:(t + 1) * 128, :], f"oxl{t % 2}", r=[("y", t)], w=[xtok])
            self.dma(z_ap, self.zs_d[t * 128:(t + 1) * 128, :], f"ozl{t % 2}", r=[("zs_d",)], w=[ztok])
            self.dma(p_ap, self.p_in[li, t * 128:(t + 1) * 128, :], f"opl{t % 2}", w=[ptok])
            self.E("pool", "tensor_tensor", out=og_ap, in0=o_all[:, t, :], in1=z_ap, op=ALU.mult, r=["big", ztok], w=[ogt])
            self.transpose8(og_ap, ogt, ogT_ap, ogTt, 0)
            for c in range(2):
                ps = self.pf[c]
                for k in range(8):
                    self.mm(ps[:, :], ogT_ap[:, k, :], Wo[:, k, c * 512:(c + 1) * 512], k == 0, k == 7, r=[ogTt, Wot],
                            w=[("pf", c)] if k == 0 else ())
                last = self.S.q["pe"][-1]
                self.S.tw[("pf", c)] = {last.stream: last}
                self.E("dve", "tensor_tensor", out=x1_ap[:, c * 512:(c + 1) * 512], in0=x_ap[:, c * 512:(c + 1) * 512],
                       in1=ps[:, :], op=ALU.add, r=[xtok, ("pf", c)], **(dict(w=[x1t]) if c == 0 else dict(pw=[x1t])))

        def S2a(t):
            x1_ap, x1t = x1[t % 2]
            r_ap, rtok = rs[t % 2]
            self.rstd_of(x1_ap, x1t, D, sqs, sqst, r_ap, rtok)
            self.E("dve", "tensor_scalar", out=xn, in0=x1_ap, scalar1=r_ap[:, 0:1], scalar2=None, op0=ALU.mult,
                   r=[x1t, rtok], w=[xnt])

        def S2b(t):
            x1_ap, x1t = x1[t % 2]
            p_ap, ptok = pt_[t % 2]
            self.transpose8(xn, xnt, xnT, xnTt, 0)
            for c in range(2):
                ps = self.pf[2 + c]
                for k in range(8):
                    self.mm(ps[:, :], xnT[:, k, :], Wg[:, k, c * 512:(c + 1) * 512], k == 0, k == 7, r=[xnTt, Wgt],
                            w=[("pf", 2 + c)] if k == 0 else ())
                last = self.S.q["pe"][-1]
                self.S.tw[("pf", 2 + c)] = {last.stream: last}
                self.act(gate[:, c * 512:(c + 1) * 512], ps[:, :], AF.Sigmoid, r=[("pf", 2 + c)],
                         **(dict(w=[gatet]) if c == 0 else dict(pw=[gatet])))
            self.E("dve", "tensor_copy", out=pbf, in_=p_ap, r=[ptok], w=[pbft])
            self.transpose8(pbf, pbft, pT, pTt, 1, nblk=2)
            x2_ap, x2tok = x2[t % 2]
            for c in range(2):
                ps = self.pf[4 + c]
                for k in range(2):
                    self.mm(ps[:, :], pT[:, k, :], Wp[:, k, c * 512:(c + 1) * 512], k == 0, k == 1, r=[pTt, Wpt],
                            w=[("pf", 4 + c)] if k == 0 else ())
                last = self.S.q["pe"][-1]
                self.S.tw[("pf", 4 + c)] = {last.stream: last}
                sl = slice(c * 512, (c + 1) * 512)
                self.E("dve", "tensor_tensor", out=x2_ap[:, sl], in0=gate[:, sl], in1=ps[:, :], op=ALU.mult,
                       r=[gatet, ("pf", 4 + c)], **(dict(w=[x2tok]) if c == 0 else dict(pw=[x2tok])))
                self.E("pool", "tensor_tensor", out=x2_ap[:, sl], in0=x2_ap[:, sl], in1=x1_ap[:, sl], op=ALU.add,
                       r=[x2tok, x1t], pw=[x2tok])
            self.dma(self.y[t * 128:(t + 1) * 128, :], x2_ap, f"oyst{t % 2}", r=[x2tok], w=[("y", t)], q="pool")

        S1(0)
        for t in range(NT):
            S2a(t)
            if t + 1 < NT:
                S1(t + 1)
            S2b(t)


def rope_tables(T):
    NT = T // 128
    half = 32
    inv_freq = (10000.0 ** (-np.arange(half, dtype=np.float32) / half)).astype(np.float32)
    pos = (np.arange(NT)[None, :] * 128 + np.arange(128)[:, None]).astype(np.float32)
    ang = pos[:, :, None] * inv_freq[None, None, :]
    cs = np.stack([np.cos(ang), np.sin(ang)], 0).astype(np.float32)
    NCT = max(1, (T // 16) // 128)
    posc = (16.0 * (np.arange(NCT)[None, :] * 128 + np.arange(128)[:, None]) + 31.0).astype(np.float32)
    angc = posc[:, :, None] * inv_freq[None, None, :]
    csc = np.stack([np.cos(angc), np.sin(angc)], 0).astype(np.float32)
    return cs, csc


_CACHE = {}


def run(inputs, T, layers, n_cores):
    key = (T, tuple(layers))
    if key not in _CACHE:
        _CACHE[key] = Builder(T, list(layers)).build()
    nc = _CACHE[key]
    cs, csc = rope_tables(T)
    shared = {k: np.ascontiguousarray(v, dtype=np.float32) for k, v in inputs.items() if k not in ("x", "p")}
    shared["rope_cs"] = cs
    shared["rope_cs_c"] = csc
    in_maps = []
    for b in range(n_cores):
        m = dict(shared)
        m["x"] = np.ascontiguousarray(inputs["x"][b], dtype=np.float32)
        m["p"] = np.ascontiguousarray(inputs["p"][:, b], dtype=np.float32)
        in_maps.append(m)
    res = run_bass_kernel_spmd(nc, in_maps, core_ids=list(range(n_cores)))
    return np.stack([r["y"] for r in res.results], 0).astype(np.float32)


def kernel(**inputs):
    return run(inputs, 4096, [0, 1, 2, 3], 8)
```

```python
import math
from contextlib import ExitStack

import numpy as np
import concourse.bass as bass
import concourse.mybir as mybir
from concourse.bass_utils import run_bass_kernel_spmd

F32 = mybir.dt.float32
BF16 = mybir.dt.bfloat16
I32 = mybir.dt.int32
AF = mybir.ActivationFunctionType
ALU = mybir.AluOpType
AX = mybir.AxisListType

D = 1024
H = 16
HD = 64
PLE = 256
EPS = 1e-6
NSA_IN = 3632
NEGB = 30000.0


class Op:
    __slots__ = ("eng", "fn", "deps", "needed", "sigval", "stream", "is_dma", "idx")


class Sched:
    ENGS = ("pe", "act", "dve", "pool", "sp")

    def __init__(self):
        self.q = {e: [] for e in self.ENGS}
        self.tw = {}
        self.tr = {}
        self.dma_cnt = {}
        self.nops = 0
        self.last = {}
        self.pending = {}

    def barrier(self):
        for e in self.ENGS:
            self.pending[e] = list(self.last.values())

    def op(self, eng, fn, reads=(), writes=(), pwrites=(), dma=None):
        o = Op()
        o.eng = eng
        o.fn = fn
        o.is_dma = dma is not None
        o.stream = ("dma", dma) if dma is not None else ("eng", eng)
        o.needed = o.is_dma
        o.sigval = None
        o.idx = self.nops
        self.nops += 1
        deps = {}

        def add(d):
            if (not o.is_dma) and (not d.is_dma) and d.eng == "pe" and eng == "pe":
                return
            k = d.stream
            if k not in deps or deps[k].idx < d.idx:
                deps[k] = d

        for d in self.pending.pop(eng, ()):
            add(d)
        for t in reads:
            for d in self.tw.get(t, {}).values():
                add(d)
        for t in writes:
            for d in self.tw.get(t, {}).values():
                add(d)
            for d in self.tr.get(t, {}).values():
                add(d)
        for t in pwrites:
            for d in self.tr.get(t, {}).values():
                add(d)
        o.deps = list(deps.values())
        for d in o.deps:
            d.needed = True
        for t in reads:
            self.tr.setdefault(t, {})[o.stream] = o
        for t in writes:
            self.tw[t] = {o.stream: o}
            self.tr[t] = {}
        for t in pwrites:
            self.tw.setdefault(t, {})[o.stream] = o
        if o.is_dma:
            c = self.dma_cnt.get(dma, 0) + 16
            self.dma_cnt[dma] = c
            o.sigval = c
        self.last[o.stream] = o
        self.q[eng].append(o)
        return o

    def emit(self, nc, stack):
        for e in self.ENGS:
            c = 0
            for o in self.q[e]:
                if not o.is_dma and o.needed:
                    c += 1
                    o.sigval = c
        sems = {}
        for e in self.ENGS:
            sems[("eng", e)] = stack.enter_context(nc.semaphore("s_" + e))
        for d in self.dma_cnt:
            sems[("dma", d)] = stack.enter_context(nc.semaphore("d_" + str(d)))
        block = stack.enter_context(nc.Block())
        q = self.q

        def run(ename, eng):
            seen = {}
            for o in q[ename]:
                for d in o.deps:
                    v = d.sigval
                    if seen.get(d.stream, 0) >= v:
                        continue
                    seen[d.stream] = v
                    eng.wait_ge(sems[d.stream], v)
                ins = o.fn(eng)
                if o.needed:
                    ins.then_inc(sems[o.stream], 16 if o.is_dma else 1)

        @block.tensor
        def _(eng):
            run("pe", eng)

        @block.scalar
        def _(eng):
            run("act", eng)

        @block.vector
        def _(eng):
            run("dve", eng)

        @block.gpsimd
        def _(eng):
            run("pool", eng)

        @block.sync
        def _(eng):
            run("sp", eng)
            for d, c in self.dma_cnt.items():
                eng.wait_ge(sems[("dma", d)], c)


class Builder:
    def __init__(self, T, layers):
        self.T = T
        self.NT = T // 128
        self.NC = T // 16
        self.NCT = max(1, self.NC // 128)
        self.layers = layers
        self.nc = bass.Bass("TRN2", target_bir_lowering=False)
        self.S = Sched()
        self.uid = 0

    def E(self, eng, meth, *a, r=(), w=(), pw=(), **kw):
        self.S.op(eng, lambda e: getattr(e, meth)(*a, **kw), reads=r, writes=w, pwrites=pw)

    def dma(self, out, in_, sem, r=(), w=(), pw=(), q="sp", **kw):
        self.S.op(q, lambda e: e.dma_start(out=out, in_=in_, **kw), reads=r, writes=w, pwrites=pw, dma=sem)

    def mm(self, out, lhsT, rhs, start, stop, r=(), w=()):
        self.S.op("pe", lambda e: e.matmul(out, lhsT=lhsT, rhs=rhs, start=start, stop=stop), reads=r, writes=w)

    def tr(self, out, in_, r=(), w=()):
        ident = self.ident
        self.S.op("pe", lambda e: e.transpose(out=out, in_=in_, identity=ident[:]), reads=tuple(r) + ("ident",), writes=w)

    def act(self, out, in_, func, r=(), w=(), pw=(), **kw):
        self.S.op("act", lambda e: e.activation(out=out, in_=in_, func=func, **kw), reads=r, writes=w, pwrites=pw)

    def persist(self, name, shape, dt):
        return self.st.enter_context(self.nc.sbuf_tensor(name, shape, dt))

    def phase(self):
        self.S.barrier()
        self.aoff = 0
        self.pid = getattr(self, "pid", 0) + 1

    def ph(self, name, shape, dt):
        esz = 4 if dt in (F32, I32) else 2
        n = 1
        for s_ in shape[1:]:
            n *= s_
        nbytes = (n * esz + 63) // 64 * 64
        a = self.aoff
        self.aoff += nbytes
        assert self.aoff <= self.ARENA_BYTES, (name, self.aoff)
        v = self.arena[:, a // 2:(a + n * esz) // 2]
        if esz == 4:
            v = v.bitcast(dt)
        if len(shape) == 3:
            v = v.rearrange("p (a b) -> p a b", b=shape[2])
        elif len(shape) == 4:
            v = v.rearrange("p (a b c) -> p a b c", b=shape[2], c=shape[3])
        return v, ("ph", self.pid, name)

    def build(self):
        nc, T, NT = self.nc, self.T, self.NT
        dt = nc.dram_tensor
        self.x_in = dt("x", [T, D], F32, kind="ExternalInput").ap()
        self.p_in = dt("p", [4, T, PLE], F32, kind="ExternalInput").ap()
        self.norm_gain = dt("norm_gain", [4, D], F32, kind="ExternalInput").ap()
        self.moba_w_in = dt("moba_w_in", [2, D, 4 * D], F32, kind="ExternalInput").ap()
        self.moba_q_gain = dt("moba_q_gain", [2, HD], F32, kind="ExternalInput").ap()
        self.moba_k_gain = dt("moba_k_gain", [2, HD], F32, kind="ExternalInput").ap()
        self.moba_w_out = dt("moba_w_out", [2, D, D], F32, kind="ExternalInput").ap()
        self.nsa_w_in = dt("nsa_w_in", [2, D, NSA_IN], F32, kind="ExternalInput").ap()
        self.nsa_q_gain = dt("nsa_q_gain", [2, HD], F32, kind="ExternalInput").ap()
        self.nsa_k_gain = dt("nsa_k_gain", [2, 3, HD], F32, kind="ExternalInput").ap()
        self.nsa_cmp_pe = dt("nsa_cmp_pe", [2, 2, 32, HD], F32, kind="ExternalInput").ap()
        self.nsa_cmp_w1 = dt("nsa_cmp_w1", [2, 2, 2048, 256], F32, kind="ExternalInput").ap()
        self.nsa_cmp_w2 = dt("nsa_cmp_w2", [2, 2, 256, HD], F32, kind="ExternalInput").ap()
        self.nsa_w_out = dt("nsa_w_out", [2, D, D], F32, kind="ExternalInput").ap()
        self.ple_w_proj = dt("ple_w_proj", [4, PLE, D], F32, kind="ExternalInput").ap()
        self.ple_gate_gain = dt("ple_gate_gain", [4, D], F32, kind="ExternalInput").ap()
        self.ple_w_gate = dt("ple_w_gate", [4, D, D], F32, kind="ExternalInput").ap()
        self.rope_cs = dt("rope_cs", [2, 128, NT, 32], F32, kind="ExternalInput").ap()
        self.rope_cs_c = dt("rope_cs_c", [2, 128, self.NCT, 32], F32, kind="ExternalInput").ap()
        self.y = dt("y", [T, D], F32, kind="ExternalOutput").ap()
        self.qT_d = dt("qT_d", [D, T], BF16).ap()
        self.kT_d = dt("kT_d", [D, T], BF16).ap()
        self.v_d = dt("v_d", [T, D], BF16).ap()
        self.zs_d = dt("zs_d", [T, D], BF16).ap()

        with ExitStack() as st:
            self.st = st
            self.ARENA_BYTES = 100 * 1024
            self.arena = self.persist("arena", [128, self.ARENA_BYTES // 2], BF16)
            self.big = self.persist("big", [128, 32768], BF16)
            self.ident = self.persist("ident", [128, 128], BF16)
            self.tri = self.persist("tri", [128, 128], BF16)
            self.atri = self.persist("atri", [128, 128], BF16)
            self.cmpm = self.persist("cmpm", [128, 17, 128], BF16)
            self.cs = self.persist("cs", [128, 2, NT, 32], F32)
            self.csc = self.persist("csc", [128, 2, self.NCT, 32], F32)
            self.g_sb = self.persist("g_sb", [128, NT, 48], F32)
            self.msel = self.persist("msel", [128, NT, 64], F32)
            self.kcTa = self.persist("kcTa", [128, 4, self.NC], BF16)
            self.vca = self.persist("vca", [128, 4, self.NCT, 128], BF16)
            self.gq = self.persist("gq", [128, HD], F32)
            self.gk = self.persist("gk", [128, 3, HD], F32)
            self.gcol = self.persist("gcol", [128, 8], F32)
            self.gcol2 = self.persist("gcol2", [128, 8], F32)
            self.pf = [st.enter_context(nc.psum_tensor(f"pf{i}", [128, 512], F32)) for i in range(6)]
            self.pb = [st.enter_context(nc.psum_tensor(f"pb{i}", [128, 1024], BF16)) for i in range(2)]
            self.consts()
            first = True
            for li in self.layers:
                src = self.x_in if first else self.y
                first = False
                if li % 2 == 0:
                    self.phase_P(li, src, moba=True)
                    self.phase_A_moba(li)
                else:
                    import os
                    stop = os.environ.get("NSA_STOP", "")
                    self.stop = stop
                    self.phase_P(li, src, moba=False)
                    if stop == "P":
                        break
                    self.phase_C(li)
                    if stop == "C":
                        break
                    self.phase_A_nsa(li)
                    if stop in ("cmp", "sel", "A"):
                        break
                self.phase_O(li, src)
            self.S.emit(nc, st)
        return nc

    def consts(self):
        self.phase()
        NT = self.NT
        tf, tft = self.ph("c_tf", [128, 128], F32)
        self.E("pool", "memset", tf, 1.0, w=[tft])
        self.E("pool", "affine_select", out=tf, in_=tf, pattern=[[1, 128]], compare_op=ALU.is_equal, fill=0.0,
               base=0, channel_multiplier=-1, r=[tft], w=[tft])
        self.E("dve", "tensor_copy", out=self.ident[:], in_=tf, r=[tft], w=["ident"])
        tf2, tf2t = self.ph("c_tf2", [128, 128], F32)
        self.E("pool", "memset", tf2, 0.0, w=[tf2t])
        self.E("pool", "affine_select", out=tf2, in_=tf2, pattern=[[1, 128]], compare_op=ALU.is_ge, fill=-NEGB,
               base=0, channel_multiplier=-1, r=[tf2t], w=[tf2t])
        self.E("dve", "tensor_copy", out=self.tri[:], in_=tf2, r=[tf2t], w=["tri"])
        tf3, tf3t = self.ph("c_tf3", [128, 128], F32)
        self.E("pool", "memset", tf3, 0.0, w=[tf3t])
        self.E("pool", "affine_select", out=tf3, in_=tf3, pattern=[[-1, 128]], compare_op=ALU.is_ge, fill=-NEGB,
               base=-1, channel_multiplier=1, r=[tf3t], w=[tf3t])
        self.E("dve", "tensor_copy", out=self.atri[:], in_=tf3, r=[tf3t], w=["atri"])
        tf4, tf4t = self.ph("c_tf4", [128, 17, 128], F32)
        self.E("pool", "memset", tf4, 0.0, w=[tf4t])
        self.E("pool", "affine_select", out=tf4, in_=tf4, pattern=[[128, 17], [1, 128]], compare_op=ALU.is_ge,
               fill=-NEGB, base=-31, channel_multiplier=-16, r=[tf4t], w=[tf4t])
        self.E("dve", "tensor_copy", out=self.cmpm[:], in_=tf4, r=[tf4t], w=["cmpm"])
        self.dma(self.cs[:, 0], self.rope_cs[0], "c_cs", w=["cs"])
        self.dma(self.cs[:, 1], self.rope_cs[1], "c_cs2", pw=["cs"])
        self.dma(self.csc[:, 0], self.rope_cs_c[0], "c_csc", w=["csc"])
        self.dma(self.csc[:, 1], self.rope_cs_c[1], "c_csc2", pw=["csc"])
        m = self.msel
        self.E("pool", "memset", m[:], 0.0, w=["msel"])
        for half in range(2):
            mh = m[half * 64:(half + 1) * 64]
            self.E("pool", "affine_select", out=mh, in_=mh, pattern=[[2, NT], [-1, 64]], compare_op=ALU.is_ge,
                   fill=1.0e4, base=half - 2, channel_multiplier=0, r=["msel"], w=["msel"])
            self.E("pool", "affine_select", out=mh, in_=mh, pattern=[[2, NT], [-1, 64]], compare_op=ALU.is_ge,
                   fill=-1.0e4, base=half, channel_multiplier=0, r=["msel"], w=["msel"])
        self.E("pool", "memset", m[:, :, 0:1], 1.0e4, r=["msel"], w=["msel"])
        ov, ovt = self.ph("c_ov", [128, self.NCT, 64], F32)
        self.E("pool", "memset", ov, 1.0, w=[ovt])
        for ct in range(self.NCT):
            o1 = ov[:, ct, :]
            self.E("pool", "affine_select", out=o1, in_=o1, pattern=[[4, 64]], compare_op=ALU.is_ge, fill=0.0,
                   base=3 - 128 * ct, channel_multiplier=-1, r=[ovt], w=[ovt])
            self.E("pool", "affine_select", out=o1, in_=o1, pattern=[[-4, 64]], compare_op=ALU.is_ge, fill=0.0,
                   base=1 + 128 * ct, channel_multiplier=1, r=[ovt], w=[ovt])
        self.E("dve", "memset", self.vca[:], 1.0, w=["vca"])
        for g in range(4):
            for ct in range(self.NCT):
                self.E("dve", "tensor_copy", out=self.vca[:, g, ct, 65:128], in_=ov[:, ct, 0:63], r=[ovt, "vca"], w=["vca"])
        self.E("dve", "memset", self.kcTa[:], 0.0, w=["kcTa"])

    def load_gcol(self, dst, tok, src_row):
        self.dma(dst[:], src_row.rearrange("(k p) -> p k", p=128), "gcol_" + tok, w=[tok],
                 allow_slow_non_contiguous=True)

    def load_w(self, dst3, dtok, w_ap, ncols, gcol, gtok, nk=8):
        if getattr(self, "_stg_pid", None) != self.pid:
            self._stg = [self.ph(f"wstg{i}", [128, 1024], F32) for i in range(2)]
            self._stg_pid = self.pid
        stg = self._stg
        i = 0
        first = True
        for k in range(nk):
            for c0 in range(0, ncols, 1024):
                cw = min(1024, ncols - c0)
                s_ap, s_tok = stg[i % 2]
                self.dma(s_ap[:, 0:cw], w_ap[k * 128:(k + 1) * 128, c0:c0 + cw], f"wst{i % 2}", w=[s_tok])
                kw = dict(r=[s_tok] + ([gtok] if gcol is not None else []))
                if first:
                    kw["w"] = [dtok]
                    first = False
                else:
                    kw["pw"] = [dtok]
                if gcol is not None:
                    self.E("dve", "tensor_scalar", out=dst3[:, k, c0:c0 + cw], in0=s_ap[:, 0:cw], scalar1=gcol[:, k:k + 1],
                           scalar2=None, op0=ALU.mult, **kw)
                else:
                    self.E("dve", "tensor_copy", out=dst3[:, k, c0:c0 + cw], in_=s_ap[:, 0:cw], **kw)
                i += 1

    def rstd_of(self, x_ap, xtok, n, scratch, stok, out_col, otok):
        self.act(scratch, x_ap, AF.Square, r=[xtok], w=[stok, otok], accum_out=out_col)
        self.act(out_col, out_col, AF.Sqrt, r=[otok], w=[otok], scale=1.0 / n, bias=EPS)
        self.E("dve", "reciprocal", out=out_col, in_=out_col, r=[otok], w=[otok])

    def transpose8(self, src_bf, stok, dstT, dtok, pbi, nblk=8):
        pbt = self.pb[pbi]
        ptok = ("pb", pbi)
        for k in range(nblk):
            kw = dict(w=[ptok]) if k == 0 else dict(w=())
            if k == 0:
                self.tr(pbt[:, k * 128:(k + 1) * 128], src_bf[:, k * 128:(k + 1) * 128], r=[stok], w=[ptok])
            else:
                self.S.op("pe", (lambda e, k=k: e.transpose(out=pbt[:, k * 128:(k + 1) * 128], in_=src_bf[:, k * 128:(k + 1) * 128],
                                                            identity=self.ident[:])), reads=[stok, "ident"], pwrites=[ptok])
        self.act(dstT.rearrange("p a b -> p (a b)") if len(dstT.shape) == 3 else dstT, pbt[:, 0:nblk * 128], AF.Copy, r=[ptok], w=[dtok])

    def norm_rope(self, src, stok, nh, gain_ap, gtok, cos_ap, sin_ap, cstok, out_bf, otok, tmp):
        (sq, sqt), (ss, sst), (A, At), (Bt_, Btt), (t13, t13t), (tsw, tswt) = tmp
        W = nh * 64
        v3 = lambda ap: ap[:, 0:W].rearrange("p (h d) -> p h d", d=64)
        v4 = lambda ap: ap[:, 0:W].rearrange("p (h a d) -> p h a d", a=2, d=32)
        self.act(sq[:, 0:W], src, AF.Square, r=[stok], w=[sqt])
        self.E("dve", "tensor_reduce", out=ss[:, 0:nh], in_=v3(sq), axis=AX.X, op=ALU.add, r=[sqt], w=[sst])
        self.act(ss[:, 0:nh], ss[:, 0:nh], AF.Sqrt, r=[sst], w=[sst], scale=1.0 / 64, bias=EPS)
        self.E("dve", "reciprocal", out=ss[:, 0:nh], in_=ss[:, 0:nh], r=[sst], w=[sst])
        self.E("dve", "tensor_tensor", out=v3(A), in0=src.rearrange("p (h d) -> p h d", d=64),
               in1=ss[:, 0:nh].unsqueeze(2).to_broadcast([128, nh, 64]), op=ALU.mult, r=[stok, sst], w=[At])
        self.E("dve", "tensor_tensor", out=v3(Bt_), in0=v3(A), in1=gain_ap.unsqueeze(1).to_broadcast([128, nh, 64]),
               op=ALU.mult, r=[At, gtok], w=[Btt])
        cosb = cos_ap.unsqueeze(1).unsqueeze(1).to_broadcast([128, nh, 2, 32])
        sinb = sin_ap.unsqueeze(1).to_broadcast([128, nh, 32])
        self.E("dve", "tensor_tensor", out=v4(t13), in0=v4(Bt_), in1=cosb, op=ALU.mult, r=[Btt, cstok], w=[t13t])
        self.E("dve", "tensor_tensor", out=v4(tsw)[:, :, 0, :], in0=v4(Bt_)[:, :, 1, :], in1=sinb, op=ALU.mult,
               r=[Btt, cstok], w=[tswt])
        self.E("dve", "tensor_tensor", out=v4(tsw)[:, :, 1, :], in0=v4(Bt_)[:, :, 0, :], in1=sinb, op=ALU.mult,
               r=[Btt, cstok, tswt], w=[tswt])
        self.E("dve", "tensor_tensor", out=v4(out_bf)[:, :, 0, :], in0=v4(t13)[:, :, 0, :], in1=v4(tsw)[:, :, 0, :],
               op=ALU.subtract, r=[t13t, tswt], w=[otok])
        self.E("dve", "tensor_tensor", out=v4(out_bf)[:, :, 1, :], in0=v4(t13)[:, :, 1, :], in1=v4(tsw)[:, :, 1, :],
               op=ALU.add, r=[t13t, tswt, otok], w=[otok])

    def phase_P(self, li, src, moba):
        self.phase()
        NT = self.NT
        j = li // 2
        ncols = 4 * D if moba else NSA_IN
        w_ap = (self.moba_w_in if moba else self.nsa_w_in)[j]
        W3 = self.big[:, 0:8 * 4096].rearrange("p (k n) -> p k n", n=4096)
        self.load_gcol(self.gcol, "gcol", self.norm_gain[li])
        self.load_w(W3, "big", w_ap, ncols, self.gcol, "gcol")
        qg = (self.moba_q_gain if moba else self.nsa_q_gain)[j]
        self.dma(self.gq[:], qg.partition_broadcast(128), "gq", w=["gq"])
        if moba:
            self.dma(self.gk[:, 0, :], self.moba_k_gain[j].partition_broadcast(128), "gk", w=["gk"])
        else:
            for b_ in range(3):
                self.dma(self.gk[:, b_, :], self.nsa_k_gain[j, b_].partition_broadcast(128), f"gk{b_}",
                         **(dict(w=["gk"]) if b_ == 0 else dict(pw=["gk"])))
        xt = [self.ph(f"xt{i}", [128, D], F32) for i in range(2)]
        sqs = self.ph("sqs", [128, D], F32)
        rs = [self.ph(f"rs{i}", [128, 1], F32) for i in range(2)]
        hb = [self.ph(f"hb{i}", [128, D], BF16) for i in range(2)]
        hT = [self.ph(f"hT{i}", [128, 8, 128], BF16) for i in range(2)]
        b1, b1t = self.ph("nr_b1", [128, 2048], F32)
        b2, b2t = self.ph("nr_b2", [128, 2048], F32)
        b3, b3t = self.ph("nr_b3", [128, 2048], F32)
        ss, sst = self.ph("nr_ss", [128, 32], F32)
        qbf = [self.ph(f"qbf{i}", [128, 2048], BF16) for i in range(2)]
        qTt = [self.ph(f"qTt{i}", [128, 16, 128], BF16) for i in range(2)]
        vbf = [self.ph(f"vbf{i}", [128, D], BF16) for i in range(2)]
        zbf = [self.ph(f"zbf{i}", [128, D], BF16) for i in range(2)]
        rawbf = [self.ph(f"rawbf{i}", [128, 512], BF16) for i in range(2)]
        if moba:
            chunks = [(c * 512, 512) for c in range(8)]
            HT_ = 32
            srcs = [(0, 0, 8, self.gq[:], "gq"), (1, 0, 8, self.gq[:], "gq"), (2, 0, 8, self.gk[:, 0, :], "gk"),
                    (3, 0, 8, self.gk[:, 0, :], "gk")]
            qk_chunks = [(0, 0), (1, 1), (2, 2), (3, 3)]
            dests = [(self.qT_d, jb * 128) for jb in range(8)] + [(self.kT_d, jb * 128) for jb in range(8)]
        else:
            chunks = [(0, 512), (512, 512), (1024, 512), (1536, 512), (2048, 512), (2560, 48), (2608, 512), (3120, 512)]
            HT_ = 24
            srcs = [(0, 0, 8, self.gq[:], "gq"), (1, 0, 8, self.gq[:], "gq"), (2, 0, 4, self.gk[:, 1, :], "gk"),
                    (3, 0, 4, self.gk[:, 2, :], "gk")]
            qk_chunks = [(0, 0), (1, 1), (3, 2), (4, 3), (2, 4)]
            dests = ([(self.qT_d, jb * 128) for jb in range(8)] + [(self.kT_d, 0), (self.kT_d, 128), (self.kT_d, 256),
                     (self.kT_d, 384)] + [(self.kT_d, 512 + jb * 128) for jb in range(4)])
        WQ = HT_ * 64

        def pre_a(t):
            x_ap, xtok = xt[t % 2]
            self.dma(x_ap, src[t * 128:(t + 1) * 128, :], f"xld{t % 2}", r=[("y", t)], w=[xtok])
            r_ap, rtok = rs[t % 2]
            self.rstd_of(x_ap, xtok, D, sqs[0], sqs[1], r_ap, rtok)
            h_ap, htok = hb[t % 2]
            self.E("dve", "tensor_scalar", out=h_ap, in0=x_ap, scalar1=r_ap[:, 0:1], scalar2=None, op0=ALU.mult,
                   r=[xtok, rtok], w=[htok])

        def pre_b(t):
            h_ap, htok = hb[t % 2]
            hT_ap, hTtok = hT[t % 2]
            self.transpose8(h_ap, htok, hT_ap, hTtok, 0)

        def mm_chunk(t, ci, bank):
            c0, cw = chunks[ci]
            hT_ap, hTtok = hT[t % 2]
            ps = self.pf[bank]
            pstok = ("pf", bank)
            for k in range(8):
                self.mm(ps[:, 0:cw], hT_ap[:, k, :], W3[:, k, c0:c0 + cw], k == 0, k == 7, r=[hTtok, "big"],
                        w=[pstok] if k == 0 else ())
            last = self.S.q["pe"][-1]
            self.S.tw[pstok] = {last.stream: last}

        def chain(t):
            q_ap, qbt = qbf[t % 2]
            v3 = lambda ap, c0, w: ap[:, c0:c0 + w].rearrange("p (h d) -> p h d", d=64)
            v4 = lambda ap, c0, w: ap[:, c0:c0 + w].rearrange("p (h a d) -> p h a d", a=2, d=32)
            c = 0
            for n_, (bank, col0, nh, g_ap, g_tok) in enumerate(srcs):
                w_ = nh * 64
                self.act(b1[:, c:c + w_], self.pf[bank][:, col0:col0 + w_], AF.Square, r=[("pf", bank)],
                         **(dict(w=[b1t]) if n_ == 0 else dict(pw=[b1t])))
                c += w_
            self.E("dve", "tensor_reduce", out=ss[:, 0:HT_], in_=v3(b1, 0, WQ), axis=AX.X, op=ALU.add, r=[b1t], w=[sst])
            self.act(ss[:, 0:HT_], ss[:, 0:HT_], AF.Sqrt, r=[sst], w=[sst], scale=1.0 / 64, bias=EPS)
            self.E("dve", "reciprocal", out=ss[:, 0:HT_], in_=ss[:, 0:HT_], r=[sst], w=[sst])
            c = 0
            h0 = 0
            for n_, (bank, col0, nh, g_ap, g_tok) in enumerate(srcs):
                w_ = nh * 64
                self.E("dve", "tensor_tensor", out=v3(b2, c, w_),
                       in0=self.pf[bank][:, col0:col0 + w_].rearrange("p (h d) -> p h d", d=64),
                       in1=ss[:, h0:h0 + nh].unsqueeze(2).to_broadcast([128, nh, 64]), op=ALU.mult,
                       r=[("pf", bank), sst], **(dict(w=[b2t]) if n_ == 0 else dict(pw=[b2t])))
                c += w_
                h0 += nh
            c = 0
            first = True
            i_ = 0
            while i_ < len(srcs):
                g_ap, g_tok = srcs[i_][3], srcs[i_][4]
                nh = srcs[i_][2]
                k_ = i_ + 1
                while k_ < len(srcs) and srcs[k_][3] is g_ap:
                    nh += srcs[k_][2]
                    k_ += 1
                w_ = nh * 64
                self.E("dve", "tensor_tensor", out=v3(b1, c, w_), in0=v3(b2, c, w_),
                       in1=g_ap.unsqueeze(1).to_broadcast([128, nh, 64]), op=ALU.mult, r=[b2t, g_tok],
                       **(dict(w=[b1t]) if first else dict(pw=[b1t])))
                first = False
                c += w_
                i_ = k_
            import os
            if "norope" in os.environ.get("PDBG", ""):
                return
            cos_ap = self.cs[:, 0, t, :]
            sin_ap = self.cs[:, 1, t, :]
            cosb = cos_ap.unsqueeze(1).unsqueeze(1).to_broadcast([128, HT_, 2, 32])
            sinb = sin_ap.unsqueeze(1).to_broadcast([128, HT_, 32])
            self.E("dve", "tensor_tensor", out=v4(b2, 0, WQ), in0=v4(b1, 0, WQ), in1=cosb, op=ALU.mult, r=[b1t, "cs"], w=[b2t])
            self.E("dve", "tensor_tensor", out=v4(b3, 0, WQ)[:, :, 0, :], in0=v4(b1, 0, WQ)[:, :, 1, :], in1=sinb, op=ALU.mult,
                   r=[b1t, "cs"], w=[b3t])
            self.E("dve", "tensor_tensor", out=v4(b3, 0, WQ)[:, :, 1, :], in0=v4(b1, 0, WQ)[:, :, 0, :], in1=sinb, op=ALU.mult,
                   r=[b1t, "cs"], pw=[b3t])
            self.E("dve", "tensor_tensor", out=v4(q_ap, 0, WQ)[:, :, 0, :], in0=v4(b2, 0, WQ)[:, :, 0, :],
                   in1=v4(b3, 0, WQ)[:, :, 0, :], op=ALU.subtract, r=[b2t, b3t], w=[qbt])
            self.E("dve", "tensor_tensor", out=v4(q_ap, 0, WQ)[:, :, 1, :], in0=v4(b2, 0, WQ)[:, :, 1, :],
                   in1=v4(b3, 0, WQ)[:, :, 1, :], op=ALU.add, r=[b2t, b3t], pw=[qbt])
            import os
            if not moba and "noraw" not in os.environ.get("PDBG", ""):
                self.E("dve", "tensor_copy", out=rawbf[t % 2][0], in_=self.pf[4][:, 0:512], r=[("pf", 4)], w=[rawbf[t % 2][1]])

        def tq(t):
            q_ap, qbt = qbf[t % 2]
            qt_, qtt = qTt[t % 2]
            for half in range(2):
                pbt = self.pb[1]
                ptok = ("pb", 1)
                for k in range(8):
                    jb = half * 8 + k
                    if (not moba) and jb >= 12:
                        s_ap, s_tok = rawbf[t % 2]
                        s_in = s_ap[:, (jb - 12) * 128:(jb - 11) * 128]
                    else:
                        s_tok = qbt
                        s_in = q_ap[:, jb * 128:(jb + 1) * 128]
                    if k == 0:
                        self.tr(pbt[:, 0:128], s_in, r=[s_tok], w=[ptok])
                    else:
                        self.S.op("pe", (lambda e, k=k, s_in=s_in: e.transpose(out=pbt[:, k * 128:(k + 1) * 128], in_=s_in,
                                                                              identity=self.ident[:])),
                                  reads=[s_tok, "ident"], pwrites=[ptok])
                self.act(qt_[:, half * 8:(half + 1) * 8, :].rearrange("p a b -> p (a b)"), pbt[:, 0:1024], AF.Copy, r=[ptok],
                         **(dict(w=[qtt]) if half == 0 else dict(pw=[qtt])))
            jb = 0
            while jb < 16:
                dram, row0 = dests[jb]
                k_ = jb + 1
                while k_ < 16 and dests[k_][0] is dram and dests[k_][1] == row0 + (k_ - jb) * 128:
                    k_ += 1
                nb = k_ - jb
                self.dma(dram[row0:row0 + nb * 128, t * 128:(t + 1) * 128].rearrange("(j p) t -> p j t", p=128),
                         qt_[:, jb:k_, :], f"qst{t % 2}", r=[qtt], pw=[("kTq_d",)], q="pool")
                jb = k_

        def mm_vz(t):
            v_ap, vtok = vbf[t % 2]
            z_ap, ztok = zbf[t % 2]
            if moba:
                for ci in (4, 5):
                    bank = ci
                    mm_chunk(t, ci, bank)
                    self.act(v_ap[:, (ci - 4) * 512:(ci - 3) * 512], self.pf[bank][:, 0:512], AF.Copy, r=[("pf", bank)],
                             **(dict(w=[vtok]) if ci == 4 else dict(pw=[vtok])))
                self.dma(self.v_d[t * 128:(t + 1) * 128, :], v_ap, f"vst{t % 2}", r=[vtok], pw=[("v_d",)], q="pool")
            else:
                self.act(v_ap[:, 0:256], self.pf[2][:, 256:512], AF.Copy, r=[("pf", 2)], w=[vtok])
                self.act(v_ap[:, 256:512], self.pf[3][:, 256:512], AF.Copy, r=[("pf", 3)], pw=[vtok])
                self.dma(self.v_d[t * 128:(t + 1) * 128, 0:512], v_ap[:, 0:512], f"vst{t % 2}", r=[vtok], pw=[("v_d",)], q="pool")
                mm_chunk(t, 5, 5)
                self.act(self.g_sb[:, t, :], self.pf[5][:, 0:48], AF.Sigmoid, r=[("pf", 5)], pw=["g_sb"])
            for ci in (6, 7):
                bank = ci - 2
                mm_chunk(t, ci, bank)
                self.act(z_ap[:, (ci - 6) * 512:(ci - 5) * 512], self.pf[bank][:, 0:512], AF.Silu, r=[("pf", bank)],
                         **(dict(w=[ztok]) if ci == 6 else dict(pw=[ztok])))
            self.dma(self.zs_d[t * 128:(t + 1) * 128, :], z_ap, f"zst{t % 2}", r=[ztok], pw=[("zs_d",)], q="pool")

        import os
        dbg = os.environ.get("PDBG", "")
        pre_a(0)
        pre_b(0)
        for t in range(NT):
            for (ci, bank) in qk_chunks:
                mm_chunk(t, ci, bank)
            if t + 1 < NT:
                pre_a(t + 1)
            if "nochain" not in dbg:
                chain(t)
            if t + 1 < NT:
                pre_b(t + 1)
            if "novz" not in dbg:
                mm_vz(t)
            if t >= 1 and "notq" not in dbg:
                tq(t - 1)
        if "notq" not in dbg:
            tq(NT - 1)

    def attention(self, qTa, qtok, kTa, ktok, Va, vtok, vw, pairs_fn, fin, pts, tag, side=None, side_delay=0):
        NT = self.NT
        qtoks = list(qtok) if isinstance(qtok, list) else [qtok]
        units = []
        for qt in range(NT):
            pairs = pairs_fn(qt)
            ng = (len(pairs) + 3) // 4
            for gi in range(ng):
                units.append((qt, pairs[gi * 4:(gi + 1) * 4], gi == 0, gi == ng - 1))
        n = len(units)
        NBK = len(pts)
        SK = NBK - 1
        banks = [0, 1, 2, 3][:NBK]
        u0 = getattr(self, "_u", 0)

        def qk(i):
            qt, grp, _, _ = units[i]
            si = banks[(u0 + i) % NBK]
            ps = self.pf[si]
            pstok = ("pf", si)
            firstw = True
            for jx, (kt, bias) in enumerate(grp):
                o_ap = ps[:, jx * 128:(jx + 1) * 128]
                self.mm(o_ap, kTa[:, kt * 128:(kt + 1) * 128], qTa[:, qt * 128:(qt + 1) * 128], True, bias is None,
                        r=[ktok] + qtoks, w=[pstok] if firstw else ())
                firstw = False
                if bias is not None:
                    self.mm(o_ap, self.ident[:], bias, False, True, r=["ident", "tri", "atri", "cmpm"])
            last = self.S.q["pe"][-1]
            self.S.tw[pstok] = {last.stream: last}

        def pv(i):
            qt, grp, isf, isl = units[i]
            si = banks[(u0 + i) % NBK]
            ps = self.pf[si]
            pstok = ("pf", si)
            pt_ap, pttok = pts[(u0 + i) % NBK]
            po = self.pf[4 + qt % 2]
            potok = ("pf", 4 + qt % 2)
            nn = len(grp) * 128
            self.act(pt_ap[:, 0:nn], ps[:, 0:nn], AF.Exp, r=[pstok], w=[pttok], scale=0.125)
            for jx, (kt, bias) in enumerate(grp):
                is_first = isf and jx == 0
                is_last = isl and jx == len(grp) - 1
                self.mm(po[:, 0:vw], pt_ap[:, jx * 128:(jx + 1) * 128], Va[:, kt, 0:vw], is_first, is_last,
                        r=[pttok, vtok], w=[potok] if is_first else ())
            last = self.S.q["pe"][-1]
            self.S.tw[potok] = {last.stream: last}
            if isl:
                fin(qt, po, potok)

        for i in range(n + SK):
            if i < n:
                qk(i)
            if i >= SK:
                pv(i - SK)
                if side is not None and i - SK >= side_delay:
                    next(side, None)
        self._u = u0 + n

    def phase_A_moba(self, li):
        self.phase()
        T, NT = self.T, self.NT
        NB = T // 256
        o_all = self.big[:, 0:NT * D].rearrange("p (t f) -> p t f", f=D)
        qTa = [self.ph(f"qTa{i}", [128, T], BF16) for i in range(2)]
        kTa = [self.ph(f"kTa{i}", [128, T], BF16) for i in range(2)]
        Va = [self.ph(f"Va{i}", [128, NT, 65], BF16) for i in range(2)]
        pts = [self.ph(f"pt{i}", [128, 512], BF16) for i in range(4)]
        kmf, kmft = self.ph("kmf", [128, 16], F32)
        kmT, kmTt = self.ph("kmT", [128, 16], BF16)
        gt_, gtt = self.ph("gate", [128, 16], F32)
        m8, m8t = self.ph("m8", [128, 8], F32)
        bq, bqt = self.ph("bq", [128, 128], BF16)
        rz, rzt = self.ph("rz", [128, 1], F32)
        self.E("dve", "memset", kmT, 0.0, w=[kmTt])
        self.E("dve", "memset", bq, 0.0, w=[bqt])
        for i in range(2):
            q_ap, qtok = qTa[i]
            k_ap, ktok = kTa[i]
            v_ap, vtok = Va[i]
            self.E("pool", "memset", q_ap, 0.0, w=[qtok])
            self.E("pool", "memset", k_ap[0:64, :], 0.0, w=[ktok])
            ke = k_ap[64:128, :]
            self.E("pool", "memset", ke, NEGB, r=[ktok], w=[ktok])
            self.E("pool", "affine_select", out=ke, in_=ke, pattern=[[1, T]], compare_op=ALU.is_ge, fill=0.0, base=0,
                   channel_multiplier=-256, r=[ktok], w=[ktok])
            self.E("pool", "affine_select", out=ke, in_=ke, pattern=[[-1, T]], compare_op=ALU.is_ge, fill=0.0, base=255,
                   channel_multiplier=256, r=[ktok], w=[ktok])
            self.E("dve", "memset", v_ap, 1.0, w=[vtok])

        def load_head(h):
            i = h % 2
            q_ap, qtok = qTa[i]
            k_ap, ktok = kTa[i]
            v_ap, vtok = Va[i]
            self.dma(q_ap[0:64, :], self.qT_d[h * 64:(h + 1) * 64, :], f"aq{i}", r=[("kTq_d",)], w=[qtok])
            self.dma(k_ap[0:64, :], self.kT_d[h * 64:(h + 1) * 64, :], f"ak{i}", r=[("kTq_d",)], w=[ktok])
            for c in range(0, NT, 8):
                n = min(8, NT - c)
                self.dma(v_ap[:, c:c + n, 0:64],
                         self.v_d[c * 128:(c + n) * 128, h * 64:(h + 1) * 64].rearrange("(t p) d -> p t d", p=128),
                         f"av{i}", r=[("v_d",)], **(dict(w=[vtok]) if c == 0 else dict(pw=[vtok])))

        def gate_steps(h):
            i = h % 2
            q_ap, qtok = qTa[i]
            k_ap, ktok = kTa[i]
            self.E("dve", "tensor_reduce", out=kmf[0:64, 0:NB], in_=k_ap[0:64, :].rearrange("p (n k) -> p n k", k=256),
                   axis=AX.X, op=ALU.add, r=[ktok], w=[kmft])
            self.E("dve", "tensor_scalar", out=kmT[0:64, 0:NB], in0=kmf[0:64, 0:NB], scalar1=1.0 / 256, scalar2=None,
                   op0=ALU.mult, r=[kmft], w=[kmTt])
            self.E("dve", "memset", gt_, -1.0e30, w=[gtt])
            yield
            for qt in range(NT):
                b = qt // 2
                if b <= 3:
                    continue
                pg = self.pb[0][:, 0:32].bitcast(F32)
                self.mm(pg[:, 0:16], q_ap[:, qt * 128:(qt + 1) * 128], kmT, True, True, r=[qtok, kmTt], w=[("pb", 0)])
                self.E("dve", "tensor_copy", out=gt_[:, 0:b], in_=pg[:, 0:b], r=[("pb", 0), gtt], w=[gtt])
                self.E("dve", "max", out=m8, in_=gt_, r=[gtt], w=[m8t])
                self.E("dve", "tensor_scalar", out=bq[:, 64:80], in0=gt_, scalar1=m8[:, 2:3], scalar2=1.0, op0=ALU.is_ge,
                       op1=ALU.subtract, r=[gtt, m8t, bqt], w=[bqt])
                self.E("dve", "memset", bq[:, 64 + b:65 + b], 0.0, r=[bqt], w=[bqt])
                yield
                yield
                pbt = self.pb[1]
                self.tr(pbt[:, 0:128], bq, r=[bqt], w=[("pb", 1)])
                self.E("dve", "tensor_copy", out=q_ap[64:128, qt * 128:(qt + 1) * 128], in_=pbt[64:128, 0:128],
                       r=[("pb", 1), qtok], w=[qtok])
                yield

        load_head(0)
        for _ in gate_steps(0):
            pass
        for h in range(H):
            side = None
            if h + 1 < H:
                load_head(h + 1)
                side = gate_steps(h + 1)
            i = h % 2
            q_ap, qtok = qTa[i]
            k_ap, ktok = kTa[i]
            v_ap, vtok = Va[i]

            def pairs_fn(qt):
                return [(kt, (self.tri[:] if kt == qt else None)) for kt in range(qt + 1)]

            def fin(qt, po, potok, h=h):
                self.E("dve", "reciprocal", out=rz, in_=po[:, 64:65], r=[potok], w=[rzt])
                self.E("dve", "tensor_scalar", out=o_all[:, qt, h * 64:(h + 1) * 64], in0=po[:, 0:64], scalar1=rz[:, 0:1],
                       scalar2=None, op0=ALU.mult, r=[potok, rzt], pw=["big"])

            self.attention(q_ap, qtok, k_ap, ktok, v_ap, vtok, 65, pairs_fn, fin, pts, f"m{h}", side=side, side_delay=min(24, self.NT))
            if side is not None:
                for _ in side:
                    pass

    def phase_C(self, li):
        self.phase()
        T, NT = self.T, self.NT
        j = li // 2
        xkp, xkpt = self.ph("xkp", [128, T + 16], BF16)
        xk16, xk16t = self.ph("xk16", [128, 2, 16, T // 16], BF16)
        W1d, W1t = self.ph("W1d", [128, 32, 256], BF16)
        w1s = [self.ph(f"w1s{i}", [128, 8, 256], F32) for i in range(2)]
        W2b, W2t = self.ph("W2b", [128, 2, 64], BF16)
        w2s, w2st = self.ph("w2s", [128, 2, 64], F32)
        peT, peTt = self.ph("peT", [128, 32], BF16)
        pes, pest = self.ph("pes", [128, 32], F32)
        bh, bht = self.ph("bh", [128, 2], F32)
        NC, NCT = self.NC, self.NCT
        xx, xxt = self.ph("xx", [128, NC], F32)
        x2, x2t = self.ph("x2", [128, NC], F32)
        sg, sgt = self.ph("sg", [128, NC], F32)
        gl, glt = self.ph("gl", [128, 2, NC], BF16)
        kcb, kcbt = self.ph("kcb", [128, 128], BF16)
        tmp = [self.ph(n, [128, 64], F32) for n in ("c_sq", "c_ss", "c_A", "c_B", "c_t13", "c_tsw")]
        self.E("pool", "memset", xkp, 0.0, w=[xkpt])
        self.E("pool", "memset", W1d, 0.0, w=[W1t])
        self.E("pool", "memset", peT, 0.0, w=[peTt])
        self.E("pool", "memset", kcb, 0.0, w=[kcbt])
        for kv in range(2):
            w1 = self.nsa_cmp_w1[j, kv].rearrange("(l d) j -> d l j", d=64)
            for c in range(4):
                s_ap, s_tok = w1s[c % 2]
                self.dma(s_ap[0:64], w1[:, c * 8:(c + 1) * 8, :], f"w1s{c % 2}", w=[s_tok])
                self.E("dve", "tensor_copy", out=W1d[0:64, c * 8:(c + 1) * 8, :], in_=s_ap[0:64], r=[s_tok],
                       **(dict(w=[W1t]) if c == 0 else dict(pw=[W1t])))
            self.dma(w2s, self.nsa_cmp_w2[j, kv].rearrange("(c p) d -> p c d", p=128), "w2s", w=[w2st])
            self.E("dve", "tensor_copy", out=W2b, in_=w2s, r=[w2st], w=[W2t])
            for q4 in range(4):
                self.dma(pes[0:64, q4 * 8:(q4 + 1) * 8], self.nsa_cmp_pe[j, kv, q4 * 8:(q4 + 1) * 8, :].rearrange("l d -> d l"),
                         "pes", **(dict(w=[pest]) if q4 == 0 else dict(pw=[pest])), allow_slow_non_contiguous=True)
            self.E("dve", "tensor_copy", out=peT[0:64, :], in_=pes[0:64, :], r=[pest], w=[peTt])
            pbias = self.pf[3]
            for jc in range(2):
                for l in range(32):
                    self.mm(pbias[:, jc:jc + 1], W1d[:, l, jc * 128:(jc + 1) * 128], peT[:, l:l + 1], l == 0, l == 31,
                            r=[W1t, peTt], w=[("pf", 3)] if (l == 0 and jc == 0) else ())
            last = self.S.q["pe"][-1]
            self.S.tw[("pf", 3)] = {last.stream: last}
            self.E("dve", "tensor_copy", out=bh, in_=pbias[:, 0:2], r=[("pf", 3)], w=[bht])
            for g in range(4):
                row0 = 512 + kv * 256 + g * 64
                self.dma(xkp[0:64, 0:T], self.kT_d[row0:row0 + 64, :], "xkp", r=[("kTq_d",)], w=[xkpt])
                self.E("dve", "tensor_copy", out=xk16[:, 0], in_=xkp[:, 0:T].rearrange("p (m r) -> p r m", r=16), r=[xkpt], w=[xk16t])
                self.E("dve", "tensor_copy", out=xk16[:, 1], in_=xkp[:, 16:T + 16].rearrange("p (m r) -> p r m", r=16), r=[xkpt],
                       pw=[xk16t])
                for jc in range(2):
                    ph_ = self.pf[jc]
                    for l in range(32):
                        self.mm(ph_[:, 0:NC], W1d[:, l, jc * 128:(jc + 1) * 128], xk16[:, l // 16, l % 16, :], l == 0, l == 31,
                                r=[W1t, xk16t], w=[("pf", jc)] if l == 0 else ())
                    last = self.S.q["pe"][-1]
                    self.S.tw[("pf", jc)] = {last.stream: last}
                    self.act(xx, ph_[:, 0:NC], AF.Identity, r=[("pf", jc), bht], w=[xxt], bias=bh[:, jc:jc + 1])
                    self.E("dve", "tensor_tensor", out=x2, in0=xx, in1=xx, op=ALU.mult, r=[xxt], w=[x2t])
                    self.E("dve", "tensor_scalar", out=x2, in0=x2, scalar1=0.044715, scalar2=1.0, op0=ALU.mult, op1=ALU.add,
                           r=[x2t], w=[x2t])
                    self.E("dve", "tensor_tensor", out=x2, in0=x2, in1=xx, op=ALU.mult, r=[x2t, xxt], w=[x2t])
                    self.act(sg, x2, AF.Sigmoid, r=[x2t], w=[sgt], scale=1.5957691216057308)
                    self.E("dve", "tensor_tensor", out=gl[:, jc, :], in0=xx, in1=sg, op=ALU.mult, r=[xxt, sgt],
                           **(dict(w=[glt]) if jc == 0 else dict(pw=[glt])))
                for ct in range(NCT):
                    pk = self.pf[2]
                    for jc in range(2):
                        self.mm(pk[:, 0:64], gl[:, jc, ct * 128:(ct + 1) * 128], W2b[:, jc, :], jc == 0, jc == 1,
                                r=[glt, W2t], w=[("pf", 2)] if jc == 0 else ())
                    last = self.S.q["pe"][-1]
                    self.S.tw[("pf", 2)] = {last.stream: last}
                    if kv == 0:
                        self.norm_rope(pk[:, 0:64], ("pf", 2), 1, self.gk[:, 0, :], "gk", self.csc[:, 0, ct, :],
                                       self.csc[:, 1, ct, :], "csc", kcb, kcbt, tmp)
                        pbt = self.pb[1]
                        self.tr(pbt[:, 0:128], kcb, r=[kcbt], w=[("pb", 1)])
                        self.E("dve", "tensor_copy", out=self.kcTa[:, g, ct * 128:(ct + 1) * 128], in_=pbt[:, 0:128],
                               r=[("pb", 1)], pw=["kcTa"])
                    else:
                        self.act(self.vca[:, g, ct, 0:64], pk[:, 0:64], AF.Copy, r=[("pf", 2)], pw=["vca"])

    def phase_A_nsa(self, li):
        self.phase()
        T, NT = self.T, self.NT
        o_all = self.big[:, 0:NT * D].rearrange("p (t f) -> p t f", f=D)
        qTa = [self.ph(f"nqTa{i}", [128, T], BF16) for i in range(4)]
        ksTa, kst = self.ph("ksTa", [128, T], BF16)
        kwTa, kwt = self.ph("kwTa", [128, T], BF16)
        Vs, Vst = self.ph("Vs", [128, NT, 65], BF16)
        Vw, Vwt = self.ph("Vw", [128, NT, 65], BF16)
        pts = [self.ph(f"npt{i}", [128, 512], BF16) for i in range(4)]
        oacc, oacct = self.ph("oacc", [128, NT, 64], F32)
        imp, impt = self.ph("imp", [128, NT, 64], F32)
        sc, sct = self.ph("sc", [128, 64], F32)
        sc2, sc2t = self.ph("sc2", [128, 64], F32)
        m8a, m8at = self.ph("m8a", [128, 8], F32)
        m8b, m8bt = self.ph("m8b", [128, 8], F32)
        bs, bst = self.ph("bs", [128, 128], BF16)
        rz, rzt = self.ph("nrz", [128, 1], F32)
        cf, cft = self.ph("ncf", [128, 1], F32)
        for hl, (q_ap, qtok) in enumerate(qTa):
            self.E("pool", "memset", q_ap, 0.0, w=[qtok, ("qhi", hl)])
        self.E("pool", "memset", kwTa, 0.0, w=[kwt])
        self.E("pool", "memset", ksTa[0:64, :], 0.0, w=[kst])
        ke = ksTa[64:128, :]
        self.E("pool", "memset", ke, NEGB, r=[kst], w=[kst])
        self.E("pool", "affine_select", out=ke, in_=ke, pattern=[[1, T]], compare_op=ALU.is_ge, fill=0.0, base=0,
               channel_multiplier=-64, r=[kst], w=[kst])
        self.E("pool", "affine_select", out=ke, in_=ke, pattern=[[-1, T]], compare_op=ALU.is_ge, fill=0.0, base=63,
               channel_multiplier=64, r=[kst], w=[kst])
        self.E("dve", "memset", Vs, 1.0, w=[Vst])
        self.E("dve", "memset", Vw, 1.0, w=[Vwt])
        self.E("dve", "memset", bs, 0.0, w=[bst])
        self.E("dve", "memset", imp, 0.0, w=[impt])

        for g in range(4):
            for hl in range(4):
                h = g * 4 + hl
                q_ap, qtok = qTa[hl]
                self.dma(q_ap[0:64, :], self.qT_d[h * 64:(h + 1) * 64, :], f"nq{hl}", r=[("kTq_d",)], w=[qtok])
            self.dma(ksTa[0:64, :], self.kT_d[g * 64:(g + 1) * 64, :], "nks", r=[("kTq_d",)], w=[kst])
            self.dma(kwTa[0:64, :], self.kT_d[256 + g * 64:256 + (g + 1) * 64, :], "nkw", r=[("kTq_d",)], w=[kwt])
            for (V_, Vt_, c0, nm) in ((Vs, Vst, g * 64, "nvs"), (Vw, Vwt, 256 + g * 64, "nvw")):
                for c in range(0, NT, 8):
                    n = min(8, NT - c)
                    self.dma(V_[:, c:c + n, 0:64],
                             self.v_d[c * 128:(c + n) * 128, c0:c0 + 64].rearrange("(t p) d -> p t d", p=128),
                             nm, r=[("v_d",)], **(dict(w=[Vt_]) if c == 0 else dict(pw=[Vt_])))

            def cmp_pairs(qt):
                out = []
                for kt in range(self.NCT):
                    dl = qt - 16 * kt
                    if dl < 0:
                        continue
                    out.append((kt, self.cmpm[:, dl, :] if dl <= 16 else None))
                return out

            def sel_A(qt):
                self.E("dve", "tensor_tensor", out=sc, in0=imp[:, qt, :], in1=self.msel[:, qt, :], op=ALU.add,
                       r=[impt, "msel"], w=[sct])
                self.E("dve", "max", out=m8a, in_=sc, r=[sct], w=[m8at])
                self.E("dve", "tensor_scalar", out=sc2, in0=sc, scalar1=m8a[:, 7:8], scalar2=-6.0e4, op0=ALU.is_ge, op1=ALU.mult,
                       r=[sct, m8at], w=[sc2t])
                self.E("dve", "tensor_tensor", out=sc2, in0=sc2, in1=sc, op=ALU.add, r=[sc2t, sct], w=[sc2t])
                self.E("dve", "max", out=m8b, in_=sc2, r=[sc2t], w=[m8bt])
                self.E("dve", "tensor_scalar", out=bs[:, 64:128], in0=sc, scalar1=m8b[:, 7:8], scalar2=1.0, op0=ALU.is_ge,
                       op1=ALU.subtract, r=[sct, m8bt, bst], w=[bst])

            def sel_B(qt):
                pbt = self.pb[1]
                self.tr(pbt[:, 0:128], bs, r=[bst], w=[("pb", 1)])
                for hl2 in range(4):
                    q2, _ = qTa[hl2]
                    self.E("dve", "tensor_copy", out=q2[64:128, qt * 128:(qt + 1) * 128], in_=pbt[64:128, 0:128],
                           r=[("pb", 1)], pw=[("qhi", hl2)])

            for hl in range(4):
                h = g * 4 + hl
                q_ap, qtok = qTa[hl]

                def fin_c(qt, po, potok, h=h, hl=hl):
                    self.E("dve", "tensor_scalar", out=rz, in0=po[:, 64:65], scalar1=1.0e-30, scalar2=None, op0=ALU.max,
                           r=[potok], w=[rzt])
                    self.E("dve", "reciprocal", out=rz, in_=rz, r=[rzt], w=[rzt])
                    self.E("dve", "tensor_tensor", out=cf, in0=rz, in1=self.g_sb[:, qt, h * 3:h * 3 + 1], op=ALU.mult,
                           r=[rzt, "g_sb"], w=[cft])
                    self.E("dve", "tensor_scalar", out=o_all[:, qt, h * 64:(h + 1) * 64], in0=po[:, 0:64], scalar1=cf[:, 0:1],
                           scalar2=None, op0=ALU.mult, r=[potok, cft], pw=["big"])
                    if hl == 0:
                        self.E("dve", "tensor_scalar", out=imp[:, qt, 0:63], in0=po[:, 65:128], scalar1=rz[:, 0:1],
                               scalar2=None, op0=ALU.mult, r=[potok, rzt], pw=[impt])
                    else:
                        self.E("dve", "scalar_tensor_tensor", out=imp[:, qt, 0:63], in0=po[:, 65:128], scalar=rz[:, 0:1],
                               in1=imp[:, qt, 0:63], op0=ALU.mult, op1=ALU.add, r=[potok, rzt, impt], pw=[impt])
                    if hl == 3:
                        if qt > 0:
                            sel_B(qt - 1)
                        sel_A(qt)

                self.attention(q_ap, qtok, self.kcTa[:, g, :], "kcTa", self.vca[:, g], "vca", 128, cmp_pairs, fin_c, pts,
                               f"c{h}")
            sel_B(NT - 1)

            if getattr(self, "stop", "") == "sel":
                break
            def sel_pairs(qt):
                return [(kt, (self.tri[:] if kt == qt else None)) for kt in range(qt + 1)]

            def win_pairs(qt):
                out = []
                for kt in range(max(0, qt - 4), qt + 1):
                    b = self.tri[:] if kt == qt else (self.atri[:] if kt == qt - 4 else None)
                    out.append((kt, b))
                return out

            for hl in range(4):
                h = g * 4 + hl
                q_ap, qtok = qTa[hl]

                def fin_s(qt, po, potok, h=h):
                    self.E("dve", "reciprocal", out=rz, in_=po[:, 64:65], r=[potok], w=[rzt])
                    self.E("dve", "tensor_tensor", out=cf, in0=rz, in1=self.g_sb[:, qt, h * 3 + 1:h * 3 + 2], op=ALU.mult,
                           r=[rzt, "g_sb"], w=[cft])
                    self.E("dve", "scalar_tensor_tensor", out=oacc[:, qt, :], in0=po[:, 0:64], scalar=cf[:, 0:1],
                           in1=o_all[:, qt, h * 64:(h + 1) * 64], op0=ALU.mult, op1=ALU.add, r=[potok, cft, "big"],
                           pw=[oacct])

                def fin_w(qt, po, potok, h=h):
                    self.E("dve", "reciprocal", out=rz, in_=po[:, 64:65], r=[potok], w=[rzt])
                    self.E("dve", "tensor_tensor", out=cf, in0=rz, in1=self.g_sb[:, qt, h * 3 + 2:h * 3 + 3], op=ALU.mult,
                           r=[rzt, "g_sb"], w=[cft])
                    self.E("dve", "scalar_tensor_tensor", out=o_all[:, qt, h * 64:(h + 1) * 64], in0=po[:, 0:64],
                           scalar=cf[:, 0:1], in1=oacc[:, qt, :], op0=ALU.mult, op1=ALU.add, r=[potok, cft, oacct],
                           pw=["big"])

                self.attention(q_ap, [qtok, ("qhi", hl)], ksTa, kst, Vs, Vst, 65, sel_pairs, fin_s, pts, f"s{h}")
                self.attention(q_ap, qtok, kwTa, kwt, Vw, Vwt, 65, win_pairs, fin_w, pts, f"w{h}")

    def phase_O(self, li, src):
        self.phase()
        NT = self.NT
        j = li // 2
        moba = (li % 2 == 0)
        o_all = self.big[:, 0:NT * D].rearrange("p (t f) -> p t f", f=D)
        Wo, Wot = self.ph("Wo", [128, 8, D], BF16)
        Wg, Wgt = self.ph("Wg", [128, 8, D], BF16)
        Wp, Wpt = self.ph("Wp", [128, 2, D], BF16)
        self.load_w(Wo, Wot, (self.moba_w_out if moba else self.nsa_w_out)[j], D, None, None)
        self.load_gcol(self.gcol2, "gcol2", self.ple_gate_gain[li])
        self.load_w(Wg, Wgt, self.ple_w_gate[li], D, self.gcol2, "gcol2")
        self.load_w(Wp, Wpt, self.ple_w_proj[li], D, None, None, nk=2)
        xt = [self.ph(f"oxt{i}", [128, D], F32) for i in range(2)]
        zt = [self.ph(f"ozt{i}", [128, D], BF16) for i in range(2)]
        pt_ = [self.ph(f"opt{i}", [128, PLE], F32) for i in range(2)]
        og = [self.ph(f"og{i}", [128, D], BF16) for i in range(2)]
        ogT = [self.ph(f"ogT{i}", [128, 8, 128], BF16) for i in range(2)]
        x1 = [self.ph(f"x1_{i}", [128, D], F32) for i in range(2)]
        sqs, sqst = self.ph("osq", [128, D], F32)
        rs = [self.ph(f"ors{i}", [128, 1], F32) for i in range(2)]
        xn, xnt = self.ph("xn", [128, D], BF16)
        xnT, xnTt = self.ph("xnT", [128, 8, 128], BF16)
        gate, gatet = self.ph("gatef", [128, D], F32)
        pbf, pbft = self.ph("pbf", [128, PLE], BF16)
        pT, pTt = self.ph("pT", [128, 2, 128], BF16)
        x2 = [self.ph(f"x2_{i}", [128, D], F32) for i in range(2)]

        def S1(t):
            x_ap, xtok = xt[t % 2]
            z_ap, ztok = zt[t % 2]
            p_ap, ptok = pt_[t % 2]
            og_ap, ogt = og[t % 2]
            ogT_ap, ogTt = ogT[t % 2]
            x1_ap, x1t = x1[t % 2]
            self.dma(x_ap, src[t * 128:(t + 1) * 128, :], f"oxl{t % 2}", r=[("y", t)], w=[xtok])
            self.dma(z_ap, self.zs_d[t * 128:(t + 1) * 128, :], f"ozl{t % 2}", r=[("zs_d",)], w=[ztok])
            self.dma(p_ap, self.p_in[li, t * 128:(t + 1) * 128, :], f"opl{t % 2}", w=[ptok])
            self.E("dve", "tensor_tensor", out=og_ap, in0=o_all[:, t, :], in1=z_ap, op=ALU.mult, r=["big", ztok], w=[ogt])
            self.transpose8(og_ap, ogt, ogT_ap, ogTt, 0)
            for c in range(2):
                ps = self.pf[c]
                for k in range(8):
                    self.mm(ps[:, :], ogT_ap[:, k, :], Wo[:, k, c * 512:(c + 1) * 512], k == 0, k == 7, r=[ogTt, Wot],
                            w=[("pf", c)] if k == 0 else ())
                last = self.S.q["pe"][-1]
                self.S.tw[("pf", c)] = {last.stream: last}
                self.E("dve", "tensor_tensor", out=x1_ap[:, c * 512:(c + 1) * 512], in0=x_ap[:, c * 512:(c + 1) * 512],
                       in1=ps[:, :], op=ALU.add, r=[xtok, ("pf", c)], **(dict(w=[x1t]) if c == 0 else dict(pw=[x1t])))

        def S2a(t):
            x1_ap, x1t = x1[t % 2]
            r_ap, rtok = rs[t % 2]
            self.rstd_of(x1_ap, x1t, D, sqs, sqst, r_ap, rtok)
            self.E("dve", "tensor_scalar", out=xn, in0=x1_ap, scalar1=r_ap[:, 0:1], scalar2=None, op0=ALU.mult,
                   r=[x1t, rtok], w=[xnt])

        def S2b(t):
            x1_ap, x1t = x1[t % 2]
            p_ap, ptok = pt_[t % 2]
            self.transpose8(xn, xnt, xnT, xnTt, 0)
            for c in range(2):
                ps = self.pf[2 + c]
                for k in range(8):
                    self.mm(ps[:, :], xnT[:, k, :], Wg[:, k, c * 512:(c + 1) * 512], k == 0, k == 7, r=[xnTt, Wgt],
                            w=[("pf", 2 + c)] if k == 0 else ())
                last = self.S.q["pe"][-1]
                self.S.tw[("pf", 2 + c)] = {last.stream: last}
                self.act(gate[:, c * 512:(c + 1) * 512], ps[:, :], AF.Sigmoid, r=[("pf", 2 + c)],
                         **(dict(w=[gatet]) if c == 0 else dict(pw=[gatet])))
            self.E("dve", "tensor_copy", out=pbf, in_=p_ap, r=[ptok], w=[pbft])
            self.transpose8(pbf, pbft, pT, pTt, 1, nblk=2)
            x2_ap, x2tok = x2[t % 2]
            for c in range(2):
                ps = self.pf[4 + c]
                for k in range(2):
                    self.mm(ps[:, :], pT[:, k, :], Wp[:, k, c * 512:(c + 1) * 512], k == 0, k == 1, r=[pTt, Wpt],
                            w=[("pf", 4 + c)] if k == 0 else ())
                last = self.S.q["pe"][-1]
                self.S.tw[("pf", 4 + c)] = {last.stream: last}
                sl = slice(c * 512, (c + 1) * 512)
                self.E("dve", "tensor_tensor", out=x2_ap[:, sl], in0=gate[:, sl], in1=ps[:, :], op=ALU.mult,
                       r=[gatet, ("pf", 4 + c)], **(dict(w=[x2tok]) if c == 0 else dict(pw=[x2tok])))
                self.E("dve", "tensor_tensor", out=x2_ap[:, sl], in0=x2_ap[:, sl], in1=x1_ap[:, sl], op=ALU.add,
                       r=[x2tok, x1t], pw=[x2tok])
            self.dma(self.y[t * 128:(t + 1) * 128, :], x2_ap, f"oyst{t % 2}", r=[x2tok], w=[("y", t)], q="pool")

        S1(0)
        for t in range(NT):
            S2a(t)
            if t + 1 < NT:
                S1(t + 1)
            S2b(t)


def rope_tables(T):
    NT = T // 128
    half = 32
    inv_freq = (10000.0 ** (-np.arange(half, dtype=np.float32) / half)).astype(np.float32)
    pos = (np.arange(NT)[None, :] * 128 + np.arange(128)[:, None]).astype(np.float32)
    ang = pos[:, :, None] * inv_freq[None, None, :]
    cs = np.stack([np.cos(ang), np.sin(ang)], 0).astype(np.float32)
    NCT = max(1, (T // 16) // 128)
    posc = (16.0 * (np.arange(NCT)[None, :] * 128 + np.arange(128)[:, None]) + 31.0).astype(np.float32)
    angc = posc[:, :, None] * inv_freq[None, None, :]
    csc = np.stack([np.cos(angc), np.sin(angc)], 0).astype(np.float32)
    return cs, csc


_CACHE = {}


def run(inputs, T, layers, n_cores):
    key = (T, tuple(layers))
    if key not in _CACHE:
        _CACHE[key] = Builder(T, list(layers)).build()
    nc = _CACHE[key]
    cs, csc = rope_tables(T)
    shared = {k: np.ascontiguousarray(v, dtype=np.float32) for k, v in inputs.items() if k not in ("x", "p")}
    shared["rope_cs"] = cs
    shared["rope_cs_c"] = csc
    in_maps = []
    for b in range(n_cores):
        m = dict(shared)
        m["x"] = np.ascontiguousarray(inputs["x"][b], dtype=np.float32)
        m["p"] = np.ascontiguousarray(inputs["p"][:, b], dtype=np.float32)
        in_maps.append(m)
    res = run_bass_kernel_spmd(nc, in_maps, core_ids=list(range(n_cores)))
    return np.stack([r["y"] for r in res.results], 0).astype(np.float32)


def kernel(**inputs):
    return run(inputs, 4096, [0, 1, 2, 3], 8)
```

```python
import math
from contextlib import ExitStack

import numpy as np
import concourse.bass as bass
import concourse.mybir as mybir
from concourse.bass_utils import run_bass_kernel_spmd

F32 = mybir.dt.float32
BF16 = mybir.dt.bfloat16
I32 = mybir.dt.int32
AF = mybir.ActivationFunctionType
ALU = mybir.AluOpType
AX = mybir.AxisListType

D = 1024
H = 16
HD = 64
PLE = 256
EPS = 1e-6
NSA_IN = 3632
NEGB = 30000.0


class Op:
    __slots__ = ("eng", "fn", "deps", "needed", "sigval", "stream", "is_dma", "idx")


class Sched:
    ENGS = ("pe", "act", "dve", "pool", "sp")

    def __init__(self):
        self.q = {e: [] for e in self.ENGS}
        self.tw = {}
        self.tr = {}
        self.dma_cnt = {}
        self.nops = 0
        self.last = {}
        self.pending = {}

    def barrier(self):
        for e in self.ENGS:
            self.pending[e] = list(self.last.values())

    def op(self, eng, fn, reads=(), writes=(), pwrites=(), dma=None):
        o = Op()
        o.eng = eng
        o.fn = fn
        o.is_dma = dma is not None
        o.stream = ("dma", dma) if dma is not None else ("eng", eng)
        o.needed = o.is_dma
        o.sigval = None
        o.idx = self.nops
        self.nops += 1
        deps = {}

        def add(d):
            if (not o.is_dma) and (not d.is_dma) and d.eng == "pe" and eng == "pe":
                return
            k = d.stream
            if k not in deps or deps[k].idx < d.idx:
                deps[k] = d

        for d in self.pending.pop(eng, ()):
            add(d)
        for t in reads:
            for d in self.tw.get(t, {}).values():
                add(d)
        for t in writes:
            for d in self.tw.get(t, {}).values():
                add(d)
            for d in self.tr.get(t, {}).values():
                add(d)
        for t in pwrites:
            for d in self.tr.get(t, {}).values():
                add(d)
        o.deps = list(deps.values())
        for d in o.deps:
            d.needed = True
        for t in reads:
            self.tr.setdefault(t, {})[o.stream] = o
        for t in writes:
            self.tw[t] = {o.stream: o}
            self.tr[t] = {}
        for t in pwrites:
            self.tw.setdefault(t, {})[o.stream] = o
        if o.is_dma:
            c = self.dma_cnt.get(dma, 0) + 16
            self.dma_cnt[dma] = c
            o.sigval = c
        self.last[o.stream] = o
        self.q[eng].append(o)
        return o

    def emit(self, nc, stack):
        for e in self.ENGS:
            c = 0
            for o in self.q[e]:
                if not o.is_dma and o.needed:
                    c += 1
                    o.sigval = c
        sems = {}
        for e in self.ENGS:
            sems[("eng", e)] = stack.enter_context(nc.semaphore("s_" + e))
        for d in self.dma_cnt:
            sems[("dma", d)] = stack.enter_context(nc.semaphore("d_" + str(d)))
        block = stack.enter_context(nc.Block())
        q = self.q

        def run(ename, eng):
            seen = {}
            for o in q[ename]:
                for d in o.deps:
                    v = d.sigval
                    if seen.get(d.stream, 0) >= v:
                        continue
                    seen[d.stream] = v
                    eng.wait_ge(sems[d.stream], v)
                ins = o.fn(eng)
                if o.needed:
                    ins.then_inc(sems[o.stream], 16 if o.is_dma else 1)

        @block.tensor
        def _(eng):
            run("pe", eng)

        @block.scalar
        def _(eng):
            run("act", eng)

        @block.vector
        def _(eng):
            run("dve", eng)

        @block.gpsimd
        def _(eng):
            run("pool", eng)

        @block.sync
        def _(eng):
            run("sp", eng)
            for d, c in self.dma_cnt.items():
                eng.wait_ge(sems[("dma", d)], c)


class Builder:
    def __init__(self, T, layers):
        self.T = T
        self.NT = T // 128
        self.NC = T // 16
        self.NCT = max(1, self.NC // 128)
        self.layers = layers
        self.nc = bass.Bass("TRN2", target_bir_lowering=False)
        self.S = Sched()
        self.uid = 0

    def E(self, eng, meth, *a, r=(), w=(), pw=(), **kw):
        self.S.op(eng, lambda e: getattr(e, meth)(*a, **kw), reads=r, writes=w, pwrites=pw)

    def dma(self, out, in_, sem, r=(), w=(), pw=(), q="sp", **kw):
        self.S.op(q, lambda e: e.dma_start(out=out, in_=in_, **kw), reads=r, writes=w, pwrites=pw, dma=sem)

    def mm(self, out, lhsT, rhs, start, stop, r=(), w=()):
        self.S.op("pe", lambda e: e.matmul(out, lhsT=lhsT, rhs=rhs, start=start, stop=stop), reads=r, writes=w)

    def tr(self, out, in_, r=(), w=()):
        ident = self.ident
        self.S.op("pe", lambda e: e.transpose(out=out, in_=in_, identity=ident[:]), reads=tuple(r) + ("ident",), writes=w)

    def act(self, out, in_, func, r=(), w=(), pw=(), **kw):
        self.S.op("act", lambda e: e.activation(out=out, in_=in_, func=func, **kw), reads=r, writes=w, pwrites=pw)

    def persist(self, name, shape, dt):
        return self.st.enter_context(self.nc.sbuf_tensor(name, shape, dt))

    def phase(self):
        self.S.barrier()
        self.aoff = 0
        self.pid = getattr(self, "pid", 0) + 1

    def ph(self, name, shape, dt):
        esz = 4 if dt in (F32, I32) else 2
        n = 1
        for s_ in shape[1:]:
            n *= s_
        nbytes = (n * esz + 63) // 64 * 64
        a = self.aoff
        self.aoff += nbytes
        assert self.aoff <= self.ARENA_BYTES, (name, self.aoff)
        v = self.arena[:, a // 2:(a + n * esz) // 2]
        if esz == 4:
            v = v.bitcast(dt)
        if len(shape) == 3:
            v = v.rearrange("p (a b) -> p a b", b=shape[2])
        elif len(shape) == 4:
            v = v.rearrange("p (a b c) -> p a b c", b=shape[2], c=shape[3])
        return v, ("ph", self.pid, name)

    def build(self):
        nc, T, NT = self.nc, self.T, self.NT
        dt = nc.dram_tensor
        self.x_in = dt("x", [T, D], F32, kind="ExternalInput").ap()
        self.p_in = dt("p", [4, T, PLE], F32, kind="ExternalInput").ap()
        self.norm_gain = dt("norm_gain", [4, D], F32, kind="ExternalInput").ap()
        self.moba_w_in = dt("moba_w_in", [2, D, 4 * D], F32, kind="ExternalInput").ap()
        self.moba_q_gain = dt("moba_q_gain", [2, HD], F32, kind="ExternalInput").ap()
        self.moba_k_gain = dt("moba_k_gain", [2, HD], F32, kind="ExternalInput").ap()
        self.moba_w_out = dt("moba_w_out", [2, D, D], F32, kind="ExternalInput").ap()
        self.nsa_w_in = dt("nsa_w_in", [2, D, NSA_IN], F32, kind="ExternalInput").ap()
        self.nsa_q_gain = dt("nsa_q_gain", [2, HD], F32, kind="ExternalInput").ap()
        self.nsa_k_gain = dt("nsa_k_gain", [2, 3, HD], F32, kind="ExternalInput").ap()
        self.nsa_cmp_pe = dt("nsa_cmp_pe", [2, 2, 32, HD], F32, kind="ExternalInput").ap()
        self.nsa_cmp_w1 = dt("nsa_cmp_w1", [2, 2, 2048, 256], F32, kind="ExternalInput").ap()
        self.nsa_cmp_w2 = dt("nsa_cmp_w2", [2, 2, 256, HD], F32, kind="ExternalInput").ap()
        self.nsa_w_out = dt("nsa_w_out", [2, D, D], F32, kind="ExternalInput").ap()
        self.ple_w_proj = dt("ple_w_proj", [4, PLE, D], F32, kind="ExternalInput").ap()
        self.ple_gate_gain = dt("ple_gate_gain", [4, D], F32, kind="ExternalInput").ap()
        self.ple_w_gate = dt("ple_w_gate", [4, D, D], F32, kind="ExternalInput").ap()
        self.rope_cs = dt("rope_cs", [2, 128, NT, 32], F32, kind="ExternalInput").ap()
        self.rope_cs_c = dt("rope_cs_c", [2, 128, self.NCT, 32], F32, kind="ExternalInput").ap()
        self.y = dt("y", [T, D], F32, kind="ExternalOutput").ap()
        self.qT_d = dt("qT_d", [D, T], BF16).ap()
        self.kT_d = dt("kT_d", [D, T], BF16).ap()
        self.v_d = dt("v_d", [T, D], BF16).ap()
        self.zs_d = dt("zs_d", [T, D], BF16).ap()

        with ExitStack() as st:
            self.st = st
            self.ARENA_BYTES = 100 * 1024
            self.arena = self.persist("arena", [128, self.ARENA_BYTES // 2], BF16)
            self.big = self.persist("big", [128, 32768], BF16)
            self.ident = self.persist("ident", [128, 128], BF16)
            self.tri = self.persist("tri", [128, 128], BF16)
            self.atri = self.persist("atri", [128, 128], BF16)
            self.cmpm = self.persist("cmpm", [128, 17, 128], BF16)
            self.cs = self.persist("cs", [128, 2, NT, 32], F32)
            self.csc = self.persist("csc", [128, 2, self.NCT, 32], F32)
            self.g_sb = self.persist("g_sb", [128, NT, 48], F32)
            self.msel = self.persist("msel", [128, NT, 64], F32)
            self.kcTa = self.persist("kcTa", [128, 4, self.NC], BF16)
            self.vca = self.persist("vca", [128, 4, self.NCT, 128], BF16)
            self.gq = self.persist("gq", [128, HD], F32)
            self.gk = self.persist("gk", [128, 3, HD], F32)
            self.gcol = self.persist("gcol", [128, 8], F32)
            self.gcol2 = self.persist("gcol2", [128, 8], F32)
            self.pf = [st.enter_context(nc.psum_tensor(f"pf{i}", [128, 512], F32)) for i in range(6)]
            self.pb = [st.enter_context(nc.psum_tensor(f"pb{i}", [128, 1024], BF16)) for i in range(2)]
            self.consts()
            first = True
            for li in self.layers:
                src = self.x_in if first else self.y
                first = False
                if li % 2 == 0:
                    self.phase_P(li, src, moba=True)
                    self.phase_A_moba(li)
                else:
                    import os
                    stop = os.environ.get("NSA_STOP", "")
                    self.stop = stop
                    self.phase_P(li, src, moba=False)
                    if stop == "P":
                        break
                    self.phase_C(li)
                    if stop == "C":
                        break
                    self.phase_A_nsa(li)
                    if stop in ("cmp", "sel", "A"):
                        break
                self.phase_O(li, src)
            self.S.emit(nc, st)
        return nc

    def consts(self):
        self.phase()
        NT = self.NT
        tf, tft = self.ph("c_tf", [128, 128], F32)
        self.E("pool", "memset", tf, 1.0, w=[tft])
        self.E("pool", "affine_select", out=tf, in_=tf, pattern=[[1, 128]], compare_op=ALU.is_equal, fill=0.0,
               base=0, channel_multiplier=-1, r=[tft], w=[tft])
        self.E("dve", "tensor_copy", out=self.ident[:], in_=tf, r=[tft], w=["ident"])
        tf2, tf2t = self.ph("c_tf2", [128, 128], F32)
        self.E("pool", "memset", tf2, 0.0, w=[tf2t])
        self.E("pool", "affine_select", out=tf2, in_=tf2, pattern=[[1, 128]], compare_op=ALU.is_ge, fill=-NEGB,
               base=0, channel_multiplier=-1, r=[tf2t], w=[tf2t])
        self.E("dve", "tensor_copy", out=self.tri[:], in_=tf2, r=[tf2t], w=["tri"])
        tf3, tf3t = self.ph("c_tf3", [128, 128], F32)
        self.E("pool", "memset", tf3, 0.0, w=[tf3t])
        self.E("pool", "affine_select", out=tf3, in_=tf3, pattern=[[-1, 128]], compare_op=ALU.is_ge, fill=-NEGB,
               base=-1, channel_multiplier=1, r=[tf3t], w=[tf3t])
        self.E("dve", "tensor_copy", out=self.atri[:], in_=tf3, r=[tf3t], w=["atri"])
        tf4, tf4t = self.ph("c_tf4", [128, 17, 128], F32)
        self.E("pool", "memset", tf4, 0.0, w=[tf4t])
        self.E("pool", "affine_select", out=tf4, in_=tf4, pattern=[[128, 17], [1, 128]], compare_op=ALU.is_ge,
               fill=-NEGB, base=-31, channel_multiplier=-16, r=[tf4t], w=[tf4t])
        self.E("dve", "tensor_copy", out=self.cmpm[:], in_=tf4, r=[tf4t], w=["cmpm"])
        self.dma(self.cs[:, 0], self.rope_cs[0], "c_cs", w=["cs"])
        self.dma(self.cs[:, 1], self.rope_cs[1], "c_cs2", pw=["cs"])
        self.dma(self.csc[:, 0], self.rope_cs_c[0], "c_csc", w=["csc"])
        self.dma(self.csc[:, 1], self.rope_cs_c[1], "c_csc2", pw=["csc"])
        m = self.msel
        self.E("pool", "memset", m[:], 0.0, w=["msel"])
        for half in range(2):
            mh = m[half * 64:(half + 1) * 64]
            self.E("pool", "affine_select", out=mh, in_=mh, pattern=[[2, NT], [-1, 64]], compare_op=ALU.is_ge,
                   fill=1.0e4, base=half - 2, channel_multiplier=0, r=["msel"], w=["msel"])
            self.E("pool", "affine_select", out=mh, in_=mh, pattern=[[2, NT], [-1, 64]], compare_op=ALU.is_ge,
                   fill=-1.0e4, base=half, channel_multiplier=0, r=["msel"], w=["msel"])
        self.E("pool", "memset", m[:, :, 0:1], 1.0e4, r=["msel"], w=["msel"])
        ov, ovt = self.ph("c_ov", [128, self.NCT, 64], F32)
        self.E("pool", "memset", ov, 1.0, w=[ovt])
        for ct in range(self.NCT):
            o1 = ov[:, ct, :]
            self.E("pool", "affine_select", out=o1, in_=o1, pattern=[[4, 64]], compare_op=ALU.is_ge, fill=0.0,
                   base=3 - 128 * ct, channel_multiplier=-1, r=[ovt], w=[ovt])
            self.E("pool", "affine_select", out=o1, in_=o1, pattern=[[-4, 64]], compare_op=ALU.is_ge, fill=0.0,
                   base=1 + 128 * ct, channel_multiplier=1, r=[ovt], w=[ovt])
        self.E("dve", "memset", self.vca[:], 1.0, w=["vca"])
        for g in range(4):
            for ct in range(self.NCT):
                self.E("dve", "tensor_copy", out=self.vca[:, g, ct, 65:128], in_=ov[:, ct, 0:63], r=[ovt, "vca"], w=["vca"])
        self.E("dve", "memset", self.kcTa[:], 0.0, w=["kcTa"])

    def load_gcol(self, dst, tok, src_row):
        self.dma(dst[:], src_row.rearrange("(k p) -> p k", p=128), "gcol_" + tok, w=[tok],
                 allow_slow_non_contiguous=True)

    def load_w(self, dst3, dtok, w_ap, ncols, gcol, gtok, nk=8):
        if getattr(self, "_stg_pid", None) != self.pid:
            self._stg = [self.ph(f"wstg{i}", [128, 1024], F32) for i in range(2)]
            self._stg_pid = self.pid
        stg = self._stg
        i = 0
        first = True
        for k in range(nk):
            for c0 in range(0, ncols, 1024):
                cw = min(1024, ncols - c0)
                s_ap, s_tok = stg[i % 2]
                self.dma(s_ap[:, 0:cw], w_ap[k * 128:(k + 1) * 128, c0:c0 + cw], f"wst{i % 2}", w=[s_tok])
                kw = dict(r=[s_tok] + ([gtok] if gcol is not None else []))
                if first:
                    kw["w"] = [dtok]
                    first = False
                else:
                    kw["pw"] = [dtok]
                if gcol is not None:
                    self.E("dve", "tensor_scalar", out=dst3[:, k, c0:c0 + cw], in0=s_ap[:, 0:cw], scalar1=gcol[:, k:k + 1],
                           scalar2=None, op0=ALU.mult, **kw)
                else:
                    self.E("dve", "tensor_copy", out=dst3[:, k, c0:c0 + cw], in_=s_ap[:, 0:cw], **kw)
                i += 1

    def rstd_of(self, x_ap, xtok, n, scratch, stok, out_col, otok):
        self.act(scratch, x_ap, AF.Square, r=[xtok], w=[stok, otok], accum_out=out_col)
        self.act(out_col, out_col, AF.Sqrt, r=[otok], w=[otok], scale=1.0 / n, bias=EPS)
        self.E("dve", "reciprocal", out=out_col, in_=out_col, r=[otok], w=[otok])

    def transpose8(self, src_bf, stok, dstT, dtok, pbi, nblk=8):
        pbt = self.pb[pbi]
        ptok = ("pb", pbi)
        for k in range(nblk):
            kw = dict(w=[ptok]) if k == 0 else dict(w=())
            if k == 0:
                self.tr(pbt[:, k * 128:(k + 1) * 128], src_bf[:, k * 128:(k + 1) * 128], r=[stok], w=[ptok])
            else:
                self.S.op("pe", (lambda e, k=k: e.transpose(out=pbt[:, k * 128:(k + 1) * 128], in_=src_bf[:, k * 128:(k + 1) * 128],
                                                            identity=self.ident[:])), reads=[stok, "ident"], pwrites=[ptok])
        self.act(dstT.rearrange("p a b -> p (a b)") if len(dstT.shape) == 3 else dstT, pbt[:, 0:nblk * 128], AF.Copy, r=[ptok], w=[dtok])

    def norm_rope(self, src, stok, nh, gain_ap, gtok, cos_ap, sin_ap, cstok, out_bf, otok, tmp):
        (sq, sqt), (ss, sst), (A, At), (Bt_, Btt), (t13, t13t), (tsw, tswt) = tmp
        W = nh * 64
        v3 = lambda ap: ap[:, 0:W].rearrange("p (h d) -> p h d", d=64)
        v4 = lambda ap: ap[:, 0:W].rearrange("p (h a d) -> p h a d", a=2, d=32)
        self.act(sq[:, 0:W], src, AF.Square, r=[stok], w=[sqt])
        self.E("dve", "tensor_reduce", out=ss[:, 0:nh], in_=v3(sq), axis=AX.X, op=ALU.add, r=[sqt], w=[sst])
        self.act(ss[:, 0:nh], ss[:, 0:nh], AF.Sqrt, r=[sst], w=[sst], scale=1.0 / 64, bias=EPS)
        self.E("dve", "reciprocal", out=ss[:, 0:nh], in_=ss[:, 0:nh], r=[sst], w=[sst])
        self.E("dve", "tensor_tensor", out=v3(A), in0=src.rearrange("p (h d) -> p h d", d=64),
               in1=ss[:, 0:nh].unsqueeze(2).to_broadcast([128, nh, 64]), op=ALU.mult, r=[stok, sst], w=[At])
        self.E("dve", "tensor_tensor", out=v3(Bt_), in0=v3(A), in1=gain_ap.unsqueeze(1).to_broadcast([128, nh, 64]),
               op=ALU.mult, r=[At, gtok], w=[Btt])
        cosb = cos_ap.unsqueeze(1).unsqueeze(1).to_broadcast([128, nh, 2, 32])
        sinb = sin_ap.unsqueeze(1).to_broadcast([128, nh, 32])
        self.E("dve", "tensor_tensor", out=v4(t13), in0=v4(Bt_), in1=cosb, op=ALU.mult, r=[Btt, cstok], w=[t13t])
        self.E("dve", "tensor_tensor", out=v4(tsw)[:, :, 0, :], in0=v4(Bt_)[:, :, 1, :], in1=sinb, op=ALU.mult,
               r=[Btt, cstok], w=[tswt])
        self.E("dve", "tensor_tensor", out=v4(tsw)[:, :, 1, :], in0=v4(Bt_)[:, :, 0, :], in1=sinb, op=ALU.mult,
               r=[Btt, cstok, tswt], w=[tswt])
        self.E("dve", "tensor_tensor", out=v4(out_bf)[:, :, 0, :], in0=v4(t13)[:, :, 0, :], in1=v4(tsw)[:, :, 0, :],
               op=ALU.subtract, r=[t13t, tswt], w=[otok])
        self.E("dve", "tensor_tensor", out=v4(out_bf)[:, :, 1, :], in0=v4(t13)[:, :, 1, :], in1=v4(tsw)[:, :, 1, :],
               op=ALU.add, r=[t13t, tswt, otok], w=[otok])

    def phase_P(self, li, src, moba):
        self.phase()
        NT = self.NT
        j = li // 2
        ncols = 4 * D if moba else NSA_IN
        w_ap = (self.moba_w_in if moba else self.nsa_w_in)[j]
        W3 = self.big[:, 0:8 * 4096].rearrange("p (k n) -> p k n", n=4096)
        self.load_gcol(self.gcol, "gcol", self.norm_gain[li])
        self.load_w(W3, "big", w_ap, ncols, self.gcol, "gcol")
        qg = (self.moba_q_gain if moba else self.nsa_q_gain)[j]
        self.dma(self.gq[:], qg.partition_broadcast(128), "gq", w=["gq"])
        if moba:
            self.dma(self.gk[:, 0, :], self.moba_k_gain[j].partition_broadcast(128), "gk", w=["gk"])
        else:
            for b_ in range(3):
                self.dma(self.gk[:, b_, :], self.nsa_k_gain[j, b_].partition_broadcast(128), f"gk{b_}",
                         **(dict(w=["gk"]) if b_ == 0 else dict(pw=["gk"])))
        xt = [self.ph(f"xt{i}", [128, D], F32) for i in range(2)]
        sqs = self.ph("sqs", [128, D], F32)
        rs = [self.ph(f"rs{i}", [128, 1], F32) for i in range(2)]
        hb = [self.ph(f"hb{i}", [128, D], BF16) for i in range(2)]
        hT = [self.ph(f"hT{i}", [128, 8, 128], BF16) for i in range(2)]
        b1, b1t = self.ph("nr_b1", [128, 2048], F32)
        b2, b2t = self.ph("nr_b2", [128, 2048], F32)
        b3, b3t = self.ph("nr_b3", [128, 2048], F32)
        ss, sst = self.ph("nr_ss", [128, 32], F32)
        qbf = [self.ph(f"qbf{i}", [128, 2048], BF16) for i in range(2)]
        qTt = [self.ph(f"qTt{i}", [128, 16, 128], BF16) for i in range(2)]
        vbf = [self.ph(f"vbf{i}", [128, D], BF16) for i in range(2)]
        zbf = [self.ph(f"zbf{i}", [128, D], BF16) for i in range(2)]
        rawbf = [self.ph(f"rawbf{i}", [128, 512], BF16) for i in range(2)]
        if moba:
            chunks = [(c * 512, 512) for c in range(8)]
            HT_ = 32
            srcs = [(0, 0, 8, self.gq[:], "gq"), (1, 0, 8, self.gq[:], "gq"), (2, 0, 8, self.gk[:, 0, :], "gk"),
                    (3, 0, 8, self.gk[:, 0, :], "gk")]
            qk_chunks = [(0, 0), (1, 1), (2, 2), (3, 3)]
            dests = [(self.qT_d, jb * 128) for jb in range(8)] + [(self.kT_d, jb * 128) for jb in range(8)]
        else:
            chunks = [(0, 512), (512, 512), (1024, 512), (1536, 512), (2048, 512), (2560, 48), (2608, 512), (3120, 512)]
            HT_ = 24
            srcs = [(0, 0, 8, self.gq[:], "gq"), (1, 0, 8, self.gq[:], "gq"), (2, 0, 4, self.gk[:, 1, :], "gk"),
                    (3, 0, 4, self.gk[:, 2, :], "gk")]
            qk_chunks = [(0, 0), (1, 1), (3, 2), (4, 3), (2, 4)]
            dests = ([(self.qT_d, jb * 128) for jb in range(8)] + [(self.kT_d, 0), (self.kT_d, 128), (self.kT_d, 256),
                     (self.kT_d, 384)] + [(self.kT_d, 512 + jb * 128) for jb in range(4)])
        WQ = HT_ * 64

        def pre_a(t):
            x_ap, xtok = xt[t % 2]
            self.dma(x_ap, src[t * 128:(t + 1) * 128, :], f"xld{t % 2}", r=[("y", t)], w=[xtok])
            r_ap, rtok = rs[t % 2]
            self.rstd_of(x_ap, xtok, D, sqs[0], sqs[1], r_ap, rtok)
            h_ap, htok = hb[t % 2]
            self.E("dve", "tensor_scalar", out=h_ap, in0=x_ap, scalar1=r_ap[:, 0:1], scalar2=None, op0=ALU.mult,
                   r=[xtok, rtok], w=[htok])

        def pre_b(t):
            h_ap, htok = hb[t % 2]
            hT_ap, hTtok = hT[t % 2]
            self.transpose8(h_ap, htok, hT_ap, hTtok, 0)

        def mm_chunk(t, ci, bank):
            c0, cw = chunks[ci]
            hT_ap, hTtok = hT[t % 2]
            ps = self.pf[bank]
            pstok = ("pf", bank)
            for k in range(8):
                self.mm(ps[:, 0:cw], hT_ap[:, k, :], W3[:, k, c0:c0 + cw], k == 0, k == 7, r=[hTtok, "big"],
                        w=[pstok] if k == 0 else ())
            last = self.S.q["pe"][-1]
            self.S.tw[pstok] = {last.stream: last}

        def chain(t):
            q_ap, qbt = qbf[t % 2]
            v3 = lambda ap, c0, w: ap[:, c0:c0 + w].rearrange("p (h d) -> p h d", d=64)
            v4 = lambda ap, c0, w: ap[:, c0:c0 + w].rearrange("p (h a d) -> p h a d", a=2, d=32)
            c = 0
            for n_, (bank, col0, nh, g_ap, g_tok) in enumerate(srcs):
                w_ = nh * 64
                self.act(b1[:, c:c + w_], self.pf[bank][:, col0:col0 + w_], AF.Square, r=[("pf", bank)],
                         **(dict(w=[b1t]) if n_ == 0 else dict(pw=[b1t])))
                c += w_
            self.E("dve", "tensor_reduce", out=ss[:, 0:HT_], in_=v3(b1, 0, WQ), axis=AX.X, op=ALU.add, r=[b1t], w=[sst])
            self.act(ss[:, 0:HT_], ss[:, 0:HT_], AF.Sqrt, r=[sst], w=[sst], scale=1.0 / 64, bias=EPS)
            self.E("dve", "reciprocal", out=ss[:, 0:HT_], in_=ss[:, 0:HT_], r=[sst], w=[sst])
            c = 0
            h0 = 0
            for n_, (bank, col0, nh, g_ap, g_tok) in enumerate(srcs):
                w_ = nh * 64
                self.E("dve", "tensor_tensor", out=v3(b2, c, w_),
                       in0=self.pf[bank][:, col0:col0 + w_].rearrange("p (h d) -> p h d", d=64),
                       in1=ss[:, h0:h0 + nh].unsqueeze(2).to_broadcast([128, nh, 64]), op=ALU.mult,
                       r=[("pf", bank), sst], **(dict(w=[b2t]) if n_ == 0 else dict(pw=[b2t])))
                c += w_
                h0 += nh
            c = 0
            first = True
            i_ = 0
            while i_ < len(srcs):
                g_ap, g_tok = srcs[i_][3], srcs[i_][4]
                nh = srcs[i_][2]
                k_ = i_ + 1
                while k_ < len(srcs) and srcs[k_][3] is g_ap:
                    nh += srcs[k_][2]
                    k_ += 1
                w_ = nh * 64
                self.E("dve", "tensor_tensor", out=v3(b1, c, w_), in0=v3(b2, c, w_),
                       in1=g_ap.unsqueeze(1).to_broadcast([128, nh, 64]), op=ALU.mult, r=[b2t, g_tok],
                       **(dict(w=[b1t]) if first else dict(pw=[b1t])))
                first = False
                c += w_
                i_ = k_
            import os
            if "norope" in os.environ.get("PDBG", ""):
                return
            cos_ap = self.cs[:, 0, t, :]
            sin_ap = self.cs[:, 1, t, :]
            cosb = cos_ap.unsqueeze(1).unsqueeze(1).to_broadcast([128, HT_, 2, 32])
            sinb = sin_ap.unsqueeze(1).to_broadcast([128, HT_, 32])
            self.E("dve", "tensor_tensor", out=v4(b2, 0, WQ), in0=v4(b1, 0, WQ), in1=cosb, op=ALU.mult, r=[b1t, "cs"], w=[b2t])
            self.E("dve", "tensor_tensor", out=v4(b3, 0, WQ)[:, :, 0, :], in0=v4(b1, 0, WQ)[:, :, 1, :], in1=sinb, op=ALU.mult,
                   r=[b1t, "cs"], w=[b3t])
            self.E("dve", "tensor_tensor", out=v4(b3, 0, WQ)[:, :, 1, :], in0=v4(b1, 0, WQ)[:, :, 0, :], in1=sinb, op=ALU.mult,
                   r=[b1t, "cs"], pw=[b3t])
            self.E("dve", "tensor_tensor", out=v4(q_ap, 0, WQ)[:, :, 0, :], in0=v4(b2, 0, WQ)[:, :, 0, :],
                   in1=v4(b3, 0, WQ)[:, :, 0, :], op=ALU.subtract, r=[b2t, b3t], w=[qbt])
            self.E("dve", "tensor_tensor", out=v4(q_ap, 0, WQ)[:, :, 1, :], in0=v4(b2, 0, WQ)[:, :, 1, :],
                   in1=v4(b3, 0, WQ)[:, :, 1, :], op=ALU.add, r=[b2t, b3t], pw=[qbt])
            import os
            if not moba and "noraw" not in os.environ.get("PDBG", ""):
                self.E("dve", "tensor_copy", out=rawbf[t % 2][0], in_=self.pf[4][:, 0:512], r=[("pf", 4)], w=[rawbf[t % 2][1]])

        def tq(t):
            q_ap, qbt = qbf[t % 2]
            qt_, qtt = qTt[t % 2]
            for half in range(2):
                pbt = self.pb[1]
                ptok = ("pb", 1)
                for k in range(8):
                    jb = half * 8 + k
                    if (not moba) and jb >= 12:
                        s_ap, s_tok = rawbf[t % 2]
                        s_in = s_ap[:, (jb - 12) * 128:(jb - 11) * 128]
                    else:
                        s_tok = qbt
                        s_in = q_ap[:, jb * 128:(jb + 1) * 128]
                    if k == 0:
                        self.tr(pbt[:, 0:128], s_in, r=[s_tok], w=[ptok])
                    else:
                        self.S.op("pe", (lambda e, k=k, s_in=s_in: e.transpose(out=pbt[:, k * 128:(k + 1) * 128], in_=s_in,
                                                                              identity=self.ident[:])),
                                  reads=[s_tok, "ident"], pwrites=[ptok])
                self.act(qt_[:, half * 8:(half + 1) * 8, :].rearrange("p a b -> p (a b)"), pbt[:, 0:1024], AF.Copy, r=[ptok],
                         **(dict(w=[qtt]) if half == 0 else dict(pw=[qtt])))
            jb = 0
            while jb < 16:
                dram, row0 = dests[jb]
                k_ = jb + 1
                while k_ < 16 and dests[k_][0] is dram and dests[k_][1] == row0 + (k_ - jb) * 128:
                    k_ += 1
                nb = k_ - jb
                self.dma(dram[row0:row0 + nb * 128, t * 128:(t + 1) * 128].rearrange("(j p) t -> p j t", p=128),
                         qt_[:, jb:k_, :], f"qst{t % 2}", r=[qtt], pw=[("kTq_d",)], q="pool")
                jb = k_

        def mm_vz(t):
            v_ap, vtok = vbf[t % 2]
            z_ap, ztok = zbf[t % 2]
            if moba:
                for ci in (4, 5):
                    bank = ci
                    mm_chunk(t, ci, bank)
                    self.act(v_ap[:, (ci - 4) * 512:(ci - 3) * 512], self.pf[bank][:, 0:512], AF.Copy, r=[("pf", bank)],
                             **(dict(w=[vtok]) if ci == 4 else dict(pw=[vtok])))
                self.dma(self.v_d[t * 128:(t + 1) * 128, :], v_ap, f"vst{t % 2}", r=[vtok], pw=[("v_d",)], q="pool")
            else:
                self.act(v_ap[:, 0:256], self.pf[2][:, 256:512], AF.Copy, r=[("pf", 2)], w=[vtok])
                self.act(v_ap[:, 256:512], self.pf[3][:, 256:512], AF.Copy, r=[("pf", 3)], pw=[vtok])
                self.dma(self.v_d[t * 128:(t + 1) * 128, 0:512], v_ap[:, 0:512], f"vst{t % 2}", r=[vtok], pw=[("v_d",)], q="pool")
                mm_chunk(t, 5, 5)
                self.act(self.g_sb[:, t, :], self.pf[5][:, 0:48], AF.Sigmoid, r=[("pf", 5)], pw=["g_sb"])
            for ci in (6, 7):
                bank = ci - 2
                mm_chunk(t, ci, bank)
                self.act(z_ap[:, (ci - 6) * 512:(ci - 5) * 512], self.pf[bank][:, 0:512], AF.Silu, r=[("pf", bank)],
                         **(dict(w=[ztok]) if ci == 6 else dict(pw=[ztok])))
            self.dma(self.zs_d[t * 128:(t + 1) * 128, :], z_ap, f"zst{t % 2}", r=[ztok], pw=[("zs_d",)], q="pool")

        import os
        dbg = os.environ.get("PDBG", "")
        pre_a(0)
        pre_b(0)
        for t in range(NT):
            for (ci, bank) in qk_chunks:
                mm_chunk(t, ci, bank)
            if t + 1 < NT:
                pre_a(t + 1)
            if "nochain" not in dbg:
                chain(t)
            if t + 1 < NT:
                pre_b(t + 1)
            if "novz" not in dbg:
                mm_vz(t)
            if t >= 1 and "notq" not in dbg:
                tq(t - 1)
        if "notq" not in dbg:
            tq(NT - 1)

    def attention(self, qTa, qtok, kTa, ktok, Va, vtok, vw, pairs_fn, fin, pts, tag, side=None, side_delay=0):
        NT = self.NT
        qtoks = list(qtok) if isinstance(qtok, list) else [qtok]
        units = []
        for qt in range(NT):
            pairs = pairs_fn(qt)
            ng = (len(pairs) + 3) // 4
            for gi in range(ng):
                units.append((qt, pairs[gi * 4:(gi + 1) * 4], gi == 0, gi == ng - 1))
        n = len(units)
        NBK = len(pts)
        SK = NBK - 1
        banks = [0, 1, 2, 3][:NBK]
        u0 = getattr(self, "_u", 0)

        def qk(i):
            qt, grp, _, _ = units[i]
            si = banks[(u0 + i) % NBK]
            ps = self.pf[si]
            pstok = ("pf", si)
            firstw = True
            for jx, (kt, bias) in enumerate(grp):
                o_ap = ps[:, jx * 128:(jx + 1) * 128]
                self.mm(o_ap, kTa[:, kt * 128:(kt + 1) * 128], qTa[:, qt * 128:(qt + 1) * 128], True, bias is None,
                        r=[ktok] + qtoks, w=[pstok] if firstw else ())
                firstw = False
                if bias is not None:
                    self.mm(o_ap, self.ident[:], bias, False, True, r=["ident", "tri", "atri", "cmpm"])
            last = self.S.q["pe"][-1]
            self.S.tw[pstok] = {last.stream: last}

        def pv(i):
            qt, grp, isf, isl = units[i]
            si = banks[(u0 + i) % NBK]
            ps = self.pf[si]
            pstok = ("pf", si)
            pt_ap, pttok = pts[(u0 + i) % NBK]
            po = self.pf[4 + qt % 2]
            potok = ("pf", 4 + qt % 2)
            nn = len(grp) * 128
            self.act(pt_ap[:, 0:nn], ps[:, 0:nn], AF.Exp, r=[pstok], w=[pttok], scale=0.125)
            for jx, (kt, bias) in enumerate(grp):
                is_first = isf and jx == 0
                is_last = isl and jx == len(grp) - 1
                self.mm(po[:, 0:vw], pt_ap[:, jx * 128:(jx + 1) * 128], Va[:, kt, 0:vw], is_first, is_last,
                        r=[pttok, vtok], w=[potok] if is_first else ())
            last = self.S.q["pe"][-1]
            self.S.tw[potok] = {last.stream: last}
            if isl:
                fin(qt, po, potok)

        for i in range(n + SK):
            if i < n:
                qk(i)
            if i >= SK:
                pv(i - SK)
                if side is not None and i - SK >= side_delay:
                    next(side, None)
        self._u = u0 + n

    def phase_A_moba(self, li):
        self.phase()
        T, NT = self.T, self.NT
        NB = T // 256
        o_all = self.big[:, 0:NT * D].rearrange("p (t f) -> p t f", f=D)
        qTa = [self.ph(f"qTa{i}", [128, T], BF16) for i in range(2)]
        kTa = [self.ph(f"kTa{i}", [128, T], BF16) for i in range(2)]
        Va = [self.ph(f"Va{i}", [128, NT, 65], BF16) for i in range(2)]
        pts = [self.ph(f"pt{i}", [128, 512], BF16) for i in range(4)]
        kmf, kmft = self.ph("kmf", [128, 16], F32)
        kmT, kmTt = self.ph("kmT", [128, 16], BF16)
        gt_, gtt = self.ph("gate", [128, 16], F32)
        m8, m8t = self.ph("m8", [128, 8], F32)
        bq, bqt = self.ph("bq", [128, 128], BF16)
        rz, rzt = self.ph("rz", [128, 1], F32)
        self.E("dve", "memset", kmT, 0.0, w=[kmTt])
        self.E("dve", "memset", bq, 0.0, w=[bqt])
        for i in range(2):
            q_ap, qtok = qTa[i]
            k_ap, ktok = kTa[i]
            v_ap, vtok = Va[i]
            self.E("pool", "memset", q_ap, 0.0, w=[qtok])
            self.E("pool", "memset", k_ap[0:64, :], 0.0, w=[ktok])
            ke = k_ap[64:128, :]
            self.E("pool", "memset", ke, NEGB, r=[ktok], w=[ktok])
            self.E("pool", "affine_select", out=ke, in_=ke, pattern=[[1, T]], compare_op=ALU.is_ge, fill=0.0, base=0,
                   channel_multiplier=-256, r=[ktok], w=[ktok])
            self.E("pool", "affine_select", out=ke, in_=ke, pattern=[[-1, T]], compare_op=ALU.is_ge, fill=0.0, base=255,
                   channel_multiplier=256, r=[ktok], w=[ktok])
            self.E("dve", "memset", v_ap, 1.0, w=[vtok])

        def load_head(h):
            i = h % 2
            q_ap, qtok = qTa[i]
            k_ap, ktok = kTa[i]
            v_ap, vtok = Va[i]
            self.dma(q_ap[0:64, :], self.qT_d[h * 64:(h + 1) * 64, :], f"aq{i}", r=[("kTq_d",)], w=[qtok])
            self.dma(k_ap[0:64, :], self.kT_d[h * 64:(h + 1) * 64, :], f"ak{i}", r=[("kTq_d",)], w=[ktok])
            for c in range(0, NT, 8):
                n = min(8, NT - c)
                self.dma(v_ap[:, c:c + n, 0:64],
                         self.v_d[c * 128:(c + n) * 128, h * 64:(h + 1) * 64].rearrange("(t p) d -> p t d", p=128),
                         f"av{i}", r=[("v_d",)], **(dict(w=[vtok]) if c == 0 else dict(pw=[vtok])))

        def gate_steps(h):
            i = h % 2
            q_ap, qtok = qTa[i]
            k_ap, ktok = kTa[i]
            self.E("dve", "tensor_reduce", out=kmf[0:64, 0:NB], in_=k_ap[0:64, :].rearrange("p (n k) -> p n k", k=256),
                   axis=AX.X, op=ALU.add, r=[ktok], w=[kmft])
            self.E("dve", "tensor_scalar", out=kmT[0:64, 0:NB], in0=kmf[0:64, 0:NB], scalar1=1.0 / 256, scalar2=None,
                   op0=ALU.mult, r=[kmft], w=[kmTt])
            self.E("dve", "memset", gt_, -1.0e30, w=[gtt])
            yield
            for qt in range(NT):
                b = qt // 2
                if b <= 3:
                    continue
                pg = self.pb[0][:, 0:32].bitcast(F32)
                self.mm(pg[:, 0:16], q_ap[:, qt * 128:(qt + 1) * 128], kmT, True, True, r=[qtok, kmTt], w=[("pb", 0)])
                self.E("dve", "tensor_copy", out=gt_[:, 0:b], in_=pg[:, 0:b], r=[("pb", 0), gtt], w=[gtt])
                self.E("dve", "max", out=m8, in_=gt_, r=[gtt], w=[m8t])
                self.E("dve", "tensor_scalar", out=bq[:, 64:80], in0=gt_, scalar1=m8[:, 2:3], scalar2=1.0, op0=ALU.is_ge,
                       op1=ALU.subtract, r=[gtt, m8t, bqt], w=[bqt])
                self.E("dve", "memset", bq[:, 64 + b:65 + b], 0.0, r=[bqt], w=[bqt])
                yield
                yield
                pbt = self.pb[1]
                self.tr(pbt[:, 0:128], bq, r=[bqt], w=[("pb", 1)])
                self.E("dve", "tensor_copy", out=q_ap[64:128, qt * 128:(qt + 1) * 128], in_=pbt[64:128, 0:128],
                       r=[("pb", 1), qtok], w=[qtok])
                yield

        load_head(0)
        for _ in gate_steps(0):
            pass
        for h in range(H):
            side = None
            if h + 1 < H:
                load_head(h + 1)
                side = gate_steps(h + 1)
            i = h % 2
            q_ap, qtok = qTa[i]
            k_ap, ktok = kTa[i]
            v_ap, vtok = Va[i]

            def pairs_fn(qt):
                return [(kt, (self.tri[:] if kt == qt else None)) for kt in range(qt + 1)]

            def fin(qt, po, potok, h=h):
                self.E("dve", "reciprocal", out=rz, in_=po[:, 64:65], r=[potok], w=[rzt])
                self.E("dve", "tensor_scalar", out=o_all[:, qt, h * 64:(h + 1) * 64], in0=po[:, 0:64], scalar1=rz[:, 0:1],
                       scalar2=None, op0=ALU.mult, r=[potok, rzt], pw=["big"])

            self.attention(q_ap, qtok, k_ap, ktok, v_ap, vtok, 65, pairs_fn, fin, pts, f"m{h}", side=side, side_delay=min(24, self.NT))
            if side is not None:
                for _ in side:
                    pass

    def phase_C(self, li):
        self.phase()
        T, NT = self.T, self.NT
        j = li // 2
        xkp, xkpt = self.ph("xkp", [128, T + 16], BF16)
        xk16, xk16t = self.ph("xk16", [128, 2, 16, T // 16], BF16)
        W1d, W1t = self.ph("W1d", [128, 32, 256], BF16)
        w1s = [self.ph(f"w1s{i}", [128, 8, 256], F32) for i in range(2)]
        W2b, W2t = self.ph("W2b", [128, 2, 64], BF16)
        w2s, w2st = self.ph("w2s", [128, 2, 64], F32)
        peT, peTt = self.ph("peT", [128, 32], BF16)
        pes, pest = self.ph("pes", [128, 32], F32)
        bh, bht = self.ph("bh", [128, 2], F32)
        NC, NCT = self.NC, self.NCT
        xx, xxt = self.ph("xx", [128, NC], F32)
        x2, x2t = self.ph("x2", [128, NC], F32)
        sg, sgt = self.ph("sg", [128, NC], F32)
        gl, glt = self.ph("gl", [128, 2, NC], BF16)
        kcb, kcbt = self.ph("kcb", [128, 128], BF16)
        tmp = [self.ph(n, [128, 64], F32) for n in ("c_sq", "c_ss", "c_A", "c_B", "c_t13", "c_tsw")]
        self.E("pool", "memset", xkp, 0.0, w=[xkpt])
        self.E("pool", "memset", W1d, 0.0, w=[W1t])
        self.E("pool", "memset", peT, 0.0, w=[peTt])
        self.E("pool", "memset", kcb, 0.0, w=[kcbt])
        for kv in range(2):
            w1 = self.nsa_cmp_w1[j, kv].rearrange("(l d) j -> d l j", d=64)
            for c in range(4):
                s_ap, s_tok = w1s[c % 2]
                self.dma(s_ap[0:64], w1[:, c * 8:(c + 1) * 8, :], f"w1s{c % 2}", w=[s_tok])
                self.E("dve", "tensor_copy", out=W1d[0:64, c * 8:(c + 1) * 8, :], in_=s_ap[0:64], r=[s_tok],
                       **(dict(w=[W1t]) if c == 0 else dict(pw=[W1t])))
            self.dma(w2s, self.nsa_cmp_w2[j, kv].rearrange("(c p) d -> p c d", p=128), "w2s", w=[w2st])
            self.E("dve", "tensor_copy", out=W2b, in_=w2s, r=[w2st], w=[W2t])
            for q4 in range(4):
                self.dma(pes[0:64, q4 * 8:(q4 + 1) * 8], self.nsa_cmp_pe[j, kv, q4 * 8:(q4 + 1) * 8, :].rearrange("l d -> d l"),
                         "pes", **(dict(w=[pest]) if q4 == 0 else dict(pw=[pest])), allow_slow_non_contiguous=True)
            self.E("dve", "tensor_copy", out=peT[0:64, :], in_=pes[0:64, :], r=[pest], w=[peTt])
            pbias = self.pf[3]
            for jc in range(2):
                for l in range(32):
                    self.mm(pbias[:, jc:jc + 1], W1d[:, l, jc * 128:(jc + 1) * 128], peT[:, l:l + 1], l == 0, l == 31,
                            r=[W1t, peTt], w=[("pf", 3)] if (l == 0 and jc == 0) else ())
            last = self.S.q["pe"][-1]
            self.S.tw[("pf", 3)] = {last.stream: last}
            self.E("dve", "tensor_copy", out=bh, in_=pbias[:, 0:2], r=[("pf", 3)], w=[bht])
            for g in range(4):
                row0 = 512 + kv * 256 + g * 64
                self.dma(xkp[0:64, 0:T], self.kT_d[row0:row0 + 64, :], "xkp", r=[("kTq_d",)], w=[xkpt])
                self.E("dve", "tensor_copy", out=xk16[:, 0], in_=xkp[:, 0:T].rearrange("p (m r) -> p r m", r=16), r=[xkpt], w=[xk16t])
                self.E("dve", "tensor_copy", out=xk16[:, 1], in_=xkp[:, 16:T + 16].rearrange("p (m r) -> p r m", r=16), r=[xkpt],
                       pw=[xk16t])
                for jc in range(2):
                    ph_ = self.pf[jc]
                    for l in range(32):
                        self.mm(ph_[:, 0:NC], W1d[:, l, jc * 128:(jc + 1) * 128], xk16[:, l // 16, l % 16, :], l == 0, l == 31,
                                r=[W1t, xk16t], w=[("pf", jc)] if l == 0 else ())
                    last = self.S.q["pe"][-1]
                    self.S.tw[("pf", jc)] = {last.stream: last}
                    self.act(xx, ph_[:, 0:NC], AF.Identity, r=[("pf", jc), bht], w=[xxt], bias=bh[:, jc:jc + 1])
                    self.E("dve", "tensor_tensor", out=x2, in0=xx, in1=xx, op=ALU.mult, r=[xxt], w=[x2t])
                    self.E("dve", "tensor_scalar", out=x2, in0=x2, scalar1=0.044715, scalar2=1.0, op0=ALU.mult, op1=ALU.add,
                           r=[x2t], w=[x2t])
                    self.E("dve", "tensor_tensor", out=x2, in0=x2, in1=xx, op=ALU.mult, r=[x2t, xxt], w=[x2t])
                    self.act(sg, x2, AF.Sigmoid, r=[x2t], w=[sgt], scale=1.5957691216057308)
                    self.E("dve", "tensor_tensor", out=gl[:, jc, :], in0=xx, in1=sg, op=ALU.mult, r=[xxt, sgt],
                           **(dict(w=[glt]) if jc == 0 else dict(pw=[glt])))
                for ct in range(NCT):
                    pk = self.pf[2]
                    for jc in range(2):
                        self.mm(pk[:, 0:64], gl[:, jc, ct * 128:(ct + 1) * 128], W2b[:, jc, :], jc == 0, jc == 1,
                                r=[glt, W2t], w=[("pf", 2)] if jc == 0 else ())
                    last = self.S.q["pe"][-1]
                    self.S.tw[("pf", 2)] = {last.stream: last}
                    if kv == 0:
                        self.norm_rope(pk[:, 0:64], ("pf", 2), 1, self.gk[:, 0, :], "gk", self.csc[:, 0, ct, :],
                                       self.csc[:, 1, ct, :], "csc", kcb, kcbt, tmp)
                        pbt = self.pb[1]
                        self.tr(pbt[:, 0:128], kcb, r=[kcbt], w=[("pb", 1)])
                        self.E("dve", "tensor_copy", out=self.kcTa[:, g, ct * 128:(ct + 1) * 128], in_=pbt[:, 0:128],
                               r=[("pb", 1)], pw=["kcTa"])
                    else:
                        self.act(self.vca[:, g, ct, 0:64], pk[:, 0:64], AF.Copy, r=[("pf", 2)], pw=["vca"])

    def phase_A_nsa(self, li):
        self.phase()
        T, NT = self.T, self.NT
        o_all = self.big[:, 0:NT * D].rearrange("p (t f) -> p t f", f=D)
        qTa = [self.ph(f"nqTa{i}", [128, T], BF16) for i in range(4)]
        ksTa, kst = self.ph("ksTa", [128, T], BF16)
        kwTa, kwt = self.ph("kwTa", [128, T], BF16)
        Vs, Vst = self.ph("Vs", [128, NT, 65], BF16)
        Vw, Vwt = self.ph("Vw", [128, NT, 65], BF16)
        pts = [self.ph(f"npt{i}", [128, 512], BF16) for i in range(4)]
        oacc, oacct = self.ph("oacc", [128, NT, 64], F32)
        imp, impt = self.ph("imp", [128, NT, 64], F32)
        sc, sct = self.ph("sc", [128, 64], F32)
        sc2, sc2t = self.ph("sc2", [128, 64], F32)
        m8a, m8at = self.ph("m8a", [128, 8], F32)
        m8b, m8bt = self.ph("m8b", [128, 8], F32)
        bs, bst = self.ph("bs", [128, 128], BF16)
        rz, rzt = self.ph("nrz", [128, 1], F32)
        cf, cft = self.ph("ncf", [128, 1], F32)
        stage, staget = self.ph("cstage", [128, NT, 128], F32)
        rzb, rzbt = self.ph("rzb", [128, NT], F32)
        cfb, cfbt = self.ph("cfb", [128, NT], F32)
        for hl, (q_ap, qtok) in enumerate(qTa):
            self.E("pool", "memset", q_ap, 0.0, w=[qtok, ("qhi", hl)])
        self.E("pool", "memset", kwTa, 0.0, w=[kwt])
        self.E("pool", "memset", ksTa[0:64, :], 0.0, w=[kst])
        ke = ksTa[64:128, :]
        self.E("pool", "memset", ke, NEGB, r=[kst], w=[kst])
        self.E("pool", "affine_select", out=ke, in_=ke, pattern=[[1, T]], compare_op=ALU.is_ge, fill=0.0, base=0,
               channel_multiplier=-64, r=[kst], w=[kst])
        self.E("pool", "affine_select", out=ke, in_=ke, pattern=[[-1, T]], compare_op=ALU.is_ge, fill=0.0, base=63,
               channel_multiplier=64, r=[kst], w=[kst])
        self.E("dve", "memset", Vs, 1.0, w=[Vst])
        self.E("dve", "memset", Vw, 1.0, w=[Vwt])
        self.E("dve", "memset", bs, 0.0, w=[bst])
        self.E("dve", "memset", imp, 0.0, w=[impt])

        for g in range(4):
            for hl in range(4):
                h = g * 4 + hl
                q_ap, qtok = qTa[hl]
                self.dma(q_ap[0:64, :], self.qT_d[h * 64:(h + 1) * 64, :], f"nq{hl}", r=[("kTq_d",)], w=[qtok])
            self.dma(ksTa[0:64, :], self.kT_d[g * 64:(g + 1) * 64, :], "nks", r=[("kTq_d",)], w=[kst])
            self.dma(kwTa[0:64, :], self.kT_d[256 + g * 64:256 + (g + 1) * 64, :], "nkw", r=[("kTq_d",)], w=[kwt])
            for (V_, Vt_, c0, nm) in ((Vs, Vst, g * 64, "nvs"), (Vw, Vwt, 256 + g * 64, "nvw")):
                for c in range(0, NT, 8):
                    n = min(8, NT - c)
                    self.dma(V_[:, c:c + n, 0:64],
                             self.v_d[c * 128:(c + n) * 128, c0:c0 + 64].rearrange("(t p) d -> p t d", p=128),
                             nm, r=[("v_d",)], **(dict(w=[Vt_]) if c == 0 else dict(pw=[Vt_])))

            def cmp_pairs(qt):
                out = []
                for kt in range(self.NCT):
                    dl = qt - 16 * kt
                    if dl < 0:
                        continue
                    out.append((kt, self.cmpm[:, dl, :] if dl <= 16 else None))
                return out

            def sel_A(qt):
                scq = imp[:, qt, :]
                self.E("dve", "max", out=m8a, in_=scq, r=[impt], w=[m8at])
                self.E("dve", "tensor_scalar", out=sc2, in0=scq, scalar1=m8a[:, 7:8], scalar2=-6.0e4, op0=ALU.is_ge, op1=ALU.mult,
                       r=[impt, m8at], w=[sc2t])
                self.E("dve", "tensor_tensor", out=sc2, in0=sc2, in1=scq, op=ALU.add, r=[sc2t, impt], w=[sc2t])
                self.E("dve", "max", out=m8b, in_=sc2, r=[sc2t], w=[m8bt])
                self.E("dve", "tensor_scalar", out=bs[:, 64:128], in0=scq, scalar1=m8b[:, 7:8], scalar2=1.0, op0=ALU.is_ge,
                       op1=ALU.subtract, r=[impt, m8bt, bst], w=[bst])

            def sel_B(qt):
                pbt = self.pb[1]
                self.tr(pbt[:, 0:128], bs, r=[bst], w=[("pb", 1)])
                for hl2 in range(4):
                    q2, _ = qTa[hl2]
                    self.E("dve", "tensor_copy", out=q2[64:128, qt * 128:(qt + 1) * 128], in_=pbt[64:128, 0:128],
                           r=[("pb", 1)], pw=[("qhi", hl2)])

            def sel_steps():
                for qt in range(NT):
                    sel_A(qt)
                    yield
                    sel_B(qt)
                    yield

            for hl in range(4):
                h = g * 4 + hl
                q_ap, qtok = qTa[hl]

                def fin_c(qt, po, potok):
                    self.E("dve", "tensor_copy", out=stage[:, qt, :], in_=po[:, 0:128], r=[potok], pw=[staget])

                self.attention(q_ap, qtok, self.kcTa[:, g, :], "kcTa", self.vca[:, g], "vca", 128, cmp_pairs, fin_c, pts,
                               f"c{h}")
                zc = stage[:, :, 64]
                self.E("dve", "tensor_scalar", out=rzb, in0=zc, scalar1=1.0e-30, scalar2=None, op0=ALU.max, r=[staget], w=[rzbt])
                self.E("dve", "reciprocal", out=rzb, in_=rzb, r=[rzbt], w=[rzbt])
                self.E("dve", "tensor_tensor", out=cfb, in0=rzb, in1=self.g_sb[:, :, h * 3], op=ALU.mult, r=[rzbt, "g_sb"],
                       w=[cfbt])
                self.E("dve", "tensor_tensor", out=o_all[:, :, h * 64:(h + 1) * 64], in0=stage[:, :, 0:64],
                       in1=cfb.unsqueeze(2).to_broadcast([128, NT, 64]), op=ALU.mult, r=[staget, cfbt], pw=["big"])
                rzb3 = rzb.unsqueeze(2).to_broadcast([128, NT, 63])
                if hl == 0:
                    self.E("dve", "memset", imp[:, :, 63:64], 0.0, w=[impt])
                    self.E("dve", "tensor_tensor", out=imp[:, :, 0:63], in0=stage[:, :, 65:128], in1=rzb3, op=ALU.mult,
                           r=[staget, rzbt], pw=[impt])
                else:
                    self.E("dve", "tensor_tensor", out=stage[:, :, 65:128], in0=stage[:, :, 65:128], in1=rzb3, op=ALU.mult,
                           r=[staget, rzbt], pw=[staget])
                    self.E("dve", "tensor_tensor", out=imp[:, :, 0:63], in0=imp[:, :, 0:63], in1=stage[:, :, 65:128], op=ALU.add,
                           r=[staget, impt], pw=[impt])
            self.E("dve", "tensor_tensor", out=imp[:, :, :], in0=imp[:, :, :], in1=self.msel[:, :, :], op=ALU.add,
                   r=[impt, "msel"], w=[impt])

            def sel_pairs(qt):
                return [(kt, (self.tri[:] if kt == qt else None)) for kt in range(qt + 1)]

            def win_pairs(qt):
                out = []
                for kt in range(max(0, qt - 4), qt + 1):
                    b = self.tri[:] if kt == qt else (self.atri[:] if kt == qt - 4 else None)
                    out.append((kt, b))
                return out

            for hl in range(4):
                h = g * 4 + hl
                q_ap, qtok = qTa[hl]

                def fin_w(qt, po, potok, h=h):
                    self.E("dve", "reciprocal", out=rz, in_=po[:, 64:65], r=[potok], w=[rzt])
                    self.E("dve", "tensor_tensor", out=cf, in0=rz, in1=self.g_sb[:, qt, h * 3 + 2:h * 3 + 3], op=ALU.mult,
                           r=[rzt, "g_sb"], w=[cft])
                    self.E("dve", "scalar_tensor_tensor", out=oacc[:, qt, :], in0=po[:, 0:64], scalar=cf[:, 0:1],
                           in1=o_all[:, qt, h * 64:(h + 1) * 64], op0=ALU.mult, op1=ALU.add, r=[potok, cft, "big"],
                           pw=[oacct])

                def fin_s(qt, po, potok, h=h):
                    self.E("dve", "reciprocal", out=rz, in_=po[:, 64:65], r=[potok], w=[rzt])
                    self.E("dve", "tensor_tensor", out=cf, in0=rz, in1=self.g_sb[:, qt, h * 3 + 1:h * 3 + 2], op=ALU.mult,
                           r=[rzt, "g_sb"], w=[cft])
                    self.E("dve", "scalar_tensor_tensor", out=o_all[:, qt, h * 64:(h + 1) * 64], in0=po[:, 0:64],
                           scalar=cf[:, 0:1], in1=oacc[:, qt, :], op0=ALU.mult, op1=ALU.add, r=[potok, cft, oacct],
                           pw=["big"])

                side = sel_steps() if hl == 0 else None
                self.attention(q_ap, qtok, kwTa, kwt, Vw, Vwt, 65, win_pairs, fin_w, pts, f"w{h}", side=side)
                if side is not None:
                    for _ in side:
                        pass
                self.attention(q_ap, [qtok, ("qhi", hl)], ksTa, kst, Vs, Vst, 65, sel_pairs, fin_s, pts, f"s{h}")

    def phase_O(self, li, src):
        self.phase()
        NT = self.NT
        j = li // 2
        moba = (li % 2 == 0)
        o_all = self.big[:, 0:NT * D].rearrange("p (t f) -> p t f", f=D)
        Wo, Wot = self.ph("Wo", [128, 8, D], BF16)
        Wg, Wgt = self.ph("Wg", [128, 8, D], BF16)
        Wp, Wpt = self.ph("Wp", [128, 2, D], BF16)
        self.load_w(Wo, Wot, (self.moba_w_out if moba else self.nsa_w_out)[j], D, None, None)
        self.load_gcol(self.gcol2, "gcol2", self.ple_gate_gain[li])
        self.load_w(Wg, Wgt, self.ple_w_gate[li], D, self.gcol2, "gcol2")
        self.load_w(Wp, Wpt, self.ple_w_proj[li], D, None, None, nk=2)
        xt = [self.ph(f"oxt{i}", [128, D], F32) for i in range(2)]
        zt = [self.ph(f"ozt{i}", [128, D], BF16) for i in range(2)]
        pt_ = [self.ph(f"opt{i}", [128, PLE], F32) for i in range(2)]
        og = [self.ph(f"og{i}", [128, D], BF16) for i in range(2)]
        ogT = [self.ph(f"ogT{i}", [128, 8, 128], BF16) for i in range(2)]
        x1 = [self.ph(f"x1_{i}", [128, D], F32) for i in range(2)]
        sqs, sqst = self.ph("osq", [128, D], F32)
        rs = [self.ph(f"ors{i}", [128, 1], F32) for i in range(2)]
        xn, xnt = self.ph("xn", [128, D], BF16)
        xnT, xnTt = self.ph("xnT", [128, 8, 128], BF16)
        gate, gatet = self.ph("gatef", [128, D], F32)
        pbf, pbft = self.ph("pbf", [128, PLE], BF16)
        pT, pTt = self.ph("pT", [128, 2, 128], BF16)
        x2 = [self.ph(f"x2_{i}", [128, D], F32) for i in range(2)]

        def S1(t):
            x_ap, xtok = xt[t % 2]
            z_ap, ztok = zt[t % 2]
            p_ap, ptok = pt_[t % 2]
            og_ap, ogt = og[t % 2]
            ogT_ap, ogTt = ogT[t % 2]
            x1_ap, x1t = x1[t % 2]
            self.dma(x_ap, src[t * 128:(t + 1) * 128, :], f"oxl{t % 2}", r=[("y", t)], w=[xtok])
            self.dma(z_ap, self.zs_d[t * 128:(t + 1) * 128, :], f"ozl{t % 2}", r=[("zs_d",)], w=[ztok])
            self.dma(p_ap, self.p_in[li, t * 128:(t + 1) * 128, :], f"opl{t % 2}", w=[ptok])
            self.E("dve", "tensor_tensor", out=og_ap, in0=o_all[:, t, :], in1=z_ap, op=ALU.mult, r=["big", ztok], w=[ogt])
            self.transpose8(og_ap, ogt, ogT_ap, ogTt, 0)
            for c in range(2):
                ps = self.pf[c]
                for k in range(8):
                    self.mm(ps[:, :], ogT_ap[:, k, :], Wo[:, k, c * 512:(c + 1) * 512], k == 0, k == 7, r=[ogTt, Wot],
                            w=[("pf", c)] if k == 0 else ())
                last = self.S.q["pe"][-1]
                self.S.tw[("pf", c)] = {last.stream: last}
                self.E("dve", "tensor_tensor", out=x1_ap[:, c * 512:(c + 1) * 512], in0=x_ap[:, c * 512:(c + 1) * 512],
                       in1=ps[:, :], op=ALU.add, r=[xtok, ("pf", c)], **(dict(w=[x1t]) if c == 0 else dict(pw=[x1t])))

        def S2a(t):
            x1_ap, x1t = x1[t % 2]
            r_ap, rtok = rs[t % 2]
            self.rstd_of(x1_ap, x1t, D, sqs, sqst, r_ap, rtok)
            self.E("dve", "tensor_scalar", out=xn, in0=x1_ap, scalar1=r_ap[:, 0:1], scalar2=None, op0=ALU.mult,
                   r=[x1t, rtok], w=[xnt])

        def S2b(t):
            x1_ap, x1t = x1[t % 2]
            p_ap, ptok = pt_[t % 2]
            self.transpose8(xn, xnt, xnT, xnTt, 0)
            for c in range(2):
                ps = self.pf[2 + c]
                for k in range(8):
                    self.mm(ps[:, :], xnT[:, k, :], Wg[:, k, c * 512:(c + 1) * 512], k == 0, k == 7, r=[xnTt, Wgt],
                            w=[("pf", 2 + c)] if k == 0 else ())
                last = self.S.q["pe"][-1]
                self.S.tw[("pf", 2 + c)] = {last.stream: last}
                self.act(gate[:, c * 512:(c + 1) * 512], ps[:, :], AF.Sigmoid, r=[("pf", 2 + c)],
                         **(dict(w=[gatet]) if c == 0 else dict(pw=[gatet])))
            self.E("dve", "tensor_copy", out=pbf, in_=p_ap, r=[ptok], w=[pbft])
            self.transpose8(pbf, pbft, pT, pTt, 1, nblk=2)
            x2_ap, x2tok = x2[t % 2]
            for c in range(2):
                ps = self.pf[4 + c]
                for k in range(2):
                    self.mm(ps[:, :], pT[:, k, :], Wp[:, k, c * 512:(c + 1) * 512], k == 0, k == 1, r=[pTt, Wpt],
                            w=[("pf", 4 + c)] if k == 0 else ())
                last = self.S.q["pe"][-1]
                self.S.tw[("pf", 4 + c)] = {last.stream: last}
                sl = slice(c * 512, (c + 1) * 512)
                self.E("dve", "tensor_tensor", out=x2_ap[:, sl], in0=gate[:, sl], in1=ps[:, :], op=ALU.mult,
                       r=[gatet, ("pf", 4 + c)], **(dict(w=[x2tok]) if c == 0 else dict(pw=[x2tok])))
                self.E("dve", "tensor_tensor", out=x2_ap[:, sl], in0=x2_ap[:, sl], in1=x1_ap[:, sl], op=ALU.add,
                       r=[x2tok, x1t], pw=[x2tok])
            self.dma(self.y[t * 128:(t + 1) * 128, :], x2_ap, f"oyst{t % 2}", r=[x2tok], w=[("y", t)], q="pool")

        S1(0)
        for t in range(NT):
            S2a(t)
            if t + 1 < NT:
                S1(t + 1)
            S2b(t)


def rope_tables(T):
    NT = T // 128
    half = 32
    inv_freq = (10000.0 ** (-np.arange(half, dtype=np.float32) / half)).astype(np.float32)
    pos = (np.arange(NT)[None, :] * 128 + np.arange(128)[:, None]).astype(np.float32)
    ang = pos[:, :, None] * inv_freq[None, None, :]
    cs = np.stack([np.cos(ang), np.sin(ang)], 0).astype(np.float32)
    NCT = max(1, (T // 16) // 128)
    posc = (16.0 * (np.arange(NCT)[None, :] * 128 + np.arange(128)[:, None]) + 31.0).astype(np.float32)
    angc = posc[:, :, None] * inv_freq[None, None, :]
    csc = np.stack([np.cos(angc), np.sin(angc)], 0).astype(np.float32)
    return cs, csc


_CACHE = {}


def run(inputs, T, layers, n_cores):
    key = (T, tuple(layers))
    if key not in _CACHE:
        _CACHE[key] = Builder(T, list(layers)).build()
    nc = _CACHE[key]
    cs, csc = rope_tables(T)
    shared = {k: np.ascontiguousarray(v, dtype=np.float32) for k, v in inputs.items() if k not in ("x", "p")}
    shared["rope_cs"] = cs
    shared["rope_cs_c"] = csc
    in_maps = []
    for b in range(n_cores):
        m = dict(shared)
        m["x"] = np.ascontiguousarray(inputs["x"][b], dtype=np.float32)
        m["p"] = np.ascontiguousarray(inputs["p"][:, b], dtype=np.float32)
        in_maps.append(m)
    res = run_bass_kernel_spmd(nc, in_maps, core_ids=list(range(n_cores)))
    return np.stack([r["y"] for r in res.results], 0).astype(np.float32)


def kernel(**inputs):
    return run(inputs, 4096, [0, 1, 2, 3], 8)
```

```python
import math
from contextlib import ExitStack

import numpy as np
import concourse.bass as bass
import concourse.mybir as mybir
from concourse.bass_utils import run_bass_kernel_spmd

F32 = mybir.dt.float32
BF16 = mybir.dt.bfloat16
I32 = mybir.dt.int32
AF = mybir.ActivationFunctionType
ALU = mybir.AluOpType
AX = mybir.AxisListType

D = 1024
H = 16
HD = 64
PLE = 256
EPS = 1e-6
NSA_IN = 3632
NEGB = 30000.0


class Op:
    __slots__ = ("eng", "fn", "deps", "needed", "sigval", "stream", "is_dma", "idx")


class Sched:
    ENGS = ("pe", "act", "dve", "pool", "sp")

    def __init__(self):
        self.q = {e: [] for e in self.ENGS}
        self.tw = {}
        self.tr = {}
        self.dma_cnt = {}
        self.nops = 0
        self.last = {}
        self.pending = {}

    def barrier(self):
        for e in self.ENGS:
            self.pending[e] = list(self.last.values())

    def op(self, eng, fn, reads=(), writes=(), pwrites=(), dma=None):
        o = Op()
        o.eng = eng
        o.fn = fn
        o.is_dma = dma is not None
        o.stream = ("dma", dma) if dma is not None else ("eng", eng)
        o.needed = o.is_dma
        o.sigval = None
        o.idx = self.nops
        self.nops += 1
        deps = {}

        def add(d):
            if (not o.is_dma) and (not d.is_dma) and d.eng == "pe" and eng == "pe":
                return
            k = d.stream
            if k not in deps or deps[k].idx < d.idx:
                deps[k] = d

        for d in self.pending.pop(eng, ()):
            add(d)
        for t in reads:
            for d in self.tw.get(t, {}).values():
                add(d)
        for t in writes:
            for d in self.tw.get(t, {}).values():
                add(d)
            for d in self.tr.get(t, {}).values():
                add(d)
        for t in pwrites:
            for d in self.tr.get(t, {}).values():
                add(d)
        o.deps = list(deps.values())
        for d in o.deps:
            d.needed = True
        for t in reads:
            self.tr.setdefault(t, {})[o.stream] = o
        for t in writes:
            self.tw[t] = {o.stream: o}
            self.tr[t] = {}
        for t in pwrites:
            self.tw.setdefault(t, {})[o.stream] = o
        if o.is_dma:
            c = self.dma_cnt.get(dma, 0) + 16
            self.dma_cnt[dma] = c
            o.sigval = c
        self.last[o.stream] = o
        self.q[eng].append(o)
        return o

    def emit(self, nc, stack):
        for e in self.ENGS:
            c = 0
            for o in self.q[e]:
                if not o.is_dma and o.needed:
                    c += 1
                    o.sigval = c
        sems = {}
        for e in self.ENGS:
            sems[("eng", e)] = stack.enter_context(nc.semaphore("s_" + e))
        for d in self.dma_cnt:
            sems[("dma", d)] = stack.enter_context(nc.semaphore("d_" + str(d)))
        block = stack.enter_context(nc.Block())
        q = self.q

        def run(ename, eng):
            seen = {}
            for o in q[ename]:
                for d in o.deps:
                    v = d.sigval
                    if seen.get(d.stream, 0) >= v:
                        continue
                    seen[d.stream] = v
                    eng.wait_ge(sems[d.stream], v)
                ins = o.fn(eng)
                if o.needed:
                    ins.then_inc(sems[o.stream], 16 if o.is_dma else 1)

        @block.tensor
        def _(eng):
            run("pe", eng)

        @block.scalar
        def _(eng):
            run("act", eng)

        @block.vector
        def _(eng):
            run("dve", eng)

        @block.gpsimd
        def _(eng):
            run("pool", eng)

        @block.sync
        def _(eng):
            run("sp", eng)
            for d, c in self.dma_cnt.items():
                eng.wait_ge(sems[("dma", d)], c)


class Builder:
    def __init__(self, T, layers):
        self.T = T
        self.NT = T // 128
        self.NC = T // 16
        self.NCT = max(1, self.NC // 128)
        self.layers = layers
        self.nc = bass.Bass("TRN2", target_bir_lowering=False)
        self.S = Sched()
        self.uid = 0

    def E(self, eng, meth, *a, r=(), w=(), pw=(), **kw):
        self.S.op(eng, lambda e: getattr(e, meth)(*a, **kw), reads=r, writes=w, pwrites=pw)

    def dma(self, out, in_, sem, r=(), w=(), pw=(), q="sp", **kw):
        self.S.op(q, lambda e: e.dma_start(out=out, in_=in_, **kw), reads=r, writes=w, pwrites=pw, dma=sem)

    def mm(self, out, lhsT, rhs, start, stop, r=(), w=()):
        self.S.op("pe", lambda e: e.matmul(out, lhsT=lhsT, rhs=rhs, start=start, stop=stop), reads=r, writes=w)

    def tr(self, out, in_, r=(), w=()):
        ident = self.ident
        self.S.op("pe", lambda e: e.transpose(out=out, in_=in_, identity=ident[:]), reads=tuple(r) + ("ident",), writes=w)

    def act(self, out, in_, func, r=(), w=(), pw=(), **kw):
        self.S.op("act", lambda e: e.activation(out=out, in_=in_, func=func, **kw), reads=r, writes=w, pwrites=pw)

    def persist(self, name, shape, dt):
        return self.st.enter_context(self.nc.sbuf_tensor(name, shape, dt))

    def phase(self):
        self.S.barrier()
        self.aoff = 0
        self.pid = getattr(self, "pid", 0) + 1

    def ph(self, name, shape, dt):
        esz = 4 if dt in (F32, I32) else 2
        n = 1
        for s_ in shape[1:]:
            n *= s_
        nbytes = (n * esz + 63) // 64 * 64
        a = self.aoff
        self.aoff += nbytes
        assert self.aoff <= self.ARENA_BYTES, (name, self.aoff)
        v = self.arena[:, a // 2:(a + n * esz) // 2]
        if esz == 4:
            v = v.bitcast(dt)
        if len(shape) == 3:
            v = v.rearrange("p (a b) -> p a b", b=shape[2])
        elif len(shape) == 4:
            v = v.rearrange("p (a b c) -> p a b c", b=shape[2], c=shape[3])
        return v, ("ph", self.pid, name)

    def build(self):
        nc, T, NT = self.nc, self.T, self.NT
        dt = nc.dram_tensor
        self.x_in = dt("x", [T, D], F32, kind="ExternalInput").ap()
        self.p_in = dt("p", [4, T, PLE], F32, kind="ExternalInput").ap()
        self.norm_gain = dt("norm_gain", [4, D], F32, kind="ExternalInput").ap()
        self.moba_w_in = dt("moba_w_in", [2, D, 4 * D], F32, kind="ExternalInput").ap()
        self.moba_q_gain = dt("moba_q_gain", [2, HD], F32, kind="ExternalInput").ap()
        self.moba_k_gain = dt("moba_k_gain", [2, HD], F32, kind="ExternalInput").ap()
        self.moba_w_out = dt("moba_w_out", [2, D, D], F32, kind="ExternalInput").ap()
        self.nsa_w_in = dt("nsa_w_in", [2, D, NSA_IN], F32, kind="ExternalInput").ap()
        self.nsa_q_gain = dt("nsa_q_gain", [2, HD], F32, kind="ExternalInput").ap()
        self.nsa_k_gain = dt("nsa_k_gain", [2, 3, HD], F32, kind="ExternalInput").ap()
        self.nsa_cmp_pe = dt("nsa_cmp_pe", [2, 2, 32, HD], F32, kind="ExternalInput").ap()
        self.nsa_cmp_w1 = dt("nsa_cmp_w1", [2, 2, 2048, 256], F32, kind="ExternalInput").ap()
        self.nsa_cmp_w2 = dt("nsa_cmp_w2", [2, 2, 256, HD], F32, kind="ExternalInput").ap()
        self.nsa_w_out = dt("nsa_w_out", [2, D, D], F32, kind="ExternalInput").ap()
        self.ple_w_proj = dt("ple_w_proj", [4, PLE, D], F32, kind="ExternalInput").ap()
        self.ple_gate_gain = dt("ple_gate_gain", [4, D], F32, kind="ExternalInput").ap()
        self.ple_w_gate = dt("ple_w_gate", [4, D, D], F32, kind="ExternalInput").ap()
        self.rope_cs = dt("rope_cs", [2, 128, NT, 32], F32, kind="ExternalInput").ap()
        self.rope_cs_c = dt("rope_cs_c", [2, 128, self.NCT, 32], F32, kind="ExternalInput").ap()
        self.y = dt("y", [T, D], F32, kind="ExternalOutput").ap()
        self.qT_d = dt("qT_d", [D, T], BF16).ap()
        self.kT_d = dt("kT_d", [D, T], BF16).ap()
        self.v_d = dt("v_d", [T, D], BF16).ap()
        self.zs_d = dt("zs_d", [T, D], BF16).ap()

        with ExitStack() as st:
            self.st = st
            self.ARENA_BYTES = 100 * 1024
            self.arena = self.persist("arena", [128, self.ARENA_BYTES // 2], BF16)
            self.big = self.persist("big", [128, 32768], BF16)
            self.ident = self.persist("ident", [128, 128], BF16)
            self.tri = self.persist("tri", [128, 128], BF16)
            self.atri = self.persist("atri", [128, 128], BF16)
            self.cmpm = self.persist("cmpm", [128, 17, 128], BF16)
            self.cs = self.persist("cs", [128, 2, NT, 32], F32)
            self.csc = self.persist("csc", [128, 2, self.NCT, 32], F32)
            self.g_sb = self.persist("g_sb", [128, NT, 48], F32)
            self.msel = self.persist("msel", [128, NT, 64], F32)
            self.kcTa = self.persist("kcTa", [128, 4, self.NC], BF16)
            self.vca = self.persist("vca", [128, 4, self.NCT, 128], BF16)
            self.gq = self.persist("gq", [128, HD], F32)
            self.gk = self.persist("gk", [128, 3, HD], F32)
            self.gcol = self.persist("gcol", [128, 8], F32)
            self.gcol2 = self.persist("gcol2", [128, 8], F32)
            self.pf = [st.enter_context(nc.psum_tensor(f"pf{i}", [128, 512], F32)) for i in range(6)]
            self.pb = [st.enter_context(nc.psum_tensor(f"pb{i}", [128, 1024], BF16)) for i in range(2)]
            self.consts()
            first = True
            for li in self.layers:
                src = self.x_in if first else self.y
                first = False
                if li % 2 == 0:
                    self.phase_P(li, src, moba=True)
                    self.phase_A_moba(li)
                else:
                    import os
                    stop = os.environ.get("NSA_STOP", "")
                    self.stop = stop
                    self.phase_P(li, src, moba=False)
                    if stop == "P":
                        break
                    self.phase_C(li)
                    if stop == "C":
                        break
                    self.phase_A_nsa(li)
                    if stop in ("cmp", "sel", "A"):
                        break
                self.phase_O(li, src)
            self.S.emit(nc, st)
        return nc

    def consts(self):
        self.phase()
        NT = self.NT
        tf, tft = self.ph("c_tf", [128, 128], F32)
        self.E("pool", "memset", tf, 1.0, w=[tft])
        self.E("pool", "affine_select", out=tf, in_=tf, pattern=[[1, 128]], compare_op=ALU.is_equal, fill=0.0,
               base=0, channel_multiplier=-1, r=[tft], w=[tft])
        self.E("dve", "tensor_copy", out=self.ident[:], in_=tf, r=[tft], w=["ident"])
        tf2, tf2t = self.ph("c_tf2", [128, 128], F32)
        self.E("pool", "memset", tf2, 0.0, w=[tf2t])
        self.E("pool", "affine_select", out=tf2, in_=tf2, pattern=[[1, 128]], compare_op=ALU.is_ge, fill=-NEGB,
               base=0, channel_multiplier=-1, r=[tf2t], w=[tf2t])
        self.E("dve", "tensor_copy", out=self.tri[:], in_=tf2, r=[tf2t], w=["tri"])
        tf3, tf3t = self.ph("c_tf3", [128, 128], F32)
        self.E("pool", "memset", tf3, 0.0, w=[tf3t])
        self.E("pool", "affine_select", out=tf3, in_=tf3, pattern=[[-1, 128]], compare_op=ALU.is_ge, fill=-NEGB,
               base=-1, channel_multiplier=1, r=[tf3t], w=[tf3t])
        self.E("dve", "tensor_copy", out=self.atri[:], in_=tf3, r=[tf3t], w=["atri"])
        tf4, tf4t = self.ph("c_tf4", [128, 17, 128], F32)
        self.E("pool", "memset", tf4, 0.0, w=[tf4t])
        self.E("pool", "affine_select", out=tf4, in_=tf4, pattern=[[128, 17], [1, 128]], compare_op=ALU.is_ge,
               fill=-NEGB, base=-31, channel_multiplier=-16, r=[tf4t], w=[tf4t])
        self.E("dve", "tensor_copy", out=self.cmpm[:], in_=tf4, r=[tf4t], w=["cmpm"])
        self.dma(self.cs[:, 0], self.rope_cs[0], "c_cs", w=["cs"])
        self.dma(self.cs[:, 1], self.rope_cs[1], "c_cs2", pw=["cs"])
        self.dma(self.csc[:, 0], self.rope_cs_c[0], "c_csc", w=["csc"])
        self.dma(self.csc[:, 1], self.rope_cs_c[1], "c_csc2", pw=["csc"])
        m = self.msel
        self.E("pool", "memset", m[:], 0.0, w=["msel"])
        for half in range(2):
            mh = m[half * 64:(half + 1) * 64]
            self.E("pool", "affine_select", out=mh, in_=mh, pattern=[[2, NT], [-1, 64]], compare_op=ALU.is_ge,
                   fill=1.0e4, base=half - 2, channel_multiplier=0, r=["msel"], w=["msel"])
            self.E("pool", "affine_select", out=mh, in_=mh, pattern=[[2, NT], [-1, 64]], compare_op=ALU.is_ge,
                   fill=-1.0e4, base=half, channel_multiplier=0, r=["msel"], w=["msel"])
        self.E("pool", "memset", m[:, :, 0:1], 1.0e4, r=["msel"], w=["msel"])
        ov, ovt = self.ph("c_ov", [128, self.NCT, 64], F32)
        self.E("pool", "memset", ov, 1.0, w=[ovt])
        for ct in range(self.NCT):
            o1 = ov[:, ct, :]
            self.E("pool", "affine_select", out=o1, in_=o1, pattern=[[4, 64]], compare_op=ALU.is_ge, fill=0.0,
                   base=3 - 128 * ct, channel_multiplier=-1, r=[ovt], w=[ovt])
            self.E("pool", "affine_select", out=o1, in_=o1, pattern=[[-4, 64]], compare_op=ALU.is_ge, fill=0.0,
                   base=1 + 128 * ct, channel_multiplier=1, r=[ovt], w=[ovt])
        self.E("dve", "memset", self.vca[:], 1.0, w=["vca"])
        for g in range(4):
            for ct in range(self.NCT):
                self.E("dve", "tensor_copy", out=self.vca[:, g, ct, 65:128], in_=ov[:, ct, 0:63], r=[ovt, "vca"], w=["vca"])
        self.E("dve", "memset", self.kcTa[:], 0.0, w=["kcTa"])

    def load_gcol(self, dst, tok, src_row):
        self.dma(dst[:], src_row.rearrange("(k p) -> p k", p=128), "gcol_" + tok, w=[tok],
                 allow_slow_non_contiguous=True)

    def load_w(self, dst3, dtok, w_ap, ncols, gcol, gtok, nk=8):
        if getattr(self, "_stg_pid", None) != self.pid:
            self._stg = [self.ph(f"wstg{i}", [128, 1024], F32) for i in range(2)]
            self._stg_pid = self.pid
        stg = self._stg
        i = 0
        first = True
        for k in range(nk):
            for c0 in range(0, ncols, 1024):
                cw = min(1024, ncols - c0)
                s_ap, s_tok = stg[i % 2]
                self.dma(s_ap[:, 0:cw], w_ap[k * 128:(k + 1) * 128, c0:c0 + cw], f"wst{i % 2}", w=[s_tok])
                kw = dict(r=[s_tok] + ([gtok] if gcol is not None else []))
                if first:
                    kw["w"] = [dtok]
                    first = False
                else:
                    kw["pw"] = [dtok]
                if gcol is not None:
                    self.E("dve", "tensor_scalar", out=dst3[:, k, c0:c0 + cw], in0=s_ap[:, 0:cw], scalar1=gcol[:, k:k + 1],
                           scalar2=None, op0=ALU.mult, **kw)
                else:
                    self.E("dve", "tensor_copy", out=dst3[:, k, c0:c0 + cw], in_=s_ap[:, 0:cw], **kw)
                i += 1

    def rstd_of(self, x_ap, xtok, n, scratch, stok, out_col, otok):
        self.act(scratch, x_ap, AF.Square, r=[xtok], w=[stok, otok], accum_out=out_col)
        self.act(out_col, out_col, AF.Sqrt, r=[otok], w=[otok], scale=1.0 / n, bias=EPS)
        self.E("dve", "reciprocal", out=out_col, in_=out_col, r=[otok], w=[otok])

    def transpose8(self, src_bf, stok, dstT, dtok, pbi, nblk=8):
        pbt = self.pb[pbi]
        ptok = ("pb", pbi)
        for k in range(nblk):
            kw = dict(w=[ptok]) if k == 0 else dict(w=())
            if k == 0:
                self.tr(pbt[:, k * 128:(k + 1) * 128], src_bf[:, k * 128:(k + 1) * 128], r=[stok], w=[ptok])
            else:
                self.S.op("pe", (lambda e, k=k: e.transpose(out=pbt[:, k * 128:(k + 1) * 128], in_=src_bf[:, k * 128:(k + 1) * 128],
                                                            identity=self.ident[:])), reads=[stok, "ident"], pwrites=[ptok])
        self.act(dstT.rearrange("p a b -> p (a b)") if len(dstT.shape) == 3 else dstT, pbt[:, 0:nblk * 128], AF.Copy, r=[ptok], w=[dtok])

    def norm_rope(self, src, stok, nh, gain_ap, gtok, cos_ap, sin_ap, cstok, out_bf, otok, tmp):
        (sq, sqt), (ss, sst), (A, At), (Bt_, Btt), (t13, t13t), (tsw, tswt) = tmp
        W = nh * 64
        v3 = lambda ap: ap[:, 0:W].rearrange("p (h d) -> p h d", d=64)
        v4 = lambda ap: ap[:, 0:W].rearrange("p (h a d) -> p h a d", a=2, d=32)
        self.act(sq[:, 0:W], src, AF.Square, r=[stok], w=[sqt])
        self.E("dve", "tensor_reduce", out=ss[:, 0:nh], in_=v3(sq), axis=AX.X, op=ALU.add, r=[sqt], w=[sst])
        self.act(ss[:, 0:nh], ss[:, 0:nh], AF.Sqrt, r=[sst], w=[sst], scale=1.0 / 64, bias=EPS)
        self.E("dve", "reciprocal", out=ss[:, 0:nh], in_=ss[:, 0:nh], r=[sst], w=[sst])
        self.E("dve", "tensor_tensor", out=v3(A), in0=src.rearrange("p (h d) -> p h d", d=64),
               in1=ss[:, 0:nh].unsqueeze(2).to_broadcast([128, nh, 64]), op=ALU.mult, r=[stok, sst], w=[At])
        self.E("dve", "tensor_tensor", out=v3(Bt_), in0=v3(A), in1=gain_ap.unsqueeze(1).to_broadcast([128, nh, 64]),
               op=ALU.mult, r=[At, gtok], w=[Btt])
        cosb = cos_ap.unsqueeze(1).unsqueeze(1).to_broadcast([128, nh, 2, 32])
        sinb = sin_ap.unsqueeze(1).to_broadcast([128, nh, 32])
        self.E("dve", "tensor_tensor", out=v4(t13), in0=v4(Bt_), in1=cosb, op=ALU.mult, r=[Btt, cstok], w=[t13t])
        self.E("dve", "tensor_tensor", out=v4(tsw)[:, :, 0, :], in0=v4(Bt_)[:, :, 1, :], in1=sinb, op=ALU.mult,
               r=[Btt, cstok], w=[tswt])
        self.E("dve", "tensor_tensor", out=v4(tsw)[:, :, 1, :], in0=v4(Bt_)[:, :, 0, :], in1=sinb, op=ALU.mult,
               r=[Btt, cstok, tswt], w=[tswt])
        self.E("dve", "tensor_tensor", out=v4(out_bf)[:, :, 0, :], in0=v4(t13)[:, :, 0, :], in1=v4(tsw)[:, :, 0, :],
               op=ALU.subtract, r=[t13t, tswt], w=[otok])
        self.E("dve", "tensor_tensor", out=v4(out_bf)[:, :, 1, :], in0=v4(t13)[:, :, 1, :], in1=v4(tsw)[:, :, 1, :],
               op=ALU.add, r=[t13t, tswt, otok], w=[otok])

    def phase_P(self, li, src, moba):
        self.phase()
        NT = self.NT
        j = li // 2
        ncols = 4 * D if moba else NSA_IN
        w_ap = (self.moba_w_in if moba else self.nsa_w_in)[j]
        W3 = self.big[:, 0:8 * 4096].rearrange("p (k n) -> p k n", n=4096)
        self.load_gcol(self.gcol, "gcol", self.norm_gain[li])
        self.load_w(W3, "big", w_ap, ncols, self.gcol, "gcol")
        qg = (self.moba_q_gain if moba else self.nsa_q_gain)[j]
        self.dma(self.gq[:], qg.partition_broadcast(128), "gq", w=["gq"])
        if moba:
            self.dma(self.gk[:, 0, :], self.moba_k_gain[j].partition_broadcast(128), "gk", w=["gk"])
        else:
            for b_ in range(3):
                self.dma(self.gk[:, b_, :], self.nsa_k_gain[j, b_].partition_broadcast(128), f"gk{b_}",
                         **(dict(w=["gk"]) if b_ == 0 else dict(pw=["gk"])))
        xt = [self.ph(f"xt{i}", [128, D], F32) for i in range(2)]
        sqs = self.ph("sqs", [128, D], F32)
        rs = [self.ph(f"rs{i}", [128, 1], F32) for i in range(2)]
        hb = [self.ph(f"hb{i}", [128, D], BF16) for i in range(2)]
        hT = [self.ph(f"hT{i}", [128, 8, 128], BF16) for i in range(2)]
        b1, b1t = self.ph("nr_b1", [128, 2048], F32)
        b2, b2t = self.ph("nr_b2", [128, 2048], F32)
        b3, b3t = self.ph("nr_b3", [128, 2048], F32)
        ss, sst = self.ph("nr_ss", [128, 32], F32)
        qbf = [self.ph(f"qbf{i}", [128, 2048], BF16) for i in range(2)]
        qTt = [self.ph(f"qTt{i}", [128, 16, 128], BF16) for i in range(2)]
        vbf = [self.ph(f"vbf{i}", [128, D], BF16) for i in range(2)]
        zbf = [self.ph(f"zbf{i}", [128, D], BF16) for i in range(2)]
        rawbf = [self.ph(f"rawbf{i}", [128, 512], BF16) for i in range(2)]
        if moba:
            chunks = [(c * 512, 512) for c in range(8)]
            HT_ = 32
            srcs = [(0, 0, 8, self.gq[:], "gq"), (1, 0, 8, self.gq[:], "gq"), (2, 0, 8, self.gk[:, 0, :], "gk"),
                    (3, 0, 8, self.gk[:, 0, :], "gk")]
            qk_chunks = [(0, 0), (1, 1), (2, 2), (3, 3)]
            dests = [(self.qT_d, jb * 128) for jb in range(8)] + [(self.kT_d, jb * 128) for jb in range(8)]
        else:
            chunks = [(0, 512), (512, 512), (1024, 512), (1536, 512), (2048, 512), (2560, 48), (2608, 512), (3120, 512)]
            HT_ = 24
            srcs = [(0, 0, 8, self.gq[:], "gq"), (1, 0, 8, self.gq[:], "gq"), (2, 0, 4, self.gk[:, 1, :], "gk"),
                    (3, 0, 4, self.gk[:, 2, :], "gk")]
            qk_chunks = [(0, 0), (1, 1), (3, 2), (4, 3), (2, 4)]
            dests = ([(self.qT_d, jb * 128) for jb in range(8)] + [(self.kT_d, 0), (self.kT_d, 128), (self.kT_d, 256),
                     (self.kT_d, 384)] + [(self.kT_d, 512 + jb * 128) for jb in range(4)])
        WQ = HT_ * 64

        def pre_a(t):
            x_ap, xtok = xt[t % 2]
            self.dma(x_ap, src[t * 128:(t + 1) * 128, :], f"xld{t % 2}", r=[("y", t)], w=[xtok])
            r_ap, rtok = rs[t % 2]
            self.rstd_of(x_ap, xtok, D, sqs[0], sqs[1], r_ap, rtok)
            h_ap, htok = hb[t % 2]
            self.act(h_ap, x_ap, AF.Copy, r=[xtok, rtok], w=[htok], scale=r_ap[:, 0:1])

        def pre_b(t):
            h_ap, htok = hb[t % 2]
            hT_ap, hTtok = hT[t % 2]
            self.transpose8(h_ap, htok, hT_ap, hTtok, 0)

        def mm_chunk(t, ci, bank):
            c0, cw = chunks[ci]
            hT_ap, hTtok = hT[t % 2]
            ps = self.pf[bank]
            pstok = ("pf", bank)
            for k in range(8):
                self.mm(ps[:, 0:cw], hT_ap[:, k, :], W3[:, k, c0:c0 + cw], k == 0, k == 7, r=[hTtok, "big"],
                        w=[pstok] if k == 0 else ())
            last = self.S.q["pe"][-1]
            self.S.tw[pstok] = {last.stream: last}

        def chain(t):
            q_ap, qbt = qbf[t % 2]
            v3 = lambda ap, c0, w: ap[:, c0:c0 + w].rearrange("p (h d) -> p h d", d=64)
            v4 = lambda ap, c0, w: ap[:, c0:c0 + w].rearrange("p (h a d) -> p h a d", a=2, d=32)
            c = 0
            for n_, (bank, col0, nh, g_ap, g_tok) in enumerate(srcs):
                w_ = nh * 64
                self.act(b1[:, c:c + w_], self.pf[bank][:, col0:col0 + w_], AF.Square, r=[("pf", bank)],
                         **(dict(w=[b1t]) if n_ == 0 else dict(pw=[b1t])))
                c += w_
            self.E("dve", "tensor_reduce", out=ss[:, 0:HT_], in_=v3(b1, 0, WQ), axis=AX.X, op=ALU.add, r=[b1t], w=[sst])
            self.act(ss[:, 0:HT_], ss[:, 0:HT_], AF.Sqrt, r=[sst], w=[sst], scale=1.0 / 64, bias=EPS)
            self.E("dve", "reciprocal", out=ss[:, 0:HT_], in_=ss[:, 0:HT_], r=[sst], w=[sst])
            c = 0
            h0 = 0
            for n_, (bank, col0, nh, g_ap, g_tok) in enumerate(srcs):
                w_ = nh * 64
                self.E("dve", "tensor_tensor", out=v3(b2, c, w_),
                       in0=self.pf[bank][:, col0:col0 + w_].rearrange("p (h d) -> p h d", d=64),
                       in1=ss[:, h0:h0 + nh].unsqueeze(2).to_broadcast([128, nh, 64]), op=ALU.mult,
                       r=[("pf", bank), sst], **(dict(w=[b2t]) if n_ == 0 else dict(pw=[b2t])))
                c += w_
                h0 += nh
            c = 0
            first = True
            i_ = 0
            while i_ < len(srcs):
                g_ap, g_tok = srcs[i_][3], srcs[i_][4]
                nh = srcs[i_][2]
                k_ = i_ + 1
                while k_ < len(srcs) and srcs[k_][3] is g_ap:
                    nh += srcs[k_][2]
                    k_ += 1
                w_ = nh * 64
                self.E("dve", "tensor_tensor", out=v3(b1, c, w_), in0=v3(b2, c, w_),
                       in1=g_ap.unsqueeze(1).to_broadcast([128, nh, 64]), op=ALU.mult, r=[b2t, g_tok],
                       **(dict(w=[b1t]) if first else dict(pw=[b1t])))
                first = False
                c += w_
                i_ = k_
            import os
            if "norope" in os.environ.get("PDBG", ""):
                return
            cos_ap = self.cs[:, 0, t, :]
            sin_ap = self.cs[:, 1, t, :]
            cosb = cos_ap.unsqueeze(1).unsqueeze(1).to_broadcast([128, HT_, 2, 32])
            sinb = sin_ap.unsqueeze(1).to_broadcast([128, HT_, 32])
            self.E("dve", "tensor_tensor", out=v4(b2, 0, WQ), in0=v4(b1, 0, WQ), in1=cosb, op=ALU.mult, r=[b1t, "cs"], w=[b2t])
            self.E("dve", "tensor_tensor", out=v4(b3, 0, WQ)[:, :, 0, :], in0=v4(b1, 0, WQ)[:, :, 1, :], in1=sinb, op=ALU.mult,
                   r=[b1t, "cs"], w=[b3t])
            self.E("dve", "tensor_tensor", out=v4(b3, 0, WQ)[:, :, 1, :], in0=v4(b1, 0, WQ)[:, :, 0, :], in1=sinb, op=ALU.mult,
                   r=[b1t, "cs"], pw=[b3t])
            self.E("dve", "tensor_tensor", out=v4(q_ap, 0, WQ)[:, :, 0, :], in0=v4(b2, 0, WQ)[:, :, 0, :],
                   in1=v4(b3, 0, WQ)[:, :, 0, :], op=ALU.subtract, r=[b2t, b3t], w=[qbt])
            self.E("dve", "tensor_tensor", out=v4(q_ap, 0, WQ)[:, :, 1, :], in0=v4(b2, 0, WQ)[:, :, 1, :],
                   in1=v4(b3, 0, WQ)[:, :, 1, :], op=ALU.add, r=[b2t, b3t], pw=[qbt])
            import os
            if not moba and "noraw" not in os.environ.get("PDBG", ""):
                self.E("dve", "tensor_copy", out=rawbf[t % 2][0], in_=self.pf[4][:, 0:512], r=[("pf", 4)], w=[rawbf[t % 2][1]])

        def tq(t):
            q_ap, qbt = qbf[t % 2]
            qt_, qtt = qTt[t % 2]
            for half in range(2):
                pbt = self.pb[1]
                ptok = ("pb", 1)
                for k in range(8):
                    jb = half * 8 + k
                    if (not moba) and jb >= 12:
                        s_ap, s_tok = rawbf[t % 2]
                        s_in = s_ap[:, (jb - 12) * 128:(jb - 11) * 128]
                    else:
                        s_tok = qbt
                        s_in = q_ap[:, jb * 128:(jb + 1) * 128]
                    if k == 0:
                        self.tr(pbt[:, 0:128], s_in, r=[s_tok], w=[ptok])
                    else:
                        self.S.op("pe", (lambda e, k=k, s_in=s_in: e.transpose(out=pbt[:, k * 128:(k + 1) * 128], in_=s_in,
                                                                              identity=self.ident[:])),
                                  reads=[s_tok, "ident"], pwrites=[ptok])
                self.act(qt_[:, half * 8:(half + 1) * 8, :].rearrange("p a b -> p (a b)"), pbt[:, 0:1024], AF.Copy, r=[ptok],
                         **(dict(w=[qtt]) if half == 0 else dict(pw=[qtt])))
            jb = 0
            while jb < 16:
                dram, row0 = dests[jb]
                k_ = jb + 1
                while k_ < 16 and dests[k_][0] is dram and dests[k_][1] == row0 + (k_ - jb) * 128:
                    k_ += 1
                nb = k_ - jb
                self.dma(dram[row0:row0 + nb * 128, t * 128:(t + 1) * 128].rearrange("(j p) t -> p j t", p=128),
                         qt_[:, jb:k_, :], f"qst{t % 2}", r=[qtt], pw=[("kTq_d",)], q="pool")
                jb = k_

        def mm_vz(t):
            v_ap, vtok = vbf[t % 2]
            z_ap, ztok = zbf[t % 2]
            if moba:
                for ci in (4, 5):
                    bank = ci
                    mm_chunk(t, ci, bank)
                    self.act(v_ap[:, (ci - 4) * 512:(ci - 3) * 512], self.pf[bank][:, 0:512], AF.Copy, r=[("pf", bank)],
                             **(dict(w=[vtok]) if ci == 4 else dict(pw=[vtok])))
                self.dma(self.v_d[t * 128:(t + 1) * 128, :], v_ap, f"vst{t % 2}", r=[vtok], pw=[("v_d",)], q="pool")
            else:
                self.act(v_ap[:, 0:256], self.pf[2][:, 256:512], AF.Copy, r=[("pf", 2)], w=[vtok])
                self.act(v_ap[:, 256:512], self.pf[3][:, 256:512], AF.Copy, r=[("pf", 3)], pw=[vtok])
                self.dma(self.v_d[t * 128:(t + 1) * 128, 0:512], v_ap[:, 0:512], f"vst{t % 2}", r=[vtok], pw=[("v_d",)], q="pool")
                mm_chunk(t, 5, 5)
                self.act(self.g_sb[:, t, :], self.pf[5][:, 0:48], AF.Sigmoid, r=[("pf", 5)], pw=["g_sb"])
            for ci in (6, 7):
                bank = ci - 2
                mm_chunk(t, ci, bank)
                self.act(z_ap[:, (ci - 6) * 512:(ci - 5) * 512], self.pf[bank][:, 0:512], AF.Silu, r=[("pf", bank)],
                         **(dict(w=[ztok]) if ci == 6 else dict(pw=[ztok])))
            self.dma(self.zs_d[t * 128:(t + 1) * 128, :], z_ap, f"zst{t % 2}", r=[ztok], pw=[("zs_d",)], q="pool")

        import os
        dbg = os.environ.get("PDBG", "")
        pre_a(0)
        pre_b(0)
        for t in range(NT):
            for (ci, bank) in qk_chunks:
                mm_chunk(t, ci, bank)
            if t + 1 < NT:
                pre_a(t + 1)
            if "nochain" not in dbg:
                chain(t)
            if t + 1 < NT:
                pre_b(t + 1)
            if "novz" not in dbg:
                mm_vz(t)
            if t >= 1 and "notq" not in dbg:
                tq(t - 1)
        if "notq" not in dbg:
            tq(NT - 1)

    def attention(self, qTa, qtok, kTa, ktok, Va, vtok, vw, pairs_fn, fin, pts, tag, side=None, side_delay=0):
        NT = self.NT
        qtoks = list(qtok) if isinstance(qtok, list) else [qtok]
        units = []
        for qt in range(NT):
            pairs = pairs_fn(qt)
            ng = (len(pairs) + 3) // 4
            for gi in range(ng):
                units.append((qt, pairs[gi * 4:(gi + 1) * 4], gi == 0, gi == ng - 1))
        n = len(units)
        NBK = len(pts)
        SK = NBK - 1
        banks = [0, 1, 2, 3][:NBK]
        u0 = getattr(self, "_u", 0)

        def qk(i):
            qt, grp, _, _ = units[i]
            si = banks[(u0 + i) % NBK]
            ps = self.pf[si]
            pstok = ("pf", si)
            firstw = True
            for jx, (kt, bias) in enumerate(grp):
                o_ap = ps[:, jx * 128:(jx + 1) * 128]
                self.mm(o_ap, kTa[:, kt * 128:(kt + 1) * 128], qTa[:, qt * 128:(qt + 1) * 128], True, bias is None,
                        r=[ktok] + qtoks, w=[pstok] if firstw else ())
                firstw = False
                if bias is not None:
                    self.mm(o_ap, self.ident[:], bias, False, True, r=["ident", "tri", "atri", "cmpm"])
            last = self.S.q["pe"][-1]
            self.S.tw[pstok] = {last.stream: last}

        def pv(i):
            qt, grp, isf, isl = units[i]
            si = banks[(u0 + i) % NBK]
            ps = self.pf[si]
            pstok = ("pf", si)
            pt_ap, pttok = pts[(u0 + i) % NBK]
            po = self.pf[4 + qt % 2]
            potok = ("pf", 4 + qt % 2)
            nn = len(grp) * 128
            self.act(pt_ap[:, 0:nn], ps[:, 0:nn], AF.Exp, r=[pstok], w=[pttok], scale=0.125)
            for jx, (kt, bias) in enumerate(grp):
                is_first = isf and jx == 0
                is_last = isl and jx == len(grp) - 1
                self.mm(po[:, 0:vw], pt_ap[:, jx * 128:(jx + 1) * 128], Va[:, kt, 0:vw], is_first, is_last,
                        r=[pttok, vtok], w=[potok] if is_first else ())
            last = self.S.q["pe"][-1]
            self.S.tw[potok] = {last.stream: last}
            if isl:
                fin(qt, po, potok)

        for i in range(n + SK):
            if i < n:
                qk(i)
            if i >= SK:
                pv(i - SK)
                if side is not None and i - SK >= side_delay:
                    next(side, None)
        self._u = u0 + n

    def phase_A_moba(self, li):
        self.phase()
        T, NT = self.T, self.NT
        NB = T // 256
        o_all = self.big[:, 0:NT * D].rearrange("p (t f) -> p t f", f=D)
        qTa = [self.ph(f"qTa{i}", [128, T], BF16) for i in range(2)]
        kTa = [self.ph(f"kTa{i}", [128, T], BF16) for i in range(2)]
        Va = [self.ph(f"Va{i}", [128, NT, 65], BF16) for i in range(2)]
        pts = [self.ph(f"pt{i}", [128, 512], BF16) for i in range(4)]
        kmf, kmft = self.ph("kmf", [128, 16], F32)
        kmT, kmTt = self.ph("kmT", [128, 16], BF16)
        gt_, gtt = self.ph("gate", [128, 16], F32)
        m8, m8t = self.ph("m8", [128, 8], F32)
        bq, bqt = self.ph("bq", [128, 128], BF16)
        rz, rzt = self.ph("rz", [128, 1], F32)
        self.E("dve", "memset", kmT, 0.0, w=[kmTt])
        self.E("dve", "memset", bq, 0.0, w=[bqt])
        for i in range(2):
            q_ap, qtok = qTa[i]
            k_ap, ktok = kTa[i]
            v_ap, vtok = Va[i]
            self.E("pool", "memset", q_ap, 0.0, w=[qtok])
            self.E("pool", "memset", k_ap[0:64, :], 0.0, w=[ktok])
            ke = k_ap[64:128, :]
            self.E("pool", "memset", ke, NEGB, r=[ktok], w=[ktok])
            self.E("pool", "affine_select", out=ke, in_=ke, pattern=[[1, T]], compare_op=ALU.is_ge, fill=0.0, base=0,
                   channel_multiplier=-256, r=[ktok], w=[ktok])
            self.E("pool", "affine_select", out=ke, in_=ke, pattern=[[-1, T]], compare_op=ALU.is_ge, fill=0.0, base=255,
                   channel_multiplier=256, r=[ktok], w=[ktok])
            self.E("dve", "memset", v_ap, 1.0, w=[vtok])

        def load_head(h):
            i = h % 2
            q_ap, qtok = qTa[i]
            k_ap, ktok = kTa[i]
            v_ap, vtok = Va[i]
            self.dma(q_ap[0:64, :], self.qT_d[h * 64:(h + 1) * 64, :], f"aq{i}", r=[("kTq_d",)], w=[qtok])
            self.dma(k_ap[0:64, :], self.kT_d[h * 64:(h + 1) * 64, :], f"ak{i}", r=[("kTq_d",)], w=[ktok])
            for c in range(0, NT, 8):
                n = min(8, NT - c)
                self.dma(v_ap[:, c:c + n, 0:64],
                         self.v_d[c * 128:(c + n) * 128, h * 64:(h + 1) * 64].rearrange("(t p) d -> p t d", p=128),
                         f"av{i}", r=[("v_d",)], **(dict(w=[vtok]) if c == 0 else dict(pw=[vtok])))

        def gate_steps(h):
            i = h % 2
            q_ap, qtok = qTa[i]
            k_ap, ktok = kTa[i]
            self.E("dve", "tensor_reduce", out=kmf[0:64, 0:NB], in_=k_ap[0:64, :].rearrange("p (n k) -> p n k", k=256),
                   axis=AX.X, op=ALU.add, r=[ktok], w=[kmft])
            self.E("dve", "tensor_scalar", out=kmT[0:64, 0:NB], in0=kmf[0:64, 0:NB], scalar1=1.0 / 256, scalar2=None,
                   op0=ALU.mult, r=[kmft], w=[kmTt])
            self.E("dve", "memset", gt_, -1.0e30, w=[gtt])
            yield
            for qt in range(NT):
                b = qt // 2
                if b <= 3:
                    continue
                pg = self.pb[0][:, 0:32].bitcast(F32)
                self.mm(pg[:, 0:16], q_ap[:, qt * 128:(qt + 1) * 128], kmT, True, True, r=[qtok, kmTt], w=[("pb", 0)])
                self.E("dve", "tensor_copy", out=gt_[:, 0:b], in_=pg[:, 0:b], r=[("pb", 0), gtt], w=[gtt])
                self.E("dve", "max", out=m8, in_=gt_, r=[gtt], w=[m8t])
                self.E("dve", "tensor_scalar", out=bq[:, 64:80], in0=gt_, scalar1=m8[:, 2:3], scalar2=1.0, op0=ALU.is_ge,
                       op1=ALU.subtract, r=[gtt, m8t, bqt], w=[bqt])
                self.E("dve", "memset", bq[:, 64 + b:65 + b], 0.0, r=[bqt], w=[bqt])
                yield
                yield
                pbt = self.pb[1]
                self.tr(pbt[:, 0:128], bq, r=[bqt], w=[("pb", 1)])
                self.E("dve", "tensor_copy", out=q_ap[64:128, qt * 128:(qt + 1) * 128], in_=pbt[64:128, 0:128],
                       r=[("pb", 1), qtok], w=[qtok])
                yield

        load_head(0)
        for _ in gate_steps(0):
            pass
        for h in range(H):
            side = None
            if h + 1 < H:
                load_head(h + 1)
                side = gate_steps(h + 1)
            i = h % 2
            q_ap, qtok = qTa[i]
            k_ap, ktok = kTa[i]
            v_ap, vtok = Va[i]

            def pairs_fn(qt):
                return [(kt, (self.tri[:] if kt == qt else None)) for kt in range(qt + 1)]

            def fin(qt, po, potok, h=h):
                self.E("dve", "reciprocal", out=rz, in_=po[:, 64:65], r=[potok], w=[rzt])
                self.E("dve", "tensor_scalar", out=o_all[:, qt, h * 64:(h + 1) * 64], in0=po[:, 0:64], scalar1=rz[:, 0:1],
                       scalar2=None, op0=ALU.mult, r=[potok, rzt], pw=["big"])

            self.attention(q_ap, qtok, k_ap, ktok, v_ap, vtok, 65, pairs_fn, fin, pts, f"m{h}", side=side, side_delay=min(24, self.NT))
            if side is not None:
                for _ in side:
                    pass

    def phase_C(self, li):
        self.phase()
        T, NT = self.T, self.NT
        j = li // 2
        xkp, xkpt = self.ph("xkp", [128, T + 16], BF16)
        xk16, xk16t = self.ph("xk16", [128, 2, 16, T // 16], BF16)
        W1d, W1t = self.ph("W1d", [128, 32, 256], BF16)
        w1s = [self.ph(f"w1s{i}", [128, 8, 256], F32) for i in range(2)]
        W2b, W2t = self.ph("W2b", [128, 2, 64], BF16)
        w2s, w2st = self.ph("w2s", [128, 2, 64], F32)
        peT, peTt = self.ph("peT", [128, 32], BF16)
        pes, pest = self.ph("pes", [128, 32], F32)
        bh, bht = self.ph("bh", [128, 2], F32)
        NC, NCT = self.NC, self.NCT
        xx, xxt = self.ph("xx", [128, NC], F32)
        x2, x2t = self.ph("x2", [128, NC], F32)
        sg, sgt = self.ph("sg", [128, NC], F32)
        gl, glt = self.ph("gl", [128, 2, NC], BF16)
        kcb, kcbt = self.ph("kcb", [128, 128], BF16)
        tmp = [self.ph(n, [128, 64], F32) for n in ("c_sq", "c_ss", "c_A", "c_B", "c_t13", "c_tsw")]
        self.E("pool", "memset", xkp, 0.0, w=[xkpt])
        self.E("pool", "memset", W1d, 0.0, w=[W1t])
        self.E("pool", "memset", peT, 0.0, w=[peTt])
        self.E("pool", "memset", kcb, 0.0, w=[kcbt])
        for kv in range(2):
            w1 = self.nsa_cmp_w1[j, kv].rearrange("(l d) j -> d l j", d=64)
            for c in range(4):
                s_ap, s_tok = w1s[c % 2]
                self.dma(s_ap[0:64], w1[:, c * 8:(c + 1) * 8, :], f"w1s{c % 2}", w=[s_tok])
                self.E("dve", "tensor_copy", out=W1d[0:64, c * 8:(c + 1) * 8, :], in_=s_ap[0:64], r=[s_tok],
                       **(dict(w=[W1t]) if c == 0 else dict(pw=[W1t])))
            self.dma(w2s, self.nsa_cmp_w2[j, kv].rearrange("(c p) d -> p c d", p=128), "w2s", w=[w2st])
            self.E("dve", "tensor_copy", out=W2b, in_=w2s, r=[w2st], w=[W2t])
            for q4 in range(4):
                self.dma(pes[0:64, q4 * 8:(q4 + 1) * 8], self.nsa_cmp_pe[j, kv, q4 * 8:(q4 + 1) * 8, :].rearrange("l d -> d l"),
                         "pes", **(dict(w=[pest]) if q4 == 0 else dict(pw=[pest])), allow_slow_non_contiguous=True)
            self.E("dve", "tensor_copy", out=peT[0:64, :], in_=pes[0:64, :], r=[pest], w=[peTt])
            pbias = self.pf[3]
            for jc in range(2):
                for l in range(32):
                    self.mm(pbias[:, jc:jc + 1], W1d[:, l, jc * 128:(jc + 1) * 128], peT[:, l:l + 1], l == 0, l == 31,
                            r=[W1t, peTt], w=[("pf", 3)] if (l == 0 and jc == 0) else ())
            last = self.S.q["pe"][-1]
            self.S.tw[("pf", 3)] = {last.stream: last}
            self.E("dve", "tensor_copy", out=bh, in_=pbias[:, 0:2], r=[("pf", 3)], w=[bht])
            for g in range(4):
                row0 = 512 + kv * 256 + g * 64
                self.dma(xkp[0:64, 0:T], self.kT_d[row0:row0 + 64, :], "xkp", r=[("kTq_d",)], w=[xkpt])
                self.E("dve", "tensor_copy", out=xk16[:, 0], in_=xkp[:, 0:T].rearrange("p (m r) -> p r m", r=16), r=[xkpt], w=[xk16t])
                self.E("dve", "tensor_copy", out=xk16[:, 1], in_=xkp[:, 16:T + 16].rearrange("p (m r) -> p r m", r=16), r=[xkpt],
                       pw=[xk16t])
                for jc in range(2):
                    ph_ = self.pf[jc]
                    for l in range(32):
                        self.mm(ph_[:, 0:NC], W1d[:, l, jc * 128:(jc + 1) * 128], xk16[:, l // 16, l % 16, :], l == 0, l == 31,
                                r=[W1t, xk16t], w=[("pf", jc)] if l == 0 else ())
                    last = self.S.q["pe"][-1]
                    self.S.tw[("pf", jc)] = {last.stream: last}
                    self.act(xx, ph_[:, 0:NC], AF.Identity, r=[("pf", jc), bht], w=[xxt], bias=bh[:, jc:jc + 1])
                    self.E("dve", "tensor_tensor", out=x2, in0=xx, in1=xx, op=ALU.mult, r=[xxt], w=[x2t])
                    self.E("dve", "tensor_scalar", out=x2, in0=x2, scalar1=0.044715, scalar2=1.0, op0=ALU.mult, op1=ALU.add,
                           r=[x2t], w=[x2t])
                    self.E("dve", "tensor_tensor", out=x2, in0=x2, in1=xx, op=ALU.mult, r=[x2t, xxt], w=[x2t])
                    self.act(sg, x2, AF.Sigmoid, r=[x2t], w=[sgt], scale=1.5957691216057308)
                    self.E("dve", "tensor_tensor", out=gl[:, jc, :], in0=xx, in1=sg, op=ALU.mult, r=[xxt, sgt],
                           **(dict(w=[glt]) if jc == 0 else dict(pw=[glt])))
                for ct in range(NCT):
                    pk = self.pf[2]
                    for jc in range(2):
                        self.mm(pk[:, 0:64], gl[:, jc, ct * 128:(ct + 1) * 128], W2b[:, jc, :], jc == 0, jc == 1,
                                r=[glt, W2t], w=[("pf", 2)] if jc == 0 else ())
                    last = self.S.q["pe"][-1]
                    self.S.tw[("pf", 2)] = {last.stream: last}
                    if kv == 0:
                        self.norm_rope(pk[:, 0:64], ("pf", 2), 1, self.gk[:, 0, :], "gk", self.csc[:, 0, ct, :],
                                       self.csc[:, 1, ct, :], "csc", kcb, kcbt, tmp)
                        pbt = self.pb[1]
                        self.tr(pbt[:, 0:128], kcb, r=[kcbt], w=[("pb", 1)])
                        self.E("dve", "tensor_copy", out=self.kcTa[:, g, ct * 128:(ct + 1) * 128], in_=pbt[:, 0:128],
                               r=[("pb", 1)], pw=["kcTa"])
                    else:
                        self.act(self.vca[:, g, ct, 0:64], pk[:, 0:64], AF.Copy, r=[("pf", 2)], pw=["vca"])

    def phase_A_nsa(self, li):
        self.phase()
        T, NT = self.T, self.NT
        o_all = self.big[:, 0:NT * D].rearrange("p (t f) -> p t f", f=D)
        qTa = [self.ph(f"nqTa{i}", [128, T], BF16) for i in range(4)]
        ksTa, kst = self.ph("ksTa", [128, T], BF16)
        kwTa, kwt = self.ph("kwTa", [128, T], BF16)
        Vs, Vst = self.ph("Vs", [128, NT, 65], BF16)
        Vw, Vwt = self.ph("Vw", [128, NT, 65], BF16)
        pts = [self.ph(f"npt{i}", [128, 512], BF16) for i in range(4)]
        oacc, oacct = self.ph("oacc", [128, NT, 64], F32)
        imp, impt = self.ph("imp", [128, NT, 64], F32)
        sc, sct = self.ph("sc", [128, 64], F32)
        sc2, sc2t = self.ph("sc2", [128, 64], F32)
        m8a, m8at = self.ph("m8a", [128, 8], F32)
        m8b, m8bt = self.ph("m8b", [128, 8], F32)
        bs, bst = self.ph("bs", [128, 128], BF16)
        rz, rzt = self.ph("nrz", [128, 1], F32)
        cf, cft = self.ph("ncf", [128, 1], F32)
        stage, staget = self.ph("cstage", [128, NT, 128], F32)
        rzb, rzbt = self.ph("rzb", [128, NT], F32)
        cfb, cfbt = self.ph("cfb", [128, NT], F32)
        for hl, (q_ap, qtok) in enumerate(qTa):
            self.E("pool", "memset", q_ap, 0.0, w=[qtok, ("qhi", hl)])
        self.E("pool", "memset", kwTa, 0.0, w=[kwt])
        self.E("pool", "memset", ksTa[0:64, :], 0.0, w=[kst])
        ke = ksTa[64:128, :]
        self.E("pool", "memset", ke, NEGB, r=[kst], w=[kst])
        self.E("pool", "affine_select", out=ke, in_=ke, pattern=[[1, T]], compare_op=ALU.is_ge, fill=0.0, base=0,
               channel_multiplier=-64, r=[kst], w=[kst])
        self.E("pool", "affine_select", out=ke, in_=ke, pattern=[[-1, T]], compare_op=ALU.is_ge, fill=0.0, base=63,
               channel_multiplier=64, r=[kst], w=[kst])
        self.E("dve", "memset", Vs, 1.0, w=[Vst])
        self.E("dve", "memset", Vw, 1.0, w=[Vwt])
        self.E("dve", "memset", bs, 0.0, w=[bst])
        self.E("dve", "memset", imp, 0.0, w=[impt])

        for g in range(4):
            for hl in range(4):
                h = g * 4 + hl
                q_ap, qtok = qTa[hl]
                self.dma(q_ap[0:64, :], self.qT_d[h * 64:(h + 1) * 64, :], f"nq{hl}", r=[("kTq_d",)], w=[qtok])
            self.dma(ksTa[0:64, :], self.kT_d[g * 64:(g + 1) * 64, :], "nks", r=[("kTq_d",)], w=[kst])
            self.dma(kwTa[0:64, :], self.kT_d[256 + g * 64:256 + (g + 1) * 64, :], "nkw", r=[("kTq_d",)], w=[kwt])
            for (V_, Vt_, c0, nm) in ((Vs, Vst, g * 64, "nvs"), (Vw, Vwt, 256 + g * 64, "nvw")):
                for c in range(0, NT, 8):
                    n = min(8, NT - c)
                    self.dma(V_[:, c:c + n, 0:64],
                             self.v_d[c * 128:(c + n) * 128, c0:c0 + 64].rearrange("(t p) d -> p t d", p=128),
                             nm, r=[("v_d",)], **(dict(w=[Vt_]) if c == 0 else dict(pw=[Vt_])))

            def cmp_pairs(qt):
                out = []
                for kt in range(self.NCT):
                    dl = qt - 16 * kt
                    if dl < 0:
                        continue
                    out.append((kt, self.cmpm[:, dl, :] if dl <= 16 else None))
                return out

            def sel_A(qt):
                scq = imp[:, qt, :]
                self.E("dve", "max", out=m8a, in_=scq, r=[impt], w=[m8at])
                self.E("dve", "tensor_scalar", out=sc2, in0=scq, scalar1=m8a[:, 7:8], scalar2=-6.0e4, op0=ALU.is_ge, op1=ALU.mult,
                       r=[impt, m8at], w=[sc2t])
                self.E("dve", "tensor_tensor", out=sc2, in0=sc2, in1=scq, op=ALU.add, r=[sc2t, impt], w=[sc2t])
                self.E("dve", "max", out=m8b, in_=sc2, r=[sc2t], w=[m8bt])
                self.E("dve", "tensor_scalar", out=bs[:, 64:128], in0=scq, scalar1=m8b[:, 7:8], scalar2=1.0, op0=ALU.is_ge,
                       op1=ALU.subtract, r=[impt, m8bt, bst], w=[bst])

            def sel_B(qt):
                pbt = self.pb[1]
                self.tr(pbt[:, 0:128], bs, r=[bst], w=[("pb", 1)])
                for hl2 in range(4):
                    q2, _ = qTa[hl2]
                    self.E("dve", "tensor_copy", out=q2[64:128, qt * 128:(qt + 1) * 128], in_=pbt[64:128, 0:128],
                           r=[("pb", 1)], pw=[("qhi", hl2)])

            def sel_steps():
                for qt in range(NT):
                    sel_A(qt)
                    yield
                    sel_B(qt)
                    yield

            for hl in range(4):
                h = g * 4 + hl
                q_ap, qtok = qTa[hl]

                def fin_c(qt, po, potok):
                    self.E("dve", "tensor_copy", out=stage[:, qt, :], in_=po[:, 0:128], r=[potok], pw=[staget])

                self.attention(q_ap, qtok, self.kcTa[:, g, :], "kcTa", self.vca[:, g], "vca", 128, cmp_pairs, fin_c, pts,
                               f"c{h}")
                zc = stage[:, :, 64]
                self.E("dve", "tensor_scalar", out=rzb, in0=zc, scalar1=1.0e-30, scalar2=None, op0=ALU.max, r=[staget], w=[rzbt])
                self.E("dve", "reciprocal", out=rzb, in_=rzb, r=[rzbt], w=[rzbt])
                self.E("dve", "tensor_tensor", out=cfb, in0=rzb, in1=self.g_sb[:, :, h * 3], op=ALU.mult, r=[rzbt, "g_sb"],
                       w=[cfbt])
                self.E("dve", "tensor_tensor", out=o_all[:, :, h * 64:(h + 1) * 64], in0=stage[:, :, 0:64],
                       in1=cfb.unsqueeze(2).to_broadcast([128, NT, 64]), op=ALU.mult, r=[staget, cfbt], pw=["big"])
                rzb3 = rzb.unsqueeze(2).to_broadcast([128, NT, 63])
                if hl == 0:
                    self.E("dve", "memset", imp[:, :, 63:64], 0.0, w=[impt])
                    self.E("dve", "tensor_tensor", out=imp[:, :, 0:63], in0=stage[:, :, 65:128], in1=rzb3, op=ALU.mult,
                           r=[staget, rzbt], pw=[impt])
                else:
                    self.E("dve", "tensor_tensor", out=stage[:, :, 65:128], in0=stage[:, :, 65:128], in1=rzb3, op=ALU.mult,
                           r=[staget, rzbt], pw=[staget])
                    self.E("dve", "tensor_tensor", out=imp[:, :, 0:63], in0=imp[:, :, 0:63], in1=stage[:, :, 65:128], op=ALU.add,
                           r=[staget, impt], pw=[impt])
            self.E("dve", "tensor_tensor", out=imp[:, :, :], in0=imp[:, :, :], in1=self.msel[:, :, :], op=ALU.add,
                   r=[impt, "msel"], w=[impt])

            def sel_pairs(qt):
                return [(kt, (self.tri[:] if kt == qt else None)) for kt in range(qt + 1)]

            def win_pairs(qt):
                out = []
                for kt in range(max(0, qt - 4), qt + 1):
                    b = self.tri[:] if kt == qt else (self.atri[:] if kt == qt - 4 else None)
                    out.append((kt, b))
                return out

            for hl in range(4):
                h = g * 4 + hl
                q_ap, qtok = qTa[hl]

                def fin_w(qt, po, potok, h=h):
                    self.E("dve", "reciprocal", out=rz, in_=po[:, 64:65], r=[potok], w=[rzt])
                    self.E("dve", "tensor_tensor", out=cf, in0=rz, in1=self.g_sb[:, qt, h * 3 + 2:h * 3 + 3], op=ALU.mult,
                           r=[rzt, "g_sb"], w=[cft])
                    self.E("dve", "scalar_tensor_tensor", out=oacc[:, qt, :], in0=po[:, 0:64], scalar=cf[:, 0:1],
                           in1=o_all[:, qt, h * 64:(h + 1) * 64], op0=ALU.mult, op1=ALU.add, r=[potok, cft, "big"],
                           pw=[oacct])

                def fin_s(qt, po, potok, h=h):
                    self.E("dve", "reciprocal", out=rz, in_=po[:, 64:65], r=[potok], w=[rzt])
                    self.E("dve", "tensor_tensor", out=cf, in0=rz, in1=self.g_sb[:, qt, h * 3 + 1:h * 3 + 2], op=ALU.mult,
                           r=[rzt, "g_sb"], w=[cft])
                    self.E("dve", "scalar_tensor_tensor", out=o_all[:, qt, h * 64:(h + 1) * 64], in0=po[:, 0:64],
                           scalar=cf[:, 0:1], in1=oacc[:, qt, :], op0=ALU.mult, op1=ALU.add, r=[potok, cft, oacct],
                           pw=["big"])

                side = sel_steps() if hl == 0 else None
                self.attention(q_ap, qtok, kwTa, kwt, Vw, Vwt, 65, win_pairs, fin_w, pts, f"w{h}", side=side)
                if side is not None:
                    for _ in side:
                        pass
                self.attention(q_ap, [qtok, ("qhi", hl)], ksTa, kst, Vs, Vst, 65, sel_pairs, fin_s, pts, f"s{h}")

    def phase_O(self, li, src):
        self.phase()
        NT = self.NT
        j = li // 2
        moba = (li % 2 == 0)
        o_all = self.big[:, 0:NT * D].rearrange("p (t f) -> p t f", f=D)
        Wo, Wot = self.ph("Wo", [128, 8, D], BF16)
        Wg, Wgt = self.ph("Wg", [128, 8, D], BF16)
        Wp, Wpt = self.ph("Wp", [128, 2, D], BF16)
        self.load_w(Wo, Wot, (self.moba_w_out if moba else self.nsa_w_out)[j], D, None, None)
        self.load_gcol(self.gcol2, "gcol2", self.ple_gate_gain[li])
        self.load_w(Wg, Wgt, self.ple_w_gate[li], D, self.gcol2, "gcol2")
        self.load_w(Wp, Wpt, self.ple_w_proj[li], D, None, None, nk=2)
        xt = [self.ph(f"oxt{i}", [128, D], F32) for i in range(2)]
        zt = [self.ph(f"ozt{i}", [128, D], BF16) for i in range(2)]
        pt_ = [self.ph(f"opt{i}", [128, PLE], F32) for i in range(2)]
        og = [self.ph(f"og{i}", [128, D], BF16) for i in range(2)]
        ogT = [self.ph(f"ogT{i}", [128, 8, 128], BF16) for i in range(2)]
        x1 = [self.ph(f"x1_{i}", [128, D], F32) for i in range(2)]
        sqs, sqst = self.ph("osq", [128, D], F32)
        rs = [self.ph(f"ors{i}", [128, 1], F32) for i in range(2)]
        xn, xnt = self.ph("xn", [128, D], BF16)
        xnT, xnTt = self.ph("xnT", [128, 8, 128], BF16)
        gate, gatet = self.ph("gatef", [128, D], F32)
        pbf, pbft = self.ph("pbf", [128, PLE], BF16)
        pT, pTt = self.ph("pT", [128, 2, 128], BF16)
        x2 = [self.ph(f"x2_{i}", [128, D], F32) for i in range(2)]

        def S1_T(t):
            x_ap, xtok = xt[t % 2]
            z_ap, ztok = zt[t % 2]
            p_ap, ptok = pt_[t % 2]
            og_ap, ogt = og[t % 2]
            ogT_ap, ogTt = ogT[t % 2]
            self.dma(x_ap, src[t * 128:(t + 1) * 128, :], f"oxl{t % 2}", r=[("y", t)], w=[xtok])
            self.dma(z_ap, self.zs_d[t * 128:(t + 1) * 128, :], f"ozl{t % 2}", r=[("zs_d",)], w=[ztok])
            self.dma(p_ap, self.p_in[li, t * 128:(t + 1) * 128, :], f"opl{t % 2}", w=[ptok])
            self.E("dve", "tensor_tensor", out=og_ap, in0=o_all[:, t, :], in1=z_ap, op=ALU.mult, r=["big", ztok], w=[ogt])
            self.transpose8(og_ap, ogt, ogT_ap, ogTt, 0)

        def S1_M(t):
            x_ap, xtok = xt[t % 2]
            ogT_ap, ogTt = ogT[t % 2]
            x1_ap, x1t = x1[t % 2]
            for c in range(2):
                ps = self.pf[c]
                for k in range(8):
                    self.mm(ps[:, :], ogT_ap[:, k, :], Wo[:, k, c * 512:(c + 1) * 512], k == 0, k == 7, r=[ogTt, Wot],
                            w=[("pf", c)] if k == 0 else ())
                last = self.S.q["pe"][-1]
                self.S.tw[("pf", c)] = {last.stream: last}
                self.E("dve", "tensor_tensor", out=x1_ap[:, c * 512:(c + 1) * 512], in0=x_ap[:, c * 512:(c + 1) * 512],
                       in1=ps[:, :], op=ALU.add, r=[xtok, ("pf", c)], **(dict(w=[x1t]) if c == 0 else dict(pw=[x1t])))

        def S2a(t):
            x1_ap, x1t = x1[t % 2]
            r_ap, rtok = rs[t % 2]
            self.rstd_of(x1_ap, x1t, D, sqs, sqst, r_ap, rtok)
            self.act(xn, x1_ap, AF.Copy, r=[x1t, rtok], w=[xnt], scale=r_ap[:, 0:1])

        def S2b_T(t):
            self.transpose8(xn, xnt, xnT, xnTt, 1)

        def S2b_M(t):
            x1_ap, x1t = x1[t % 2]
            p_ap, ptok = pt_[t % 2]
            for c in range(2):
                ps = self.pf[2 + c]
                for k in range(8):
                    self.mm(ps[:, :], xnT[:, k, :], Wg[:, k, c * 512:(c + 1) * 512], k == 0, k == 7, r=[xnTt, Wgt],
                            w=[("pf", 2 + c)] if k == 0 else ())
                last = self.S.q["pe"][-1]
                self.S.tw[("pf", 2 + c)] = {last.stream: last}
                self.act(gate[:, c * 512:(c + 1) * 512], ps[:, :], AF.Sigmoid, r=[("pf", 2 + c)],
                         **(dict(w=[gatet]) if c == 0 else dict(pw=[gatet])))
            self.E("dve", "tensor_copy", out=pbf, in_=p_ap, r=[ptok], w=[pbft])
            self.transpose8(pbf, pbft, pT, pTt, 1, nblk=2)
            x2_ap, x2tok = x2[t % 2]
            for c in range(2):
                ps = self.pf[4 + c]
                for k in range(2):
                    self.mm(ps[:, :], pT[:, k, :], Wp[:, k, c * 512:(c + 1) * 512], k == 0, k == 1, r=[pTt, Wpt],
                            w=[("pf", 4 + c)] if k == 0 else ())
                last = self.S.q["pe"][-1]
                self.S.tw[("pf", 4 + c)] = {last.stream: last}
                sl = slice(c * 512, (c + 1) * 512)
                self.E("dve", "tensor_tensor", out=x2_ap[:, sl], in0=gate[:, sl], in1=ps[:, :], op=ALU.mult,
                       r=[gatet, ("pf", 4 + c)], **(dict(w=[x2tok]) if c == 0 else dict(pw=[x2tok])))
                self.E("dve", "tensor_tensor", out=x2_ap[:, sl], in0=x2_ap[:, sl], in1=x1_ap[:, sl], op=ALU.add,
                       r=[x2tok, x1t], pw=[x2tok])
            self.dma(self.y[t * 128:(t + 1) * 128, :], x2_ap, f"oyst{t % 2}", r=[x2tok], w=[("y", t)], q="pool")

        S1_T(0)
        S1_M(0)
        for t in range(NT):
            S2a(t)
            if t + 1 < NT:
                S1_T(t + 1)
            S2b_T(t)
            if t + 1 < NT:
                S1_M(t + 1)
            S2b_M(t)


def rope_tables(T):
    NT = T // 128
    half = 32
    inv_freq = (10000.0 ** (-np.arange(half, dtype=np.float32) / half)).astype(np.float32)
    pos = (np.arange(NT)[None, :] * 128 + np.arange(128)[:, None]).astype(np.float32)
    ang = pos[:, :, None] * inv_freq[None, None, :]
    cs = np.stack([np.cos(ang), np.sin(ang)], 0).astype(np.float32)
    NCT = max(1, (T // 16) // 128)
    posc = (16.0 * (np.arange(NCT)[None, :] * 128 + np.arange(128)[:, None]) + 31.0).astype(np.float32)
    angc = posc[:, :, None] * inv_freq[None, None, :]
    csc = np.stack([np.cos(angc), np.sin(angc)], 0).astype(np.float32)
    return cs, csc


_CACHE = {}


def run(inputs, T, layers, n_cores):
    key = (T, tuple(layers))
    if key not in _CACHE:
        _CACHE[key] = Builder(T, list(layers)).build()
    nc = _CACHE[key]
    cs, csc = rope_tables(T)
    shared = {k: np.ascontiguousarray(v, dtype=np.float32) for k, v in inputs.items() if k not in ("x", "p")}
    shared["rope_cs"] = cs
    shared["rope_cs_c"] = csc
    in_maps = []
    for b in range(n_cores):
        m = dict(shared)
        m["x"] = np.ascontiguousarray(inputs["x"][b], dtype=np.float32)
        m["p"] = np.ascontiguousarray(inputs["p"][:, b], dtype=np.float32)
        in_maps.append(m)
    res = run_bass_kernel_spmd(nc, in_maps, core_ids=list(range(n_cores)))
    return np.stack([r["y"] for r in res.results], 0).astype(np.float32)


def kernel(**inputs):
    return run(inputs, 4096, [0, 1, 2, 3], 8)
```

```python
import math
from contextlib import ExitStack

import numpy as np
import concourse.bass as bass
import concourse.mybir as mybir
from concourse.bass_utils import run_bass_kernel_spmd

F32 = mybir.dt.float32
BF16 = mybir.dt.bfloat16
I32 = mybir.dt.int32
AF = mybir.ActivationFunctionType
ALU = mybir.AluOpType
AX = mybir.AxisListType

D = 1024
H = 16
HD = 64
PLE = 256
EPS = 1e-6
NSA_IN = 3632
NEGB = 30000.0


class Op:
    __slots__ = ("eng", "fn", "deps", "needed", "sigval", "stream", "is_dma", "idx")


class Sched:
    ENGS = ("pe", "act", "dve", "pool", "sp")

    def __init__(self):
        self.q = {e: [] for e in self.ENGS}
        self.tw = {}
        self.tr = {}
        self.dma_cnt = {}
        self.nops = 0
        self.last = {}
        self.pending = {}

    def barrier(self):
        for e in self.ENGS:
            self.pending[e] = list(self.last.values())

    def op(self, eng, fn, reads=(), writes=(), pwrites=(), dma=None):
        o = Op()
        o.eng = eng
        o.fn = fn
        o.is_dma = dma is not None
        o.stream = ("dma", dma) if dma is not None else ("eng", eng)
        o.needed = o.is_dma
        o.sigval = None
        o.idx = self.nops
        self.nops += 1
        deps = {}

        def add(d):
            if (not o.is_dma) and (not d.is_dma) and d.eng == "pe" and eng == "pe":
                return
            k = d.stream
            if k not in deps or deps[k].idx < d.idx:
                deps[k] = d

        for d in self.pending.pop(eng, ()):
            add(d)
        for t in reads:
            for d in self.tw.get(t, {}).values():
                add(d)
        for t in writes:
            for d in self.tw.get(t, {}).values():
                add(d)
            for d in self.tr.get(t, {}).values():
                add(d)
        for t in pwrites:
            for d in self.tr.get(t, {}).values():
                add(d)
        o.deps = list(deps.values())
        for d in o.deps:
            d.needed = True
        for t in reads:
            self.tr.setdefault(t, {})[o.stream] = o
        for t in writes:
            self.tw[t] = {o.stream: o}
            self.tr[t] = {}
        for t in pwrites:
            self.tw.setdefault(t, {})[o.stream] = o
        if o.is_dma:
            c = self.dma_cnt.get(dma, 0) + 16
            self.dma_cnt[dma] = c
            o.sigval = c
        self.last[o.stream] = o
        self.q[eng].append(o)
        return o

    def emit(self, nc, stack):
        for e in self.ENGS:
            c = 0
            for o in self.q[e]:
                if not o.is_dma and o.needed:
                    c += 1
                    o.sigval = c
        sems = {}
        for e in self.ENGS:
            sems[("eng", e)] = stack.enter_context(nc.semaphore("s_" + e))
        for d in self.dma_cnt:
            sems[("dma", d)] = stack.enter_context(nc.semaphore("d_" + str(d)))
        block = stack.enter_context(nc.Block())
        q = self.q

        def run(ename, eng):
            seen = {}
            for o in q[ename]:
                for d in o.deps:
                    v = d.sigval
                    if seen.get(d.stream, 0) >= v:
                        continue
                    seen[d.stream] = v
                    eng.wait_ge(sems[d.stream], v)
                ins = o.fn(eng)
                if o.needed:
                    ins.then_inc(sems[o.stream], 16 if o.is_dma else 1)

        @block.tensor
        def _(eng):
            run("pe", eng)

        @block.scalar
        def _(eng):
            run("act", eng)

        @block.vector
        def _(eng):
            run("dve", eng)

        @block.gpsimd
        def _(eng):
            run("pool", eng)

        @block.sync
        def _(eng):
            run("sp", eng)
            for d, c in self.dma_cnt.items():
                eng.wait_ge(sems[("dma", d)], c)


class Builder:
    def __init__(self, T, layers):
        self.T = T
        self.NT = T // 128
        self.NC = T // 16
        self.NCT = max(1, self.NC // 128)
        self.layers = layers
        self.nc = bass.Bass("TRN2", target_bir_lowering=False)
        self.S = Sched()
        self.uid = 0

    def E(self, eng, meth, *a, r=(), w=(), pw=(), **kw):
        self.S.op(eng, lambda e: getattr(e, meth)(*a, **kw), reads=r, writes=w, pwrites=pw)

    def dma(self, out, in_, sem, r=(), w=(), pw=(), q="sp", **kw):
        self.S.op(q, lambda e: e.dma_start(out=out, in_=in_, **kw), reads=r, writes=w, pwrites=pw, dma=sem)

    def mm(self, out, lhsT, rhs, start, stop, r=(), w=()):
        self.S.op("pe", lambda e: e.matmul(out, lhsT=lhsT, rhs=rhs, start=start, stop=stop), reads=r, writes=w)

    def tr(self, out, in_, r=(), w=()):
        ident = self.ident
        self.S.op("pe", lambda e: e.transpose(out=out, in_=in_, identity=ident[:]), reads=tuple(r) + ("ident",), writes=w)

    def act(self, out, in_, func, r=(), w=(), pw=(), **kw):
        self.S.op("act", lambda e: e.activation(out=out, in_=in_, func=func, **kw), reads=r, writes=w, pwrites=pw)

    def persist(self, name, shape, dt):
        return self.st.enter_context(self.nc.sbuf_tensor(name, shape, dt))

    def phase(self):
        self.S.barrier()
        self.aoff = 0
        self.pid = getattr(self, "pid", 0) + 1

    def ph(self, name, shape, dt):
        esz = 4 if dt in (F32, I32) else 2
        n = 1
        for s_ in shape[1:]:
            n *= s_
        nbytes = (n * esz + 63) // 64 * 64
        a = self.aoff
        self.aoff += nbytes
        assert self.aoff <= self.ARENA_BYTES, (name, self.aoff)
        v = self.arena[:, a // 2:(a + n * esz) // 2]
        if esz == 4:
            v = v.bitcast(dt)
        if len(shape) == 3:
            v = v.rearrange("p (a b) -> p a b", b=shape[2])
        elif len(shape) == 4:
            v = v.rearrange("p (a b c) -> p a b c", b=shape[2], c=shape[3])
        return v, ("ph", self.pid, name)

    def build(self):
        nc, T, NT = self.nc, self.T, self.NT
        dt = nc.dram_tensor
        self.x_in = dt("x", [T, D], F32, kind="ExternalInput").ap()
        self.p_in = dt("p", [4, T, PLE], F32, kind="ExternalInput").ap()
        self.norm_gain = dt("norm_gain", [4, D], F32, kind="ExternalInput").ap()
        self.moba_w_in = dt("moba_w_in", [2, D, 4 * D], F32, kind="ExternalInput").ap()
        self.moba_q_gain = dt("moba_q_gain", [2, HD], F32, kind="ExternalInput").ap()
        self.moba_k_gain = dt("moba_k_gain", [2, HD], F32, kind="ExternalInput").ap()
        self.moba_w_out = dt("moba_w_out", [2, D, D], F32, kind="ExternalInput").ap()
        self.nsa_w_in = dt("nsa_w_in", [2, D, NSA_IN], F32, kind="ExternalInput").ap()
        self.nsa_q_gain = dt("nsa_q_gain", [2, HD], F32, kind="ExternalInput").ap()
        self.nsa_k_gain = dt("nsa_k_gain", [2, 3, HD], F32, kind="ExternalInput").ap()
        self.nsa_cmp_pe = dt("nsa_cmp_pe", [2, 2, 32, HD], F32, kind="ExternalInput").ap()
        self.nsa_cmp_w1 = dt("nsa_cmp_w1", [2, 2, 2048, 256], F32, kind="ExternalInput").ap()
        self.nsa_cmp_w2 = dt("nsa_cmp_w2", [2, 2, 256, HD], F32, kind="ExternalInput").ap()
        self.nsa_w_out = dt("nsa_w_out", [2, D, D], F32, kind="ExternalInput").ap()
        self.ple_w_proj = dt("ple_w_proj", [4, PLE, D], F32, kind="ExternalInput").ap()
        self.ple_gate_gain = dt("ple_gate_gain", [4, D], F32, kind="ExternalInput").ap()
        self.ple_w_gate = dt("ple_w_gate", [4, D, D], F32, kind="ExternalInput").ap()
        self.rope_cs = dt("rope_cs", [2, 128, NT, 32], F32, kind="ExternalInput").ap()
        self.rope_cs_c = dt("rope_cs_c", [2, 128, self.NCT, 32], F32, kind="ExternalInput").ap()
        self.y = dt("y", [T, D], F32, kind="ExternalOutput").ap()
        self.qT_d = dt("qT_d", [D, T], BF16).ap()
        self.kT_d = dt("kT_d", [D, T], BF16).ap()
        self.v_d = dt("v_d", [T, D], BF16).ap()
        self.zs_d = dt("zs_d", [T, D], BF16).ap()

        with ExitStack() as st:
            self.st = st
            self.ARENA_BYTES = 100 * 1024
            self.arena = self.persist("arena", [128, self.ARENA_BYTES // 2], BF16)
            self.big = self.persist("big", [128, 32768], BF16)
            self.ident = self.persist("ident", [128, 128], BF16)
            self.tri = self.persist("tri", [128, 128], BF16)
            self.atri = self.persist("atri", [128, 128], BF16)
            self.cmpm = self.persist("cmpm", [128, 17, 128], BF16)
            self.cs = self.persist("cs", [128, 2, NT, 32], F32)
            self.csc = self.persist("csc", [128, 2, self.NCT, 32], F32)
            self.g_sb = self.persist("g_sb", [128, NT, 48], F32)
            self.msel = self.persist("msel", [128, NT, 64], F32)
            self.kcTa = self.persist("kcTa", [128, 4, self.NC], BF16)
            self.vca = self.persist("vca", [128, 4, self.NCT, 128], BF16)
            self.gq = self.persist("gq", [128, HD], F32)
            self.gk = self.persist("gk", [128, 3, HD], F32)
            self.gcol = self.persist("gcol", [128, 8], F32)
            self.gcol2 = self.persist("gcol2", [128, 8], F32)
            self.pf = [st.enter_context(nc.psum_tensor(f"pf{i}", [128, 512], F32)) for i in range(6)]
            self.pb = [st.enter_context(nc.psum_tensor(f"pb{i}", [128, 1024], BF16)) for i in range(2)]
            self.consts()
            first = True
            for li in self.layers:
                src = self.x_in if first else self.y
                first = False
                if li % 2 == 0:
                    self.phase_P(li, src, moba=True)
                    self.phase_A_moba(li)
                else:
                    import os
                    stop = os.environ.get("NSA_STOP", "")
                    self.stop = stop
                    self.phase_P(li, src, moba=False)
                    if stop == "P":
                        break
                    self.phase_C(li)
                    if stop == "C":
                        break
                    self.phase_A_nsa(li)
                    if stop in ("cmp", "sel", "A"):
                        break
                self.phase_O(li, src)
            self.S.emit(nc, st)
        return nc

    def consts(self):
        self.phase()
        NT = self.NT
        tf, tft = self.ph("c_tf", [128, 128], F32)
        self.E("pool", "memset", tf, 1.0, w=[tft])
        self.E("pool", "affine_select", out=tf, in_=tf, pattern=[[1, 128]], compare_op=ALU.is_equal, fill=0.0,
               base=0, channel_multiplier=-1, r=[tft], w=[tft])
        self.E("dve", "tensor_copy", out=self.ident[:], in_=tf, r=[tft], w=["ident"])
        tf2, tf2t = self.ph("c_tf2", [128, 128], F32)
        self.E("pool", "memset", tf2, 0.0, w=[tf2t])
        self.E("pool", "affine_select", out=tf2, in_=tf2, pattern=[[1, 128]], compare_op=ALU.is_ge, fill=-NEGB,
               base=0, channel_multiplier=-1, r=[tf2t], w=[tf2t])
        self.E("dve", "tensor_copy", out=self.tri[:], in_=tf2, r=[tf2t], w=["tri"])
        tf3, tf3t = self.ph("c_tf3", [128, 128], F32)
        self.E("pool", "memset", tf3, 0.0, w=[tf3t])
        self.E("pool", "affine_select", out=tf3, in_=tf3, pattern=[[-1, 128]], compare_op=ALU.is_ge, fill=-NEGB,
               base=-1, channel_multiplier=1, r=[tf3t], w=[tf3t])
        self.E("dve", "tensor_copy", out=self.atri[:], in_=tf3, r=[tf3t], w=["atri"])
        tf4, tf4t = self.ph("c_tf4", [128, 17, 128], F32)
        self.E("pool", "memset", tf4, 0.0, w=[tf4t])
        self.E("pool", "affine_select", out=tf4, in_=tf4, pattern=[[128, 17], [1, 128]], compare_op=ALU.is_ge,
               fill=-NEGB, base=-31, channel_multiplier=-16, r=[tf4t], w=[tf4t])
        self.E("dve", "tensor_copy", out=self.cmpm[:], in_=tf4, r=[tf4t], w=["cmpm"])
        self.dma(self.cs[:, 0], self.rope_cs[0], "c_cs", w=["cs"])
        self.dma(self.cs[:, 1], self.rope_cs[1], "c_cs2", pw=["cs"])
        self.dma(self.csc[:, 0], self.rope_cs_c[0], "c_csc", w=["csc"])
        self.dma(self.csc[:, 1], self.rope_cs_c[1], "c_csc2", pw=["csc"])
        m = self.msel
        self.E("pool", "memset", m[:], 0.0, w=["msel"])
        for half in range(2):
            mh = m[half * 64:(half + 1) * 64]
            self.E("pool", "affine_select", out=mh, in_=mh, pattern=[[2, NT], [-1, 64]], compare_op=ALU.is_ge,
                   fill=1.0e4, base=half - 2, channel_multiplier=0, r=["msel"], w=["msel"])
            self.E("pool", "affine_select", out=mh, in_=mh, pattern=[[2, NT], [-1, 64]], compare_op=ALU.is_ge,
                   fill=-1.0e4, base=half, channel_multiplier=0, r=["msel"], w=["msel"])
        self.E("pool", "memset", m[:, :, 0:1], 1.0e4, r=["msel"], w=["msel"])
        ov, ovt = self.ph("c_ov", [128, self.NCT, 64], F32)
        self.E("pool", "memset", ov, 1.0, w=[ovt])
        for ct in range(self.NCT):
            o1 = ov[:, ct, :]
            self.E("pool", "affine_select", out=o1, in_=o1, pattern=[[4, 64]], compare_op=ALU.is_ge, fill=0.0,
                   base=3 - 128 * ct, channel_multiplier=-1, r=[ovt], w=[ovt])
            self.E("pool", "affine_select", out=o1, in_=o1, pattern=[[-4, 64]], compare_op=ALU.is_ge, fill=0.0,
                   base=1 + 128 * ct, channel_multiplier=1, r=[ovt], w=[ovt])
        self.E("dve", "memset", self.vca[:], 1.0, w=["vca"])
        for g in range(4):
            for ct in range(self.NCT):
                self.E("dve", "tensor_copy", out=self.vca[:, g, ct, 65:128], in_=ov[:, ct, 0:63], r=[ovt, "vca"], w=["vca"])
        self.E("dve", "memset", self.kcTa[:], 0.0, w=["kcTa"])

    def load_gcol(self, dst, tok, src_row):
        self.dma(dst[:], src_row.rearrange("(k p) -> p k", p=128), "gcol_" + tok, w=[tok],
                 allow_slow_non_contiguous=True)

    def load_w(self, dst3, dtok, w_ap, ncols, gcol, gtok, nk=8):
        if getattr(self, "_stg_pid", None) != self.pid:
            self._stg = [self.ph(f"wstg{i}", [128, 512], F32) for i in range(4)]
            self._stg_pid = self.pid
        stg = self._stg
        i = 0
        first = True
        for k in range(nk):
            for c0 in range(0, ncols, 512):
                cw = min(512, ncols - c0)
                s_ap, s_tok = stg[i % 4]
                self.dma(s_ap[:, 0:cw], w_ap[k * 128:(k + 1) * 128, c0:c0 + cw], f"wst{i % 4}", w=[s_tok])
                kw = dict(r=[s_tok] + ([gtok] if gcol is not None else []))
                if first:
                    kw["w"] = [dtok]
                    first = False
                else:
                    kw["pw"] = [dtok]
                if gcol is not None:
                    self.E("dve", "tensor_scalar", out=dst3[:, k, c0:c0 + cw], in0=s_ap[:, 0:cw], scalar1=gcol[:, k:k + 1],
                           scalar2=None, op0=ALU.mult, **kw)
                else:
                    self.E("dve", "tensor_copy", out=dst3[:, k, c0:c0 + cw], in_=s_ap[:, 0:cw], **kw)
                i += 1

    def rstd_of(self, x_ap, xtok, n, scratch, stok, out_col, otok):
        self.act(scratch, x_ap, AF.Square, r=[xtok], w=[stok, otok], accum_out=out_col)
        self.act(out_col, out_col, AF.Sqrt, r=[otok], w=[otok], scale=1.0 / n, bias=EPS)
        self.E("dve", "reciprocal", out=out_col, in_=out_col, r=[otok], w=[otok])

    def transpose8(self, src_bf, stok, dstT, dtok, pbi, nblk=8):
        pbt = self.pb[pbi]
        ptok = ("pb", pbi)
        for k in range(nblk):
            kw = dict(w=[ptok]) if k == 0 else dict(w=())
            if k == 0:
                self.tr(pbt[:, k * 128:(k + 1) * 128], src_bf[:, k * 128:(k + 1) * 128], r=[stok], w=[ptok])
            else:
                self.S.op("pe", (lambda e, k=k: e.transpose(out=pbt[:, k * 128:(k + 1) * 128], in_=src_bf[:, k * 128:(k + 1) * 128],
                                                            identity=self.ident[:])), reads=[stok, "ident"], pwrites=[ptok])
        self.act(dstT.rearrange("p a b -> p (a b)") if len(dstT.shape) == 3 else dstT, pbt[:, 0:nblk * 128], AF.Copy, r=[ptok], w=[dtok])

    def norm_rope(self, src, stok, nh, gain_ap, gtok, cos_ap, sin_ap, cstok, out_bf, otok, tmp):
        (sq, sqt), (ss, sst), (A, At), (Bt_, Btt), (t13, t13t), (tsw, tswt) = tmp
        W = nh * 64
        v3 = lambda ap: ap[:, 0:W].rearrange("p (h d) -> p h d", d=64)
        v4 = lambda ap: ap[:, 0:W].rearrange("p (h a d) -> p h a d", a=2, d=32)
        self.act(sq[:, 0:W], src, AF.Square, r=[stok], w=[sqt])
        self.E("dve", "tensor_reduce", out=ss[:, 0:nh], in_=v3(sq), axis=AX.X, op=ALU.add, r=[sqt], w=[sst])
        self.act(ss[:, 0:nh], ss[:, 0:nh], AF.Sqrt, r=[sst], w=[sst], scale=1.0 / 64, bias=EPS)
        self.E("dve", "reciprocal", out=ss[:, 0:nh], in_=ss[:, 0:nh], r=[sst], w=[sst])
        self.E("dve", "tensor_tensor", out=v3(A), in0=src.rearrange("p (h d) -> p h d", d=64),
               in1=ss[:, 0:nh].unsqueeze(2).to_broadcast([128, nh, 64]), op=ALU.mult, r=[stok, sst], w=[At])
        self.E("dve", "tensor_tensor", out=v3(Bt_), in0=v3(A), in1=gain_ap.unsqueeze(1).to_broadcast([128, nh, 64]),
               op=ALU.mult, r=[At, gtok], w=[Btt])
        cosb = cos_ap.unsqueeze(1).unsqueeze(1).to_broadcast([128, nh, 2, 32])
        sinb = sin_ap.unsqueeze(1).to_broadcast([128, nh, 32])
        self.E("dve", "tensor_tensor", out=v4(t13), in0=v4(Bt_), in1=cosb, op=ALU.mult, r=[Btt, cstok], w=[t13t])
        self.E("dve", "tensor_tensor", out=v4(tsw)[:, :, 0, :], in0=v4(Bt_)[:, :, 1, :], in1=sinb, op=ALU.mult,
               r=[Btt, cstok], w=[tswt])
        self.E("dve", "tensor_tensor", out=v4(tsw)[:, :, 1, :], in0=v4(Bt_)[:, :, 0, :], in1=sinb, op=ALU.mult,
               r=[Btt, cstok, tswt], w=[tswt])
        self.E("dve", "tensor_tensor", out=v4(out_bf)[:, :, 0, :], in0=v4(t13)[:, :, 0, :], in1=v4(tsw)[:, :, 0, :],
               op=ALU.subtract, r=[t13t, tswt], w=[otok])
        self.E("dve", "tensor_tensor", out=v4(out_bf)[:, :, 1, :], in0=v4(t13)[:, :, 1, :], in1=v4(tsw)[:, :, 1, :],
               op=ALU.add, r=[t13t, tswt, otok], w=[otok])

    def phase_P(self, li, src, moba):
        self.phase()
        NT = self.NT
        j = li // 2
        ncols = 4 * D if moba else NSA_IN
        w_ap = (self.moba_w_in if moba else self.nsa_w_in)[j]
        W3 = self.big[:, 0:8 * 4096].rearrange("p (k n) -> p k n", n=4096)
        self.load_gcol(self.gcol, "gcol", self.norm_gain[li])
        self.load_w(W3, "big", w_ap, ncols, self.gcol, "gcol")
        qg = (self.moba_q_gain if moba else self.nsa_q_gain)[j]
        self.dma(self.gq[:], qg.partition_broadcast(128), "gq", w=["gq"])
        if moba:
            self.dma(self.gk[:, 0, :], self.moba_k_gain[j].partition_broadcast(128), "gk", w=["gk"])
        else:
            for b_ in range(3):
                self.dma(self.gk[:, b_, :], self.nsa_k_gain[j, b_].partition_broadcast(128), f"gk{b_}",
                         **(dict(w=["gk"]) if b_ == 0 else dict(pw=["gk"])))
        xt = [self.ph(f"xt{i}", [128, D], F32) for i in range(2)]
        sqs = self.ph("sqs", [128, D], F32)
        rs = [self.ph(f"rs{i}", [128, 1], F32) for i in range(2)]
        hb = [self.ph(f"hb{i}", [128, D], BF16) for i in range(2)]
        hT = [self.ph(f"hT{i}", [128, 8, 128], BF16) for i in range(2)]
        b1, b1t = self.ph("nr_b1", [128, 2048], F32)
        b2, b2t = self.ph("nr_b2", [128, 2048], F32)
        b3, b3t = self.ph("nr_b3", [128, 2048], F32)
        ss, sst = self.ph("nr_ss", [128, 32], F32)
        qbf = [self.ph(f"qbf{i}", [128, 2048], BF16) for i in range(2)]
        qTt = [self.ph(f"qTt{i}", [128, 16, 128], BF16) for i in range(2)]
        vbf = [self.ph(f"vbf{i}", [128, D], BF16) for i in range(2)]
        zbf = [self.ph(f"zbf{i}", [128, D], BF16) for i in range(2)]
        rawbf = [self.ph(f"rawbf{i}", [128, 512], BF16) for i in range(2)]
        if moba:
            chunks = [(c * 512, 512) for c in range(8)]
            HT_ = 32
            srcs = [(0, 0, 8, self.gq[:], "gq"), (1, 0, 8, self.gq[:], "gq"), (2, 0, 8, self.gk[:, 0, :], "gk"),
                    (3, 0, 8, self.gk[:, 0, :], "gk")]
            qk_chunks = [(0, 0), (1, 1), (2, 2), (3, 3)]
            dests = [(self.qT_d, jb * 128) for jb in range(8)] + [(self.kT_d, jb * 128) for jb in range(8)]
        else:
            chunks = [(0, 512), (512, 512), (1024, 512), (1536, 512), (2048, 512), (2560, 48), (2608, 512), (3120, 512)]
            HT_ = 24
            srcs = [(0, 0, 8, self.gq[:], "gq"), (1, 0, 8, self.gq[:], "gq"), (2, 0, 4, self.gk[:, 1, :], "gk"),
                    (3, 0, 4, self.gk[:, 2, :], "gk")]
            qk_chunks = [(0, 0), (1, 1), (3, 2), (4, 3), (2, 4)]
            dests = ([(self.qT_d, jb * 128) for jb in range(8)] + [(self.kT_d, 0), (self.kT_d, 128), (self.kT_d, 256),
                     (self.kT_d, 384)] + [(self.kT_d, 512 + jb * 128) for jb in range(4)])
        WQ = HT_ * 64

        def pre_a(t):
            x_ap, xtok = xt[t % 2]
            self.dma(x_ap, src[t * 128:(t + 1) * 128, :], f"xld{t % 2}", r=[("y", t)], w=[xtok])
            r_ap, rtok = rs[t % 2]
            self.rstd_of(x_ap, xtok, D, sqs[0], sqs[1], r_ap, rtok)
            h_ap, htok = hb[t % 2]
            self.act(h_ap, x_ap, AF.Copy, r=[xtok, rtok], w=[htok], scale=r_ap[:, 0:1])

        def pre_b(t):
            h_ap, htok = hb[t % 2]
            hT_ap, hTtok = hT[t % 2]
            self.transpose8(h_ap, htok, hT_ap, hTtok, 0)

        def mm_chunk(t, ci, bank):
            c0, cw = chunks[ci]
            hT_ap, hTtok = hT[t % 2]
            ps = self.pf[bank]
            pstok = ("pf", bank)
            for k in range(8):
                self.mm(ps[:, 0:cw], hT_ap[:, k, :], W3[:, k, c0:c0 + cw], k == 0, k == 7, r=[hTtok, "big"],
                        w=[pstok] if k == 0 else ())
            last = self.S.q["pe"][-1]
            self.S.tw[pstok] = {last.stream: last}

        def chain(t):
            q_ap, qbt = qbf[t % 2]
            v3 = lambda ap, c0, w: ap[:, c0:c0 + w].rearrange("p (h d) -> p h d", d=64)
            v4 = lambda ap, c0, w: ap[:, c0:c0 + w].rearrange("p (h a d) -> p h a d", a=2, d=32)
            c = 0
            for n_, (bank, col0, nh, g_ap, g_tok) in enumerate(srcs):
                w_ = nh * 64
                self.act(b1[:, c:c + w_], self.pf[bank][:, col0:col0 + w_], AF.Square, r=[("pf", bank)],
                         **(dict(w=[b1t]) if n_ == 0 else dict(pw=[b1t])))
                c += w_
            self.E("dve", "tensor_reduce", out=ss[:, 0:HT_], in_=v3(b1, 0, WQ), axis=AX.X, op=ALU.add, r=[b1t], w=[sst])
            self.act(ss[:, 0:HT_], ss[:, 0:HT_], AF.Sqrt, r=[sst], w=[sst], scale=1.0 / 64, bias=EPS)
            self.E("dve", "reciprocal", out=ss[:, 0:HT_], in_=ss[:, 0:HT_], r=[sst], w=[sst])
            c = 0
            h0 = 0
            for n_, (bank, col0, nh, g_ap, g_tok) in enumerate(srcs):
                w_ = nh * 64
                self.E("dve", "tensor_tensor", out=v3(b2, c, w_),
                       in0=self.pf[bank][:, col0:col0 + w_].rearrange("p (h d) -> p h d", d=64),
                       in1=ss[:, h0:h0 + nh].unsqueeze(2).to_broadcast([128, nh, 64]), op=ALU.mult,
                       r=[("pf", bank), sst], **(dict(w=[b2t]) if n_ == 0 else dict(pw=[b2t])))
                c += w_
                h0 += nh
            c = 0
            first = True
            i_ = 0
            while i_ < len(srcs):
                g_ap, g_tok = srcs[i_][3], srcs[i_][4]
                nh = srcs[i_][2]
                k_ = i_ + 1
                while k_ < len(srcs) and srcs[k_][3] is g_ap:
                    nh += srcs[k_][2]
                    k_ += 1
                w_ = nh * 64
                self.E("dve", "tensor_tensor", out=v3(b1, c, w_), in0=v3(b2, c, w_),
                       in1=g_ap.unsqueeze(1).to_broadcast([128, nh, 64]), op=ALU.mult, r=[b2t, g_tok],
                       **(dict(w=[b1t]) if first else dict(pw=[b1t])))
                first = False
                c += w_
                i_ = k_
            import os
            if "norope" in os.environ.get("PDBG", ""):
                return
            cos_ap = self.cs[:, 0, t, :]
            sin_ap = self.cs[:, 1, t, :]
            cosb = cos_ap.unsqueeze(1).unsqueeze(1).to_broadcast([128, HT_, 2, 32])
            sinb = sin_ap.unsqueeze(1).to_broadcast([128, HT_, 32])
            self.E("dve", "tensor_tensor", out=v4(b2, 0, WQ), in0=v4(b1, 0, WQ), in1=cosb, op=ALU.mult, r=[b1t, "cs"], w=[b2t])
            self.E("dve", "tensor_tensor", out=v4(b3, 0, WQ)[:, :, 0, :], in0=v4(b1, 0, WQ)[:, :, 1, :], in1=sinb, op=ALU.mult,
                   r=[b1t, "cs"], w=[b3t])
            self.E("dve", "tensor_tensor", out=v4(b3, 0, WQ)[:, :, 1, :], in0=v4(b1, 0, WQ)[:, :, 0, :], in1=sinb, op=ALU.mult,
                   r=[b1t, "cs"], pw=[b3t])
            self.E("dve", "tensor_tensor", out=v4(q_ap, 0, WQ)[:, :, 0, :], in0=v4(b2, 0, WQ)[:, :, 0, :],
                   in1=v4(b3, 0, WQ)[:, :, 0, :], op=ALU.subtract, r=[b2t, b3t], w=[qbt])
            self.E("dve", "tensor_tensor", out=v4(q_ap, 0, WQ)[:, :, 1, :], in0=v4(b2, 0, WQ)[:, :, 1, :],
                   in1=v4(b3, 0, WQ)[:, :, 1, :], op=ALU.add, r=[b2t, b3t], pw=[qbt])
            import os
            if not moba and "noraw" not in os.environ.get("PDBG", ""):
                self.E("dve", "tensor_copy", out=rawbf[t % 2][0], in_=self.pf[4][:, 0:512], r=[("pf", 4)], w=[rawbf[t % 2][1]])

        def tq(t):
            q_ap, qbt = qbf[t % 2]
            qt_, qtt = qTt[t % 2]
            for half in range(2):
                pbt = self.pb[1]
                ptok = ("pb", 1)
                for k in range(8):
                    jb = half * 8 + k
                    if (not moba) and jb >= 12:
                        s_ap, s_tok = rawbf[t % 2]
                        s_in = s_ap[:, (jb - 12) * 128:(jb - 11) * 128]
                    else:
                        s_tok = qbt
                        s_in = q_ap[:, jb * 128:(jb + 1) * 128]
                    if k == 0:
                        self.tr(pbt[:, 0:128], s_in, r=[s_tok], w=[ptok])
                    else:
                        self.S.op("pe", (lambda e, k=k, s_in=s_in: e.transpose(out=pbt[:, k * 128:(k + 1) * 128], in_=s_in,
                                                                              identity=self.ident[:])),
                                  reads=[s_tok, "ident"], pwrites=[ptok])
                self.act(qt_[:, half * 8:(half + 1) * 8, :].rearrange("p a b -> p (a b)"), pbt[:, 0:1024], AF.Copy, r=[ptok],
                         **(dict(w=[qtt]) if half == 0 else dict(pw=[qtt])))
            jb = 0
            while jb < 16:
                dram, row0 = dests[jb]
                k_ = jb + 1
                while k_ < 16 and dests[k_][0] is dram and dests[k_][1] == row0 + (k_ - jb) * 128:
                    k_ += 1
                nb = k_ - jb
                self.dma(dram[row0:row0 + nb * 128, t * 128:(t + 1) * 128].rearrange("(j p) t -> p j t", p=128),
                         qt_[:, jb:k_, :], f"qst{t % 2}", r=[qtt], pw=[("kTq_d",)], q="pool")
                jb = k_

        def mm_vz(t):
            v_ap, vtok = vbf[t % 2]
            z_ap, ztok = zbf[t % 2]
            if moba:
                for ci in (4, 5):
                    bank = ci
                    mm_chunk(t, ci, bank)
                    self.act(v_ap[:, (ci - 4) * 512:(ci - 3) * 512], self.pf[bank][:, 0:512], AF.Copy, r=[("pf", bank)],
                             **(dict(w=[vtok]) if ci == 4 else dict(pw=[vtok])))
                self.dma(self.v_d[t * 128:(t + 1) * 128, :], v_ap, f"vst{t % 2}", r=[vtok], pw=[("v_d",)], q="pool")
            else:
                self.act(v_ap[:, 0:256], self.pf[2][:, 256:512], AF.Copy, r=[("pf", 2)], w=[vtok])
                self.act(v_ap[:, 256:512], self.pf[3][:, 256:512], AF.Copy, r=[("pf", 3)], pw=[vtok])
                self.dma(self.v_d[t * 128:(t + 1) * 128, 0:512], v_ap[:, 0:512], f"vst{t % 2}", r=[vtok], pw=[("v_d",)], q="pool")
                mm_chunk(t, 5, 5)
                self.act(self.g_sb[:, t, :], self.pf[5][:, 0:48], AF.Sigmoid, r=[("pf", 5)], pw=["g_sb"])
            for ci in (6, 7):
                bank = ci - 2
                mm_chunk(t, ci, bank)
                self.act(z_ap[:, (ci - 6) * 512:(ci - 5) * 512], self.pf[bank][:, 0:512], AF.Silu, r=[("pf", bank)],
                         **(dict(w=[ztok]) if ci == 6 else dict(pw=[ztok])))
            self.dma(self.zs_d[t * 128:(t + 1) * 128, :], z_ap, f"zst{t % 2}", r=[ztok], pw=[("zs_d",)], q="pool")

        import os
        dbg = os.environ.get("PDBG", "")
        pre_a(0)
        pre_b(0)
        for t in range(NT):
            for (ci, bank) in qk_chunks:
                mm_chunk(t, ci, bank)
            if t + 1 < NT:
                pre_a(t + 1)
            if "nochain" not in dbg:
                chain(t)
            if t + 1 < NT:
                pre_b(t + 1)
            if "novz" not in dbg:
                mm_vz(t)
            if t >= 1 and "notq" not in dbg:
                tq(t - 1)
        if "notq" not in dbg:
            tq(NT - 1)

    def attention(self, qTa, qtok, kTa, ktok, Va, vtok, vw, pairs_fn, fin, pts, tag, side=None, side_delay=0):
        NT = self.NT
        qtoks = list(qtok) if isinstance(qtok, list) else [qtok]
        units = []
        for qt in range(NT):
            pairs = pairs_fn(qt)
            ng = (len(pairs) + 3) // 4
            for gi in range(ng):
                units.append((qt, pairs[gi * 4:(gi + 1) * 4], gi == 0, gi == ng - 1))
        n = len(units)
        NBK = len(pts)
        SK = NBK - 1
        banks = [0, 1, 2, 3][:NBK]
        u0 = getattr(self, "_u", 0)

        def qk(i):
            qt, grp, _, _ = units[i]
            si = banks[(u0 + i) % NBK]
            ps = self.pf[si]
            pstok = ("pf", si)
            firstw = True
            for jx, (kt, bias) in enumerate(grp):
                o_ap = ps[:, jx * 128:(jx + 1) * 128]
                self.mm(o_ap, kTa[:, kt * 128:(kt + 1) * 128], qTa[:, qt * 128:(qt + 1) * 128], True, bias is None,
                        r=[ktok] + qtoks, w=[pstok] if firstw else ())
                firstw = False
                if bias is not None:
                    self.mm(o_ap, self.ident[:], bias, False, True, r=["ident", "tri", "atri", "cmpm"])
            last = self.S.q["pe"][-1]
            self.S.tw[pstok] = {last.stream: last}

        def pv(i):
            qt, grp, isf, isl = units[i]
            si = banks[(u0 + i) % NBK]
            ps = self.pf[si]
            pstok = ("pf", si)
            pt_ap, pttok = pts[(u0 + i) % NBK]
            po = self.pf[4 + qt % 2]
            potok = ("pf", 4 + qt % 2)
            nn = len(grp) * 128
            self.act(pt_ap[:, 0:nn], ps[:, 0:nn], AF.Exp, r=[pstok], w=[pttok], scale=0.125)
            for jx, (kt, bias) in enumerate(grp):
                is_first = isf and jx == 0
                is_last = isl and jx == len(grp) - 1
                self.mm(po[:, 0:vw], pt_ap[:, jx * 128:(jx + 1) * 128], Va[:, kt, 0:vw], is_first, is_last,
                        r=[pttok, vtok], w=[potok] if is_first else ())
            last = self.S.q["pe"][-1]
            self.S.tw[potok] = {last.stream: last}
            if isl:
                fin(qt, po, potok)

        for i in range(n + SK):
            if i < n:
                qk(i)
            if i >= SK:
                pv(i - SK)
                if side is not None and i - SK >= side_delay:
                    next(side, None)
        self._u = u0 + n

    def phase_A_moba(self, li):
        self.phase()
        T, NT = self.T, self.NT
        NB = T // 256
        o_all = self.big[:, 0:NT * D].rearrange("p (t f) -> p t f", f=D)
        qTa = [self.ph(f"qTa{i}", [128, T], BF16) for i in range(2)]
        kTa = [self.ph(f"kTa{i}", [128, T], BF16) for i in range(2)]
        Va = [self.ph(f"Va{i}", [128, NT, 65], BF16) for i in range(2)]
        pts = [self.ph(f"pt{i}", [128, 512], BF16) for i in range(4)]
        kmf, kmft = self.ph("kmf", [128, 16], F32)
        kmT, kmTt = self.ph("kmT", [128, 16], BF16)
        gt_, gtt = self.ph("gate", [128, 16], F32)
        m8, m8t = self.ph("m8", [128, 8], F32)
        bq, bqt = self.ph("bq", [128, 128], BF16)
        rz, rzt = self.ph("rz", [128, 1], F32)
        self.E("dve", "memset", kmT, 0.0, w=[kmTt])
        self.E("dve", "memset", bq, 0.0, w=[bqt])
        for i in range(2):
            q_ap, qtok = qTa[i]
            k_ap, ktok = kTa[i]
            v_ap, vtok = Va[i]
            self.E("pool", "memset", q_ap, 0.0, w=[qtok])
            self.E("pool", "memset", k_ap[0:64, :], 0.0, w=[ktok])
            ke = k_ap[64:128, :]
            self.E("pool", "memset", ke, NEGB, r=[ktok], w=[ktok])
            self.E("pool", "affine_select", out=ke, in_=ke, pattern=[[1, T]], compare_op=ALU.is_ge, fill=0.0, base=0,
                   channel_multiplier=-256, r=[ktok], w=[ktok])
            self.E("pool", "affine_select", out=ke, in_=ke, pattern=[[-1, T]], compare_op=ALU.is_ge, fill=0.0, base=255,
                   channel_multiplier=256, r=[ktok], w=[ktok])
            self.E("dve", "memset", v_ap, 1.0, w=[vtok])

        def load_head(h):
            i = h % 2
            q_ap, qtok = qTa[i]
            k_ap, ktok = kTa[i]
            v_ap, vtok = Va[i]
            self.dma(q_ap[0:64, :], self.qT_d[h * 64:(h + 1) * 64, :], f"aq{i}", r=[("kTq_d",)], w=[qtok])
            self.dma(k_ap[0:64, :], self.kT_d[h * 64:(h + 1) * 64, :], f"ak{i}", r=[("kTq_d",)], w=[ktok])
            for c in range(0, NT, 8):
                n = min(8, NT - c)
                self.dma(v_ap[:, c:c + n, 0:64],
                         self.v_d[c * 128:(c + n) * 128, h * 64:(h + 1) * 64].rearrange("(t p) d -> p t d", p=128),
                         f"av{i}", r=[("v_d",)], **(dict(w=[vtok]) if c == 0 else dict(pw=[vtok])))

        def gate_steps(h):
            i = h % 2
            q_ap, qtok = qTa[i]
            k_ap, ktok = kTa[i]
            self.E("dve", "tensor_reduce", out=kmf[0:64, 0:NB], in_=k_ap[0:64, :].rearrange("p (n k) -> p n k", k=256),
                   axis=AX.X, op=ALU.add, r=[ktok], w=[kmft])
            self.E("dve", "tensor_scalar", out=kmT[0:64, 0:NB], in0=kmf[0:64, 0:NB], scalar1=1.0 / 256, scalar2=None,
                   op0=ALU.mult, r=[kmft], w=[kmTt])
            self.E("dve", "memset", gt_, -1.0e30, w=[gtt])
            yield
            for qt in range(NT):
                b = qt // 2
                if b <= 3:
                    continue
                pg = self.pb[0][:, 0:32].bitcast(F32)
                self.mm(pg[:, 0:16], q_ap[:, qt * 128:(qt + 1) * 128], kmT, True, True, r=[qtok, kmTt], w=[("pb", 0)])
                self.E("dve", "tensor_copy", out=gt_[:, 0:b], in_=pg[:, 0:b], r=[("pb", 0), gtt], w=[gtt])
                self.E("dve", "max", out=m8, in_=gt_, r=[gtt], w=[m8t])
                self.E("dve", "tensor_scalar", out=bq[:, 64:80], in0=gt_, scalar1=m8[:, 2:3], scalar2=1.0, op0=ALU.is_ge,
                       op1=ALU.subtract, r=[gtt, m8t, bqt], w=[bqt])
                self.E("dve", "memset", bq[:, 64 + b:65 + b], 0.0, r=[bqt], w=[bqt])
                yield
                yield
                pbt = self.pb[1]
                self.tr(pbt[:, 0:128], bq, r=[bqt], w=[("pb", 1)])
                self.E("dve", "tensor_copy", out=q_ap[64:128, qt * 128:(qt + 1) * 128], in_=pbt[64:128, 0:128],
                       r=[("pb", 1), qtok], w=[qtok])
                yield

        load_head(0)
        for _ in gate_steps(0):
            pass
        for h in range(H):
            side = None
            if h + 1 < H:
                load_head(h + 1)
                side = gate_steps(h + 1)
            i = h % 2
            q_ap, qtok = qTa[i]
            k_ap, ktok = kTa[i]
            v_ap, vtok = Va[i]

            def pairs_fn(qt):
                return [(kt, (self.tri[:] if kt == qt else None)) for kt in range(qt + 1)]

            def fin(qt, po, potok, h=h):
                self.E("dve", "reciprocal", out=rz, in_=po[:, 64:65], r=[potok], w=[rzt])
                self.E("dve", "tensor_scalar", out=o_all[:, qt, h * 64:(h + 1) * 64], in0=po[:, 0:64], scalar1=rz[:, 0:1],
                       scalar2=None, op0=ALU.mult, r=[potok, rzt], pw=["big"])

            self.attention(q_ap, qtok, k_ap, ktok, v_ap, vtok, 65, pairs_fn, fin, pts, f"m{h}", side=side, side_delay=min(24, self.NT))
            if side is not None:
                for _ in side:
                    pass

    def phase_C(self, li):
        self.phase()
        T, NT = self.T, self.NT
        j = li // 2
        xkp, xkpt = self.ph("xkp", [128, T + 16], BF16)
        xk16, xk16t = self.ph("xk16", [128, 2, 16, T // 16], BF16)
        W1d, W1t = self.ph("W1d", [128, 32, 256], BF16)
        w1s = [self.ph(f"w1s{i}", [128, 8, 256], F32) for i in range(2)]
        W2b, W2t = self.ph("W2b", [128, 2, 64], BF16)
        w2s, w2st = self.ph("w2s", [128, 2, 64], F32)
        peT, peTt = self.ph("peT", [128, 32], BF16)
        pes, pest = self.ph("pes", [128, 32], F32)
        bh, bht = self.ph("bh", [128, 2], F32)
        NC, NCT = self.NC, self.NCT
        xx, xxt = self.ph("xx", [128, NC], F32)
        x2, x2t = self.ph("x2", [128, NC], F32)
        sg, sgt = self.ph("sg", [128, NC], F32)
        gl, glt = self.ph("gl", [128, 2, NC], BF16)
        kcb, kcbt = self.ph("kcb", [128, 128], BF16)
        tmp = [self.ph(n, [128, 64], F32) for n in ("c_sq", "c_ss", "c_A", "c_B", "c_t13", "c_tsw")]
        self.E("pool", "memset", xkp, 0.0, w=[xkpt])
        self.E("pool", "memset", W1d, 0.0, w=[W1t])
        self.E("pool", "memset", peT, 0.0, w=[peTt])
        self.E("pool", "memset", kcb, 0.0, w=[kcbt])
        for kv in range(2):
            w1 = self.nsa_cmp_w1[j, kv].rearrange("(l d) j -> d l j", d=64)
            for c in range(4):
                s_ap, s_tok = w1s[c % 2]
                self.dma(s_ap[0:64], w1[:, c * 8:(c + 1) * 8, :], f"w1s{c % 2}", w=[s_tok])
                self.E("dve", "tensor_copy", out=W1d[0:64, c * 8:(c + 1) * 8, :], in_=s_ap[0:64], r=[s_tok],
                       **(dict(w=[W1t]) if c == 0 else dict(pw=[W1t])))
            self.dma(w2s, self.nsa_cmp_w2[j, kv].rearrange("(c p) d -> p c d", p=128), "w2s", w=[w2st])
            self.E("dve", "tensor_copy", out=W2b, in_=w2s, r=[w2st], w=[W2t])
            for q4 in range(4):
                self.dma(pes[0:64, q4 * 8:(q4 + 1) * 8], self.nsa_cmp_pe[j, kv, q4 * 8:(q4 + 1) * 8, :].rearrange("l d -> d l"),
                         "pes", **(dict(w=[pest]) if q4 == 0 else dict(pw=[pest])), allow_slow_non_contiguous=True)
            self.E("dve", "tensor_copy", out=peT[0:64, :], in_=pes[0:64, :], r=[pest], w=[peTt])
            pbias = self.pf[3]
            for jc in range(2):
                for l in range(32):
                    self.mm(pbias[:, jc:jc + 1], W1d[:, l, jc * 128:(jc + 1) * 128], peT[:, l:l + 1], l == 0, l == 31,
                            r=[W1t, peTt], w=[("pf", 3)] if (l == 0 and jc == 0) else ())
            last = self.S.q["pe"][-1]
            self.S.tw[("pf", 3)] = {last.stream: last}
            self.E("dve", "tensor_copy", out=bh, in_=pbias[:, 0:2], r=[("pf", 3)], w=[bht])
            for g in range(4):
                row0 = 512 + kv * 256 + g * 64
                self.dma(xkp[0:64, 0:T], self.kT_d[row0:row0 + 64, :], "xkp", r=[("kTq_d",)], w=[xkpt])
                self.E("dve", "tensor_copy", out=xk16[:, 0], in_=xkp[:, 0:T].rearrange("p (m r) -> p r m", r=16), r=[xkpt], w=[xk16t])
                self.E("dve", "tensor_copy", out=xk16[:, 1], in_=xkp[:, 16:T + 16].rearrange("p (m r) -> p r m", r=16), r=[xkpt],
                       pw=[xk16t])
                for jc in range(2):
                    ph_ = self.pf[jc]
                    for l in range(32):
                        self.mm(ph_[:, 0:NC], W1d[:, l, jc * 128:(jc + 1) * 128], xk16[:, l // 16, l % 16, :], l == 0, l == 31,
                                r=[W1t, xk16t], w=[("pf", jc)] if l == 0 else ())
                    last = self.S.q["pe"][-1]
                    self.S.tw[("pf", jc)] = {last.stream: last}
                    self.act(xx, ph_[:, 0:NC], AF.Identity, r=[("pf", jc), bht], w=[xxt], bias=bh[:, jc:jc + 1])
                    self.E("dve", "tensor_tensor", out=x2, in0=xx, in1=xx, op=ALU.mult, r=[xxt], w=[x2t])
                    self.E("dve", "tensor_scalar", out=x2, in0=x2, scalar1=0.044715, scalar2=1.0, op0=ALU.mult, op1=ALU.add,
                           r=[x2t], w=[x2t])
                    self.E("dve", "tensor_tensor", out=x2, in0=x2, in1=xx, op=ALU.mult, r=[x2t, xxt], w=[x2t])
                    self.act(sg, x2, AF.Sigmoid, r=[x2t], w=[sgt], scale=1.5957691216057308)
                    self.E("dve", "tensor_tensor", out=gl[:, jc, :], in0=xx, in1=sg, op=ALU.mult, r=[xxt, sgt],
                           **(dict(w=[glt]) if jc == 0 else dict(pw=[glt])))
                for ct in range(NCT):
                    pk = self.pf[2]
                    for jc in range(2):
                        self.mm(pk[:, 0:64], gl[:, jc, ct * 128:(ct + 1) * 128], W2b[:, jc, :], jc == 0, jc == 1,
                                r=[glt, W2t], w=[("pf", 2)] if jc == 0 else ())
                    last = self.S.q["pe"][-1]
                    self.S.tw[("pf", 2)] = {last.stream: last}
                    if kv == 0:
                        self.norm_rope(pk[:, 0:64], ("pf", 2), 1, self.gk[:, 0, :], "gk", self.csc[:, 0, ct, :],
                                       self.csc[:, 1, ct, :], "csc", kcb, kcbt, tmp)
                        pbt = self.pb[1]
                        self.tr(pbt[:, 0:128], kcb, r=[kcbt], w=[("pb", 1)])
                        self.E("dve", "tensor_copy", out=self.kcTa[:, g, ct * 128:(ct + 1) * 128], in_=pbt[:, 0:128],
                               r=[("pb", 1)], pw=["kcTa"])
                    else:
                        self.act(self.vca[:, g, ct, 0:64], pk[:, 0:64], AF.Copy, r=[("pf", 2)], pw=["vca"])

    def phase_A_nsa(self, li):
        self.phase()
        T, NT = self.T, self.NT
        o_all = self.big[:, 0:NT * D].rearrange("p (t f) -> p t f", f=D)
        qTa = [self.ph(f"nqTa{i}", [128, T], BF16) for i in range(4)]
        ksTa, kst = self.ph("ksTa", [128, T], BF16)
        kwTa, kwt = self.ph("kwTa", [128, T], BF16)
        Vs, Vst = self.ph("Vs", [128, NT, 65], BF16)
        Vw, Vwt = self.ph("Vw", [128, NT, 65], BF16)
        pts = [self.ph(f"npt{i}", [128, 512], BF16) for i in range(4)]
        oacc, oacct = self.ph("oacc", [128, NT, 64], F32)
        imp, impt = self.ph("imp", [128, NT, 64], F32)
        sc, sct = self.ph("sc", [128, 64], F32)
        sc2, sc2t = self.ph("sc2", [128, 64], F32)
        m8a, m8at = self.ph("m8a", [128, 8], F32)
        m8b, m8bt = self.ph("m8b", [128, 8], F32)
        bs, bst = self.ph("bs", [128, 128], BF16)
        rz, rzt = self.ph("nrz", [128, 1], F32)
        cf, cft = self.ph("ncf", [128, 1], F32)
        stage, staget = self.ph("cstage", [128, NT, 128], F32)
        rzb, rzbt = self.ph("rzb", [128, NT], F32)
        cfb, cfbt = self.ph("cfb", [128, NT], F32)
        for hl, (q_ap, qtok) in enumerate(qTa):
            self.E("pool", "memset", q_ap, 0.0, w=[qtok, ("qhi", hl)])
        self.E("pool", "memset", kwTa, 0.0, w=[kwt])
        self.E("pool", "memset", ksTa[0:64, :], 0.0, w=[kst])
        ke = ksTa[64:128, :]
        self.E("pool", "memset", ke, NEGB, r=[kst], w=[kst])
        self.E("pool", "affine_select", out=ke, in_=ke, pattern=[[1, T]], compare_op=ALU.is_ge, fill=0.0, base=0,
               channel_multiplier=-64, r=[kst], w=[kst])
        self.E("pool", "affine_select", out=ke, in_=ke, pattern=[[-1, T]], compare_op=ALU.is_ge, fill=0.0, base=63,
               channel_multiplier=64, r=[kst], w=[kst])
        self.E("dve", "memset", Vs, 1.0, w=[Vst])
        self.E("dve", "memset", Vw, 1.0, w=[Vwt])
        self.E("dve", "memset", bs, 0.0, w=[bst])
        self.E("dve", "memset", imp, 0.0, w=[impt])

        for g in range(4):
            for hl in range(4):
                h = g * 4 + hl
                q_ap, qtok = qTa[hl]
                self.dma(q_ap[0:64, :], self.qT_d[h * 64:(h + 1) * 64, :], f"nq{hl}", r=[("kTq_d",)], w=[qtok])
            self.dma(ksTa[0:64, :], self.kT_d[g * 64:(g + 1) * 64, :], "nks", r=[("kTq_d",)], w=[kst])
            self.dma(kwTa[0:64, :], self.kT_d[256 + g * 64:256 + (g + 1) * 64, :], "nkw", r=[("kTq_d",)], w=[kwt])
            for (V_, Vt_, c0, nm) in ((Vs, Vst, g * 64, "nvs"), (Vw, Vwt, 256 + g * 64, "nvw")):
                for c in range(0, NT, 8):
                    n = min(8, NT - c)
                    self.dma(V_[:, c:c + n, 0:64],
                             self.v_d[c * 128:(c + n) * 128, c0:c0 + 64].rearrange("(t p) d -> p t d", p=128),
                             nm, r=[("v_d",)], **(dict(w=[Vt_]) if c == 0 else dict(pw=[Vt_])))

            def cmp_pairs(qt):
                out = []
                for kt in range(self.NCT):
                    dl = qt - 16 * kt
                    if dl < 0:
                        continue
                    out.append((kt, self.cmpm[:, dl, :] if dl <= 16 else None))
                return out

            def sel_A(qt):
                scq = imp[:, qt, :]
                self.E("dve", "max", out=m8a, in_=scq, r=[impt], w=[m8at])
                self.E("dve", "tensor_scalar", out=sc2, in0=scq, scalar1=m8a[:, 7:8], scalar2=-6.0e4, op0=ALU.is_ge, op1=ALU.mult,
                       r=[impt, m8at], w=[sc2t])
                self.E("dve", "tensor_tensor", out=sc2, in0=sc2, in1=scq, op=ALU.add, r=[sc2t, impt], w=[sc2t])
                self.E("dve", "max", out=m8b, in_=sc2, r=[sc2t], w=[m8bt])
                self.E("dve", "tensor_scalar", out=bs[:, 64:128], in0=scq, scalar1=m8b[:, 7:8], scalar2=1.0, op0=ALU.is_ge,
                       op1=ALU.subtract, r=[impt, m8bt, bst], w=[bst])

            def sel_B(qt):
                pbt = self.pb[1]
                self.tr(pbt[:, 0:128], bs, r=[bst], w=[("pb", 1)])
                for hl2 in range(4):
                    q2, _ = qTa[hl2]
                    self.E("dve", "tensor_copy", out=q2[64:128, qt * 128:(qt + 1) * 128], in_=pbt[64:128, 0:128],
                           r=[("pb", 1)], pw=[("qhi", hl2)])

            def sel_steps():
                for qt in range(NT):
                    sel_A(qt)
                    yield
                    sel_B(qt)
                    yield

            for hl in range(4):
                h = g * 4 + hl
                q_ap, qtok = qTa[hl]

                def fin_c(qt, po, potok):
                    self.E("dve", "tensor_copy", out=stage[:, qt, :], in_=po[:, 0:128], r=[potok], pw=[staget])

                self.attention(q_ap, qtok, self.kcTa[:, g, :], "kcTa", self.vca[:, g], "vca", 128, cmp_pairs, fin_c, pts,
                               f"c{h}")
                zc = stage[:, :, 64]
                self.E("dve", "tensor_scalar", out=rzb, in0=zc, scalar1=1.0e-30, scalar2=None, op0=ALU.max, r=[staget], w=[rzbt])
                self.E("dve", "reciprocal", out=rzb, in_=rzb, r=[rzbt], w=[rzbt])
                self.E("dve", "tensor_tensor", out=cfb, in0=rzb, in1=self.g_sb[:, :, h * 3], op=ALU.mult, r=[rzbt, "g_sb"],
                       w=[cfbt])
                self.E("dve", "tensor_tensor", out=o_all[:, :, h * 64:(h + 1) * 64], in0=stage[:, :, 0:64],
                       in1=cfb.unsqueeze(2).to_broadcast([128, NT, 64]), op=ALU.mult, r=[staget, cfbt], pw=["big"])
                rzb3 = rzb.unsqueeze(2).to_broadcast([128, NT, 63])
                if hl == 0:
                    self.E("dve", "memset", imp[:, :, 63:64], 0.0, w=[impt])
                    self.E("dve", "tensor_tensor", out=imp[:, :, 0:63], in0=stage[:, :, 65:128], in1=rzb3, op=ALU.mult,
                           r=[staget, rzbt], pw=[impt])
                else:
                    self.E("dve", "tensor_tensor", out=stage[:, :, 65:128], in0=stage[:, :, 65:128], in1=rzb3, op=ALU.mult,
                           r=[staget, rzbt], pw=[staget])
                    self.E("dve", "tensor_tensor", out=imp[:, :, 0:63], in0=imp[:, :, 0:63], in1=stage[:, :, 65:128], op=ALU.add,
                           r=[staget, impt], pw=[impt])
            self.E("dve", "tensor_tensor", out=imp[:, :, :], in0=imp[:, :, :], in1=self.msel[:, :, :], op=ALU.add,
                   r=[impt, "msel"], w=[impt])

            def sel_pairs(qt):
                return [(kt, (self.tri[:] if kt == qt else None)) for kt in range(qt + 1)]

            def win_pairs(qt):
                out = []
                for kt in range(max(0, qt - 4), qt + 1):
                    b = self.tri[:] if kt == qt else (self.atri[:] if kt == qt - 4 else None)
                    out.append((kt, b))
                return out

            for hl in range(4):
                h = g * 4 + hl
                q_ap, qtok = qTa[hl]

                def fin_w(qt, po, potok, h=h):
                    self.E("dve", "reciprocal", out=rz, in_=po[:, 64:65], r=[potok], w=[rzt])
                    self.E("dve", "tensor_tensor", out=cf, in0=rz, in1=self.g_sb[:, qt, h * 3 + 2:h * 3 + 3], op=ALU.mult,
                           r=[rzt, "g_sb"], w=[cft])
                    self.E("dve", "scalar_tensor_tensor", out=oacc[:, qt, :], in0=po[:, 0:64], scalar=cf[:, 0:1],
                           in1=o_all[:, qt, h * 64:(h + 1) * 64], op0=ALU.mult, op1=ALU.add, r=[potok, cft, "big"],
                           pw=[oacct])

                def fin_s(qt, po, potok, h=h):
                    self.E("dve", "reciprocal", out=rz, in_=po[:, 64:65], r=[potok], w=[rzt])
                    self.E("dve", "tensor_tensor", out=cf, in0=rz, in1=self.g_sb[:, qt, h * 3 + 1:h * 3 + 2], op=ALU.mult,
                           r=[rzt, "g_sb"], w=[cft])
                    self.E("dve", "scalar_tensor_tensor", out=o_all[:, qt, h * 64:(h + 1) * 64], in0=po[:, 0:64],
                           scalar=cf[:, 0:1], in1=oacc[:, qt, :], op0=ALU.mult, op1=ALU.add, r=[potok, cft, oacct],
                           pw=["big"])

                side = sel_steps() if hl == 0 else None
                self.attention(q_ap, qtok, kwTa, kwt, Vw, Vwt, 65, win_pairs, fin_w, pts, f"w{h}", side=side)
                if side is not None:
                    for _ in side:
                        pass
                self.attention(q_ap, [qtok, ("qhi", hl)], ksTa, kst, Vs, Vst, 65, sel_pairs, fin_s, pts, f"s{h}")

    def phase_O(self, li, src):
        self.phase()
        NT = self.NT
        j = li // 2
        moba = (li % 2 == 0)
        o_all = self.big[:, 0:NT * D].rearrange("p (t f) -> p t f", f=D)
        Wo, Wot = self.ph("Wo", [128, 8, D], BF16)
        Wg, Wgt = self.ph("Wg", [128, 8, D], BF16)
        Wp, Wpt = self.ph("Wp", [128, 2, D], BF16)
        self.load_w(Wo, Wot, (self.moba_w_out if moba else self.nsa_w_out)[j], D, None, None)
        self.load_gcol(self.gcol2, "gcol2", self.ple_gate_gain[li])
        self.load_w(Wg, Wgt, self.ple_w_gate[li], D, self.gcol2, "gcol2")
        self.load_w(Wp, Wpt, self.ple_w_proj[li], D, None, None, nk=2)
        xt = [self.ph(f"oxt{i}", [128, D], F32) for i in range(2)]
        zt = [self.ph(f"ozt{i}", [128, D], BF16) for i in range(2)]
        pt_ = [self.ph(f"opt{i}", [128, PLE], F32) for i in range(2)]
        og = [self.ph(f"og{i}", [128, D], BF16) for i in range(2)]
        ogT = [self.ph(f"ogT{i}", [128, 8, 128], BF16) for i in range(2)]
        x1 = [self.ph(f"x1_{i}", [128, D], F32) for i in range(2)]
        sqs, sqst = self.ph("osq", [128, D], F32)
        rs = [self.ph(f"ors{i}", [128, 1], F32) for i in range(2)]
        xn, xnt = self.ph("xn", [128, D], BF16)
        xnT, xnTt = self.ph("xnT", [128, 8, 128], BF16)
        gate, gatet = self.ph("gatef", [128, D], F32)
        pbf, pbft = self.ph("pbf", [128, PLE], BF16)
        pT, pTt = self.ph("pT", [128, 2, 128], BF16)
        x2 = [self.ph(f"x2_{i}", [128, D], F32) for i in range(2)]

        def S1_T(t):
            x_ap, xtok = xt[t % 2]
            z_ap, ztok = zt[t % 2]
            p_ap, ptok = pt_[t % 2]
            og_ap, ogt = og[t % 2]
            ogT_ap, ogTt = ogT[t % 2]
            self.dma(x_ap, src[t * 128:(t + 1) * 128, :], f"oxl{t % 2}", r=[("y", t)], w=[xtok])
            self.dma(z_ap, self.zs_d[t * 128:(t + 1) * 128, :], f"ozl{t % 2}", r=[("zs_d",)], w=[ztok])
            self.dma(p_ap, self.p_in[li, t * 128:(t + 1) * 128, :], f"opl{t % 2}", w=[ptok])
            self.E("dve", "tensor_tensor", out=og_ap, in0=o_all[:, t, :], in1=z_ap, op=ALU.mult, r=["big", ztok], w=[ogt])
            self.transpose8(og_ap, ogt, ogT_ap, ogTt, 0)

        def S1_M(t):
            x_ap, xtok = xt[t % 2]
            ogT_ap, ogTt = ogT[t % 2]
            x1_ap, x1t = x1[t % 2]
            for c in range(2):
                ps = self.pf[c]
                for k in range(8):
                    self.mm(ps[:, :], ogT_ap[:, k, :], Wo[:, k, c * 512:(c + 1) * 512], k == 0, k == 7, r=[ogTt, Wot],
                            w=[("pf", c)] if k == 0 else ())
                last = self.S.q["pe"][-1]
                self.S.tw[("pf", c)] = {last.stream: last}
                self.E("dve", "tensor_tensor", out=x1_ap[:, c * 512:(c + 1) * 512], in0=x_ap[:, c * 512:(c + 1) * 512],
                       in1=ps[:, :], op=ALU.add, r=[xtok, ("pf", c)], **(dict(w=[x1t]) if c == 0 else dict(pw=[x1t])))

        def S2a(t):
            x1_ap, x1t = x1[t % 2]
            r_ap, rtok = rs[t % 2]
            self.rstd_of(x1_ap, x1t, D, sqs, sqst, r_ap, rtok)
            self.act(xn, x1_ap, AF.Copy, r=[x1t, rtok], w=[xnt], scale=r_ap[:, 0:1])

        def S2b_T(t):
            self.transpose8(xn, xnt, xnT, xnTt, 1)

        def S2b_M(t):
            x1_ap, x1t = x1[t % 2]
            p_ap, ptok = pt_[t % 2]
            for c in range(2):
                ps = self.pf[2 + c]
                for k in range(8):
                    self.mm(ps[:, :], xnT[:, k, :], Wg[:, k, c * 512:(c + 1) * 512], k == 0, k == 7, r=[xnTt, Wgt],
                            w=[("pf", 2 + c)] if k == 0 else ())
                last = self.S.q["pe"][-1]
                self.S.tw[("pf", 2 + c)] = {last.stream: last}
                self.act(gate[:, c * 512:(c + 1) * 512], ps[:, :], AF.Sigmoid, r=[("pf", 2 + c)],
                         **(dict(w=[gatet]) if c == 0 else dict(pw=[gatet])))
            self.E("dve", "tensor_copy", out=pbf, in_=p_ap, r=[ptok], w=[pbft])
            self.transpose8(pbf, pbft, pT, pTt, 1, nblk=2)
            x2_ap, x2tok = x2[t % 2]
            for c in range(2):
                ps = self.pf[4 + c]
                for k in range(2):
                    self.mm(ps[:, :], pT[:, k, :], Wp[:, k, c * 512:(c + 1) * 512], k == 0, k == 1, r=[pTt, Wpt],
                            w=[("pf", 4 + c)] if k == 0 else ())
                last = self.S.q["pe"][-1]
                self.S.tw[("pf", 4 + c)] = {last.stream: last}
                sl = slice(c * 512, (c + 1) * 512)
                self.E("dve", "tensor_tensor", out=x2_ap[:, sl], in0=gate[:, sl], in1=ps[:, :], op=ALU.mult,
                       r=[gatet, ("pf", 4 + c)], **(dict(w=[x2tok]) if c == 0 else dict(pw=[x2tok])))
                self.E("dve", "tensor_tensor", out=x2_ap[:, sl], in0=x2_ap[:, sl], in1=x1_ap[:, sl], op=ALU.add,
                       r=[x2tok, x1t], pw=[x2tok])
            self.dma(self.y[t * 128:(t + 1) * 128, :], x2_ap, f"oyst{t % 2}", r=[x2tok], w=[("y", t)], q="pool")

        S1_T(0)
        S1_M(0)
        for t in range(NT):
            S2a(t)
            if t + 1 < NT:
                S1_T(t + 1)
            S2b_T(t)
            if t + 1 < NT:
                S1_M(t + 1)
            S2b_M(t)


def rope_tables(T):
    NT = T // 128
    half = 32
    inv_freq = (10000.0 ** (-np.arange(half, dtype=np.float32) / half)).astype(np.float32)
    pos = (np.arange(NT)[None, :] * 128 + np.arange(128)[:, None]).astype(np.float32)
    ang = pos[:, :, None] * inv_freq[None, None, :]
    cs = np.stack([np.cos(ang), np.sin(ang)], 0).astype(np.float32)
    NCT = max(1, (T // 16) // 128)
    posc = (16.0 * (np.arange(NCT)[None, :] * 128 + np.arange(128)[:, None]) + 31.0).astype(np.float32)
    angc = posc[:, :, None] * inv_freq[None, None, :]
    csc = np.stack([np.cos(angc), np.sin(angc)], 0).astype(np.float32)
    return cs, csc


_CACHE = {}


def run(inputs, T, layers, n_cores):
    key = (T, tuple(layers))
    if key not in _CACHE:
        _CACHE[key] = Builder(T, list(layers)).build()
    nc = _CACHE[key]
    cs, csc = rope_tables(T)
    shared = {k: np.ascontiguousarray(v, dtype=np.float32) for k, v in inputs.items() if k not in ("x", "p")}
    shared["rope_cs"] = cs
    shared["rope_cs_c"] = csc
    in_maps = []
    for b in range(n_cores):
        m = dict(shared)
        m["x"] = np.ascontiguousarray(inputs["x"][b], dtype=np.float32)
        m["p"] = np.ascontiguousarray(inputs["p"][:, b], dtype=np.float32)
        in_maps.append(m)
    res = run_bass_kernel_spmd(nc, in_maps, core_ids=list(range(n_cores)))
    return np.stack([r["y"] for r in res.results], 0).astype(np.float32)


def kernel(**inputs):
    return run(inputs, 4096, [0, 1, 2, 3], 8)
```

```python
import math
from contextlib import ExitStack

import numpy as np
import concourse.bass as bass
import concourse.mybir as mybir
from concourse.bass_utils import run_bass_kernel_spmd

F32 = mybir.dt.float32
BF16 = mybir.dt.bfloat16
I32 = mybir.dt.int32
AF = mybir.ActivationFunctionType
ALU = mybir.AluOpType
AX = mybir.AxisListType

D = 1024
H = 16
HD = 64
PLE = 256
EPS = 1e-6
NSA_IN = 3632
NEGB = 30000.0


class Op:
    __slots__ = ("eng", "fn", "deps", "needed", "sigval", "stream", "is_dma", "idx")


class Sched:
    ENGS = ("pe", "act", "dve", "pool", "sp")

    def __init__(self):
        self.q = {e: [] for e in self.ENGS}
        self.tw = {}
        self.tr = {}
        self.dma_cnt = {}
        self.nops = 0
        self.last = {}
        self.pending = {}

    def barrier(self):
        for e in self.ENGS:
            self.pending[e] = list(self.last.values())

    def op(self, eng, fn, reads=(), writes=(), pwrites=(), dma=None):
        o = Op()
        o.eng = eng
        o.fn = fn
        o.is_dma = dma is not None
        o.stream = ("dma", dma) if dma is not None else ("eng", eng)
        o.needed = o.is_dma
        o.sigval = None
        o.idx = self.nops
        self.nops += 1
        deps = {}

        def add(d):
            if (not o.is_dma) and (not d.is_dma) and d.eng == "pe" and eng == "pe":
                return
            k = d.stream
            if k not in deps or deps[k].idx < d.idx:
                deps[k] = d

        for d in self.pending.pop(eng, ()):
            add(d)
        for t in reads:
            for d in self.tw.get(t, {}).values():
                add(d)
        for t in writes:
            for d in self.tw.get(t, {}).values():
                add(d)
            for d in self.tr.get(t, {}).values():
                add(d)
        for t in pwrites:
            for d in self.tr.get(t, {}).values():
                add(d)
        o.deps = list(deps.values())
        for d in o.deps:
            d.needed = True
        for t in reads:
            self.tr.setdefault(t, {})[o.stream] = o
        for t in writes:
            self.tw[t] = {o.stream: o}
            self.tr[t] = {}
        for t in pwrites:
            self.tw.setdefault(t, {})[o.stream] = o
        if o.is_dma:
            c = self.dma_cnt.get(dma, 0) + 16
            self.dma_cnt[dma] = c
            o.sigval = c
        self.last[o.stream] = o
        self.q[eng].append(o)
        return o

    def emit(self, nc, stack):
        for e in self.ENGS:
            c = 0
            for o in self.q[e]:
                if not o.is_dma and o.needed:
                    c += 1
                    o.sigval = c
        sems = {}
        for e in self.ENGS:
            sems[("eng", e)] = stack.enter_context(nc.semaphore("s_" + e))
        for d in self.dma_cnt:
            sems[("dma", d)] = stack.enter_context(nc.semaphore("d_" + str(d)))
        block = stack.enter_context(nc.Block())
        q = self.q

        def run(ename, eng):
            seen = {}
            for o in q[ename]:
                for d in o.deps:
                    v = d.sigval
                    if seen.get(d.stream, 0) >= v:
                        continue
                    seen[d.stream] = v
                    eng.wait_ge(sems[d.stream], v)
                ins = o.fn(eng)
                if o.needed:
                    ins.then_inc(sems[o.stream], 16 if o.is_dma else 1)

        @block.tensor
        def _(eng):
            run("pe", eng)

        @block.scalar
        def _(eng):
            run("act", eng)

        @block.vector
        def _(eng):
            run("dve", eng)

        @block.gpsimd
        def _(eng):
            run("pool", eng)

        @block.sync
        def _(eng):
            run("sp", eng)
            for d, c in self.dma_cnt.items():
                eng.wait_ge(sems[("dma", d)], c)


class Builder:
    def __init__(self, T, layers):
        self.T = T
        self.NT = T // 128
        self.NC = T // 16
        self.NCT = max(1, self.NC // 128)
        self.layers = layers
        self.nc = bass.Bass("TRN2", target_bir_lowering=False)
        self.S = Sched()
        self.uid = 0

    def E(self, eng, meth, *a, r=(), w=(), pw=(), **kw):
        self.S.op(eng, lambda e: getattr(e, meth)(*a, **kw), reads=r, writes=w, pwrites=pw)

    def dma(self, out, in_, sem, r=(), w=(), pw=(), q="sp", **kw):
        self.S.op(q, lambda e: e.dma_start(out=out, in_=in_, **kw), reads=r, writes=w, pwrites=pw, dma=sem)

    def mm(self, out, lhsT, rhs, start, stop, r=(), w=()):
        self.S.op("pe", lambda e: e.matmul(out, lhsT=lhsT, rhs=rhs, start=start, stop=stop), reads=r, writes=w)

    def tr(self, out, in_, r=(), w=()):
        ident = self.ident
        self.S.op("pe", lambda e: e.transpose(out=out, in_=in_, identity=ident[:]), reads=tuple(r) + ("ident",), writes=w)

    def act(self, out, in_, func, r=(), w=(), pw=(), **kw):
        self.S.op("act", lambda e: e.activation(out=out, in_=in_, func=func, **kw), reads=r, writes=w, pwrites=pw)

    def persist(self, name, shape, dt):
        return self.st.enter_context(self.nc.sbuf_tensor(name, shape, dt))

    def phase(self):
        self.S.barrier()
        self.aoff = 0
        self.pid = getattr(self, "pid", 0) + 1

    def ph(self, name, shape, dt):
        esz = 4 if dt in (F32, I32) else 2
        n = 1
        for s_ in shape[1:]:
            n *= s_
        nbytes = (n * esz + 63) // 64 * 64
        a = self.aoff
        self.aoff += nbytes
        assert self.aoff <= self.ARENA_BYTES, (name, self.aoff)
        v = self.arena[:, a // 2:(a + n * esz) // 2]
        if esz == 4:
            v = v.bitcast(dt)
        if len(shape) == 3:
            v = v.rearrange("p (a b) -> p a b", b=shape[2])
        elif len(shape) == 4:
            v = v.rearrange("p (a b c) -> p a b c", b=shape[2], c=shape[3])
        return v, ("ph", self.pid, name)

    def build(self):
        nc, T, NT = self.nc, self.T, self.NT
        dt = nc.dram_tensor
        self.x_in = dt("x", [T, D], F32, kind="ExternalInput").ap()
        self.p_in = dt("p", [4, T, PLE], F32, kind="ExternalInput").ap()
        self.norm_gain = dt("norm_gain", [4, D], F32, kind="ExternalInput").ap()
        self.moba_w_in = dt("moba_w_in", [2, D, 4 * D], F32, kind="ExternalInput").ap()
        self.moba_q_gain = dt("moba_q_gain", [2, HD], F32, kind="ExternalInput").ap()
        self.moba_k_gain = dt("moba_k_gain", [2, HD], F32, kind="ExternalInput").ap()
        self.moba_w_out = dt("moba_w_out", [2, D, D], F32, kind="ExternalInput").ap()
        self.nsa_w_in = dt("nsa_w_in", [2, D, NSA_IN], F32, kind="ExternalInput").ap()
        self.nsa_q_gain = dt("nsa_q_gain", [2, HD], F32, kind="ExternalInput").ap()
        self.nsa_k_gain = dt("nsa_k_gain", [2, 3, HD], F32, kind="ExternalInput").ap()
        self.nsa_cmp_pe = dt("nsa_cmp_pe", [2, 2, 32, HD], F32, kind="ExternalInput").ap()
        self.nsa_cmp_w1 = dt("nsa_cmp_w1", [2, 2, 2048, 256], F32, kind="ExternalInput").ap()
        self.nsa_cmp_w2 = dt("nsa_cmp_w2", [2, 2, 256, HD], F32, kind="ExternalInput").ap()
        self.nsa_w_out = dt("nsa_w_out", [2, D, D], F32, kind="ExternalInput").ap()
        self.ple_w_proj = dt("ple_w_proj", [4, PLE, D], F32, kind="ExternalInput").ap()
        self.ple_gate_gain = dt("ple_gate_gain", [4, D], F32, kind="ExternalInput").ap()
        self.ple_w_gate = dt("ple_w_gate", [4, D, D], F32, kind="ExternalInput").ap()
        self.rope_cs = dt("rope_cs", [2, 128, NT, 32], F32, kind="ExternalInput").ap()
        self.rope_cs_c = dt("rope_cs_c", [2, 128, self.NCT, 32], F32, kind="ExternalInput").ap()
        self.y = dt("y", [T, D], F32, kind="ExternalOutput").ap()
        self.qT_d = dt("qT_d", [D, T], BF16).ap()
        self.kT_d = dt("kT_d", [D, T], BF16).ap()
        self.v_d = dt("v_d", [T, D], BF16).ap()
        self.zs_d = dt("zs_d", [T, D], BF16).ap()

        with ExitStack() as st:
            self.st = st
            self.ARENA_BYTES = 100 * 1024
            self.arena = self.persist("arena", [128, self.ARENA_BYTES // 2], BF16)
            self.big = self.persist("big", [128, 32768], BF16)
            self.ident = self.persist("ident", [128, 128], BF16)
            self.tri = self.persist("tri", [128, 128], BF16)
            self.atri = self.persist("atri", [128, 128], BF16)
            self.cmpm = self.persist("cmpm", [128, 17, 128], BF16)
            self.cs = self.persist("cs", [128, 2, NT, 32], F32)
            self.csc = self.persist("csc", [128, 2, self.NCT, 32], F32)
            self.g_sb = self.persist("g_sb", [128, NT, 48], F32)
            self.msel = self.persist("msel", [128, NT, 64], F32)
            self.kcTa = self.persist("kcTa", [128, 4, self.NC], BF16)
            self.vca = self.persist("vca", [128, 4, self.NCT, 128], BF16)
            self.gq = self.persist("gq", [128, HD], F32)
            self.gk = self.persist("gk", [128, 3, HD], F32)
            self.gcol = self.persist("gcol", [128, 8], F32)
            self.gcol2 = self.persist("gcol2", [128, 8], F32)
            self.pf = [st.enter_context(nc.psum_tensor(f"pf{i}", [128, 512], F32)) for i in range(6)]
            self.pb = [st.enter_context(nc.psum_tensor(f"pb{i}", [128, 1024], BF16)) for i in range(2)]
            self.consts()
            first = True
            for li in self.layers:
                src = self.x_in if first else self.y
                first = False
                if li % 2 == 0:
                    self.phase_P(li, src, moba=True)
                    self.phase_A_moba(li)
                else:
                    import os
                    stop = os.environ.get("NSA_STOP", "")
                    self.stop = stop
                    self.phase_P(li, src, moba=False)
                    if stop == "P":
                        break
                    self.phase_C(li)
                    if stop == "C":
                        break
                    self.phase_A_nsa(li)
                    if stop in ("cmp", "sel", "A"):
                        break
                self.phase_O(li, src)
            self.S.emit(nc, st)
        return nc

    def consts(self):
        self.phase()
        NT = self.NT
        tf, tft = self.ph("c_tf", [128, 128], F32)
        self.E("pool", "memset", tf, 1.0, w=[tft])
        self.E("pool", "affine_select", out=tf, in_=tf, pattern=[[1, 128]], compare_op=ALU.is_equal, fill=0.0,
               base=0, channel_multiplier=-1, r=[tft], w=[tft])
        self.E("dve", "tensor_copy", out=self.ident[:], in_=tf, r=[tft], w=["ident"])
        tf2, tf2t = self.ph("c_tf2", [128, 128], F32)
        self.E("pool", "memset", tf2, 0.0, w=[tf2t])
        self.E("pool", "affine_select", out=tf2, in_=tf2, pattern=[[1, 128]], compare_op=ALU.is_ge, fill=-NEGB,
               base=0, channel_multiplier=-1, r=[tf2t], w=[tf2t])
        self.E("dve", "tensor_copy", out=self.tri[:], in_=tf2, r=[tf2t], w=["tri"])
        tf3, tf3t = self.ph("c_tf3", [128, 128], F32)
        self.E("pool", "memset", tf3, 0.0, w=[tf3t])
        self.E("pool", "affine_select", out=tf3, in_=tf3, pattern=[[-1, 128]], compare_op=ALU.is_ge, fill=-NEGB,
               base=-1, channel_multiplier=1, r=[tf3t], w=[tf3t])
        self.E("dve", "tensor_copy", out=self.atri[:], in_=tf3, r=[tf3t], w=["atri"])
        tf4, tf4t = self.ph("c_tf4", [128, 17, 128], F32)
        self.E("pool", "memset", tf4, 0.0, w=[tf4t])
        self.E("pool", "affine_select", out=tf4, in_=tf4, pattern=[[128, 17], [1, 128]], compare_op=ALU.is_ge,
               fill=-NEGB, base=-31, channel_multiplier=-16, r=[tf4t], w=[tf4t])
        self.E("dve", "tensor_copy", out=self.cmpm[:], in_=tf4, r=[tf4t], w=["cmpm"])
        self.dma(self.cs[:, 0], self.rope_cs[0], "c_cs", w=["cs"])
        self.dma(self.cs[:, 1], self.rope_cs[1], "c_cs2", pw=["cs"])
        self.dma(self.csc[:, 0], self.rope_cs_c[0], "c_csc", w=["csc"])
        self.dma(self.csc[:, 1], self.rope_cs_c[1], "c_csc2", pw=["csc"])
        m = self.msel
        self.E("pool", "memset", m[:], 0.0, w=["msel"])
        for half in range(2):
            mh = m[half * 64:(half + 1) * 64]
            self.E("pool", "affine_select", out=mh, in_=mh, pattern=[[2, NT], [-1, 64]], compare_op=ALU.is_ge,
                   fill=1.0e4, base=half - 2, channel_multiplier=0, r=["msel"], w=["msel"])
            self.E("pool", "affine_select", out=mh, in_=mh, pattern=[[2, NT], [-1, 64]], compare_op=ALU.is_ge,
                   fill=-1.0e4, base=half, channel_multiplier=0, r=["msel"], w=["msel"])
        self.E("pool", "memset", m[:, :, 0:1], 1.0e4, r=["msel"], w=["msel"])
        ov, ovt = self.ph("c_ov", [128, self.NCT, 64], F32)
        self.E("pool", "memset", ov, 1.0, w=[ovt])
        for ct in range(self.NCT):
            o1 = ov[:, ct, :]
            self.E("pool", "affine_select", out=o1, in_=o1, pattern=[[4, 64]], compare_op=ALU.is_ge, fill=0.0,
                   base=3 - 128 * ct, channel_multiplier=-1, r=[ovt], w=[ovt])
            self.E("pool", "affine_select", out=o1, in_=o1, pattern=[[-4, 64]], compare_op=ALU.is_ge, fill=0.0,
                   base=1 + 128 * ct, channel_multiplier=1, r=[ovt], w=[ovt])
        self.E("dve", "memset", self.vca[:], 1.0, w=["vca"])
        for g in range(4):
            for ct in range(self.NCT):
                self.E("dve", "tensor_copy", out=self.vca[:, g, ct, 65:128], in_=ov[:, ct, 0:63], r=[ovt, "vca"], w=["vca"])
        self.E("dve", "memset", self.kcTa[:], 0.0, w=["kcTa"])

    def load_gcol(self, dst, tok, src_row):
        self.dma(dst[:], src_row.rearrange("(k p) -> p k", p=128), "gcol_" + tok, w=[tok],
                 allow_slow_non_contiguous=True)

    def load_w(self, dst3, dtok, w_ap, ncols, gcol, gtok, nk=8):
        if getattr(self, "_stg_pid", None) != self.pid:
            self._stg = [self.ph(f"wstg{i}", [128, 1024], F32) for i in range(2)]
            self._stg_pid = self.pid
        stg = self._stg
        i = 0
        first = True
        for k in range(nk):
            for c0 in range(0, ncols, 1024):
                cw = min(1024, ncols - c0)
                s_ap, s_tok = stg[i % 2]
                self.dma(s_ap[:, 0:cw], w_ap[k * 128:(k + 1) * 128, c0:c0 + cw], f"wst{i % 2}", w=[s_tok])
                kw = dict(r=[s_tok] + ([gtok] if gcol is not None else []))
                if first:
                    kw["w"] = [dtok]
                    first = False
                else:
                    kw["pw"] = [dtok]
                if gcol is not None:
                    self.E("dve", "tensor_scalar", out=dst3[:, k, c0:c0 + cw], in0=s_ap[:, 0:cw], scalar1=gcol[:, k:k + 1],
                           scalar2=None, op0=ALU.mult, **kw)
                else:
                    self.E("dve", "tensor_copy", out=dst3[:, k, c0:c0 + cw], in_=s_ap[:, 0:cw], **kw)
                i += 1

    def rstd_of(self, x_ap, xtok, n, scratch, stok, out_col, otok):
        self.act(scratch, x_ap, AF.Square, r=[xtok], w=[stok, otok], accum_out=out_col)
        self.act(out_col, out_col, AF.Sqrt, r=[otok], w=[otok], scale=1.0 / n, bias=EPS)
        self.E("dve", "reciprocal", out=out_col, in_=out_col, r=[otok], w=[otok])

    def transpose8(self, src_bf, stok, dstT, dtok, pbi, nblk=8):
        pbt = self.pb[pbi]
        ptok = ("pb", pbi)
        for k in range(nblk):
            kw = dict(w=[ptok]) if k == 0 else dict(w=())
            if k == 0:
                self.tr(pbt[:, k * 128:(k + 1) * 128], src_bf[:, k * 128:(k + 1) * 128], r=[stok], w=[ptok])
            else:
                self.S.op("pe", (lambda e, k=k: e.transpose(out=pbt[:, k * 128:(k + 1) * 128], in_=src_bf[:, k * 128:(k + 1) * 128],
                                                            identity=self.ident[:])), reads=[stok, "ident"], pwrites=[ptok])
        self.act(dstT.rearrange("p a b -> p (a b)") if len(dstT.shape) == 3 else dstT, pbt[:, 0:nblk * 128], AF.Copy, r=[ptok], w=[dtok])

    def norm_rope(self, src, stok, nh, gain_ap, gtok, cos_ap, sin_ap, cstok, out_bf, otok, tmp):
        (sq, sqt), (ss, sst), (A, At), (Bt_, Btt), (t13, t13t), (tsw, tswt) = tmp
        W = nh * 64
        v3 = lambda ap: ap[:, 0:W].rearrange("p (h d) -> p h d", d=64)
        v4 = lambda ap: ap[:, 0:W].rearrange("p (h a d) -> p h a d", a=2, d=32)
        self.act(sq[:, 0:W], src, AF.Square, r=[stok], w=[sqt])
        self.E("dve", "tensor_reduce", out=ss[:, 0:nh], in_=v3(sq), axis=AX.X, op=ALU.add, r=[sqt], w=[sst])
        self.act(ss[:, 0:nh], ss[:, 0:nh], AF.Sqrt, r=[sst], w=[sst], scale=1.0 / 64, bias=EPS)
        self.E("dve", "reciprocal", out=ss[:, 0:nh], in_=ss[:, 0:nh], r=[sst], w=[sst])
        self.E("dve", "tensor_tensor", out=v3(A), in0=src.rearrange("p (h d) -> p h d", d=64),
               in1=ss[:, 0:nh].unsqueeze(2).to_broadcast([128, nh, 64]), op=ALU.mult, r=[stok, sst], w=[At])
        self.E("dve", "tensor_tensor", out=v3(Bt_), in0=v3(A), in1=gain_ap.unsqueeze(1).to_broadcast([128, nh, 64]),
               op=ALU.mult, r=[At, gtok], w=[Btt])
        cosb = cos_ap.unsqueeze(1).unsqueeze(1).to_broadcast([128, nh, 2, 32])
        sinb = sin_ap.unsqueeze(1).to_broadcast([128, nh, 32])
        self.E("dve", "tensor_tensor", out=v4(t13), in0=v4(Bt_), in1=cosb, op=ALU.mult, r=[Btt, cstok], w=[t13t])
        self.E("dve", "tensor_tensor", out=v4(tsw)[:, :, 0, :], in0=v4(Bt_)[:, :, 1, :], in1=sinb, op=ALU.mult,
               r=[Btt, cstok], w=[tswt])
        self.E("dve", "tensor_tensor", out=v4(tsw)[:, :, 1, :], in0=v4(Bt_)[:, :, 0, :], in1=sinb, op=ALU.mult,
               r=[Btt, cstok, tswt], w=[tswt])
        self.E("dve", "tensor_tensor", out=v4(out_bf)[:, :, 0, :], in0=v4(t13)[:, :, 0, :], in1=v4(tsw)[:, :, 0, :],
               op=ALU.subtract, r=[t13t, tswt], w=[otok])
        self.E("dve", "tensor_tensor", out=v4(out_bf)[:, :, 1, :], in0=v4(t13)[:, :, 1, :], in1=v4(tsw)[:, :, 1, :],
               op=ALU.add, r=[t13t, tswt, otok], w=[otok])

    def phase_P(self, li, src, moba):
        self.phase()
        NT = self.NT
        j = li // 2
        ncols = 4 * D if moba else NSA_IN
        w_ap = (self.moba_w_in if moba else self.nsa_w_in)[j]
        W3 = self.big[:, 0:8 * 4096].rearrange("p (k n) -> p k n", n=4096)
        self.load_gcol(self.gcol, "gcol", self.norm_gain[li])
        self.load_w(W3, "big", w_ap, ncols, self.gcol, "gcol")
        qg = (self.moba_q_gain if moba else self.nsa_q_gain)[j]
        self.dma(self.gq[:], qg.partition_broadcast(128), "gq", w=["gq"])
        if moba:
            self.dma(self.gk[:, 0, :], self.moba_k_gain[j].partition_broadcast(128), "gk", w=["gk"])
        else:
            for b_ in range(3):
                self.dma(self.gk[:, b_, :], self.nsa_k_gain[j, b_].partition_broadcast(128), f"gk{b_}",
                         **(dict(w=["gk"]) if b_ == 0 else dict(pw=["gk"])))
        xt = [self.ph(f"xt{i}", [128, D], F32) for i in range(2)]
        sqs = self.ph("sqs", [128, D], F32)
        rs = [self.ph(f"rs{i}", [128, 1], F32) for i in range(2)]
        hb = [self.ph(f"hb{i}", [128, D], BF16) for i in range(2)]
        hT = [self.ph(f"hT{i}", [128, 8, 128], BF16) for i in range(2)]
        b1, b1t = self.ph("nr_b1", [128, 2048], F32)
        b2, b2t = self.ph("nr_b2", [128, 2048], F32)
        b3, b3t = self.ph("nr_b3", [128, 2048], F32)
        ss, sst = self.ph("nr_ss", [128, 32], F32)
        qbf = [self.ph(f"qbf{i}", [128, 2048], BF16) for i in range(2)]
        qTt = [self.ph(f"qTt{i}", [128, 16, 128], BF16) for i in range(2)]
        vbf = [self.ph(f"vbf{i}", [128, D], BF16) for i in range(2)]
        zbf = [self.ph(f"zbf{i}", [128, D], BF16) for i in range(2)]
        rawbf = [self.ph(f"rawbf{i}", [128, 512], BF16) for i in range(2)]
        if moba:
            chunks = [(c * 512, 512) for c in range(8)]
            HT_ = 32
            srcs = [(0, 0, 8, self.gq[:], "gq"), (1, 0, 8, self.gq[:], "gq"), (2, 0, 8, self.gk[:, 0, :], "gk"),
                    (3, 0, 8, self.gk[:, 0, :], "gk")]
            qk_chunks = [(0, 0), (1, 1), (2, 2), (3, 3)]
            dests = [(self.qT_d, jb * 128) for jb in range(8)] + [(self.kT_d, jb * 128) for jb in range(8)]
        else:
            chunks = [(0, 512), (512, 512), (1024, 512), (1536, 512), (2048, 512), (2560, 48), (2608, 512), (3120, 512)]
            HT_ = 24
            srcs = [(0, 0, 8, self.gq[:], "gq"), (1, 0, 8, self.gq[:], "gq"), (2, 0, 4, self.gk[:, 1, :], "gk"),
                    (3, 0, 4, self.gk[:, 2, :], "gk")]
            qk_chunks = [(0, 0), (1, 1), (3, 2), (4, 3), (2, 4)]
            dests = ([(self.qT_d, jb * 128) for jb in range(8)] + [(self.kT_d, 0), (self.kT_d, 128), (self.kT_d, 256),
                     (self.kT_d, 384)] + [(self.kT_d, 512 + jb * 128) for jb in range(4)])
        WQ = HT_ * 64

        def pre_a(t):
            x_ap, xtok = xt[t % 2]
            self.dma(x_ap, src[t * 128:(t + 1) * 128, :], f"xld{t % 2}", r=[("y", t)], w=[xtok])
            r_ap, rtok = rs[t % 2]
            self.rstd_of(x_ap, xtok, D, sqs[0], sqs[1], r_ap, rtok)
            h_ap, htok = hb[t % 2]
            self.act(h_ap, x_ap, AF.Copy, r=[xtok, rtok], w=[htok], scale=r_ap[:, 0:1])

        def pre_b(t):
            h_ap, htok = hb[t % 2]
            hT_ap, hTtok = hT[t % 2]
            self.transpose8(h_ap, htok, hT_ap, hTtok, 0)

        def mm_chunk(t, ci, bank):
            c0, cw = chunks[ci]
            hT_ap, hTtok = hT[t % 2]
            ps = self.pf[bank]
            pstok = ("pf", bank)
            for k in range(8):
                self.mm(ps[:, 0:cw], hT_ap[:, k, :], W3[:, k, c0:c0 + cw], k == 0, k == 7, r=[hTtok, "big"],
                        w=[pstok] if k == 0 else ())
            last = self.S.q["pe"][-1]
            self.S.tw[pstok] = {last.stream: last}

        def chain(t):
            q_ap, qbt = qbf[t % 2]
            v3 = lambda ap, c0, w: ap[:, c0:c0 + w].rearrange("p (h d) -> p h d", d=64)
            v4 = lambda ap, c0, w: ap[:, c0:c0 + w].rearrange("p (h a d) -> p h a d", a=2, d=32)
            c = 0
            for n_, (bank, col0, nh, g_ap, g_tok) in enumerate(srcs):
                w_ = nh * 64
                self.act(b1[:, c:c + w_], self.pf[bank][:, col0:col0 + w_], AF.Square, r=[("pf", bank)],
                         **(dict(w=[b1t]) if n_ == 0 else dict(pw=[b1t])))
                c += w_
            self.E("dve", "tensor_reduce", out=ss[:, 0:HT_], in_=v3(b1, 0, WQ), axis=AX.X, op=ALU.add, r=[b1t], w=[sst])
            self.act(ss[:, 0:HT_], ss[:, 0:HT_], AF.Sqrt, r=[sst], w=[sst], scale=1.0 / 64, bias=EPS)
            self.E("dve", "reciprocal", out=ss[:, 0:HT_], in_=ss[:, 0:HT_], r=[sst], w=[sst])
            c = 0
            h0 = 0
            for n_, (bank, col0, nh, g_ap, g_tok) in enumerate(srcs):
                w_ = nh * 64
                self.E("dve", "tensor_tensor", out=v3(b2, c, w_),
                       in0=self.pf[bank][:, col0:col0 + w_].rearrange("p (h d) -> p h d", d=64),
                       in1=ss[:, h0:h0 + nh].unsqueeze(2).to_broadcast([128, nh, 64]), op=ALU.mult,
                       r=[("pf", bank), sst], **(dict(w=[b2t]) if n_ == 0 else dict(pw=[b2t])))
                c += w_
                h0 += nh
            c = 0
            first = True
            i_ = 0
            while i_ < len(srcs):
                g_ap, g_tok = srcs[i_][3], srcs[i_][4]
                nh = srcs[i_][2]
                k_ = i_ + 1
                while k_ < len(srcs) and srcs[k_][3] is g_ap:
                    nh += srcs[k_][2]
                    k_ += 1
                w_ = nh * 64
                self.E("dve", "tensor_tensor", out=v3(b1, c, w_), in0=v3(b2, c, w_),
                       in1=g_ap.unsqueeze(1).to_broadcast([128, nh, 64]), op=ALU.mult, r=[b2t, g_tok],
                       **(dict(w=[b1t]) if first else dict(pw=[b1t])))
                first = False
                c += w_
                i_ = k_
            import os
            if "norope" in os.environ.get("PDBG", ""):
                return
            cos_ap = self.cs[:, 0, t, :]
            sin_ap = self.cs[:, 1, t, :]
            cosb = cos_ap.unsqueeze(1).unsqueeze(1).to_broadcast([128, HT_, 2, 32])
            sinb = sin_ap.unsqueeze(1).to_broadcast([128, HT_, 32])
            self.E("dve", "tensor_tensor", out=v4(b2, 0, WQ), in0=v4(b1, 0, WQ), in1=cosb, op=ALU.mult, r=[b1t, "cs"], w=[b2t])
            self.E("dve", "tensor_tensor", out=v4(b3, 0, WQ)[:, :, 0, :], in0=v4(b1, 0, WQ)[:, :, 1, :], in1=sinb, op=ALU.mult,
                   r=[b1t, "cs"], w=[b3t])
            self.E("dve", "tensor_tensor", out=v4(b3, 0, WQ)[:, :, 1, :], in0=v4(b1, 0, WQ)[:, :, 0, :], in1=sinb, op=ALU.mult,
                   r=[b1t, "cs"], pw=[b3t])
            self.E("dve", "tensor_tensor", out=v4(q_ap, 0, WQ)[:, :, 0, :], in0=v4(b2, 0, WQ)[:, :, 0, :],
                   in1=v4(b3, 0, WQ)[:, :, 0, :], op=ALU.subtract, r=[b2t, b3t], w=[qbt])
            self.E("dve", "tensor_tensor", out=v4(q_ap, 0, WQ)[:, :, 1, :], in0=v4(b2, 0, WQ)[:, :, 1, :],
                   in1=v4(b3, 0, WQ)[:, :, 1, :], op=ALU.add, r=[b2t, b3t], pw=[qbt])
            import os
            if not moba and "noraw" not in os.environ.get("PDBG", ""):
                self.E("dve", "tensor_copy", out=rawbf[t % 2][0], in_=self.pf[4][:, 0:512], r=[("pf", 4)], w=[rawbf[t % 2][1]])

        def tq(t):
            q_ap, qbt = qbf[t % 2]
            qt_, qtt = qTt[t % 2]
            for half in range(2):
                pbt = self.pb[1]
                ptok = ("pb", 1)
                for k in range(8):
                    jb = half * 8 + k
                    if (not moba) and jb >= 12:
                        s_ap, s_tok = rawbf[t % 2]
                        s_in = s_ap[:, (jb - 12) * 128:(jb - 11) * 128]
                    else:
                        s_tok = qbt
                        s_in = q_ap[:, jb * 128:(jb + 1) * 128]
                    if k == 0:
                        self.tr(pbt[:, 0:128], s_in, r=[s_tok], w=[ptok])
                    else:
                        self.S.op("pe", (lambda e, k=k, s_in=s_in: e.transpose(out=pbt[:, k * 128:(k + 1) * 128], in_=s_in,
                                                                              identity=self.ident[:])),
                                  reads=[s_tok, "ident"], pwrites=[ptok])
                self.act(qt_[:, half * 8:(half + 1) * 8, :].rearrange("p a b -> p (a b)"), pbt[:, 0:1024], AF.Copy, r=[ptok],
                         **(dict(w=[qtt]) if half == 0 else dict(pw=[qtt])))
            jb = 0
            while jb < 16:
                dram, row0 = dests[jb]
                k_ = jb + 1
                while k_ < 16 and dests[k_][0] is dram and dests[k_][1] == row0 + (k_ - jb) * 128:
                    k_ += 1
                nb = k_ - jb
                self.dma(dram[row0:row0 + nb * 128, t * 128:(t + 1) * 128].rearrange("(j p) t -> p j t", p=128),
                         qt_[:, jb:k_, :], f"qst{t % 2}", r=[qtt], pw=[("kTq_d",)], q="pool")
                jb = k_

        def mm_vz(t):
            v_ap, vtok = vbf[t % 2]
            z_ap, ztok = zbf[t % 2]
            if moba:
                for ci in (4, 5):
                    bank = ci
                    mm_chunk(t, ci, bank)
                    self.act(v_ap[:, (ci - 4) * 512:(ci - 3) * 512], self.pf[bank][:, 0:512], AF.Copy, r=[("pf", bank)],
                             **(dict(w=[vtok]) if ci == 4 else dict(pw=[vtok])))
                self.dma(self.v_d[t * 128:(t + 1) * 128, :], v_ap, f"vst{t % 2}", r=[vtok], pw=[("v_d",)], q="pool")
            else:
                self.act(v_ap[:, 0:256], self.pf[2][:, 256:512], AF.Copy, r=[("pf", 2)], w=[vtok])
                self.act(v_ap[:, 256:512], self.pf[3][:, 256:512], AF.Copy, r=[("pf", 3)], pw=[vtok])
                self.dma(self.v_d[t * 128:(t + 1) * 128, 0:512], v_ap[:, 0:512], f"vst{t % 2}", r=[vtok], pw=[("v_d",)], q="pool")
                mm_chunk(t, 5, 5)
                self.act(self.g_sb[:, t, :], self.pf[5][:, 0:48], AF.Sigmoid, r=[("pf", 5)], pw=["g_sb"])
            for ci in (6, 7):
                bank = ci - 2
                mm_chunk(t, ci, bank)
                self.act(z_ap[:, (ci - 6) * 512:(ci - 5) * 512], self.pf[bank][:, 0:512], AF.Silu, r=[("pf", bank)],
                         **(dict(w=[ztok]) if ci == 6 else dict(pw=[ztok])))
            self.dma(self.zs_d[t * 128:(t + 1) * 128, :], z_ap, f"zst{t % 2}", r=[ztok], pw=[("zs_d",)], q="pool")

        import os
        dbg = os.environ.get("PDBG", "")
        pre_a(0)
        pre_b(0)
        for t in range(NT):
            for (ci, bank) in qk_chunks:
                mm_chunk(t, ci, bank)
            if t + 1 < NT:
                pre_a(t + 1)
            if "nochain" not in dbg:
                chain(t)
            if t + 1 < NT:
                pre_b(t + 1)
            if "novz" not in dbg:
                mm_vz(t)
            if t >= 1 and "notq" not in dbg:
                tq(t - 1)
        if "notq" not in dbg:
            tq(NT - 1)

    def attention(self, qTa, qtok, kTa, ktok, Va, vtok, vw, pairs_fn, fin, pts, tag, side=None, side_delay=0):
        NT = self.NT
        qtoks = list(qtok) if isinstance(qtok, list) else [qtok]
        allp = []
        for qt in range(NT):
            prs = pairs_fn(qt)
            for j_, (kt, bias) in enumerate(prs):
                allp.append((qt, kt, bias, j_ == 0, j_ == len(prs) - 1))
        units = []
        cur = []
        for p_ in allp:
            qts = []
            for x_ in cur:
                if x_[0] not in qts:
                    qts.append(x_[0])
            if len(cur) == 4 or (p_[0] not in qts and len(qts) == 2):
                units.append(cur)
                cur = []
            cur.append(p_)
        if cur:
            units.append(cur)
        n = len(units)
        NBK = len(pts)
        SK = NBK - 1
        banks = [0, 1, 2, 3][:NBK]
        u0 = getattr(self, "_u", 0)

        def qk(i):
            grp = units[i]
            si = banks[(u0 + i) % NBK]
            ps = self.pf[si]
            pstok = ("pf", si)
            firstw = True
            for jx, (qt, kt, bias, _, _) in enumerate(grp):
                o_ap = ps[:, jx * 128:(jx + 1) * 128]
                self.mm(o_ap, kTa[:, kt * 128:(kt + 1) * 128], qTa[:, qt * 128:(qt + 1) * 128], True, bias is None,
                        r=[ktok] + qtoks, w=[pstok] if firstw else ())
                firstw = False
                if bias is not None:
                    self.mm(o_ap, self.ident[:], bias, False, True, r=["ident", "tri", "atri", "cmpm"])
            last = self.S.q["pe"][-1]
            self.S.tw[pstok] = {last.stream: last}

        def pv(i):
            grp = units[i]
            si = banks[(u0 + i) % NBK]
            ps = self.pf[si]
            pstok = ("pf", si)
            pt_ap, pttok = pts[(u0 + i) % NBK]
            nn = len(grp) * 128
            self.act(pt_ap[:, 0:nn], ps[:, 0:nn], AF.Exp, r=[pstok], w=[pttok], scale=0.125)
            done = []
            for jx, (qt, kt, bias, isf, isl) in enumerate(grp):
                po = self.pf[4 + qt % 2]
                potok = ("pf", 4 + qt % 2)
                self.mm(po[:, 0:vw], pt_ap[:, jx * 128:(jx + 1) * 128], Va[:, kt, 0:vw], isf, isl,
                        r=[pttok, vtok], w=[potok] if isf else ())
                last = self.S.q["pe"][-1]
                self.S.tw[potok] = {last.stream: last}
                if isl:
                    done.append(qt)
            for qt in done:
                fin(qt, self.pf[4 + qt % 2], ("pf", 4 + qt % 2))

        for i in range(n + SK):
            if i < n:
                qk(i)
            if i >= SK:
                pv(i - SK)
                if side is not None and i - SK >= side_delay:
                    next(side, None)
        self._u = u0 + n

    def phase_A_moba(self, li):
        self.phase()
        T, NT = self.T, self.NT
        NB = T // 256
        o_all = self.big[:, 0:NT * D].rearrange("p (t f) -> p t f", f=D)
        qTa = [self.ph(f"qTa{i}", [128, T], BF16) for i in range(2)]
        kTa = [self.ph(f"kTa{i}", [128, T], BF16) for i in range(2)]
        Va = [self.ph(f"Va{i}", [128, NT, 65], BF16) for i in range(2)]
        pts = [self.ph(f"pt{i}", [128, 512], BF16) for i in range(4)]
        kmf, kmft = self.ph("kmf", [128, 16], F32)
        kmT, kmTt = self.ph("kmT", [128, 16], BF16)
        gt_, gtt = self.ph("gate", [128, 16], F32)
        m8, m8t = self.ph("m8", [128, 8], F32)
        bq, bqt = self.ph("bq", [128, 128], BF16)
        rz, rzt = self.ph("rz", [128, 1], F32)
        self.E("dve", "memset", kmT, 0.0, w=[kmTt])
        self.E("dve", "memset", bq, 0.0, w=[bqt])
        for i in range(2):
            q_ap, qtok = qTa[i]
            k_ap, ktok = kTa[i]
            v_ap, vtok = Va[i]
            self.E("pool", "memset", q_ap, 0.0, w=[qtok])
            self.E("pool", "memset", k_ap[0:64, :], 0.0, w=[ktok])
            ke = k_ap[64:128, :]
            self.E("pool", "memset", ke, NEGB, r=[ktok], w=[ktok])
            self.E("pool", "affine_select", out=ke, in_=ke, pattern=[[1, T]], compare_op=ALU.is_ge, fill=0.0, base=0,
                   channel_multiplier=-256, r=[ktok], w=[ktok])
            self.E("pool", "affine_select", out=ke, in_=ke, pattern=[[-1, T]], compare_op=ALU.is_ge, fill=0.0, base=255,
                   channel_multiplier=256, r=[ktok], w=[ktok])
            self.E("dve", "memset", v_ap, 1.0, w=[vtok])

        def load_head(h):
            i = h % 2
            q_ap, qtok = qTa[i]
            k_ap, ktok = kTa[i]
            v_ap, vtok = Va[i]
            self.dma(q_ap[0:64, :], self.qT_d[h * 64:(h + 1) * 64, :], f"aq{i}", r=[("kTq_d",)], w=[qtok])
            self.dma(k_ap[0:64, :], self.kT_d[h * 64:(h + 1) * 64, :], f"ak{i}", r=[("kTq_d",)], w=[ktok])
            for c in range(0, NT, 8):
                n = min(8, NT - c)
                self.dma(v_ap[:, c:c + n, 0:64],
                         self.v_d[c * 128:(c + n) * 128, h * 64:(h + 1) * 64].rearrange("(t p) d -> p t d", p=128),
                         f"av{i}", r=[("v_d",)], **(dict(w=[vtok]) if c == 0 else dict(pw=[vtok])))

        def gate_steps(h):
            i = h % 2
            q_ap, qtok = qTa[i]
            k_ap, ktok = kTa[i]
            self.E("dve", "tensor_reduce", out=kmf[0:64, 0:NB], in_=k_ap[0:64, :].rearrange("p (n k) -> p n k", k=256),
                   axis=AX.X, op=ALU.add, r=[ktok], w=[kmft])
            self.E("dve", "tensor_scalar", out=kmT[0:64, 0:NB], in0=kmf[0:64, 0:NB], scalar1=1.0 / 256, scalar2=None,
                   op0=ALU.mult, r=[kmft], w=[kmTt])
            self.E("dve", "memset", gt_, -1.0e30, w=[gtt])
            yield
            for qt in range(NT):
                b = qt // 2
                if b <= 3:
                    continue
                pg = self.pb[0][:, 0:32].bitcast(F32)
                self.mm(pg[:, 0:16], q_ap[:, qt * 128:(qt + 1) * 128], kmT, True, True, r=[qtok, kmTt], w=[("pb", 0)])
                self.E("dve", "tensor_copy", out=gt_[:, 0:b], in_=pg[:, 0:b], r=[("pb", 0), gtt], w=[gtt])
                self.E("dve", "max", out=m8, in_=gt_, r=[gtt], w=[m8t])
                self.E("dve", "tensor_scalar", out=bq[:, 64:80], in0=gt_, scalar1=m8[:, 2:3], scalar2=1.0, op0=ALU.is_ge,
                       op1=ALU.subtract, r=[gtt, m8t, bqt], w=[bqt])
                self.E("dve", "memset", bq[:, 64 + b:65 + b], 0.0, r=[bqt], w=[bqt])
                yield
                yield
                pbt = self.pb[1]
                self.tr(pbt[:, 0:128], bq, r=[bqt], w=[("pb", 1)])
                self.E("dve", "tensor_copy", out=q_ap[64:128, qt * 128:(qt + 1) * 128], in_=pbt[64:128, 0:128],
                       r=[("pb", 1), qtok], w=[qtok])
                yield

        load_head(0)
        for _ in gate_steps(0):
            pass
        for h in range(H):
            side = None
            if h + 1 < H:
                load_head(h + 1)
                side = gate_steps(h + 1)
            i = h % 2
            q_ap, qtok = qTa[i]
            k_ap, ktok = kTa[i]
            v_ap, vtok = Va[i]

            def pairs_fn(qt):
                return [(kt, (self.tri[:] if kt == qt else None)) for kt in range(qt + 1)]

            def fin(qt, po, potok, h=h):
                self.E("dve", "reciprocal", out=rz, in_=po[:, 64:65], r=[potok], w=[rzt])
                self.E("dve", "tensor_scalar", out=o_all[:, qt, h * 64:(h + 1) * 64], in0=po[:, 0:64], scalar1=rz[:, 0:1],
                       scalar2=None, op0=ALU.mult, r=[potok, rzt], pw=["big"])

            self.attention(q_ap, qtok, k_ap, ktok, v_ap, vtok, 65, pairs_fn, fin, pts, f"m{h}", side=side, side_delay=min(24, self.NT))
            if side is not None:
                for _ in side:
                    pass

    def phase_C(self, li):
        self.phase()
        T, NT = self.T, self.NT
        j = li // 2
        xkp, xkpt = self.ph("xkp", [128, T + 16], BF16)
        xk16, xk16t = self.ph("xk16", [128, 2, 16, T // 16], BF16)
        W1d, W1t = self.ph("W1d", [128, 32, 256], BF16)
        w1s = [self.ph(f"w1s{i}", [128, 8, 256], F32) for i in range(2)]
        W2b, W2t = self.ph("W2b", [128, 2, 64], BF16)
        w2s, w2st = self.ph("w2s", [128, 2, 64], F32)
        peT, peTt = self.ph("peT", [128, 32], BF16)
        pes, pest = self.ph("pes", [128, 32], F32)
        bh, bht = self.ph("bh", [128, 2], F32)
        NC, NCT = self.NC, self.NCT
        xx, xxt = self.ph("xx", [128, NC], F32)
        x2, x2t = self.ph("x2", [128, NC], F32)
        sg, sgt = self.ph("sg", [128, NC], F32)
        gl, glt = self.ph("gl", [128, 2, NC], BF16)
        kcb, kcbt = self.ph("kcb", [128, 128], BF16)
        tmp = [self.ph(n, [128, 64], F32) for n in ("c_sq", "c_ss", "c_A", "c_B", "c_t13", "c_tsw")]
        self.E("pool", "memset", xkp, 0.0, w=[xkpt])
        self.E("pool", "memset", W1d, 0.0, w=[W1t])
        self.E("pool", "memset", peT, 0.0, w=[peTt])
        self.E("pool", "memset", kcb, 0.0, w=[kcbt])
        for kv in range(2):
            w1 = self.nsa_cmp_w1[j, kv].rearrange("(l d) j -> d l j", d=64)
            for c in range(4):
                s_ap, s_tok = w1s[c % 2]
                self.dma(s_ap[0:64], w1[:, c * 8:(c + 1) * 8, :], f"w1s{c % 2}", w=[s_tok])
                self.E("dve", "tensor_copy", out=W1d[0:64, c * 8:(c + 1) * 8, :], in_=s_ap[0:64], r=[s_tok],
                       **(dict(w=[W1t]) if c == 0 else dict(pw=[W1t])))
            self.dma(w2s, self.nsa_cmp_w2[j, kv].rearrange("(c p) d -> p c d", p=128), "w2s", w=[w2st])
            self.E("dve", "tensor_copy", out=W2b, in_=w2s, r=[w2st], w=[W2t])
            for q4 in range(4):
                self.dma(pes[0:64, q4 * 8:(q4 + 1) * 8], self.nsa_cmp_pe[j, kv, q4 * 8:(q4 + 1) * 8, :].rearrange("l d -> d l"),
                         "pes", **(dict(w=[pest]) if q4 == 0 else dict(pw=[pest])), allow_slow_non_contiguous=True)
            self.E("dve", "tensor_copy", out=peT[0:64, :], in_=pes[0:64, :], r=[pest], w=[peTt])
            pbias = self.pf[3]
            for jc in range(2):
                for l in range(32):
                    self.mm(pbias[:, jc:jc + 1], W1d[:, l, jc * 128:(jc + 1) * 128], peT[:, l:l + 1], l == 0, l == 31,
                            r=[W1t, peTt], w=[("pf", 3)] if (l == 0 and jc == 0) else ())
            last = self.S.q["pe"][-1]
            self.S.tw[("pf", 3)] = {last.stream: last}
            self.E("dve", "tensor_copy", out=bh, in_=pbias[:, 0:2], r=[("pf", 3)], w=[bht])
            for g in range(4):
                row0 = 512 + kv * 256 + g * 64
                self.dma(xkp[0:64, 0:T], self.kT_d[row0:row0 + 64, :], "xkp", r=[("kTq_d",)], w=[xkpt])
                self.E("dve", "tensor_copy", out=xk16[:, 0], in_=xkp[:, 0:T].rearrange("p (m r) -> p r m", r=16), r=[xkpt], w=[xk16t])
                self.E("dve", "tensor_copy", out=xk16[:, 1], in_=xkp[:, 16:T + 16].rearrange("p (m r) -> p r m", r=16), r=[xkpt],
                       pw=[xk16t])
                for jc in range(2):
                    ph_ = self.pf[jc]
                    for l in range(32):
                        self.mm(ph_[:, 0:NC], W1d[:, l, jc * 128:(jc + 1) * 128], xk16[:, l // 16, l % 16, :], l == 0, l == 31,
                                r=[W1t, xk16t], w=[("pf", jc)] if l == 0 else ())
                    last = self.S.q["pe"][-1]
                    self.S.tw[("pf", jc)] = {last.stream: last}
                    self.act(xx, ph_[:, 0:NC], AF.Identity, r=[("pf", jc), bht], w=[xxt], bias=bh[:, jc:jc + 1])
                    self.E("dve", "tensor_tensor", out=x2, in0=xx, in1=xx, op=ALU.mult, r=[xxt], w=[x2t])
                    self.E("dve", "tensor_scalar", out=x2, in0=x2, scalar1=0.044715, scalar2=1.0, op0=ALU.mult, op1=ALU.add,
                           r=[x2t], w=[x2t])
                    self.E("dve", "tensor_tensor", out=x2, in0=x2, in1=xx, op=ALU.mult, r=[x2t, xxt], w=[x2t])
                    self.act(sg, x2, AF.Sigmoid, r=[x2t], w=[sgt], scale=1.5957691216057308)
                    self.E("dve", "tensor_tensor", out=gl[:, jc, :], in0=xx, in1=sg, op=ALU.mult, r=[xxt, sgt],
                           **(dict(w=[glt]) if jc == 0 else dict(pw=[glt])))
                for ct in range(NCT):
                    pk = self.pf[2]
                    for jc in range(2):
                        self.mm(pk[:, 0:64], gl[:, jc, ct * 128:(ct + 1) * 128], W2b[:, jc, :], jc == 0, jc == 1,
                                r=[glt, W2t], w=[("pf", 2)] if jc == 0 else ())
                    last = self.S.q["pe"][-1]
                    self.S.tw[("pf", 2)] = {last.stream: last}
                    if kv == 0:
                        self.norm_rope(pk[:, 0:64], ("pf", 2), 1, self.gk[:, 0, :], "gk", self.csc[:, 0, ct, :],
                                       self.csc[:, 1, ct, :], "csc", kcb, kcbt, tmp)
                        pbt = self.pb[1]
                        self.tr(pbt[:, 0:128], kcb, r=[kcbt], w=[("pb", 1)])
                        self.E("dve", "tensor_copy", out=self.kcTa[:, g, ct * 128:(ct + 1) * 128], in_=pbt[:, 0:128],
                               r=[("pb", 1)], pw=["kcTa"])
                    else:
                        self.act(self.vca[:, g, ct, 0:64], pk[:, 0:64], AF.Copy, r=[("pf", 2)], pw=["vca"])

    def phase_A_nsa(self, li):
        self.phase()
        T, NT = self.T, self.NT
        o_all = self.big[:, 0:NT * D].rearrange("p (t f) -> p t f", f=D)
        qTa = [self.ph(f"nqTa{i}", [128, T], BF16) for i in range(4)]
        ksTa, kst = self.ph("ksTa", [128, T], BF16)
        kwTa, kwt = self.ph("kwTa", [128, T], BF16)
        Vs, Vst = self.ph("Vs", [128, NT, 65], BF16)
        Vw, Vwt = self.ph("Vw", [128, NT, 65], BF16)
        pts = [self.ph(f"npt{i}", [128, 512], BF16) for i in range(4)]
        oacc, oacct = self.ph("oacc", [128, NT, 64], F32)
        imp, impt = self.ph("imp", [128, NT, 64], F32)
        sc, sct = self.ph("sc", [128, 64], F32)
        sc2, sc2t = self.ph("sc2", [128, 64], F32)
        m8a, m8at = self.ph("m8a", [128, 8], F32)
        m8b, m8bt = self.ph("m8b", [128, 8], F32)
        bs2 = [self.ph(f"bs{i}", [128, 128], BF16) for i in range(2)]
        rz, rzt = self.ph("nrz", [128, 1], F32)
        cf, cft = self.ph("ncf", [128, 1], F32)
        stage, staget = self.ph("cstage", [128, NT, 128], F32)
        rzb, rzbt = self.ph("rzb", [128, NT], F32)
        cfb, cfbt = self.ph("cfb", [128, NT], F32)
        for hl, (q_ap, qtok) in enumerate(qTa):
            self.E("pool", "memset", q_ap, 0.0, w=[qtok, ("qhi", hl)])
        self.E("pool", "memset", kwTa, 0.0, w=[kwt])
        self.E("pool", "memset", ksTa[0:64, :], 0.0, w=[kst])
        ke = ksTa[64:128, :]
        self.E("pool", "memset", ke, NEGB, r=[kst], w=[kst])
        self.E("pool", "affine_select", out=ke, in_=ke, pattern=[[1, T]], compare_op=ALU.is_ge, fill=0.0, base=0,
               channel_multiplier=-64, r=[kst], w=[kst])
        self.E("pool", "affine_select", out=ke, in_=ke, pattern=[[-1, T]], compare_op=ALU.is_ge, fill=0.0, base=63,
               channel_multiplier=64, r=[kst], w=[kst])
        self.E("dve", "memset", Vs, 1.0, w=[Vst])
        self.E("dve", "memset", Vw, 1.0, w=[Vwt])
        for (b_ap, b_tok) in bs2:
            self.E("dve", "memset", b_ap, 0.0, w=[b_tok])
        self.E("dve", "memset", imp, 0.0, w=[impt])

        for g in range(4):
            for hl in range(4):
                h = g * 4 + hl
                q_ap, qtok = qTa[hl]
                self.dma(q_ap[0:64, :], self.qT_d[h * 64:(h + 1) * 64, :], f"nq{hl}", r=[("kTq_d",)], w=[qtok])
            self.dma(ksTa[0:64, :], self.kT_d[g * 64:(g + 1) * 64, :], "nks", r=[("kTq_d",)], w=[kst])
            self.dma(kwTa[0:64, :], self.kT_d[256 + g * 64:256 + (g + 1) * 64, :], "nkw", r=[("kTq_d",)], w=[kwt])
            for (V_, Vt_, c0, nm) in ((Vs, Vst, g * 64, "nvs"), (Vw, Vwt, 256 + g * 64, "nvw")):
                for c in range(0, NT, 8):
                    n = min(8, NT - c)
                    self.dma(V_[:, c:c + n, 0:64],
                             self.v_d[c * 128:(c + n) * 128, c0:c0 + 64].rearrange("(t p) d -> p t d", p=128),
                             nm, r=[("v_d",)], **(dict(w=[Vt_]) if c == 0 else dict(pw=[Vt_])))

            def cmp_pairs(qt):
                out = []
                for kt in range(self.NCT):
                    dl = qt - 16 * kt
                    if dl < 0:
                        continue
                    out.append((kt, self.cmpm[:, dl, :] if dl <= 16 else None))
                return out

            def sel_A(qt):
                scq = imp[:, qt, :]
                self.E("dve", "max", out=m8a, in_=scq, r=[impt], w=[m8at])
                self.E("dve", "tensor_scalar", out=sc2, in0=scq, scalar1=m8a[:, 7:8], scalar2=-6.0e4, op0=ALU.is_ge, op1=ALU.mult,
                       r=[impt, m8at], w=[sc2t])
                self.E("dve", "tensor_tensor", out=sc2, in0=sc2, in1=scq, op=ALU.add, r=[sc2t, impt], w=[sc2t])
                self.E("dve", "max", out=m8b, in_=sc2, r=[sc2t], w=[m8bt])
                bs, bst = bs2[qt % 2]
                self.E("dve", "tensor_scalar", out=bs[:, 64:128], in0=scq, scalar1=m8b[:, 7:8], scalar2=1.0, op0=ALU.is_ge,
                       op1=ALU.subtract, r=[impt, m8bt, bst], w=[bst])

            def sel_B(qt):
                pbt = self.pb[1]
                bs, bst = bs2[qt % 2]
                self.tr(pbt[:, 0:128], bs, r=[bst], w=[("pb", 1)])
                for hl2 in range(4):
                    q2, _ = qTa[hl2]
                    self.E("dve", "tensor_copy", out=q2[64:128, qt * 128:(qt + 1) * 128], in_=pbt[64:128, 0:128],
                           r=[("pb", 1)], pw=[("qhi", hl2)])

            def sel_steps():
                sel_A(0)
                yield
                for qt in range(NT):
                    if qt + 1 < NT:
                        sel_A(qt + 1)
                    sel_B(qt)
                    yield

            for hl in range(4):
                h = g * 4 + hl
                q_ap, qtok = qTa[hl]

                def fin_c(qt, po, potok):
                    self.E("dve", "tensor_copy", out=stage[:, qt, :], in_=po[:, 0:128], r=[potok], pw=[staget])

                self.attention(q_ap, qtok, self.kcTa[:, g, :], "kcTa", self.vca[:, g], "vca", 128, cmp_pairs, fin_c, pts,
                               f"c{h}")
                zc = stage[:, :, 64]
                self.E("dve", "tensor_scalar", out=rzb, in0=zc, scalar1=1.0e-30, scalar2=None, op0=ALU.max, r=[staget], w=[rzbt])
                self.E("dve", "reciprocal", out=rzb, in_=rzb, r=[rzbt], w=[rzbt])
                self.E("dve", "tensor_tensor", out=cfb, in0=rzb, in1=self.g_sb[:, :, h * 3], op=ALU.mult, r=[rzbt, "g_sb"],
                       w=[cfbt])
                self.E("dve", "tensor_tensor", out=o_all[:, :, h * 64:(h + 1) * 64], in0=stage[:, :, 0:64],
                       in1=cfb.unsqueeze(2).to_broadcast([128, NT, 64]), op=ALU.mult, r=[staget, cfbt], pw=["big"])
                rzb3 = rzb.unsqueeze(2).to_broadcast([128, NT, 63])
                if hl == 0:
                    self.E("dve", "memset", imp[:, :, 63:64], 0.0, w=[impt])
                    self.E("dve", "tensor_tensor", out=imp[:, :, 0:63], in0=stage[:, :, 65:128], in1=rzb3, op=ALU.mult,
                           r=[staget, rzbt], pw=[impt])
                else:
                    self.E("dve", "tensor_tensor", out=stage[:, :, 65:128], in0=stage[:, :, 65:128], in1=rzb3, op=ALU.mult,
                           r=[staget, rzbt], pw=[staget])
                    self.E("dve", "tensor_tensor", out=imp[:, :, 0:63], in0=imp[:, :, 0:63], in1=stage[:, :, 65:128], op=ALU.add,
                           r=[staget, impt], pw=[impt])
            self.E("dve", "tensor_tensor", out=imp[:, :, :], in0=imp[:, :, :], in1=self.msel[:, :, :], op=ALU.add,
                   r=[impt, "msel"], w=[impt])

            def sel_pairs(qt):
                return [(kt, (self.tri[:] if kt == qt else None)) for kt in range(qt + 1)]

            def win_pairs(qt):
                out = []
                for kt in range(max(0, qt - 4), qt + 1):
                    b = self.tri[:] if kt == qt else (self.atri[:] if kt == qt - 4 else None)
                    out.append((kt, b))
                return out

            for hl in range(4):
                h = g * 4 + hl
                q_ap, qtok = qTa[hl]

                def fin_w(qt, po, potok, h=h):
                    self.E("dve", "reciprocal", out=rz, in_=po[:, 64:65], r=[potok], w=[rzt])
                    self.E("dve", "tensor_tensor", out=cf, in0=rz, in1=self.g_sb[:, qt, h * 3 + 2:h * 3 + 3], op=ALU.mult,
                           r=[rzt, "g_sb"], w=[cft])
                    self.E("dve", "scalar_tensor_tensor", out=oacc[:, qt, :], in0=po[:, 0:64], scalar=cf[:, 0:1],
                           in1=o_all[:, qt, h * 64:(h + 1) * 64], op0=ALU.mult, op1=ALU.add, r=[potok, cft, "big"],
                           pw=[oacct])

                def fin_s(qt, po, potok, h=h):
                    self.E("dve", "reciprocal", out=rz, in_=po[:, 64:65], r=[potok], w=[rzt])
                    self.E("dve", "tensor_tensor", out=cf, in0=rz, in1=self.g_sb[:, qt, h * 3 + 1:h * 3 + 2], op=ALU.mult,
                           r=[rzt, "g_sb"], w=[cft])
                    self.E("dve", "scalar_tensor_tensor", out=o_all[:, qt, h * 64:(h + 1) * 64], in0=po[:, 0:64],
                           scalar=cf[:, 0:1], in1=oacc[:, qt, :], op0=ALU.mult, op1=ALU.add, r=[potok, cft, oacct],
                           pw=["big"])

                side = sel_steps() if hl == 0 else None
                self.attention(q_ap, qtok, kwTa, kwt, Vw, Vwt, 65, win_pairs, fin_w, pts, f"w{h}", side=side)
                if side is not None:
                    for _ in side:
                        pass
                self.attention(q_ap, [qtok, ("qhi", hl)], ksTa, kst, Vs, Vst, 65, sel_pairs, fin_s, pts, f"s{h}")

    def phase_O(self, li, src):
        self.phase()
        NT = self.NT
        j = li // 2
        moba = (li % 2 == 0)
        o_all = self.big[:, 0:NT * D].rearrange("p (t f) -> p t f", f=D)
        Wo, Wot = self.ph("Wo", [128, 8, D], BF16)
        Wg, Wgt = self.ph("Wg", [128, 8, D], BF16)
        Wp, Wpt = self.ph("Wp", [128, 2, D], BF16)
        self.load_w(Wo, Wot, (self.moba_w_out if moba else self.nsa_w_out)[j], D, None, None)
        self.load_gcol(self.gcol2, "gcol2", self.ple_gate_gain[li])
        self.load_w(Wg, Wgt, self.ple_w_gate[li], D, self.gcol2, "gcol2")
        self.load_w(Wp, Wpt, self.ple_w_proj[li], D, None, None, nk=2)
        xt = [self.ph(f"oxt{i}", [128, D], F32) for i in range(2)]
        zt = [self.ph(f"ozt{i}", [128, D], BF16) for i in range(2)]
        pt_ = [self.ph(f"opt{i}", [128, PLE], F32) for i in range(2)]
        og = [self.ph(f"og{i}", [128, D], BF16) for i in range(2)]
        ogT = [self.ph(f"ogT{i}", [128, 8, 128], BF16) for i in range(2)]
        x1 = [self.ph(f"x1_{i}", [128, D], F32) for i in range(2)]
        sqs, sqst = self.ph("osq", [128, D], F32)
        rs = [self.ph(f"ors{i}", [128, 1], F32) for i in range(2)]
        xn, xnt = self.ph("xn", [128, D], BF16)
        xnT, xnTt = self.ph("xnT", [128, 8, 128], BF16)
        gate, gatet = self.ph("gatef", [128, D], F32)
        pbf, pbft = self.ph("pbf", [128, PLE], BF16)
        pT, pTt = self.ph("pT", [128, 2, 128], BF16)
        x2 = [self.ph(f"x2_{i}", [128, D], F32) for i in range(2)]

        def S1_T(t):
            x_ap, xtok = xt[t % 2]
            z_ap, ztok = zt[t % 2]
            p_ap, ptok = pt_[t % 2]
            og_ap, ogt = og[t % 2]
            ogT_ap, ogTt = ogT[t % 2]
            self.dma(x_ap, src[t * 128:(t + 1) * 128, :], f"oxl{t % 2}", r=[("y", t)], w=[xtok])
            self.dma(z_ap, self.zs_d[t * 128:(t + 1) * 128, :], f"ozl{t % 2}", r=[("zs_d",)], w=[ztok])
            self.dma(p_ap, self.p_in[li, t * 128:(t + 1) * 128, :], f"opl{t % 2}", w=[ptok])
            self.E("dve", "tensor_tensor", out=og_ap, in0=o_all[:, t, :], in1=z_ap, op=ALU.mult, r=["big", ztok], w=[ogt])
            self.transpose8(og_ap, ogt, ogT_ap, ogTt, 0)

        def S1_M(t):
            x_ap, xtok = xt[t % 2]
            ogT_ap, ogTt = ogT[t % 2]
            x1_ap, x1t = x1[t % 2]
            for c in range(2):
                ps = self.pf[c]
                for k in range(8):
                    self.mm(ps[:, :], ogT_ap[:, k, :], Wo[:, k, c * 512:(c + 1) * 512], k == 0, k == 7, r=[ogTt, Wot],
                            w=[("pf", c)] if k == 0 else ())
                last = self.S.q["pe"][-1]
                self.S.tw[("pf", c)] = {last.stream: last}
                self.E("dve", "tensor_tensor", out=x1_ap[:, c * 512:(c + 1) * 512], in0=x_ap[:, c * 512:(c + 1) * 512],
                       in1=ps[:, :], op=ALU.add, r=[xtok, ("pf", c)], **(dict(w=[x1t]) if c == 0 else dict(pw=[x1t])))

        def S2a(t):
            x1_ap, x1t = x1[t % 2]
            r_ap, rtok = rs[t % 2]
            self.rstd_of(x1_ap, x1t, D, sqs, sqst, r_ap, rtok)
            self.act(xn, x1_ap, AF.Copy, r=[x1t, rtok], w=[xnt], scale=r_ap[:, 0:1])

        def S2b_T(t):
            self.transpose8(xn, xnt, xnT, xnTt, 1)

        def S2b_M(t):
            x1_ap, x1t = x1[t % 2]
            p_ap, ptok = pt_[t % 2]
            for c in range(2):
                ps = self.pf[2 + c]
                for k in range(8):
                    self.mm(ps[:, :], xnT[:, k, :], Wg[:, k, c * 512:(c + 1) * 512], k == 0, k == 7, r=[xnTt, Wgt],
                            w=[("pf", 2 + c)] if k == 0 else ())
                last = self.S.q["pe"][-1]
                self.S.tw[("pf", 2 + c)] = {last.stream: last}
                self.act(gate[:, c * 512:(c + 1) * 512], ps[:, :], AF.Sigmoid, r=[("pf", 2 + c)],
                         **(dict(w=[gatet]) if c == 0 else dict(pw=[gatet])))
            self.E("dve", "tensor_copy", out=pbf, in_=p_ap, r=[ptok], w=[pbft])
            self.transpose8(pbf, pbft, pT, pTt, 1, nblk=2)
            x2_ap, x2tok = x2[t % 2]
            for c in range(2):
                ps = self.pf[4 + c]
                for k in range(2):
                    self.mm(ps[:, :], pT[:, k, :], Wp[:, k, c * 512:(c + 1) * 512], k == 0, k == 1, r=[pTt, Wpt],
                            w=[("pf", 4 + c)] if k == 0 else ())
                last = self.S.q["pe"][-1]
                self.S.tw[("pf", 4 + c)] = {last.stream: last}
                sl = slice(c * 512, (c + 1) * 512)
                self.E("dve", "tensor_tensor", out=x2_ap[:, sl], in0=gate[:, sl], in1=ps[:, :], op=ALU.mult,
                       r=[gatet, ("pf", 4 + c)], **(dict(w=[x2tok]) if c == 0 else dict(pw=[x2tok])))
                self.E("dve", "tensor_tensor", out=x2_ap[:, sl], in0=x2_ap[:, sl], in1=x1_ap[:, sl], op=ALU.add,
                       r=[x2tok, x1t], pw=[x2tok])
            self.dma(self.y[t * 128:(t + 1) * 128, :], x2_ap, f"oyst{t % 2}", r=[x2tok], w=[("y", t)], q="pool")

        S1_T(0)
        S1_M(0)
        for t in range(NT):
            S2a(t)
            if t + 1 < NT:
                S1_T(t + 1)
            S2b_T(t)
            if t + 1 < NT:
                S1_M(t + 1)
            S2b_M(t)


def rope_tables(T):
    NT = T // 128
    half = 32
    inv_freq = (10000.0 ** (-np.arange(half, dtype=np.float32) / half)).astype(np.float32)
    pos = (np.arange(NT)[None, :] * 128 + np.arange(128)[:, None]).astype(np.float32)
    ang = pos[:, :, None] * inv_freq[None, None, :]
    cs = np.stack([np.cos(ang), np.sin(ang)], 0).astype(np.float32)
    NCT = max(1, (T // 16) // 128)
    posc = (16.0 * (np.arange(NCT)[None, :] * 128 + np.arange(128)[:, None]) + 31.0).astype(np.float32)
    angc = posc[:, :, None] * inv_freq[None, None, :]
    csc = np.stack([np.cos(angc), np.sin(angc)], 0).astype(np.float32)
    return cs, csc


_CACHE = {}


def run(inputs, T, layers, n_cores):
    key = (T, tuple(layers))
    if key not in _CACHE:
        _CACHE[key] = Builder(T, list(layers)).build()
    nc = _CACHE[key]
    cs, csc = rope_tables(T)
    shared = {k: np.ascontiguousarray(v, dtype=np.float32) for k, v in inputs.items() if k not in ("x", "p")}
    shared["rope_cs"] = cs
    shared["rope_cs_c"] = csc
    in_maps = []
    for b in range(n_cores):
        m = dict(shared)
        m["x"] = np.ascontiguousarray(inputs["x"][b], dtype=np.float32)
        m["p"] = np.ascontiguousarray(inputs["p"][:, b], dtype=np.float32)
        in_maps.append(m)
    res = run_bass_kernel_spmd(nc, in_maps, core_ids=list(range(n_cores)))
    return np.stack([r["y"] for r in res.results], 0).astype(np.float32)


def kernel(**inputs):
    return run(inputs, 4096, [0, 1, 2, 3], 8)
```

```python
import math
from contextlib import ExitStack

import numpy as np
import concourse.bass as bass
import concourse.mybir as mybir
from concourse.bass_utils import run_bass_kernel_spmd

F32 = mybir.dt.float32
BF16 = mybir.dt.bfloat16
I32 = mybir.dt.int32
AF = mybir.ActivationFunctionType
ALU = mybir.AluOpType
AX = mybir.AxisListType

D = 1024
H = 16
HD = 64
PLE = 256
EPS = 1e-6
NSA_IN = 3632
NEGB = 30000.0


class Op:
    __slots__ = ("eng", "fn", "deps", "needed", "sigval", "stream", "is_dma", "idx")


class Sched:
    ENGS = ("pe", "act", "dve", "pool", "sp")

    def __init__(self):
        self.q = {e: [] for e in self.ENGS}
        self.tw = {}
        self.tr = {}
        self.dma_cnt = {}
        self.nops = 0
        self.last = {}
        self.pending = {}

    def barrier(self):
        for e in self.ENGS:
            self.pending[e] = list(self.last.values())

    def op(self, eng, fn, reads=(), writes=(), pwrites=(), dma=None):
        o = Op()
        o.eng = eng
        o.fn = fn
        o.is_dma = dma is not None
        o.stream = ("dma", dma) if dma is not None else ("eng", eng)
        o.needed = o.is_dma
        o.sigval = None
        o.idx = self.nops
        self.nops += 1
        deps = {}

        def add(d):
            if (not o.is_dma) and (not d.is_dma) and d.eng == "pe" and eng == "pe":
                return
            k = d.stream
            if k not in deps or deps[k].idx < d.idx:
                deps[k] = d

        for d in self.pending.pop(eng, ()):
            add(d)
        for t in reads:
            for d in self.tw.get(t, {}).values():
                add(d)
        for t in writes:
            for d in self.tw.get(t, {}).values():
                add(d)
            for d in self.tr.get(t, {}).values():
                add(d)
        for t in pwrites:
            for d in self.tr.get(t, {}).values():
                add(d)
        o.deps = list(deps.values())
        for d in o.deps:
            d.needed = True
        for t in reads:
            self.tr.setdefault(t, {})[o.stream] = o
        for t in writes:
            self.tw[t] = {o.stream: o}
            self.tr[t] = {}
        for t in pwrites:
            self.tw.setdefault(t, {})[o.stream] = o
        if o.is_dma:
            c = self.dma_cnt.get(dma, 0) + 16
            self.dma_cnt[dma] = c
            o.sigval = c
        self.last[o.stream] = o
        self.q[eng].append(o)
        return o

    def emit(self, nc, stack):
        for e in self.ENGS:
            c = 0
            for o in self.q[e]:
                if not o.is_dma and o.needed:
                    c += 1
                    o.sigval = c
        sems = {}
        for e in self.ENGS:
            sems[("eng", e)] = stack.enter_context(nc.semaphore("s_" + e))
        for d in self.dma_cnt:
            sems[("dma", d)] = stack.enter_context(nc.semaphore("d_" + str(d)))
        block = stack.enter_context(nc.Block())
        q = self.q

        def run(ename, eng):
            seen = {}
            for o in q[ename]:
                for d in o.deps:
                    v = d.sigval
                    if seen.get(d.stream, 0) >= v:
                        continue
                    seen[d.stream] = v
                    eng.wait_ge(sems[d.stream], v)
                ins = o.fn(eng)
                if o.needed:
                    ins.then_inc(sems[o.stream], 16 if o.is_dma else 1)

        @block.tensor
        def _(eng):
            run("pe", eng)

        @block.scalar
        def _(eng):
            run("act", eng)

        @block.vector
        def _(eng):
            run("dve", eng)

        @block.gpsimd
        def _(eng):
            run("pool", eng)

        @block.sync
        def _(eng):
            run("sp", eng)
            for d, c in self.dma_cnt.items():
                eng.wait_ge(sems[("dma", d)], c)


class Builder:
    def __init__(self, T, layers):
        self.T = T
        self.NT = T // 128
        self.NC = T // 16
        self.NCT = max(1, self.NC // 128)
        self.layers = layers
        self.nc = bass.Bass("TRN2", target_bir_lowering=False)
        self.S = Sched()
        self.uid = 0

    def E(self, eng, meth, *a, r=(), w=(), pw=(), **kw):
        self.S.op(eng, lambda e: getattr(e, meth)(*a, **kw), reads=r, writes=w, pwrites=pw)

    def dma(self, out, in_, sem, r=(), w=(), pw=(), q="sp", **kw):
        self.S.op(q, lambda e: e.dma_start(out=out, in_=in_, **kw), reads=r, writes=w, pwrites=pw, dma=sem)

    def mm(self, out, lhsT, rhs, start, stop, r=(), w=()):
        self.S.op("pe", lambda e: e.matmul(out, lhsT=lhsT, rhs=rhs, start=start, stop=stop), reads=r, writes=w)

    def tr(self, out, in_, r=(), w=()):
        ident = self.ident
        self.S.op("pe", lambda e: e.transpose(out=out, in_=in_, identity=ident[:]), reads=tuple(r) + ("ident",), writes=w)

    def act(self, out, in_, func, r=(), w=(), pw=(), **kw):
        self.S.op("act", lambda e: e.activation(out=out, in_=in_, func=func, **kw), reads=r, writes=w, pwrites=pw)

    def persist(self, name, shape, dt):
        return self.st.enter_context(self.nc.sbuf_tensor(name, shape, dt))

    def phase(self):
        self.S.barrier()
        self.aoff = 0
        self.pid = getattr(self, "pid", 0) + 1

    def ph(self, name, shape, dt):
        esz = 4 if dt in (F32, I32) else 2
        n = 1
        for s_ in shape[1:]:
            n *= s_
        nbytes = (n * esz + 63) // 64 * 64
        a = self.aoff
        self.aoff += nbytes
        assert self.aoff <= self.ARENA_BYTES, (name, self.aoff)
        v = self.arena[:, a // 2:(a + n * esz) // 2]
        if esz == 4:
            v = v.bitcast(dt)
        if len(shape) == 3:
            v = v.rearrange("p (a b) -> p a b", b=shape[2])
        elif len(shape) == 4:
            v = v.rearrange("p (a b c) -> p a b c", b=shape[2], c=shape[3])
        return v, ("ph", self.pid, name)

    def build(self):
        nc, T, NT = self.nc, self.T, self.NT
        dt = nc.dram_tensor
        self.x_in = dt("x", [T, D], F32, kind="ExternalInput").ap()
        self.p_in = dt("p", [4, T, PLE], F32, kind="ExternalInput").ap()
        self.norm_gain = dt("norm_gain", [4, D], F32, kind="ExternalInput").ap()
        self.moba_w_in = dt("moba_w_in", [2, D, 4 * D], F32, kind="ExternalInput").ap()
        self.moba_q_gain = dt("moba_q_gain", [2, HD], F32, kind="ExternalInput").ap()
        self.moba_k_gain = dt("moba_k_gain", [2, HD], F32, kind="ExternalInput").ap()
        self.moba_w_out = dt("moba_w_out", [2, D, D], F32, kind="ExternalInput").ap()
        self.nsa_w_in = dt("nsa_w_in", [2, D, NSA_IN], F32, kind="ExternalInput").ap()
        self.nsa_q_gain = dt("nsa_q_gain", [2, HD], F32, kind="ExternalInput").ap()
        self.nsa_k_gain = dt("nsa_k_gain", [2, 3, HD], F32, kind="ExternalInput").ap()
        self.nsa_cmp_pe = dt("nsa_cmp_pe", [2, 2, 32, HD], F32, kind="ExternalInput").ap()
        self.nsa_cmp_w1 = dt("nsa_cmp_w1", [2, 2, 2048, 256], F32, kind="ExternalInput").ap()
        self.nsa_cmp_w2 = dt("nsa_cmp_w2", [2, 2, 256, HD], F32, kind="ExternalInput").ap()
        self.nsa_w_out = dt("nsa_w_out", [2, D, D], F32, kind="ExternalInput").ap()
        self.ple_w_proj = dt("ple_w_proj", [4, PLE, D], F32, kind="ExternalInput").ap()
        self.ple_gate_gain = dt("ple_gate_gain", [4, D], F32, kind="ExternalInput").ap()
        self.ple_w_gate = dt("ple_w_gate", [4, D, D], F32, kind="ExternalInput").ap()
        self.rope_cs = dt("rope_cs", [2, 128, NT, 32], F32, kind="ExternalInput").ap()
        self.rope_cs_c = dt("rope_cs_c", [2, 128, self.NCT, 32], F32, kind="ExternalInput").ap()
        self.y = dt("y", [T, D], F32, kind="ExternalOutput").ap()
        self.qT_d = dt("qT_d", [D, T], BF16).ap()
        self.kT_d = dt("kT_d", [D, T], BF16).ap()
        self.v_d = dt("v_d", [T, D], BF16).ap()
        self.zs_d = dt("zs_d", [T, D], BF16).ap()

        with ExitStack() as st:
            self.st = st
            self.ARENA_BYTES = 100 * 1024
            self.arena = self.persist("arena", [128, self.ARENA_BYTES // 2], BF16)
            self.big = self.persist("big", [128, 32768], BF16)
            self.ident = self.persist("ident", [128, 128], BF16)
            self.tri = self.persist("tri", [128, 128], BF16)
            self.atri = self.persist("atri", [128, 128], BF16)
            self.cmpm = self.persist("cmpm", [128, 17, 128], BF16)
            self.cs = self.persist("cs", [128, 2, NT, 32], F32)
            self.csc = self.persist("csc", [128, 2, self.NCT, 32], F32)
            self.g_sb = self.persist("g_sb", [128, NT, 48], F32)
            self.msel = self.persist("msel", [128, NT, 64], F32)
            self.kcTa = self.persist("kcTa", [128, 4, self.NC], BF16)
            self.vca = self.persist("vca", [128, 4, self.NCT, 128], BF16)
            self.gq = self.persist("gq", [128, HD], F32)
            self.gk = self.persist("gk", [128, 3, HD], F32)
            self.gcol = self.persist("gcol", [128, 8], F32)
            self.gcol2 = self.persist("gcol2", [128, 8], F32)
            self.pf = [st.enter_context(nc.psum_tensor(f"pf{i}", [128, 512], F32)) for i in range(6)]
            self.pb = [st.enter_context(nc.psum_tensor(f"pb{i}", [128, 1024], BF16)) for i in range(2)]
            self.consts()
            first = True
            for li in self.layers:
                src = self.x_in if first else self.y
                first = False
                if li % 2 == 0:
                    self.phase_P(li, src, moba=True)
                    self.phase_A_moba(li)
                else:
                    import os
                    stop = os.environ.get("NSA_STOP", "")
                    self.stop = stop
                    self.phase_P(li, src, moba=False)
                    if stop == "P":
                        break
                    self.phase_C(li)
                    if stop == "C":
                        break
                    self.phase_A_nsa(li)
                    if stop in ("cmp", "sel", "A"):
                        break
                self.phase_O(li, src)
            self.S.emit(nc, st)
        return nc

    def consts(self):
        self.phase()
        NT = self.NT
        tf, tft = self.ph("c_tf", [128, 128], F32)
        self.E("pool", "memset", tf, 1.0, w=[tft])
        self.E("pool", "affine_select", out=tf, in_=tf, pattern=[[1, 128]], compare_op=ALU.is_equal, fill=0.0,
               base=0, channel_multiplier=-1, r=[tft], w=[tft])
        self.E("dve", "tensor_copy", out=self.ident[:], in_=tf, r=[tft], w=["ident"])
        tf2, tf2t = self.ph("c_tf2", [128, 128], F32)
        self.E("pool", "memset", tf2, 0.0, w=[tf2t])
        self.E("pool", "affine_select", out=tf2, in_=tf2, pattern=[[1, 128]], compare_op=ALU.is_ge, fill=-NEGB,
               base=0, channel_multiplier=-1, r=[tf2t], w=[tf2t])
        self.E("dve", "tensor_copy", out=self.tri[:], in_=tf2, r=[tf2t], w=["tri"])
        tf3, tf3t = self.ph("c_tf3", [128, 128], F32)
        self.E("pool", "memset", tf3, 0.0, w=[tf3t])
        self.E("pool", "affine_select", out=tf3, in_=tf3, pattern=[[-1, 128]], compare_op=ALU.is_ge, fill=-NEGB,
               base=-1, channel_multiplier=1, r=[tf3t], w=[tf3t])
        self.E("dve", "tensor_copy", out=self.atri[:], in_=tf3, r=[tf3t], w=["atri"])
        tf4, tf4t = self.ph("c_tf4", [128, 17, 128], F32)
        self.E("pool", "memset", tf4, 0.0, w=[tf4t])
        self.E("pool", "affine_select", out=tf4, in_=tf4, pattern=[[128, 17], [1, 128]], compare_op=ALU.is_ge,
               fill=-NEGB, base=-31, channel_multiplier=-16, r=[tf4t], w=[tf4t])
        self.E("dve", "tensor_copy", out=self.cmpm[:], in_=tf4, r=[tf4t], w=["cmpm"])
        self.dma(self.cs[:, 0], self.rope_cs[0], "c_cs", w=["cs"])
        self.dma(self.cs[:, 1], self.rope_cs[1], "c_cs2", pw=["cs"])
        self.dma(self.csc[:, 0], self.rope_cs_c[0], "c_csc", w=["csc"])
        self.dma(self.csc[:, 1], self.rope_cs_c[1], "c_csc2", pw=["csc"])
        m = self.msel
        self.E("pool", "memset", m[:], 0.0, w=["msel"])
        for half in range(2):
            mh = m[half * 64:(half + 1) * 64]
            self.E("pool", "affine_select", out=mh, in_=mh, pattern=[[2, NT], [-1, 64]], compare_op=ALU.is_ge,
                   fill=1.0e4, base=half - 2, channel_multiplier=0, r=["msel"], w=["msel"])
            self.E("pool", "affine_select", out=mh, in_=mh, pattern=[[2, NT], [-1, 64]], compare_op=ALU.is_ge,
                   fill=-1.0e4, base=half, channel_multiplier=0, r=["msel"], w=["msel"])
        self.E("pool", "memset", m[:, :, 0:1], 1.0e4, r=["msel"], w=["msel"])
        ov, ovt = self.ph("c_ov", [128, self.NCT, 64], F32)
        self.E("pool", "memset", ov, 1.0, w=[ovt])
        for ct in range(self.NCT):
            o1 = ov[:, ct, :]
            self.E("pool", "affine_select", out=o1, in_=o1, pattern=[[4, 64]], compare_op=ALU.is_ge, fill=0.0,
                   base=3 - 128 * ct, channel_multiplier=-1, r=[ovt], w=[ovt])
            self.E("pool", "affine_select", out=o1, in_=o1, pattern=[[-4, 64]], compare_op=ALU.is_ge, fill=0.0,
                   base=1 + 128 * ct, channel_multiplier=1, r=[ovt], w=[ovt])
        self.E("dve", "memset", self.vca[:], 1.0, w=["vca"])
        for g in range(4):
            for ct in range(self.NCT):
                self.E("dve", "tensor_copy", out=self.vca[:, g, ct, 65:128], in_=ov[:, ct, 0:63], r=[ovt, "vca"], w=["vca"])
        self.E("dve", "memset", self.kcTa[:], 0.0, w=["kcTa"])

    def load_gcol(self, dst, tok, src_row):
        self.dma(dst[:], src_row.rearrange("(k p) -> p k", p=128), "gcol_" + tok, w=[tok],
                 allow_slow_non_contiguous=True)

    def load_w(self, dst3, dtok, w_ap, ncols, gcol, gtok, nk=8):
        if getattr(self, "_stg_pid", None) != self.pid:
            self._stg = [self.ph(f"wstg{i}", [128, 512], F32) for i in range(4)]
            self._stg_pid = self.pid
        stg = self._stg
        i = 0
        first = True
        for k in range(nk):
            for c0 in range(0, ncols, 512):
                cw = min(512, ncols - c0)
                s_ap, s_tok = stg[i % 4]
                self.dma(s_ap[:, 0:cw], w_ap[k * 128:(k + 1) * 128, c0:c0 + cw], f"wst{i % 4}", w=[s_tok])
                kw = dict(r=[s_tok] + ([gtok] if gcol is not None else []))
                if first:
                    kw["w"] = [dtok]
                    first = False
                else:
                    kw["pw"] = [dtok]
                if i % 2 == 0:
                    if gcol is not None:
                        self.E("dve", "tensor_scalar", out=dst3[:, k, c0:c0 + cw], in0=s_ap[:, 0:cw],
                               scalar1=gcol[:, k:k + 1], scalar2=None, op0=ALU.mult, **kw)
                    else:
                        self.E("dve", "tensor_copy", out=dst3[:, k, c0:c0 + cw], in_=s_ap[:, 0:cw], **kw)
                else:
                    if gcol is not None:
                        self.act(dst3[:, k, c0:c0 + cw], s_ap[:, 0:cw], AF.Copy, scale=gcol[:, k:k + 1], **kw)
                    else:
                        self.act(dst3[:, k, c0:c0 + cw], s_ap[:, 0:cw], AF.Copy, **kw)
                i += 1

    def rstd_of(self, x_ap, xtok, n, scratch, stok, out_col, otok):
        self.act(scratch, x_ap, AF.Square, r=[xtok], w=[stok, otok], accum_out=out_col)
        self.act(out_col, out_col, AF.Sqrt, r=[otok], w=[otok], scale=1.0 / n, bias=EPS)
        self.E("dve", "reciprocal", out=out_col, in_=out_col, r=[otok], w=[otok])

    def transpose8(self, src_bf, stok, dstT, dtok, pbi, nblk=8):
        pbt = self.pb[pbi]
        ptok = ("pb", pbi)
        for k in range(nblk):
            kw = dict(w=[ptok]) if k == 0 else dict(w=())
            if k == 0:
                self.tr(pbt[:, k * 128:(k + 1) * 128], src_bf[:, k * 128:(k + 1) * 128], r=[stok], w=[ptok])
            else:
                self.S.op("pe", (lambda e, k=k: e.transpose(out=pbt[:, k * 128:(k + 1) * 128], in_=src_bf[:, k * 128:(k + 1) * 128],
                                                            identity=self.ident[:])), reads=[stok, "ident"], pwrites=[ptok])
        self.act(dstT.rearrange("p a b -> p (a b)") if len(dstT.shape) == 3 else dstT, pbt[:, 0:nblk * 128], AF.Copy, r=[ptok], w=[dtok])

    def norm_rope(self, src, stok, nh, gain_ap, gtok, cos_ap, sin_ap, cstok, out_bf, otok, tmp):
        (sq, sqt), (ss, sst), (A, At), (Bt_, Btt), (t13, t13t), (tsw, tswt) = tmp
        W = nh * 64
        v3 = lambda ap: ap[:, 0:W].rearrange("p (h d) -> p h d", d=64)
        v4 = lambda ap: ap[:, 0:W].rearrange("p (h a d) -> p h a d", a=2, d=32)
        self.act(sq[:, 0:W], src, AF.Square, r=[stok], w=[sqt])
        self.E("dve", "tensor_reduce", out=ss[:, 0:nh], in_=v3(sq), axis=AX.X, op=ALU.add, r=[sqt], w=[sst])
        self.act(ss[:, 0:nh], ss[:, 0:nh], AF.Sqrt, r=[sst], w=[sst], scale=1.0 / 64, bias=EPS)
        self.E("dve", "reciprocal", out=ss[:, 0:nh], in_=ss[:, 0:nh], r=[sst], w=[sst])
        self.E("dve", "tensor_tensor", out=v3(A), in0=src.rearrange("p (h d) -> p h d", d=64),
               in1=ss[:, 0:nh].unsqueeze(2).to_broadcast([128, nh, 64]), op=ALU.mult, r=[stok, sst], w=[At])
        self.E("dve", "tensor_tensor", out=v3(Bt_), in0=v3(A), in1=gain_ap.unsqueeze(1).to_broadcast([128, nh, 64]),
               op=ALU.mult, r=[At, gtok], w=[Btt])
        cosb = cos_ap.unsqueeze(1).unsqueeze(1).to_broadcast([128, nh, 2, 32])
        sinb = sin_ap.unsqueeze(1).to_broadcast([128, nh, 32])
        self.E("dve", "tensor_tensor", out=v4(t13), in0=v4(Bt_), in1=cosb, op=ALU.mult, r=[Btt, cstok], w=[t13t])
        self.E("dve", "tensor_tensor", out=v4(tsw)[:, :, 0, :], in0=v4(Bt_)[:, :, 1, :], in1=sinb, op=ALU.mult,
               r=[Btt, cstok], w=[tswt])
        self.E("dve", "tensor_tensor", out=v4(tsw)[:, :, 1, :], in0=v4(Bt_)[:, :, 0, :], in1=sinb, op=ALU.mult,
               r=[Btt, cstok, tswt], w=[tswt])
        self.E("dve", "tensor_tensor", out=v4(out_bf)[:, :, 0, :], in0=v4(t13)[:, :, 0, :], in1=v4(tsw)[:, :, 0, :],
               op=ALU.subtract, r=[t13t, tswt], w=[otok])
        self.E("dve", "tensor_tensor", out=v4(out_bf)[:, :, 1, :], in0=v4(t13)[:, :, 1, :], in1=v4(tsw)[:, :, 1, :],
               op=ALU.add, r=[t13t, tswt, otok], w=[otok])

    def phase_P(self, li, src, moba):
        self.phase()
        NT = self.NT
        j = li // 2
        ncols = 4 * D if moba else NSA_IN
        w_ap = (self.moba_w_in if moba else self.nsa_w_in)[j]
        W3 = self.big[:, 0:8 * 4096].rearrange("p (k n) -> p k n", n=4096)
        self.load_gcol(self.gcol, "gcol", self.norm_gain[li])
        self.load_w(W3, "big", w_ap, ncols, self.gcol, "gcol")
        qg = (self.moba_q_gain if moba else self.nsa_q_gain)[j]
        self.dma(self.gq[:], qg.partition_broadcast(128), "gq", w=["gq"])
        if moba:
            self.dma(self.gk[:, 0, :], self.moba_k_gain[j].partition_broadcast(128), "gk", w=["gk"])
        else:
            for b_ in range(3):
                self.dma(self.gk[:, b_, :], self.nsa_k_gain[j, b_].partition_broadcast(128), f"gk{b_}",
                         **(dict(w=["gk"]) if b_ == 0 else dict(pw=["gk"])))
        xt = [self.ph(f"xt{i}", [128, D], F32) for i in range(2)]
        sqs = self.ph("sqs", [128, D], F32)
        rs = [self.ph(f"rs{i}", [128, 1], F32) for i in range(2)]
        hb = [self.ph(f"hb{i}", [128, D], BF16) for i in range(2)]
        hT = [self.ph(f"hT{i}", [128, 8, 128], BF16) for i in range(2)]
        b1, b1t = self.ph("nr_b1", [128, 2048], F32)
        b2, b2t = self.ph("nr_b2", [128, 2048], F32)
        b3, b3t = self.ph("nr_b3", [128, 2048], F32)
        ss, sst = self.ph("nr_ss", [128, 32], F32)
        qbf = [self.ph(f"qbf{i}", [128, 2048], BF16) for i in range(2)]
        qTt = [self.ph(f"qTt{i}", [128, 16, 128], BF16) for i in range(2)]
        vbf = [self.ph(f"vbf{i}", [128, D], BF16) for i in range(2)]
        zbf = [self.ph(f"zbf{i}", [128, D], BF16) for i in range(2)]
        rawbf = [self.ph(f"rawbf{i}", [128, 512], BF16) for i in range(2)]
        if moba:
            chunks = [(c * 512, 512) for c in range(8)]
            HT_ = 32
            srcs = [(0, 0, 8, self.gq[:], "gq"), (1, 0, 8, self.gq[:], "gq"), (2, 0, 8, self.gk[:, 0, :], "gk"),
                    (3, 0, 8, self.gk[:, 0, :], "gk")]
            qk_chunks = [(0, 0), (1, 1), (2, 2), (3, 3)]
            dests = [(self.qT_d, jb * 128) for jb in range(8)] + [(self.kT_d, jb * 128) for jb in range(8)]
        else:
            chunks = [(0, 512), (512, 512), (1024, 512), (1536, 512), (2048, 512), (2560, 48), (2608, 512), (3120, 512)]
            HT_ = 24
            srcs = [(0, 0, 8, self.gq[:], "gq"), (1, 0, 8, self.gq[:], "gq"), (2, 0, 4, self.gk[:, 1, :], "gk"),
                    (3, 0, 4, self.gk[:, 2, :], "gk")]
            qk_chunks = [(0, 0), (1, 1), (3, 2), (4, 3), (2, 4)]
            dests = ([(self.qT_d, jb * 128) for jb in range(8)] + [(self.kT_d, 0), (self.kT_d, 128), (self.kT_d, 256),
                     (self.kT_d, 384)] + [(self.kT_d, 512 + jb * 128) for jb in range(4)])
        WQ = HT_ * 64

        def pre_a(t):
            x_ap, xtok = xt[t % 2]
            self.dma(x_ap, src[t * 128:(t + 1) * 128, :], f"xld{t % 2}", r=[("y", t)], w=[xtok])
            r_ap, rtok = rs[t % 2]
            self.rstd_of(x_ap, xtok, D, sqs[0], sqs[1], r_ap, rtok)
            h_ap, htok = hb[t % 2]
            self.act(h_ap, x_ap, AF.Copy, r=[xtok, rtok], w=[htok], scale=r_ap[:, 0:1])

        def pre_b(t):
            h_ap, htok = hb[t % 2]
            hT_ap, hTtok = hT[t % 2]
            self.transpose8(h_ap, htok, hT_ap, hTtok, 0)

        def mm_chunk(t, ci, bank):
            c0, cw = chunks[ci]
            hT_ap, hTtok = hT[t % 2]
            ps = self.pf[bank]
            pstok = ("pf", bank)
            for k in range(8):
                self.mm(ps[:, 0:cw], hT_ap[:, k, :], W3[:, k, c0:c0 + cw], k == 0, k == 7, r=[hTtok, "big"],
                        w=[pstok] if k == 0 else ())
            last = self.S.q["pe"][-1]
            self.S.tw[pstok] = {last.stream: last}

        def chain(t):
            q_ap, qbt = qbf[t % 2]
            v3 = lambda ap, c0, w: ap[:, c0:c0 + w].rearrange("p (h d) -> p h d", d=64)
            v4 = lambda ap, c0, w: ap[:, c0:c0 + w].rearrange("p (h a d) -> p h a d", a=2, d=32)
            c = 0
            for n_, (bank, col0, nh, g_ap, g_tok) in enumerate(srcs):
                w_ = nh * 64
                self.act(b1[:, c:c + w_], self.pf[bank][:, col0:col0 + w_], AF.Square, r=[("pf", bank)],
                         **(dict(w=[b1t]) if n_ == 0 else dict(pw=[b1t])))
                c += w_
            self.E("dve", "tensor_reduce", out=ss[:, 0:HT_], in_=v3(b1, 0, WQ), axis=AX.X, op=ALU.add, r=[b1t], w=[sst])
            self.act(ss[:, 0:HT_], ss[:, 0:HT_], AF.Sqrt, r=[sst], w=[sst], scale=1.0 / 64, bias=EPS)
            self.E("dve", "reciprocal", out=ss[:, 0:HT_], in_=ss[:, 0:HT_], r=[sst], w=[sst])
            c = 0
            h0 = 0
            for n_, (bank, col0, nh, g_ap, g_tok) in enumerate(srcs):
                w_ = nh * 64
                self.E("dve", "tensor_tensor", out=v3(b2, c, w_),
                       in0=self.pf[bank][:, col0:col0 + w_].rearrange("p (h d) -> p h d", d=64),
                       in1=ss[:, h0:h0 + nh].unsqueeze(2).to_broadcast([128, nh, 64]), op=ALU.mult,
                       r=[("pf", bank), sst], **(dict(w=[b2t]) if n_ == 0 else dict(pw=[b2t])))
                c += w_
                h0 += nh
            c = 0
            first = True
            i_ = 0
            while i_ < len(srcs):
                g_ap, g_tok = srcs[i_][3], srcs[i_][4]
                nh = srcs[i_][2]
                k_ = i_ + 1
                while k_ < len(srcs) and srcs[k_][3] is g_ap:
                    nh += srcs[k_][2]
                    k_ += 1
                w_ = nh * 64
                self.E("dve", "tensor_tensor", out=v3(b1, c, w_), in0=v3(b2, c, w_),
                       in1=g_ap.unsqueeze(1).to_broadcast([128, nh, 64]), op=ALU.mult, r=[b2t, g_tok],
                       **(dict(w=[b1t]) if first else dict(pw=[b1t])))
                first = False
                c += w_
                i_ = k_
            import os
            if "norope" in os.environ.get("PDBG", ""):
                return
            cos_ap = self.cs[:, 0, t, :]
            sin_ap = self.cs[:, 1, t, :]
            cosb = cos_ap.unsqueeze(1).unsqueeze(1).to_broadcast([128, HT_, 2, 32])
            sinb = sin_ap.unsqueeze(1).to_broadcast([128, HT_, 32])
            self.E("dve", "tensor_tensor", out=v4(b2, 0, WQ), in0=v4(b1, 0, WQ), in1=cosb, op=ALU.mult, r=[b1t, "cs"], w=[b2t])
            self.E("dve", "tensor_tensor", out=v4(b3, 0, WQ)[:, :, 0, :], in0=v4(b1, 0, WQ)[:, :, 1, :], in1=sinb, op=ALU.mult,
                   r=[b1t, "cs"], w=[b3t])
            self.E("dve", "tensor_tensor", out=v4(b3, 0, WQ)[:, :, 1, :], in0=v4(b1, 0, WQ)[:, :, 0, :], in1=sinb, op=ALU.mult,
                   r=[b1t, "cs"], pw=[b3t])
            self.E("dve", "tensor_tensor", out=v4(q_ap, 0, WQ)[:, :, 0, :], in0=v4(b2, 0, WQ)[:, :, 0, :],
                   in1=v4(b3, 0, WQ)[:, :, 0, :], op=ALU.subtract, r=[b2t, b3t], w=[qbt])
            self.E("dve", "tensor_tensor", out=v4(q_ap, 0, WQ)[:, :, 1, :], in0=v4(b2, 0, WQ)[:, :, 1, :],
                   in1=v4(b3, 0, WQ)[:, :, 1, :], op=ALU.add, r=[b2t, b3t], pw=[qbt])
            import os
            if not moba and "noraw" not in os.environ.get("PDBG", ""):
                self.E("dve", "tensor_copy", out=rawbf[t % 2][0], in_=self.pf[4][:, 0:512], r=[("pf", 4)], w=[rawbf[t % 2][1]])

        def tq(t):
            q_ap, qbt = qbf[t % 2]
            qt_, qtt = qTt[t % 2]
            for half in range(2):
                pbt = self.pb[1]
                ptok = ("pb", 1)
                for k in range(8):
                    jb = half * 8 + k
                    if (not moba) and jb >= 12:
                        s_ap, s_tok = rawbf[t % 2]
                        s_in = s_ap[:, (jb - 12) * 128:(jb - 11) * 128]
                    else:
                        s_tok = qbt
                        s_in = q_ap[:, jb * 128:(jb + 1) * 128]
                    if k == 0:
                        self.tr(pbt[:, 0:128], s_in, r=[s_tok], w=[ptok])
                    else:
                        self.S.op("pe", (lambda e, k=k, s_in=s_in: e.transpose(out=pbt[:, k * 128:(k + 1) * 128], in_=s_in,
                                                                              identity=self.ident[:])),
                                  reads=[s_tok, "ident"], pwrites=[ptok])
                self.act(qt_[:, half * 8:(half + 1) * 8, :].rearrange("p a b -> p (a b)"), pbt[:, 0:1024], AF.Copy, r=[ptok],
                         **(dict(w=[qtt]) if half == 0 else dict(pw=[qtt])))
            jb = 0
            while jb < 16:
                dram, row0 = dests[jb]
                k_ = jb + 1
                while k_ < 16 and dests[k_][0] is dram and dests[k_][1] == row0 + (k_ - jb) * 128:
                    k_ += 1
                nb = k_ - jb
                self.dma(dram[row0:row0 + nb * 128, t * 128:(t + 1) * 128].rearrange("(j p) t -> p j t", p=128),
                         qt_[:, jb:k_, :], f"qst{t % 2}", r=[qtt], pw=[("kTq_d",)], q="pool")
                jb = k_

        def mm_vz(t):
            v_ap, vtok = vbf[t % 2]
            z_ap, ztok = zbf[t % 2]
            if moba:
                for ci in (4, 5):
                    bank = ci
                    mm_chunk(t, ci, bank)
                    self.act(v_ap[:, (ci - 4) * 512:(ci - 3) * 512], self.pf[bank][:, 0:512], AF.Copy, r=[("pf", bank)],
                             **(dict(w=[vtok]) if ci == 4 else dict(pw=[vtok])))
                self.dma(self.v_d[t * 128:(t + 1) * 128, :], v_ap, f"vst{t % 2}", r=[vtok], pw=[("v_d",)], q="pool")
            else:
                self.act(v_ap[:, 0:256], self.pf[2][:, 256:512], AF.Copy, r=[("pf", 2)], w=[vtok])
                self.act(v_ap[:, 256:512], self.pf[3][:, 256:512], AF.Copy, r=[("pf", 3)], pw=[vtok])
                self.dma(self.v_d[t * 128:(t + 1) * 128, 0:512], v_ap[:, 0:512], f"vst{t % 2}", r=[vtok], pw=[("v_d",)], q="pool")
                mm_chunk(t, 5, 5)
                self.act(self.g_sb[:, t, :], self.pf[5][:, 0:48], AF.Sigmoid, r=[("pf", 5)], pw=["g_sb"])
            for ci in (6, 7):
                bank = ci - 2
                mm_chunk(t, ci, bank)
                self.act(z_ap[:, (ci - 6) * 512:(ci - 5) * 512], self.pf[bank][:, 0:512], AF.Silu, r=[("pf", bank)],
                         **(dict(w=[ztok]) if ci == 6 else dict(pw=[ztok])))
            self.dma(self.zs_d[t * 128:(t + 1) * 128, :], z_ap, f"zst{t % 2}", r=[ztok], pw=[("zs_d",)], q="pool")

        import os
        dbg = os.environ.get("PDBG", "")
        pre_a(0)
        pre_b(0)
        for t in range(NT):
            for (ci, bank) in qk_chunks:
                mm_chunk(t, ci, bank)
            if t + 1 < NT:
                pre_a(t + 1)
            if "nochain" not in dbg:
                chain(t)
            if t + 1 < NT:
                pre_b(t + 1)
            if "novz" not in dbg:
                mm_vz(t)
            if t >= 1 and "notq" not in dbg:
                tq(t - 1)
        if "notq" not in dbg:
            tq(NT - 1)

    def attention(self, qTa, qtok, kTa, ktok, Va, vtok, vw, pairs_fn, fin, pts, tag, side=None, side_delay=0):
        NT = self.NT
        qtoks = list(qtok) if isinstance(qtok, list) else [qtok]
        allp = []
        for qt in range(NT):
            prs = pairs_fn(qt)
            for j_, (kt, bias) in enumerate(prs):
                allp.append((qt, kt, bias, j_ == 0, j_ == len(prs) - 1))
        units = []
        cur = []
        for p_ in allp:
            qts = []
            for x_ in cur:
                if x_[0] not in qts:
                    qts.append(x_[0])
            if len(cur) == 4 or (p_[0] not in qts and len(qts) == 2):
                units.append(cur)
                cur = []
            cur.append(p_)
        if cur:
            units.append(cur)
        n = len(units)
        NBK = len(pts)
        SK = NBK - 1
        banks = [0, 1, 2, 3][:NBK]
        u0 = getattr(self, "_u", 0)

        def qk(i):
            grp = units[i]
            si = banks[(u0 + i) % NBK]
            ps = self.pf[si]
            pstok = ("pf", si)
            firstw = True
            for jx, (qt, kt, bias, _, _) in enumerate(grp):
                o_ap = ps[:, jx * 128:(jx + 1) * 128]
                self.mm(o_ap, kTa[:, kt * 128:(kt + 1) * 128], qTa[:, qt * 128:(qt + 1) * 128], True, bias is None,
                        r=[ktok] + qtoks, w=[pstok] if firstw else ())
                firstw = False
                if bias is not None:
                    self.mm(o_ap, self.ident[:], bias, False, True, r=["ident", "tri", "atri", "cmpm"])
            last = self.S.q["pe"][-1]
            self.S.tw[pstok] = {last.stream: last}

        def pv(i):
            grp = units[i]
            si = banks[(u0 + i) % NBK]
            ps = self.pf[si]
            pstok = ("pf", si)
            pt_ap, pttok = pts[(u0 + i) % NBK]
            nn = len(grp) * 128
            self.act(pt_ap[:, 0:nn], ps[:, 0:nn], AF.Exp, r=[pstok], w=[pttok], scale=0.125)
            done = []
            for jx, (qt, kt, bias, isf, isl) in enumerate(grp):
                po = self.pf[4 + qt % 2]
                potok = ("pf", 4 + qt % 2)
                self.mm(po[:, 0:vw], pt_ap[:, jx * 128:(jx + 1) * 128], Va[:, kt, 0:vw], isf, isl,
                        r=[pttok, vtok], w=[potok] if isf else ())
                last = self.S.q["pe"][-1]
                self.S.tw[potok] = {last.stream: last}
                if isl:
                    done.append(qt)
            for qt in done:
                fin(qt, self.pf[4 + qt % 2], ("pf", 4 + qt % 2))

        for i in range(n + SK):
            if i < n:
                qk(i)
            if i >= SK:
                pv(i - SK)
                if side is not None and i - SK >= side_delay:
                    next(side, None)
        self._u = u0 + n

    def phase_A_moba(self, li):
        self.phase()
        T, NT = self.T, self.NT
        NB = T // 256
        o_all = self.big[:, 0:NT * D].rearrange("p (t f) -> p t f", f=D)
        qTa = [self.ph(f"qTa{i}", [128, T], BF16) for i in range(2)]
        kTa = [self.ph(f"kTa{i}", [128, T], BF16) for i in range(2)]
        Va = [self.ph(f"Va{i}", [128, NT, 65], BF16) for i in range(2)]
        pts = [self.ph(f"pt{i}", [128, 512], BF16) for i in range(4)]
        kmf, kmft = self.ph("kmf", [128, 16], F32)
        kmT, kmTt = self.ph("kmT", [128, 16], BF16)
        gt_, gtt = self.ph("gate", [128, 16], F32)
        m8, m8t = self.ph("m8", [128, 8], F32)
        bq, bqt = self.ph("bq", [128, 128], BF16)
        rz, rzt = self.ph("rz", [128, 1], F32)
        self.E("dve", "memset", kmT, 0.0, w=[kmTt])
        self.E("dve", "memset", bq, 0.0, w=[bqt])
        for i in range(2):
            q_ap, qtok = qTa[i]
            k_ap, ktok = kTa[i]
            v_ap, vtok = Va[i]
            self.E("pool", "memset", q_ap, 0.0, w=[qtok])
            self.E("pool", "memset", k_ap[0:64, :], 0.0, w=[ktok])
            ke = k_ap[64:128, :]
            self.E("pool", "memset", ke, NEGB, r=[ktok], w=[ktok])
            self.E("pool", "affine_select", out=ke, in_=ke, pattern=[[1, T]], compare_op=ALU.is_ge, fill=0.0, base=0,
                   channel_multiplier=-256, r=[ktok], w=[ktok])
            self.E("pool", "affine_select", out=ke, in_=ke, pattern=[[-1, T]], compare_op=ALU.is_ge, fill=0.0, base=255,
                   channel_multiplier=256, r=[ktok], w=[ktok])
            self.E("dve", "memset", v_ap, 1.0, w=[vtok])

        def load_head(h):
            i = h % 2
            q_ap, qtok = qTa[i]
            k_ap, ktok = kTa[i]
            v_ap, vtok = Va[i]
            self.dma(q_ap[0:64, :], self.qT_d[h * 64:(h + 1) * 64, :], f"aq{i}", r=[("kTq_d",)], w=[qtok])
            self.dma(k_ap[0:64, :], self.kT_d[h * 64:(h + 1) * 64, :], f"ak{i}", r=[("kTq_d",)], w=[ktok])
            for c in range(0, NT, 8):
                n = min(8, NT - c)
                self.dma(v_ap[:, c:c + n, 0:64],
                         self.v_d[c * 128:(c + n) * 128, h * 64:(h + 1) * 64].rearrange("(t p) d -> p t d", p=128),
                         f"av{i}", r=[("v_d",)], **(dict(w=[vtok]) if c == 0 else dict(pw=[vtok])))

        def gate_steps(h):
            i = h % 2
            q_ap, qtok = qTa[i]
            k_ap, ktok = kTa[i]
            self.E("dve", "tensor_reduce", out=kmf[0:64, 0:NB], in_=k_ap[0:64, :].rearrange("p (n k) -> p n k", k=256),
                   axis=AX.X, op=ALU.add, r=[ktok], w=[kmft])
            self.E("dve", "tensor_scalar", out=kmT[0:64, 0:NB], in0=kmf[0:64, 0:NB], scalar1=1.0 / 256, scalar2=None,
                   op0=ALU.mult, r=[kmft], w=[kmTt])
            self.E("dve", "memset", gt_, -1.0e30, w=[gtt])
            yield
            for qt in range(NT):
                b = qt // 2
                if b <= 3:
                    continue
                pg = self.pb[0][:, 0:32].bitcast(F32)
                self.mm(pg[:, 0:16], q_ap[:, qt * 128:(qt + 1) * 128], kmT, True, True, r=[qtok, kmTt], w=[("pb", 0)])
                self.E("dve", "tensor_copy", out=gt_[:, 0:b], in_=pg[:, 0:b], r=[("pb", 0), gtt], w=[gtt])
                self.E("dve", "max", out=m8, in_=gt_, r=[gtt], w=[m8t])
                self.E("dve", "tensor_scalar", out=bq[:, 64:80], in0=gt_, scalar1=m8[:, 2:3], scalar2=1.0, op0=ALU.is_ge,
                       op1=ALU.subtract, r=[gtt, m8t, bqt], w=[bqt])
                self.E("dve", "memset", bq[:, 64 + b:65 + b], 0.0, r=[bqt], w=[bqt])
                yield
                yield
                pbt = self.pb[1]
                self.tr(pbt[:, 0:128], bq, r=[bqt], w=[("pb", 1)])
                self.E("dve", "tensor_copy", out=q_ap[64:128, qt * 128:(qt + 1) * 128], in_=pbt[64:128, 0:128],
                       r=[("pb", 1), qtok], w=[qtok])
                yield

        load_head(0)
        for _ in gate_steps(0):
            pass
        for h in range(H):
            side = None
            if h + 1 < H:
                load_head(h + 1)
                side = gate_steps(h + 1)
            i = h % 2
            q_ap, qtok = qTa[i]
            k_ap, ktok = kTa[i]
            v_ap, vtok = Va[i]

            def pairs_fn(qt):
                return [(kt, (self.tri[:] if kt == qt else None)) for kt in range(qt + 1)]

            def fin(qt, po, potok, h=h):
                self.E("dve", "reciprocal", out=rz, in_=po[:, 64:65], r=[potok], w=[rzt])
                self.E("dve", "tensor_scalar", out=o_all[:, qt, h * 64:(h + 1) * 64], in0=po[:, 0:64], scalar1=rz[:, 0:1],
                       scalar2=None, op0=ALU.mult, r=[potok, rzt], pw=["big"])

            self.attention(q_ap, qtok, k_ap, ktok, v_ap, vtok, 65, pairs_fn, fin, pts, f"m{h}", side=side, side_delay=min(24, self.NT))
            if side is not None:
                for _ in side:
                    pass

    def phase_C(self, li):
        self.phase()
        T, NT = self.T, self.NT
        j = li // 2
        xkp, xkpt = self.ph("xkp", [128, T + 16], BF16)
        xk16, xk16t = self.ph("xk16", [128, 2, 16, T // 16], BF16)
        W1d, W1t = self.ph("W1d", [128, 32, 256], BF16)
        w1s = [self.ph(f"w1s{i}", [128, 8, 256], F32) for i in range(2)]
        W2b, W2t = self.ph("W2b", [128, 2, 64], BF16)
        w2s, w2st = self.ph("w2s", [128, 2, 64], F32)
        peT, peTt = self.ph("peT", [128, 32], BF16)
        pes, pest = self.ph("pes", [128, 32], F32)
        bh, bht = self.ph("bh", [128, 2], F32)
        NC, NCT = self.NC, self.NCT
        xx, xxt = self.ph("xx", [128, NC], F32)
        x2, x2t = self.ph("x2", [128, NC], F32)
        sg, sgt = self.ph("sg", [128, NC], F32)
        gl, glt = self.ph("gl", [128, 2, NC], BF16)
        kcb, kcbt = self.ph("kcb", [128, 128], BF16)
        tmp = [self.ph(n, [128, 64], F32) for n in ("c_sq", "c_ss", "c_A", "c_B", "c_t13", "c_tsw")]
        self.E("pool", "memset", xkp, 0.0, w=[xkpt])
        self.E("pool", "memset", W1d, 0.0, w=[W1t])
        self.E("pool", "memset", peT, 0.0, w=[peTt])
        self.E("pool", "memset", kcb, 0.0, w=[kcbt])
        for kv in range(2):
            w1 = self.nsa_cmp_w1[j, kv].rearrange("(l d) j -> d l j", d=64)
            for c in range(4):
                s_ap, s_tok = w1s[c % 2]
                self.dma(s_ap[0:64], w1[:, c * 8:(c + 1) * 8, :], f"w1s{c % 2}", w=[s_tok])
                self.E("dve", "tensor_copy", out=W1d[0:64, c * 8:(c + 1) * 8, :], in_=s_ap[0:64], r=[s_tok],
                       **(dict(w=[W1t]) if c == 0 else dict(pw=[W1t])))
            self.dma(w2s, self.nsa_cmp_w2[j, kv].rearrange("(c p) d -> p c d", p=128), "w2s", w=[w2st])
            self.E("dve", "tensor_copy", out=W2b, in_=w2s, r=[w2st], w=[W2t])
            for q4 in range(4):
                self.dma(pes[0:64, q4 * 8:(q4 + 1) * 8], self.nsa_cmp_pe[j, kv, q4 * 8:(q4 + 1) * 8, :].rearrange("l d -> d l"),
                         "pes", **(dict(w=[pest]) if q4 == 0 else dict(pw=[pest])), allow_slow_non_contiguous=True)
            self.E("dve", "tensor_copy", out=peT[0:64, :], in_=pes[0:64, :], r=[pest], w=[peTt])
            pbias = self.pf[3]
            for jc in range(2):
                for l in range(32):
                    self.mm(pbias[:, jc:jc + 1], W1d[:, l, jc * 128:(jc + 1) * 128], peT[:, l:l + 1], l == 0, l == 31,
                            r=[W1t, peTt], w=[("pf", 3)] if (l == 0 and jc == 0) else ())
            last = self.S.q["pe"][-1]
            self.S.tw[("pf", 3)] = {last.stream: last}
            self.E("dve", "tensor_copy", out=bh, in_=pbias[:, 0:2], r=[("pf", 3)], w=[bht])
            for g in range(4):
                row0 = 512 + kv * 256 + g * 64
                self.dma(xkp[0:64, 0:T], self.kT_d[row0:row0 + 64, :], "xkp", r=[("kTq_d",)], w=[xkpt])
                self.E("dve", "tensor_copy", out=xk16[:, 0], in_=xkp[:, 0:T].rearrange("p (m r) -> p r m", r=16), r=[xkpt], w=[xk16t])
                self.E("dve", "tensor_copy", out=xk16[:, 1], in_=xkp[:, 16:T + 16].rearrange("p (m r) -> p r m", r=16), r=[xkpt],
                       pw=[xk16t])
                for jc in range(2):
                    ph_ = self.pf[jc]
                    for l in range(32):
                        self.mm(ph_[:, 0:NC], W1d[:, l, jc * 128:(jc + 1) * 128], xk16[:, l // 16, l % 16, :], l == 0, l == 31,
                                r=[W1t, xk16t], w=[("pf", jc)] if l == 0 else ())
                    last = self.S.q["pe"][-1]
                    self.S.tw[("pf", jc)] = {last.stream: last}
                    self.act(xx, ph_[:, 0:NC], AF.Identity, r=[("pf", jc), bht], w=[xxt], bias=bh[:, jc:jc + 1])
                    self.E("dve", "tensor_tensor", out=x2, in0=xx, in1=xx, op=ALU.mult, r=[xxt], w=[x2t])
                    self.E("dve", "tensor_scalar", out=x2, in0=x2, scalar1=0.044715, scalar2=1.0, op0=ALU.mult, op1=ALU.add,
                           r=[x2t], w=[x2t])
                    self.E("dve", "tensor_tensor", out=x2, in0=x2, in1=xx, op=ALU.mult, r=[x2t, xxt], w=[x2t])
                    self.act(sg, x2, AF.Sigmoid, r=[x2t], w=[sgt], scale=1.5957691216057308)
                    self.E("dve", "tensor_tensor", out=gl[:, jc, :], in0=xx, in1=sg, op=ALU.mult, r=[xxt, sgt],
                           **(dict(w=[glt]) if jc == 0 else dict(pw=[glt])))
                for ct in range(NCT):
                    pk = self.pf[2]
                    for jc in range(2):
                        self.mm(pk[:, 0:64], gl[:, jc, ct * 128:(ct + 1) * 128], W2b[:, jc, :], jc == 0, jc == 1,
                                r=[glt, W2t], w=[("pf", 2)] if jc == 0 else ())
                    last = self.S.q["pe"][-1]
                    self.S.tw[("pf", 2)] = {last.stream: last}
                    if kv == 0:
                        self.norm_rope(pk[:, 0:64], ("pf", 2), 1, self.gk[:, 0, :], "gk", self.csc[:, 0, ct, :],
                                       self.csc[:, 1, ct, :], "csc", kcb, kcbt, tmp)
                        pbt = self.pb[1]
                        self.tr(pbt[:, 0:128], kcb, r=[kcbt], w=[("pb", 1)])
                        self.E("dve", "tensor_copy", out=self.kcTa[:, g, ct * 128:(ct + 1) * 128], in_=pbt[:, 0:128],
                               r=[("pb", 1)], pw=["kcTa"])
                    else:
                        self.act(self.vca[:, g, ct, 0:64], pk[:, 0:64], AF.Copy, r=[("pf", 2)], pw=["vca"])

    def phase_A_nsa(self, li):
        self.phase()
        T, NT = self.T, self.NT
        o_all = self.big[:, 0:NT * D].rearrange("p (t f) -> p t f", f=D)
        qTa = [self.ph(f"nqTa{i}", [128, T], BF16) for i in range(4)]
        ksTa, kst = self.ph("ksTa", [128, T], BF16)
        kwTa, kwt = self.ph("kwTa", [128, T], BF16)
        Vs, Vst = self.ph("Vs", [128, NT, 65], BF16)
        Vw, Vwt = self.ph("Vw", [128, NT, 65], BF16)
        pts = [self.ph(f"npt{i}", [128, 512], BF16) for i in range(4)]
        oacc, oacct = self.ph("oacc", [128, NT, 64], F32)
        imp, impt = self.ph("imp", [128, NT, 64], F32)
        sc, sct = self.ph("sc", [128, 64], F32)
        sc2, sc2t = self.ph("sc2", [128, 64], F32)
        m8a, m8at = self.ph("m8a", [128, 8], F32)
        m8b, m8bt = self.ph("m8b", [128, 8], F32)
        bs2 = [self.ph(f"bs{i}", [128, 128], BF16) for i in range(2)]
        rz, rzt = self.ph("nrz", [128, 1], F32)
        cf, cft = self.ph("ncf", [128, 1], F32)
        stage, staget = self.ph("cstage", [128, NT, 128], F32)
        rzb, rzbt = self.ph("rzb", [128, NT], F32)
        cfb, cfbt = self.ph("cfb", [128, NT], F32)
        for hl, (q_ap, qtok) in enumerate(qTa):
            self.E("pool", "memset", q_ap, 0.0, w=[qtok, ("qhi", hl)])
        self.E("pool", "memset", kwTa, 0.0, w=[kwt])
        self.E("pool", "memset", ksTa[0:64, :], 0.0, w=[kst])
        ke = ksTa[64:128, :]
        self.E("pool", "memset", ke, NEGB, r=[kst], w=[kst])
        self.E("pool", "affine_select", out=ke, in_=ke, pattern=[[1, T]], compare_op=ALU.is_ge, fill=0.0, base=0,
               channel_multiplier=-64, r=[kst], w=[kst])
        self.E("pool", "affine_select", out=ke, in_=ke, pattern=[[-1, T]], compare_op=ALU.is_ge, fill=0.0, base=63,
               channel_multiplier=64, r=[kst], w=[kst])
        self.E("dve", "memset", Vs, 1.0, w=[Vst])
        self.E("dve", "memset", Vw, 1.0, w=[Vwt])
        for (b_ap, b_tok) in bs2:
            self.E("dve", "memset", b_ap, 0.0, w=[b_tok])
        self.E("dve", "memset", imp, 0.0, w=[impt])

        for g in range(4):
            for hl in range(4):
                h = g * 4 + hl
                q_ap, qtok = qTa[hl]
                self.dma(q_ap[0:64, :], self.qT_d[h * 64:(h + 1) * 64, :], f"nq{hl}", r=[("kTq_d",)], w=[qtok])
            self.dma(ksTa[0:64, :], self.kT_d[g * 64:(g + 1) * 64, :], "nks", r=[("kTq_d",)], w=[kst])
            self.dma(kwTa[0:64, :], self.kT_d[256 + g * 64:256 + (g + 1) * 64, :], "nkw", r=[("kTq_d",)], w=[kwt])
            for (V_, Vt_, c0, nm) in ((Vs, Vst, g * 64, "nvs"), (Vw, Vwt, 256 + g * 64, "nvw")):
                for c in range(0, NT, 8):
                    n = min(8, NT - c)
                    self.dma(V_[:, c:c + n, 0:64],
                             self.v_d[c * 128:(c + n) * 128, c0:c0 + 64].rearrange("(t p) d -> p t d", p=128),
                             nm, r=[("v_d",)], **(dict(w=[Vt_]) if c == 0 else dict(pw=[Vt_])))

            def cmp_pairs(qt):
                out = []
                for kt in range(self.NCT):
                    dl = qt - 16 * kt
                    if dl < 0:
                        continue
                    out.append((kt, self.cmpm[:, dl, :] if dl <= 16 else None))
                return out

            def sel_A(qt):
                scq = imp[:, qt, :]
                self.E("dve", "max", out=m8a, in_=scq, r=[impt], w=[m8at])
                self.E("dve", "tensor_scalar", out=sc2, in0=scq, scalar1=m8a[:, 7:8], scalar2=-6.0e4, op0=ALU.is_ge, op1=ALU.mult,
                       r=[impt, m8at], w=[sc2t])
                self.E("dve", "tensor_tensor", out=sc2, in0=sc2, in1=scq, op=ALU.add, r=[sc2t, impt], w=[sc2t])
                self.E("dve", "max", out=m8b, in_=sc2, r=[sc2t], w=[m8bt])
                bs, bst = bs2[qt % 2]
                self.E("dve", "tensor_scalar", out=bs[:, 64:128], in0=scq, scalar1=m8b[:, 7:8], scalar2=1.0, op0=ALU.is_ge,
                       op1=ALU.subtract, r=[impt, m8bt, bst], w=[bst])

            def sel_B(qt):
                pbt = self.pb[1]
                bs, bst = bs2[qt % 2]
                self.tr(pbt[:, 0:128], bs, r=[bst], w=[("pb", 1)])
                for hl2 in range(4):
                    q2, _ = qTa[hl2]
                    self.E("dve", "tensor_copy", out=q2[64:128, qt * 128:(qt + 1) * 128], in_=pbt[64:128, 0:128],
                           r=[("pb", 1)], pw=[("qhi", hl2)])

            def sel_steps():
                sel_A(0)
                yield
                for qt in range(NT):
                    if qt + 1 < NT:
                        sel_A(qt + 1)
                    sel_B(qt)
                    yield

            for hl in range(4):
                h = g * 4 + hl
                q_ap, qtok = qTa[hl]

                def fin_c(qt, po, potok):
                    self.E("dve", "tensor_copy", out=stage[:, qt, :], in_=po[:, 0:128], r=[potok], pw=[staget])

                self.attention(q_ap, qtok, self.kcTa[:, g, :], "kcTa", self.vca[:, g], "vca", 128, cmp_pairs, fin_c, pts,
                               f"c{h}")
                zc = stage[:, :, 64]
                self.E("dve", "tensor_scalar", out=rzb, in0=zc, scalar1=1.0e-30, scalar2=None, op0=ALU.max, r=[staget], w=[rzbt])
                self.E("dve", "reciprocal", out=rzb, in_=rzb, r=[rzbt], w=[rzbt])
                self.E("dve", "tensor_tensor", out=cfb, in0=rzb, in1=self.g_sb[:, :, h * 3], op=ALU.mult, r=[rzbt, "g_sb"],
                       w=[cfbt])
                self.E("dve", "tensor_tensor", out=o_all[:, :, h * 64:(h + 1) * 64], in0=stage[:, :, 0:64],
                       in1=cfb.unsqueeze(2).to_broadcast([128, NT, 64]), op=ALU.mult, r=[staget, cfbt], pw=["big"])
                rzb3 = rzb.unsqueeze(2).to_broadcast([128, NT, 63])
                if hl == 0:
                    self.E("dve", "memset", imp[:, :, 63:64], 0.0, w=[impt])
                    self.E("dve", "tensor_tensor", out=imp[:, :, 0:63], in0=stage[:, :, 65:128], in1=rzb3, op=ALU.mult,
                           r=[staget, rzbt], pw=[impt])
                else:
                    self.E("dve", "tensor_tensor", out=stage[:, :, 65:128], in0=stage[:, :, 65:128], in1=rzb3, op=ALU.mult,
                           r=[staget, rzbt], pw=[staget])
                    self.E("dve", "tensor_tensor", out=imp[:, :, 0:63], in0=imp[:, :, 0:63], in1=stage[:, :, 65:128], op=ALU.add,
                           r=[staget, impt], pw=[impt])
            self.E("dve", "tensor_tensor", out=imp[:, :, :], in0=imp[:, :, :], in1=self.msel[:, :, :], op=ALU.add,
                   r=[impt, "msel"], w=[impt])

            def sel_pairs(qt):
                return [(kt, (self.tri[:] if kt == qt else None)) for kt in range(qt + 1)]

            def win_pairs(qt):
                out = []
                for kt in range(max(0, qt - 4), qt + 1):
                    b = self.tri[:] if kt == qt else (self.atri[:] if kt == qt - 4 else None)
                    out.append((kt, b))
                return out

            for hl in range(4):
                h = g * 4 + hl
                q_ap, qtok = qTa[hl]

                def fin_w(qt, po, potok, h=h):
                    self.E("dve", "reciprocal", out=rz, in_=po[:, 64:65], r=[potok], w=[rzt])
                    self.E("dve", "tensor_tensor", out=cf, in0=rz, in1=self.g_sb[:, qt, h * 3 + 2:h * 3 + 3], op=ALU.mult,
                           r=[rzt, "g_sb"], w=[cft])
                    self.E("dve", "scalar_tensor_tensor", out=oacc[:, qt, :], in0=po[:, 0:64], scalar=cf[:, 0:1],
                           in1=o_all[:, qt, h * 64:(h + 1) * 64], op0=ALU.mult, op1=ALU.add, r=[potok, cft, "big"],
                           pw=[oacct])

                def fin_s(qt, po, potok, h=h):
                    self.E("dve", "reciprocal", out=rz, in_=po[:, 64:65], r=[potok], w=[rzt])
                    self.E("dve", "tensor_tensor", out=cf, in0=rz, in1=self.g_sb[:, qt, h * 3 + 1:h * 3 + 2], op=ALU.mult,
                           r=[rzt, "g_sb"], w=[cft])
                    self.E("dve", "scalar_tensor_tensor", out=o_all[:, qt, h * 64:(h + 1) * 64], in0=po[:, 0:64],
                           scalar=cf[:, 0:1], in1=oacc[:, qt, :], op0=ALU.mult, op1=ALU.add, r=[potok, cft, oacct],
                           pw=["big"])

                side = sel_steps() if hl == 0 else None
                self.attention(q_ap, qtok, kwTa, kwt, Vw, Vwt, 65, win_pairs, fin_w, pts, f"w{h}", side=side)
                if side is not None:
                    for _ in side:
                        pass
                self.attention(q_ap, [qtok, ("qhi", hl)], ksTa, kst, Vs, Vst, 65, sel_pairs, fin_s, pts, f"s{h}")

    def phase_O(self, li, src):
        self.phase()
        NT = self.NT
        j = li // 2
        moba = (li % 2 == 0)
        o_all = self.big[:, 0:NT * D].rearrange("p (t f) -> p t f", f=D)
        Wo, Wot = self.ph("Wo", [128, 8, D], BF16)
        Wg, Wgt = self.ph("Wg", [128, 8, D], BF16)
        Wp, Wpt = self.ph("Wp", [128, 2, D], BF16)
        self.load_w(Wo, Wot, (self.moba_w_out if moba else self.nsa_w_out)[j], D, None, None)
        self.load_gcol(self.gcol2, "gcol2", self.ple_gate_gain[li])
        self.load_w(Wg, Wgt, self.ple_w_gate[li], D, self.gcol2, "gcol2")
        self.load_w(Wp, Wpt, self.ple_w_proj[li], D, None, None, nk=2)
        xt = [self.ph(f"oxt{i}", [128, D], F32) for i in range(2)]
        zt = [self.ph(f"ozt{i}", [128, D], BF16) for i in range(2)]
        pt_ = [self.ph(f"opt{i}", [128, PLE], F32) for i in range(2)]
        og = [self.ph(f"og{i}", [128, D], BF16) for i in range(2)]
        ogT = [self.ph(f"ogT{i}", [128, 8, 128], BF16) for i in range(2)]
        x1 = [self.ph(f"x1_{i}", [128, D], F32) for i in range(2)]
        sqs, sqst = self.ph("osq", [128, D], F32)
        rs = [self.ph(f"ors{i}", [128, 1], F32) for i in range(2)]
        xn, xnt = self.ph("xn", [128, D], BF16)
        xnT, xnTt = self.ph("xnT", [128, 8, 128], BF16)
        gate, gatet = self.ph("gatef", [128, D], F32)
        pbf, pbft = self.ph("pbf", [128, PLE], BF16)
        pT, pTt = self.ph("pT", [128, 2, 128], BF16)
        x2 = [self.ph(f"x2_{i}", [128, D], F32) for i in range(2)]

        def S1_T(t):
            x_ap, xtok = xt[t % 2]
            z_ap, ztok = zt[t % 2]
            p_ap, ptok = pt_[t % 2]
            og_ap, ogt = og[t % 2]
            ogT_ap, ogTt = ogT[t % 2]
            self.dma(x_ap, src[t * 128:(t + 1) * 128, :], f"oxl{t % 2}", r=[("y", t)], w=[xtok])
            self.dma(z_ap, self.zs_d[t * 128:(t + 1) * 128, :], f"ozl{t % 2}", r=[("zs_d",)], w=[ztok])
            self.dma(p_ap, self.p_in[li, t * 128:(t + 1) * 128, :], f"opl{t % 2}", w=[ptok])
            self.E("dve", "tensor_tensor", out=og_ap, in0=o_all[:, t, :], in1=z_ap, op=ALU.mult, r=["big", ztok], w=[ogt])
            self.transpose8(og_ap, ogt, ogT_ap, ogTt, 0)

        def S1_M(t):
            x_ap, xtok = xt[t % 2]
            ogT_ap, ogTt = ogT[t % 2]
            x1_ap, x1t = x1[t % 2]
            for c in range(2):
                ps = self.pf[c]
                for k in range(8):
                    self.mm(ps[:, :], ogT_ap[:, k, :], Wo[:, k, c * 512:(c + 1) * 512], k == 0, k == 7, r=[ogTt, Wot],
                            w=[("pf", c)] if k == 0 else ())
                last = self.S.q["pe"][-1]
                self.S.tw[("pf", c)] = {last.stream: last}
                self.E("dve", "tensor_tensor", out=x1_ap[:, c * 512:(c + 1) * 512], in0=x_ap[:, c * 512:(c + 1) * 512],
                       in1=ps[:, :], op=ALU.add, r=[xtok, ("pf", c)], **(dict(w=[x1t]) if c == 0 else dict(pw=[x1t])))

        def S2a(t):
            x1_ap, x1t = x1[t % 2]
            r_ap, rtok = rs[t % 2]
            self.rstd_of(x1_ap, x1t, D, sqs, sqst, r_ap, rtok)
            self.act(xn, x1_ap, AF.Copy, r=[x1t, rtok], w=[xnt], scale=r_ap[:, 0:1])

        def S2b_T(t):
            self.transpose8(xn, xnt, xnT, xnTt, 1)

        def S2b_M(t):
            x1_ap, x1t = x1[t % 2]
            p_ap, ptok = pt_[t % 2]
            for c in range(2):
                ps = self.pf[2 + c]
                for k in range(8):
                    self.mm(ps[:, :], xnT[:, k, :], Wg[:, k, c * 512:(c + 1) * 512], k == 0, k == 7, r=[xnTt, Wgt],
                            w=[("pf", 2 + c)] if k == 0 else ())
                last = self.S.q["pe"][-1]
                self.S.tw[("pf", 2 + c)] = {last.stream: last}
                self.act(gate[:, c * 512:(c + 1) * 512], ps[:, :], AF.Sigmoid, r=[("pf", 2 + c)],
                         **(dict(w=[gatet]) if c == 0 else dict(pw=[gatet])))
            self.E("dve", "tensor_copy", out=pbf, in_=p_ap, r=[ptok], w=[pbft])
            self.transpose8(pbf, pbft, pT, pTt, 1, nblk=2)
            x2_ap, x2tok = x2[t % 2]
            for c in range(2):
                ps = self.pf[4 + c]
                for k in range(2):
                    self.mm(ps[:, :], pT[:, k, :], Wp[:, k, c * 512:(c + 1) * 512], k == 0, k == 1, r=[pTt, Wpt],
                            w=[("pf", 4 + c)] if k == 0 else ())
                last = self.S.q["pe"][-1]
                self.S.tw[("pf", 4 + c)] = {last.stream: last}
                sl = slice(c * 512, (c + 1) * 512)
                self.E("dve", "tensor_tensor", out=x2_ap[:, sl], in0=gate[:, sl], in1=ps[:, :], op=ALU.mult,
                       r=[gatet, ("pf", 4 + c)], **(dict(w=[x2tok]) if c == 0 else dict(pw=[x2tok])))
                self.E("dve", "tensor_tensor", out=x2_ap[:, sl], in0=x2_ap[:, sl], in1=x1_ap[:, sl], op=ALU.add,
                       r=[x2tok, x1t], pw=[x2tok])
            self.dma(self.y[t * 128:(t + 1) * 128, :], x2_ap, f"oyst{t % 2}", r=[x2tok], w=[("y", t)], q="pool")

        S1_T(0)
        S1_M(0)
        for t in range(NT):
            S2a(t)
            if t + 1 < NT:
                S1_T(t + 1)
            S2b_T(t)
            if t + 1 < NT:
                S1_M(t + 1)
            S2b_M(t)


def rope_tables(T):
    NT = T // 128
    half = 32
    inv_freq = (10000.0 ** (-np.arange(half, dtype=np.float32) / half)).astype(np.float32)
    pos = (np.arange(NT)[None, :] * 128 + np.arange(128)[:, None]).astype(np.float32)
    ang = pos[:, :, None] * inv_freq[None, None, :]
    cs = np.stack([np.cos(ang), np.sin(ang)], 0).astype(np.float32)
    NCT = max(1, (T // 16) // 128)
    posc = (16.0 * (np.arange(NCT)[None, :] * 128 + np.arange(128)[:, None]) + 31.0).astype(np.float32)
    angc = posc[:, :, None] * inv_freq[None, None, :]
    csc = np.stack([np.cos(angc), np.sin(angc)], 0).astype(np.float32)
    return cs, csc


_CACHE = {}


def run(inputs, T, layers, n_cores):
    key = (T, tuple(layers))
    if key not in _CACHE:
        _CACHE[key] = Builder(T, list(layers)).build()
    nc = _CACHE[key]
    cs, csc = rope_tables(T)
    shared = {k: np.ascontiguousarray(v, dtype=np.float32) for k, v in inputs.items() if k not in ("x", "p")}
    shared["rope_cs"] = cs
    shared["rope_cs_c"] = csc
    in_maps = []
    for b in range(n_cores):
        m = dict(shared)
        m["x"] = np.ascontiguousarray(inputs["x"][b], dtype=np.float32)
        m["p"] = np.ascontiguousarray(inputs["p"][:, b], dtype=np.float32)
        in_maps.append(m)
    res = run_bass_kernel_spmd(nc, in_maps, core_ids=list(range(n_cores)))
    return np.stack([r["y"] for r in res.results], 0).astype(np.float32)


def kernel(**inputs):
    return run(inputs, 4096, [0, 1, 2, 3], 8)
```

```python
import math
from contextlib import ExitStack

import numpy as np
import concourse.bass as bass
import concourse.mybir as mybir
from concourse.bass_utils import run_bass_kernel_spmd

F32 = mybir.dt.float32
BF16 = mybir.dt.bfloat16
I32 = mybir.dt.int32
AF = mybir.ActivationFunctionType
ALU = mybir.AluOpType
AX = mybir.AxisListType

D = 1024
H = 16
HD = 64
PLE = 256
EPS = 1e-6
NSA_IN = 3632
NEGB = 30000.0


class Op:
    __slots__ = ("eng", "fn", "deps", "needed", "sigval", "stream", "is_dma", "idx")


class Sched:
    ENGS = ("pe", "act", "dve", "pool", "sp")

    def __init__(self):
        self.q = {e: [] for e in self.ENGS}
        self.tw = {}
        self.tr = {}
        self.dma_cnt = {}
        self.nops = 0
        self.last = {}
        self.pending = {}

    def barrier(self):
        for e in self.ENGS:
            self.pending[e] = list(self.last.values())

    def op(self, eng, fn, reads=(), writes=(), pwrites=(), dma=None):
        o = Op()
        o.eng = eng
        o.fn = fn
        o.is_dma = dma is not None
        o.stream = ("dma", dma) if dma is not None else ("eng", eng)
        o.needed = o.is_dma
        o.sigval = None
        o.idx = self.nops
        self.nops += 1
        deps = {}

        def add(d):
            if (not o.is_dma) and (not d.is_dma) and d.eng == "pe" and eng == "pe":
                return
            k = d.stream
            if k not in deps or deps[k].idx < d.idx:
                deps[k] = d

        for d in self.pending.pop(eng, ()):
            add(d)
        for t in reads:
            for d in self.tw.get(t, {}).values():
                add(d)
        for t in writes:
            for d in self.tw.get(t, {}).values():
                add(d)
            for d in self.tr.get(t, {}).values():
                add(d)
        for t in pwrites:
            for d in self.tr.get(t, {}).values():
                add(d)
        o.deps = list(deps.values())
        for d in o.deps:
            d.needed = True
        for t in reads:
            self.tr.setdefault(t, {})[o.stream] = o
        for t in writes:
            self.tw[t] = {o.stream: o}
            self.tr[t] = {}
        for t in pwrites:
            self.tw.setdefault(t, {})[o.stream] = o
        if o.is_dma:
            c = self.dma_cnt.get(dma, 0) + 16
            self.dma_cnt[dma] = c
            o.sigval = c
        self.last[o.stream] = o
        self.q[eng].append(o)
        return o

    def emit(self, nc, stack):
        for e in self.ENGS:
            c = 0
            for o in self.q[e]:
                if not o.is_dma and o.needed:
                    c += 1
                    o.sigval = c
        sems = {}
        for e in self.ENGS:
            sems[("eng", e)] = stack.enter_context(nc.semaphore("s_" + e))
        for d in self.dma_cnt:
            sems[("dma", d)] = stack.enter_context(nc.semaphore("d_" + str(d)))
        block = stack.enter_context(nc.Block())
        q = self.q

        def run(ename, eng):
            seen = {}
            for o in q[ename]:
                for d in o.deps:
                    v = d.sigval
                    if seen.get(d.stream, 0) >= v:
                        continue
                    seen[d.stream] = v
                    eng.wait_ge(sems[d.stream], v)
                ins = o.fn(eng)
                if o.needed:
                    ins.then_inc(sems[o.stream], 16 if o.is_dma else 1)

        @block.tensor
        def _(eng):
            run("pe", eng)

        @block.scalar
        def _(eng):
            run("act", eng)

        @block.vector
        def _(eng):
            run("dve", eng)

        @block.gpsimd
        def _(eng):
            run("pool", eng)

        @block.sync
        def _(eng):
            run("sp", eng)
            for d, c in self.dma_cnt.items():
                eng.wait_ge(sems[("dma", d)], c)


class Builder:
    def __init__(self, T, layers):
        self.T = T
        self.NT = T // 128
        self.NC = T // 16
        self.NCT = max(1, self.NC // 128)
        self.layers = layers
        self.nc = bass.Bass("TRN2", target_bir_lowering=False)
        self.S = Sched()
        self.uid = 0

    def E(self, eng, meth, *a, r=(), w=(), pw=(), **kw):
        self.S.op(eng, lambda e: getattr(e, meth)(*a, **kw), reads=r, writes=w, pwrites=pw)

    def dma(self, out, in_, sem, r=(), w=(), pw=(), q="sp", **kw):
        self.S.op(q, lambda e: e.dma_start(out=out, in_=in_, **kw), reads=r, writes=w, pwrites=pw, dma=sem)

    def mm(self, out, lhsT, rhs, start, stop, r=(), w=()):
        self.S.op("pe", lambda e: e.matmul(out, lhsT=lhsT, rhs=rhs, start=start, stop=stop), reads=r, writes=w)

    def tr(self, out, in_, r=(), w=()):
        ident = self.ident
        self.S.op("pe", lambda e: e.transpose(out=out, in_=in_, identity=ident[:]), reads=tuple(r) + ("ident",), writes=w)

    def act(self, out, in_, func, r=(), w=(), pw=(), **kw):
        self.S.op("act", lambda e: e.activation(out=out, in_=in_, func=func, **kw), reads=r, writes=w, pwrites=pw)

    def persist(self, name, shape, dt):
        return self.st.enter_context(self.nc.sbuf_tensor(name, shape, dt))

    def phase(self):
        self.S.barrier()
        self.aoff = 0
        self.pid = getattr(self, "pid", 0) + 1

    def ph(self, name, shape, dt):
        esz = 4 if dt in (F32, I32) else 2
        n = 1
        for s_ in shape[1:]:
            n *= s_
        nbytes = (n * esz + 63) // 64 * 64
        a = self.aoff
        self.aoff += nbytes
        assert self.aoff <= self.ARENA_BYTES, (name, self.aoff)
        v = self.arena[:, a // 2:(a + n * esz) // 2]
        if esz == 4:
            v = v.bitcast(dt)
        if len(shape) == 3:
            v = v.rearrange("p (a b) -> p a b", b=shape[2])
        elif len(shape) == 4:
            v = v.rearrange("p (a b c) -> p a b c", b=shape[2], c=shape[3])
        return v, ("ph", self.pid, name)

    def build(self):
        nc, T, NT = self.nc, self.T, self.NT
        dt = nc.dram_tensor
        self.x_in = dt("x", [T, D], F32, kind="ExternalInput").ap()
        self.p_in = dt("p", [4, T, PLE], F32, kind="ExternalInput").ap()
        self.norm_gain = dt("norm_gain", [4, D], F32, kind="ExternalInput").ap()
        self.moba_w_in = dt("moba_w_in", [2, D, 4 * D], F32, kind="ExternalInput").ap()
        self.moba_q_gain = dt("moba_q_gain", [2, HD], F32, kind="ExternalInput").ap()
        self.moba_k_gain = dt("moba_k_gain", [2, HD], F32, kind="ExternalInput").ap()
        self.moba_w_out = dt("moba_w_out", [2, D, D], F32, kind="ExternalInput").ap()
        self.nsa_w_in = dt("nsa_w_in", [2, D, NSA_IN], F32, kind="ExternalInput").ap()
        self.nsa_q_gain = dt("nsa_q_gain", [2, HD], F32, kind="ExternalInput").ap()
        self.nsa_k_gain = dt("nsa_k_gain", [2, 3, HD], F32, kind="ExternalInput").ap()
        self.nsa_cmp_pe = dt("nsa_cmp_pe", [2, 2, 32, HD], F32, kind="ExternalInput").ap()
        self.nsa_cmp_w1 = dt("nsa_cmp_w1", [2, 2, 2048, 256], F32, kind="ExternalInput").ap()
        self.nsa_cmp_w2 = dt("nsa_cmp_w2", [2, 2, 256, HD], F32, kind="ExternalInput").ap()
        self.nsa_w_out = dt("nsa_w_out", [2, D, D], F32, kind="ExternalInput").ap()
        self.ple_w_proj = dt("ple_w_proj", [4, PLE, D], F32, kind="ExternalInput").ap()
        self.ple_gate_gain = dt("ple_gate_gain", [4, D], F32, kind="ExternalInput").ap()
        self.ple_w_gate = dt("ple_w_gate", [4, D, D], F32, kind="ExternalInput").ap()
        self.rope_cs = dt("rope_cs", [2, 128, NT, 32], F32, kind="ExternalInput").ap()
        self.rope_cs_c = dt("rope_cs_c", [2, 128, self.NCT, 32], F32, kind="ExternalInput").ap()
        self.y = dt("y", [T, D], F32, kind="ExternalOutput").ap()
        self.qT_d = dt("qT_d", [D, T], BF16).ap()
        self.kT_d = dt("kT_d", [D, T], BF16).ap()
        self.v_d = dt("v_d", [T, D], BF16).ap()
        self.zs_d = dt("zs_d", [T, D], BF16).ap()

        with ExitStack() as st:
            self.st = st
            self.ARENA_BYTES = 100 * 1024
            self.arena = self.persist("arena", [128, self.ARENA_BYTES // 2], BF16)
            self.big = self.persist("big", [128, 32768], BF16)
            self.ident = self.persist("ident", [128, 128], BF16)
            self.tri = self.persist("tri", [128, 128], BF16)
            self.atri = self.persist("atri", [128, 128], BF16)
            self.cmpm = self.persist("cmpm", [128, 17, 128], BF16)
            self.cs = self.persist("cs", [128, 2, NT, 32], F32)
            self.csc = self.persist("csc", [128, 2, self.NCT, 32], F32)
            self.g_sb = self.persist("g_sb", [128, NT, 48], F32)
            self.msel = self.persist("msel", [128, NT, 64], F32)
            self.kcTa = self.persist("kcTa", [128, 4, self.NC], BF16)
            self.vca = self.persist("vca", [128, 4, self.NCT, 128], BF16)
            self.gq = self.persist("gq", [128, HD], F32)
            self.gk = self.persist("gk", [128, 3, HD], F32)
            self.gcol = self.persist("gcol", [128, 8], F32)
            self.gcol2 = self.persist("gcol2", [128, 8], F32)
            self.pf = [st.enter_context(nc.psum_tensor(f"pf{i}", [128, 512], F32)) for i in range(6)]
            self.pb = [st.enter_context(nc.psum_tensor(f"pb{i}", [128, 1024], BF16)) for i in range(2)]
            self.consts()
            first = True
            for li in self.layers:
                src = self.x_in if first else self.y
                first = False
                if li % 2 == 0:
                    self.phase_P(li, src, moba=True)
                    self.phase_A_moba(li)
                else:
                    import os
                    stop = os.environ.get("NSA_STOP", "")
                    self.stop = stop
                    self.phase_P(li, src, moba=False)
                    if stop == "P":
                        break
                    self.phase_C(li)
                    if stop == "C":
                        break
                    self.phase_A_nsa(li)
                    if stop in ("cmp", "sel", "A"):
                        break
                self.phase_O(li, src)
            self.S.emit(nc, st)
        return nc

    def consts(self):
        self.phase()
        NT = self.NT
        tf, tft = self.ph("c_tf", [128, 128], F32)
        self.E("pool", "memset", tf, 1.0, w=[tft])
        self.E("pool", "affine_select", out=tf, in_=tf, pattern=[[1, 128]], compare_op=ALU.is_equal, fill=0.0,
               base=0, channel_multiplier=-1, r=[tft], w=[tft])
        self.E("dve", "tensor_copy", out=self.ident[:], in_=tf, r=[tft], w=["ident"])
        tf2, tf2t = self.ph("c_tf2", [128, 128], F32)
        self.E("pool", "memset", tf2, 0.0, w=[tf2t])
        self.E("pool", "affine_select", out=tf2, in_=tf2, pattern=[[1, 128]], compare_op=ALU.is_ge, fill=-NEGB,
               base=0, channel_multiplier=-1, r=[tf2t], w=[tf2t])
        self.E("dve", "tensor_copy", out=self.tri[:], in_=tf2, r=[tf2t], w=["tri"])
        tf3, tf3t = self.ph("c_tf3", [128, 128], F32)
        self.E("pool", "memset", tf3, 0.0, w=[tf3t])
        self.E("pool", "affine_select", out=tf3, in_=tf3, pattern=[[-1, 128]], compare_op=ALU.is_ge, fill=-NEGB,
               base=-1, channel_multiplier=1, r=[tf3t], w=[tf3t])
        self.E("dve", "tensor_copy", out=self.atri[:], in_=tf3, r=[tf3t], w=["atri"])
        tf4, tf4t = self.ph("c_tf4", [128, 17, 128], F32)
        self.E("pool", "memset", tf4, 0.0, w=[tf4t])
        self.E("pool", "affine_select", out=tf4, in_=tf4, pattern=[[128, 17], [1, 128]], compare_op=ALU.is_ge,
               fill=-NEGB, base=-31, channel_multiplier=-16, r=[tf4t], w=[tf4t])
        self.E("dve", "tensor_copy", out=self.cmpm[:], in_=tf4, r=[tf4t], w=["cmpm"])
        self.dma(self.cs[:, 0], self.rope_cs[0], "c_cs", w=["cs"])
        self.dma(self.cs[:, 1], self.rope_cs[1], "c_cs2", pw=["cs"])
        self.dma(self.csc[:, 0], self.rope_cs_c[0], "c_csc", w=["csc"])
        self.dma(self.csc[:, 1], self.rope_cs_c[1], "c_csc2", pw=["csc"])
        m = self.msel
        self.E("pool", "memset", m[:], 0.0, w=["msel"])
        for half in range(2):
            mh = m[half * 64:(half + 1) * 64]
            self.E("pool", "affine_select", out=mh, in_=mh, pattern=[[2, NT], [-1, 64]], compare_op=ALU.is_ge,
                   fill=1.0e4, base=half - 2, channel_multiplier=0, r=["msel"], w=["msel"])
            self.E("pool", "affine_select", out=mh, in_=mh, pattern=[[2, NT], [-1, 64]], compare_op=ALU.is_ge,
                   fill=-1.0e4, base=half, channel_multiplier=0, r=["msel"], w=["msel"])
        self.E("pool", "memset", m[:, :, 0:1], 1.0e4, r=["msel"], w=["msel"])
        ov, ovt = self.ph("c_ov", [128, self.NCT, 64], F32)
        self.E("pool", "memset", ov, 1.0, w=[ovt])
        for ct in range(self.NCT):
            o1 = ov[:, ct, :]
            self.E("pool", "affine_select", out=o1, in_=o1, pattern=[[4, 64]], compare_op=ALU.is_ge, fill=0.0,
                   base=3 - 128 * ct, channel_multiplier=-1, r=[ovt], w=[ovt])
            self.E("pool", "affine_select", out=o1, in_=o1, pattern=[[-4, 64]], compare_op=ALU.is_ge, fill=0.0,
                   base=1 + 128 * ct, channel_multiplier=1, r=[ovt], w=[ovt])
        self.E("dve", "memset", self.vca[:], 1.0, w=["vca"])
        for g in range(4):
            for ct in range(self.NCT):
                self.E("dve", "tensor_copy", out=self.vca[:, g, ct, 65:128], in_=ov[:, ct, 0:63], r=[ovt, "vca"], w=["vca"])
        self.E("dve", "memset", self.kcTa[:], 0.0, w=["kcTa"])

    def load_gcol(self, dst, tok, src_row):
        self.dma(dst[:], src_row.rearrange("(k p) -> p k", p=128), "gcol_" + tok, w=[tok],
                 allow_slow_non_contiguous=True)

    def load_w(self, dst3, dtok, w_ap, ncols, gcol, gtok, nk=8):
        if getattr(self, "_stg_pid", None) != self.pid:
            self._stg = [self.ph(f"wstg{i}", [128, 512], F32) for i in range(6)]
            self._stg_pid = self.pid
        stg = self._stg
        i = 0
        first = True
        for k in range(nk):
            for c0 in range(0, ncols, 512):
                cw = min(512, ncols - c0)
                s_ap, s_tok = stg[i % 6]
                self.dma(s_ap[:, 0:cw], w_ap[k * 128:(k + 1) * 128, c0:c0 + cw], f"wst{i % 6}", w=[s_tok])
                kw = dict(r=[s_tok] + ([gtok] if gcol is not None else []))
                if first:
                    kw["w"] = [dtok]
                    first = False
                else:
                    kw["pw"] = [dtok]
                if i % 2 == 0:
                    if gcol is not None:
                        self.E("dve", "tensor_scalar", out=dst3[:, k, c0:c0 + cw], in0=s_ap[:, 0:cw],
                               scalar1=gcol[:, k:k + 1], scalar2=None, op0=ALU.mult, **kw)
                    else:
                        self.E("dve", "tensor_copy", out=dst3[:, k, c0:c0 + cw], in_=s_ap[:, 0:cw], **kw)
                else:
                    if gcol is not None:
                        self.act(dst3[:, k, c0:c0 + cw], s_ap[:, 0:cw], AF.Copy, scale=gcol[:, k:k + 1], **kw)
                    else:
                        self.act(dst3[:, k, c0:c0 + cw], s_ap[:, 0:cw], AF.Copy, **kw)
                i += 1

    def rstd_of(self, x_ap, xtok, n, scratch, stok, out_col, otok):
        self.act(scratch, x_ap, AF.Square, r=[xtok], w=[stok, otok], accum_out=out_col)
        self.act(out_col, out_col, AF.Sqrt, r=[otok], w=[otok], scale=1.0 / n, bias=EPS)
        self.E("dve", "reciprocal", out=out_col, in_=out_col, r=[otok], w=[otok])

    def transpose8(self, src_bf, stok, dstT, dtok, pbi, nblk=8):
        pbt = self.pb[pbi]
        ptok = ("pb", pbi)
        for k in range(nblk):
            kw = dict(w=[ptok]) if k == 0 else dict(w=())
            if k == 0:
                self.tr(pbt[:, k * 128:(k + 1) * 128], src_bf[:, k * 128:(k + 1) * 128], r=[stok], w=[ptok])
            else:
                self.S.op("pe", (lambda e, k=k: e.transpose(out=pbt[:, k * 128:(k + 1) * 128], in_=src_bf[:, k * 128:(k + 1) * 128],
                                                            identity=self.ident[:])), reads=[stok, "ident"], pwrites=[ptok])
        self.act(dstT.rearrange("p a b -> p (a b)") if len(dstT.shape) == 3 else dstT, pbt[:, 0:nblk * 128], AF.Copy, r=[ptok], w=[dtok])

    def norm_rope(self, src, stok, nh, gain_ap, gtok, cos_ap, sin_ap, cstok, out_bf, otok, tmp):
        (sq, sqt), (ss, sst), (A, At), (Bt_, Btt), (t13, t13t), (tsw, tswt) = tmp
        W = nh * 64
        v3 = lambda ap: ap[:, 0:W].rearrange("p (h d) -> p h d", d=64)
        v4 = lambda ap: ap[:, 0:W].rearrange("p (h a d) -> p h a d", a=2, d=32)
        self.act(sq[:, 0:W], src, AF.Square, r=[stok], w=[sqt])
        self.E("dve", "tensor_reduce", out=ss[:, 0:nh], in_=v3(sq), axis=AX.X, op=ALU.add, r=[sqt], w=[sst])
        self.act(ss[:, 0:nh], ss[:, 0:nh], AF.Sqrt, r=[sst], w=[sst], scale=1.0 / 64, bias=EPS)
        self.E("dve", "reciprocal", out=ss[:, 0:nh], in_=ss[:, 0:nh], r=[sst], w=[sst])
        self.E("dve", "tensor_tensor", out=v3(A), in0=src.rearrange("p (h d) -> p h d", d=64),
               in1=ss[:, 0:nh].unsqueeze(2).to_broadcast([128, nh, 64]), op=ALU.mult, r=[stok, sst], w=[At])
        self.E("dve", "tensor_tensor", out=v3(Bt_), in0=v3(A), in1=gain_ap.unsqueeze(1).to_broadcast([128, nh, 64]),
               op=ALU.mult, r=[At, gtok], w=[Btt])
        cosb = cos_ap.unsqueeze(1).unsqueeze(1).to_broadcast([128, nh, 2, 32])
        sinb = sin_ap.unsqueeze(1).to_broadcast([128, nh, 32])
        self.E("dve", "tensor_tensor", out=v4(t13), in0=v4(Bt_), in1=cosb, op=ALU.mult, r=[Btt, cstok], w=[t13t])
        self.E("dve", "tensor_tensor", out=v4(tsw)[:, :, 0, :], in0=v4(Bt_)[:, :, 1, :], in1=sinb, op=ALU.mult,
               r=[Btt, cstok], w=[tswt])
        self.E("dve", "tensor_tensor", out=v4(tsw)[:, :, 1, :], in0=v4(Bt_)[:, :, 0, :], in1=sinb, op=ALU.mult,
               r=[Btt, cstok, tswt], w=[tswt])
        self.E("dve", "tensor_tensor", out=v4(out_bf)[:, :, 0, :], in0=v4(t13)[:, :, 0, :], in1=v4(tsw)[:, :, 0, :],
               op=ALU.subtract, r=[t13t, tswt], w=[otok])
        self.E("dve", "tensor_tensor", out=v4(out_bf)[:, :, 1, :], in0=v4(t13)[:, :, 1, :], in1=v4(tsw)[:, :, 1, :],
               op=ALU.add, r=[t13t, tswt, otok], w=[otok])

    def phase_P(self, li, src, moba):
        self.phase()
        NT = self.NT
        j = li // 2
        ncols = 4 * D if moba else NSA_IN
        w_ap = (self.moba_w_in if moba else self.nsa_w_in)[j]
        W3 = self.big[:, 0:8 * 4096].rearrange("p (k n) -> p k n", n=4096)
        self.load_gcol(self.gcol, "gcol", self.norm_gain[li])
        self.load_w(W3, "big", w_ap, ncols, self.gcol, "gcol")
        qg = (self.moba_q_gain if moba else self.nsa_q_gain)[j]
        self.dma(self.gq[:], qg.partition_broadcast(128), "gq", w=["gq"])
        if moba:
            self.dma(self.gk[:, 0, :], self.moba_k_gain[j].partition_broadcast(128), "gk", w=["gk"])
        else:
            for b_ in range(3):
                self.dma(self.gk[:, b_, :], self.nsa_k_gain[j, b_].partition_broadcast(128), f"gk{b_}",
                         **(dict(w=["gk"]) if b_ == 0 else dict(pw=["gk"])))
        xt = [self.ph(f"xt{i}", [128, D], F32) for i in range(2)]
        sqs = self.ph("sqs", [128, D], F32)
        rs = [self.ph(f"rs{i}", [128, 1], F32) for i in range(2)]
        hb = [self.ph(f"hb{i}", [128, D], BF16) for i in range(2)]
        hT = [self.ph(f"hT{i}", [128, 8, 128], BF16) for i in range(2)]
        b1, b1t = self.ph("nr_b1", [128, 2048], F32)
        b2, b2t = self.ph("nr_b2", [128, 2048], F32)
        b3, b3t = self.ph("nr_b3", [128, 2048], F32)
        ss, sst = self.ph("nr_ss", [128, 32], F32)
        qbf = [self.ph(f"qbf{i}", [128, 2048], BF16) for i in range(2)]
        qTt = [self.ph(f"qTt{i}", [128, 16, 128], BF16) for i in range(2)]
        vbf = [self.ph(f"vbf{i}", [128, D], BF16) for i in range(2)]
        zbf = [self.ph(f"zbf{i}", [128, D], BF16) for i in range(2)]
        rawbf = [self.ph(f"rawbf{i}", [128, 512], BF16) for i in range(2)]
        if moba:
            chunks = [(c * 512, 512) for c in range(8)]
            HT_ = 32
            srcs = [(0, 0, 8, self.gq[:], "gq"), (1, 0, 8, self.gq[:], "gq"), (2, 0, 8, self.gk[:, 0, :], "gk"),
                    (3, 0, 8, self.gk[:, 0, :], "gk")]
            qk_chunks = [(0, 0), (1, 1), (2, 2), (3, 3)]
            dests = [(self.qT_d, jb * 128) for jb in range(8)] + [(self.kT_d, jb * 128) for jb in range(8)]
        else:
            chunks = [(0, 512), (512, 512), (1024, 512), (1536, 512), (2048, 512), (2560, 48), (2608, 512), (3120, 512)]
            HT_ = 24
            srcs = [(0, 0, 8, self.gq[:], "gq"), (1, 0, 8, self.gq[:], "gq"), (2, 0, 4, self.gk[:, 1, :], "gk"),
                    (3, 0, 4, self.gk[:, 2, :], "gk")]
            qk_chunks = [(0, 0), (1, 1), (3, 2), (4, 3), (2, 4)]
            dests = ([(self.qT_d, jb * 128) for jb in range(8)] + [(self.kT_d, 0), (self.kT_d, 128), (self.kT_d, 256),
                     (self.kT_d, 384)] + [(self.kT_d, 512 + jb * 128) for jb in range(4)])
        WQ = HT_ * 64

        def pre_a(t):
            x_ap, xtok = xt[t % 2]
            self.dma(x_ap, src[t * 128:(t + 1) * 128, :], f"xld{t % 2}", r=[("y", t)], w=[xtok])
            r_ap, rtok = rs[t % 2]
            self.rstd_of(x_ap, xtok, D, sqs[0], sqs[1], r_ap, rtok)
            h_ap, htok = hb[t % 2]
            self.act(h_ap, x_ap, AF.Copy, r=[xtok, rtok], w=[htok], scale=r_ap[:, 0:1])

        def pre_b(t):
            h_ap, htok = hb[t % 2]
            hT_ap, hTtok = hT[t % 2]
            self.transpose8(h_ap, htok, hT_ap, hTtok, 0)

        def mm_chunk(t, ci, bank):
            c0, cw = chunks[ci]
            hT_ap, hTtok = hT[t % 2]
            ps = self.pf[bank]
            pstok = ("pf", bank)
            for k in range(8):
                self.mm(ps[:, 0:cw], hT_ap[:, k, :], W3[:, k, c0:c0 + cw], k == 0, k == 7, r=[hTtok, "big"],
                        w=[pstok] if k == 0 else ())
            last = self.S.q["pe"][-1]
            self.S.tw[pstok] = {last.stream: last}

        def chain(t):
            q_ap, qbt = qbf[t % 2]
            v3 = lambda ap, c0, w: ap[:, c0:c0 + w].rearrange("p (h d) -> p h d", d=64)
            v4 = lambda ap, c0, w: ap[:, c0:c0 + w].rearrange("p (h a d) -> p h a d", a=2, d=32)
            c = 0
            for n_, (bank, col0, nh, g_ap, g_tok) in enumerate(srcs):
                w_ = nh * 64
                self.act(b1[:, c:c + w_], self.pf[bank][:, col0:col0 + w_], AF.Square, r=[("pf", bank)],
                         **(dict(w=[b1t]) if n_ == 0 else dict(pw=[b1t])))
                c += w_
            self.E("dve", "tensor_reduce", out=ss[:, 0:HT_], in_=v3(b1, 0, WQ), axis=AX.X, op=ALU.add, r=[b1t], w=[sst])
            self.act(ss[:, 0:HT_], ss[:, 0:HT_], AF.Sqrt, r=[sst], w=[sst], scale=1.0 / 64, bias=EPS)
            self.E("dve", "reciprocal", out=ss[:, 0:HT_], in_=ss[:, 0:HT_], r=[sst], w=[sst])
            c = 0
            h0 = 0
            for n_, (bank, col0, nh, g_ap, g_tok) in enumerate(srcs):
                w_ = nh * 64
                self.E("dve", "tensor_tensor", out=v3(b2, c, w_),
                       in0=self.pf[bank][:, col0:col0 + w_].rearrange("p (h d) -> p h d", d=64),
                       in1=ss[:, h0:h0 + nh].unsqueeze(2).to_broadcast([128, nh, 64]), op=ALU.mult,
                       r=[("pf", bank), sst], **(dict(w=[b2t]) if n_ == 0 else dict(pw=[b2t])))
                c += w_
                h0 += nh
            c = 0
            first = True
            i_ = 0
            while i_ < len(srcs):
                g_ap, g_tok = srcs[i_][3], srcs[i_][4]
                nh = srcs[i_][2]
                k_ = i_ + 1
                while k_ < len(srcs) and srcs[k_][3] is g_ap:
                    nh += srcs[k_][2]
                    k_ += 1
                w_ = nh * 64
                self.E("dve", "tensor_tensor", out=v3(b1, c, w_), in0=v3(b2, c, w_),
                       in1=g_ap.unsqueeze(1).to_broadcast([128, nh, 64]), op=ALU.mult, r=[b2t, g_tok],
                       **(dict(w=[b1t]) if first else dict(pw=[b1t])))
                first = False
                c += w_
                i_ = k_
            import os
            if "norope" in os.environ.get("PDBG", ""):
                return
            cos_ap = self.cs[:, 0, t, :]
            sin_ap = self.cs[:, 1, t, :]
            cosb = cos_ap.unsqueeze(1).unsqueeze(1).to_broadcast([128, HT_, 2, 32])
            sinb = sin_ap.unsqueeze(1).to_broadcast([128, HT_, 32])
            self.E("dve", "tensor_tensor", out=v4(b2, 0, WQ), in0=v4(b1, 0, WQ), in1=cosb, op=ALU.mult, r=[b1t, "cs"], w=[b2t])
            self.E("dve", "tensor_tensor", out=v4(b3, 0, WQ)[:, :, 0, :], in0=v4(b1, 0, WQ)[:, :, 1, :], in1=sinb, op=ALU.mult,
                   r=[b1t, "cs"], w=[b3t])
            self.E("dve", "tensor_tensor", out=v4(b3, 0, WQ)[:, :, 1, :], in0=v4(b1, 0, WQ)[:, :, 0, :], in1=sinb, op=ALU.mult,
                   r=[b1t, "cs"], pw=[b3t])
            self.E("dve", "tensor_tensor", out=v4(q_ap, 0, WQ)[:, :, 0, :], in0=v4(b2, 0, WQ)[:, :, 0, :],
                   in1=v4(b3, 0, WQ)[:, :, 0, :], op=ALU.subtract, r=[b2t, b3t], w=[qbt])
            self.E("dve", "tensor_tensor", out=v4(q_ap, 0, WQ)[:, :, 1, :], in0=v4(b2, 0, WQ)[:, :, 1, :],
                   in1=v4(b3, 0, WQ)[:, :, 1, :], op=ALU.add, r=[b2t, b3t], pw=[qbt])
            import os
            if not moba and "noraw" not in os.environ.get("PDBG", ""):
                self.E("dve", "tensor_copy", out=rawbf[t % 2][0], in_=self.pf[4][:, 0:512], r=[("pf", 4)], w=[rawbf[t % 2][1]])

        def tq(t):
            q_ap, qbt = qbf[t % 2]
            qt_, qtt = qTt[t % 2]
            for half in range(2):
                pbt = self.pb[1]
                ptok = ("pb", 1)
                for k in range(8):
                    jb = half * 8 + k
                    if (not moba) and jb >= 12:
                        s_ap, s_tok = rawbf[t % 2]
                        s_in = s_ap[:, (jb - 12) * 128:(jb - 11) * 128]
                    else:
                        s_tok = qbt
                        s_in = q_ap[:, jb * 128:(jb + 1) * 128]
                    if k == 0:
                        self.tr(pbt[:, 0:128], s_in, r=[s_tok], w=[ptok])
                    else:
                        self.S.op("pe", (lambda e, k=k, s_in=s_in: e.transpose(out=pbt[:, k * 128:(k + 1) * 128], in_=s_in,
                                                                              identity=self.ident[:])),
                                  reads=[s_tok, "ident"], pwrites=[ptok])
                self.act(qt_[:, half * 8:(half + 1) * 8, :].rearrange("p a b -> p (a b)"), pbt[:, 0:1024], AF.Copy, r=[ptok],
                         **(dict(w=[qtt]) if half == 0 else dict(pw=[qtt])))
            jb = 0
            while jb < 16:
                dram, row0 = dests[jb]
                k_ = jb + 1
                while k_ < 16 and dests[k_][0] is dram and dests[k_][1] == row0 + (k_ - jb) * 128:
                    k_ += 1
                nb = k_ - jb
                self.dma(dram[row0:row0 + nb * 128, t * 128:(t + 1) * 128].rearrange("(j p) t -> p j t", p=128),
                         qt_[:, jb:k_, :], f"qst{t % 2}", r=[qtt], pw=[("kTq_d",)], q="pool")
                jb = k_

        def mm_vz(t):
            v_ap, vtok = vbf[t % 2]
            z_ap, ztok = zbf[t % 2]
            if moba:
                for ci in (4, 5):
                    bank = ci
                    mm_chunk(t, ci, bank)
                    self.act(v_ap[:, (ci - 4) * 512:(ci - 3) * 512], self.pf[bank][:, 0:512], AF.Copy, r=[("pf", bank)],
                             **(dict(w=[vtok]) if ci == 4 else dict(pw=[vtok])))
                self.dma(self.v_d[t * 128:(t + 1) * 128, :], v_ap, f"vst{t % 2}", r=[vtok], pw=[("v_d",)], q="pool")
            else:
                self.act(v_ap[:, 0:256], self.pf[2][:, 256:512], AF.Copy, r=[("pf", 2)], w=[vtok])
                self.act(v_ap[:, 256:512], self.pf[3][:, 256:512], AF.Copy, r=[("pf", 3)], pw=[vtok])
                self.dma(self.v_d[t * 128:(t + 1) * 128, 0:512], v_ap[:, 0:512], f"vst{t % 2}", r=[vtok], pw=[("v_d",)], q="pool")
                mm_chunk(t, 5, 5)
                self.act(self.g_sb[:, t, :], self.pf[5][:, 0:48], AF.Sigmoid, r=[("pf", 5)], pw=["g_sb"])
            for ci in (6, 7):
                bank = ci - 2
                mm_chunk(t, ci, bank)
                self.act(z_ap[:, (ci - 6) * 512:(ci - 5) * 512], self.pf[bank][:, 0:512], AF.Silu, r=[("pf", bank)],
                         **(dict(w=[ztok]) if ci == 6 else dict(pw=[ztok])))
            self.dma(self.zs_d[t * 128:(t + 1) * 128, :], z_ap, f"zst{t % 2}", r=[ztok], pw=[("zs_d",)], q="pool")

        import os
        dbg = os.environ.get("PDBG", "")
        pre_a(0)
        pre_b(0)
        for t in range(NT):
            for (ci, bank) in qk_chunks:
                mm_chunk(t, ci, bank)
            if t + 1 < NT:
                pre_a(t + 1)
            if "nochain" not in dbg:
                chain(t)
            if t + 1 < NT:
                pre_b(t + 1)
            if "novz" not in dbg:
                mm_vz(t)
            if t >= 1 and "notq" not in dbg:
                tq(t - 1)
        if "notq" not in dbg:
            tq(NT - 1)

    def attention(self, qTa, qtok, kTa, ktok, Va, vtok, vw, pairs_fn, fin, pts, tag, side=None, side_delay=0):
        NT = self.NT
        qtoks = list(qtok) if isinstance(qtok, list) else [qtok]
        allp = []
        for qt in range(NT):
            prs = pairs_fn(qt)
            for j_, (kt, bias) in enumerate(prs):
                allp.append((qt, kt, bias, j_ == 0, j_ == len(prs) - 1))
        units = []
        cur = []
        for p_ in allp:
            qts = []
            for x_ in cur:
                if x_[0] not in qts:
                    qts.append(x_[0])
            if len(cur) == 4 or (p_[0] not in qts and len(qts) == 2):
                units.append(cur)
                cur = []
            cur.append(p_)
        if cur:
            units.append(cur)
        n = len(units)
        NBK = len(pts)
        SK = NBK - 1
        banks = [0, 1, 2, 3][:NBK]
        u0 = getattr(self, "_u", 0)

        def qk(i):
            grp = units[i]
            si = banks[(u0 + i) % NBK]
            ps = self.pf[si]
            pstok = ("pf", si)
            firstw = True
            for jx, (qt, kt, bias, _, _) in enumerate(grp):
                o_ap = ps[:, jx * 128:(jx + 1) * 128]
                self.mm(o_ap, kTa[:, kt * 128:(kt + 1) * 128], qTa[:, qt * 128:(qt + 1) * 128], True, bias is None,
                        r=[ktok] + qtoks, w=[pstok] if firstw else ())
                firstw = False
                if bias is not None:
                    self.mm(o_ap, self.ident[:], bias, False, True, r=["ident", "tri", "atri", "cmpm"])
            last = self.S.q["pe"][-1]
            self.S.tw[pstok] = {last.stream: last}

        def pv(i):
            grp = units[i]
            si = banks[(u0 + i) % NBK]
            ps = self.pf[si]
            pstok = ("pf", si)
            pt_ap, pttok = pts[(u0 + i) % NBK]
            nn = len(grp) * 128
            self.act(pt_ap[:, 0:nn], ps[:, 0:nn], AF.Exp, r=[pstok], w=[pttok], scale=0.125)
            done = []
            for jx, (qt, kt, bias, isf, isl) in enumerate(grp):
                po = self.pf[4 + qt % 2]
                potok = ("pf", 4 + qt % 2)
                self.mm(po[:, 0:vw], pt_ap[:, jx * 128:(jx + 1) * 128], Va[:, kt, 0:vw], isf, isl,
                        r=[pttok, vtok], w=[potok] if isf else ())
                last = self.S.q["pe"][-1]
                self.S.tw[potok] = {last.stream: last}
                if isl:
                    done.append(qt)
            for qt in done:
                fin(qt, self.pf[4 + qt % 2], ("pf", 4 + qt % 2))

        for i in range(n + SK):
            if i < n:
                qk(i)
            if i >= SK:
                pv(i - SK)
                if side is not None and i - SK >= side_delay:
                    next(side, None)
        self._u = u0 + n

    def phase_A_moba(self, li):
        self.phase()
        T, NT = self.T, self.NT
        NB = T // 256
        o_all = self.big[:, 0:NT * D].rearrange("p (t f) -> p t f", f=D)
        qTa = [self.ph(f"qTa{i}", [128, T], BF16) for i in range(2)]
        kTa = [self.ph(f"kTa{i}", [128, T], BF16) for i in range(2)]
        Va = [self.ph(f"Va{i}", [128, NT, 65], BF16) for i in range(2)]
        pts = [self.ph(f"pt{i}", [128, 512], BF16) for i in range(4)]
        kmf, kmft = self.ph("kmf", [128, 16], F32)
        kmT, kmTt = self.ph("kmT", [128, 16], BF16)
        gt_, gtt = self.ph("gate", [128, 16], F32)
        m8, m8t = self.ph("m8", [128, 8], F32)
        bq, bqt = self.ph("bq", [128, 128], BF16)
        rz, rzt = self.ph("rz", [128, 1], F32)
        self.E("dve", "memset", kmT, 0.0, w=[kmTt])
        self.E("dve", "memset", bq, 0.0, w=[bqt])
        for i in range(2):
            q_ap, qtok = qTa[i]
            k_ap, ktok = kTa[i]
            v_ap, vtok = Va[i]
            self.E("pool", "memset", q_ap, 0.0, w=[qtok])
            self.E("pool", "memset", k_ap[0:64, :], 0.0, w=[ktok])
            ke = k_ap[64:128, :]
            self.E("pool", "memset", ke, NEGB, r=[ktok], w=[ktok])
            self.E("pool", "affine_select", out=ke, in_=ke, pattern=[[1, T]], compare_op=ALU.is_ge, fill=0.0, base=0,
                   channel_multiplier=-256, r=[ktok], w=[ktok])
            self.E("pool", "affine_select", out=ke, in_=ke, pattern=[[-1, T]], compare_op=ALU.is_ge, fill=0.0, base=255,
                   channel_multiplier=256, r=[ktok], w=[ktok])
            self.E("dve", "memset", v_ap, 1.0, w=[vtok])

        def load_head(h):
            i = h % 2
            q_ap, qtok = qTa[i]
            k_ap, ktok = kTa[i]
            v_ap, vtok = Va[i]
            self.dma(q_ap[0:64, :], self.qT_d[h * 64:(h + 1) * 64, :], f"aq{i}", r=[("kTq_d",)], w=[qtok])
            self.dma(k_ap[0:64, :], self.kT_d[h * 64:(h + 1) * 64, :], f"ak{i}", r=[("kTq_d",)], w=[ktok])
            for c in range(0, NT, 8):
                n = min(8, NT - c)
                self.dma(v_ap[:, c:c + n, 0:64],
                         self.v_d[c * 128:(c + n) * 128, h * 64:(h + 1) * 64].rearrange("(t p) d -> p t d", p=128),
                         f"av{i}", r=[("v_d",)], **(dict(w=[vtok]) if c == 0 else dict(pw=[vtok])))

        def gate_steps(h):
            i = h % 2
            q_ap, qtok = qTa[i]
            k_ap, ktok = kTa[i]
            self.E("dve", "tensor_reduce", out=kmf[0:64, 0:NB], in_=k_ap[0:64, :].rearrange("p (n k) -> p n k", k=256),
                   axis=AX.X, op=ALU.add, r=[ktok], w=[kmft])
            self.E("dve", "tensor_scalar", out=kmT[0:64, 0:NB], in0=kmf[0:64, 0:NB], scalar1=1.0 / 256, scalar2=None,
                   op0=ALU.mult, r=[kmft], w=[kmTt])
            self.E("dve", "memset", gt_, -1.0e30, w=[gtt])
            yield
            for qt in range(NT):
                b = qt // 2
                if b <= 3:
                    continue
                pg = self.pb[0][:, 0:32].bitcast(F32)
                self.mm(pg[:, 0:16], q_ap[:, qt * 128:(qt + 1) * 128], kmT, True, True, r=[qtok, kmTt], w=[("pb", 0)])
                self.E("dve", "tensor_copy", out=gt_[:, 0:b], in_=pg[:, 0:b], r=[("pb", 0), gtt], w=[gtt])
                self.E("dve", "max", out=m8, in_=gt_, r=[gtt], w=[m8t])
                self.E("dve", "tensor_scalar", out=bq[:, 64:80], in0=gt_, scalar1=m8[:, 2:3], scalar2=1.0, op0=ALU.is_ge,
                       op1=ALU.subtract, r=[gtt, m8t, bqt], w=[bqt])
                self.E("dve", "memset", bq[:, 64 + b:65 + b], 0.0, r=[bqt], w=[bqt])
                yield
                yield
                pbt = self.pb[1]
                self.tr(pbt[:, 0:128], bq, r=[bqt], w=[("pb", 1)])
                self.E("dve", "tensor_copy", out=q_ap[64:128, qt * 128:(qt + 1) * 128], in_=pbt[64:128, 0:128],
                       r=[("pb", 1), qtok], w=[qtok])
                yield

        load_head(0)
        for _ in gate_steps(0):
            pass
        for h in range(H):
            side = None
            if h + 1 < H:
                load_head(h + 1)
                side = gate_steps(h + 1)
            i = h % 2
            q_ap, qtok = qTa[i]
            k_ap, ktok = kTa[i]
            v_ap, vtok = Va[i]

            def pairs_fn(qt):
                return [(kt, (self.tri[:] if kt == qt else None)) for kt in range(qt + 1)]

            def fin(qt, po, potok, h=h):
                self.E("dve", "reciprocal", out=rz, in_=po[:, 64:65], r=[potok], w=[rzt])
                self.E("dve", "tensor_scalar", out=o_all[:, qt, h * 64:(h + 1) * 64], in0=po[:, 0:64], scalar1=rz[:, 0:1],
                       scalar2=None, op0=ALU.mult, r=[potok, rzt], pw=["big"])

            self.attention(q_ap, qtok, k_ap, ktok, v_ap, vtok, 65, pairs_fn, fin, pts, f"m{h}", side=side, side_delay=min(24, self.NT))
            if side is not None:
                for _ in side:
                    pass

    def phase_C(self, li):
        self.phase()
        T, NT = self.T, self.NT
        j = li // 2
        xkp, xkpt = self.ph("xkp", [128, T + 16], BF16)
        xk16, xk16t = self.ph("xk16", [128, 2, 16, T // 16], BF16)
        W1d, W1t = self.ph("W1d", [128, 32, 256], BF16)
        w1s = [self.ph(f"w1s{i}", [128, 8, 256], F32) for i in range(2)]
        W2b, W2t = self.ph("W2b", [128, 2, 64], BF16)
        w2s, w2st = self.ph("w2s", [128, 2, 64], F32)
        peT, peTt = self.ph("peT", [128, 32], BF16)
        pes, pest = self.ph("pes", [128, 32], F32)
        bh, bht = self.ph("bh", [128, 2], F32)
        NC, NCT = self.NC, self.NCT
        xx, xxt = self.ph("xx", [128, NC], F32)
        x2, x2t = self.ph("x2", [128, NC], F32)
        sg, sgt = self.ph("sg", [128, NC], F32)
        gl, glt = self.ph("gl", [128, 2, NC], BF16)
        kcb, kcbt = self.ph("kcb", [128, 128], BF16)
        tmp = [self.ph(n, [128, 64], F32) for n in ("c_sq", "c_ss", "c_A", "c_B", "c_t13", "c_tsw")]
        self.E("pool", "memset", xkp, 0.0, w=[xkpt])
        self.E("pool", "memset", W1d, 0.0, w=[W1t])
        self.E("pool", "memset", peT, 0.0, w=[peTt])
        self.E("pool", "memset", kcb, 0.0, w=[kcbt])
        for kv in range(2):
            w1 = self.nsa_cmp_w1[j, kv].rearrange("(l d) j -> d l j", d=64)
            for c in range(4):
                s_ap, s_tok = w1s[c % 2]
                self.dma(s_ap[0:64], w1[:, c * 8:(c + 1) * 8, :], f"w1s{c % 2}", w=[s_tok])
                self.E("dve", "tensor_copy", out=W1d[0:64, c * 8:(c + 1) * 8, :], in_=s_ap[0:64], r=[s_tok],
                       **(dict(w=[W1t]) if c == 0 else dict(pw=[W1t])))
            self.dma(w2s, self.nsa_cmp_w2[j, kv].rearrange("(c p) d -> p c d", p=128), "w2s", w=[w2st])
            self.E("dve", "tensor_copy", out=W2b, in_=w2s, r=[w2st], w=[W2t])
            for q4 in range(4):
                self.dma(pes[0:64, q4 * 8:(q4 + 1) * 8], self.nsa_cmp_pe[j, kv, q4 * 8:(q4 + 1) * 8, :].rearrange("l d -> d l"),
                         "pes", **(dict(w=[pest]) if q4 == 0 else dict(pw=[pest])), allow_slow_non_contiguous=True)
            self.E("dve", "tensor_copy", out=peT[0:64, :], in_=pes[0:64, :], r=[pest], w=[peTt])
            pbias = self.pf[3]
            for jc in range(2):
                for l in range(32):
                    self.mm(pbias[:, jc:jc + 1], W1d[:, l, jc * 128:(jc + 1) * 128], peT[:, l:l + 1], l == 0, l == 31,
                            r=[W1t, peTt], w=[("pf", 3)] if (l == 0 and jc == 0) else ())
            last = self.S.q["pe"][-1]
            self.S.tw[("pf", 3)] = {last.stream: last}
            self.E("dve", "tensor_copy", out=bh, in_=pbias[:, 0:2], r=[("pf", 3)], w=[bht])
            for g in range(4):
                row0 = 512 + kv * 256 + g * 64
                self.dma(xkp[0:64, 0:T], self.kT_d[row0:row0 + 64, :], "xkp", r=[("kTq_d",)], w=[xkpt])
                self.E("dve", "tensor_copy", out=xk16[:, 0], in_=xkp[:, 0:T].rearrange("p (m r) -> p r m", r=16), r=[xkpt], w=[xk16t])
                self.E("dve", "tensor_copy", out=xk16[:, 1], in_=xkp[:, 16:T + 16].rearrange("p (m r) -> p r m", r=16), r=[xkpt],
                       pw=[xk16t])
                for jc in range(2):
                    ph_ = self.pf[jc]
                    for l in range(32):
                        self.mm(ph_[:, 0:NC], W1d[:, l, jc * 128:(jc + 1) * 128], xk16[:, l // 16, l % 16, :], l == 0, l == 31,
                                r=[W1t, xk16t], w=[("pf", jc)] if l == 0 else ())
                    last = self.S.q["pe"][-1]
                    self.S.tw[("pf", jc)] = {last.stream: last}
                    self.act(xx, ph_[:, 0:NC], AF.Identity, r=[("pf", jc), bht], w=[xxt], bias=bh[:, jc:jc + 1])
                    self.E("dve", "tensor_tensor", out=x2, in0=xx, in1=xx, op=ALU.mult, r=[xxt], w=[x2t])
                    self.E("dve", "tensor_scalar", out=x2, in0=x2, scalar1=0.044715, scalar2=1.0, op0=ALU.mult, op1=ALU.add,
                           r=[x2t], w=[x2t])
                    self.E("dve", "tensor_tensor", out=x2, in0=x2, in1=xx, op=ALU.mult, r=[x2t, xxt], w=[x2t])
                    self.act(sg, x2, AF.Sigmoid, r=[x2t], w=[sgt], scale=1.5957691216057308)
                    self.E("dve", "tensor_tensor", out=gl[:, jc, :], in0=xx, in1=sg, op=ALU.mult, r=[xxt, sgt],
                           **(dict(w=[glt]) if jc == 0 else dict(pw=[glt])))
                for ct in range(NCT):
                    pk = self.pf[2]
                    for jc in range(2):
                        self.mm(pk[:, 0:64], gl[:, jc, ct * 128:(ct + 1) * 128], W2b[:, jc, :], jc == 0, jc == 1,
                                r=[glt, W2t], w=[("pf", 2)] if jc == 0 else ())
                    last = self.S.q["pe"][-1]
                    self.S.tw[("pf", 2)] = {last.stream: last}
                    if kv == 0:
                        self.norm_rope(pk[:, 0:64], ("pf", 2), 1, self.gk[:, 0, :], "gk", self.csc[:, 0, ct, :],
                                       self.csc[:, 1, ct, :], "csc", kcb, kcbt, tmp)
                        pbt = self.pb[1]
                        self.tr(pbt[:, 0:128], kcb, r=[kcbt], w=[("pb", 1)])
                        self.E("dve", "tensor_copy", out=self.kcTa[:, g, ct * 128:(ct + 1) * 128], in_=pbt[:, 0:128],
                               r=[("pb", 1)], pw=["kcTa"])
                    else:
                        self.act(self.vca[:, g, ct, 0:64], pk[:, 0:64], AF.Copy, r=[("pf", 2)], pw=["vca"])

    def phase_A_nsa(self, li):
        self.phase()
        T, NT = self.T, self.NT
        o_all = self.big[:, 0:NT * D].rearrange("p (t f) -> p t f", f=D)
        qTa = [self.ph(f"nqTa{i}", [128, T], BF16) for i in range(4)]
        ksTa, kst = self.ph("ksTa", [128, T], BF16)
        kwTa, kwt = self.ph("kwTa", [128, T], BF16)
        Vs, Vst = self.ph("Vs", [128, NT, 65], BF16)
        Vw, Vwt = self.ph("Vw", [128, NT, 65], BF16)
        pts = [self.ph(f"npt{i}", [128, 512], BF16) for i in range(4)]
        oacc, oacct = self.ph("oacc", [128, NT, 64], F32)
        imp, impt = self.ph("imp", [128, NT, 64], F32)
        sc, sct = self.ph("sc", [128, 64], F32)
        sc2, sc2t = self.ph("sc2", [128, 64], F32)
        m8a, m8at = self.ph("m8a", [128, 8], F32)
        m8b, m8bt = self.ph("m8b", [128, 8], F32)
        bs2 = [self.ph(f"bs{i}", [128, 128], BF16) for i in range(2)]
        rz, rzt = self.ph("nrz", [128, 1], F32)
        cf, cft = self.ph("ncf", [128, 1], F32)
        stage, staget = self.ph("cstage", [128, NT, 128], F32)
        rzb, rzbt = self.ph("rzb", [128, NT], F32)
        cfb, cfbt = self.ph("cfb", [128, NT], F32)
        for hl, (q_ap, qtok) in enumerate(qTa):
            self.E("pool", "memset", q_ap, 0.0, w=[qtok, ("qhi", hl)])
        self.E("pool", "memset", kwTa, 0.0, w=[kwt])
        self.E("pool", "memset", ksTa[0:64, :], 0.0, w=[kst])
        ke = ksTa[64:128, :]
        self.E("pool", "memset", ke, NEGB, r=[kst], w=[kst])
        self.E("pool", "affine_select", out=ke, in_=ke, pattern=[[1, T]], compare_op=ALU.is_ge, fill=0.0, base=0,
               channel_multiplier=-64, r=[kst], w=[kst])
        self.E("pool", "affine_select", out=ke, in_=ke, pattern=[[-1, T]], compare_op=ALU.is_ge, fill=0.0, base=63,
               channel_multiplier=64, r=[kst], w=[kst])
        self.E("dve", "memset", Vs, 1.0, w=[Vst])
        self.E("dve", "memset", Vw, 1.0, w=[Vwt])
        for (b_ap, b_tok) in bs2:
            self.E("dve", "memset", b_ap, 0.0, w=[b_tok])
        self.E("dve", "memset", imp, 0.0, w=[impt])

        for g in range(4):
            for hl in range(4):
                h = g * 4 + hl
                q_ap, qtok = qTa[hl]
                self.dma(q_ap[0:64, :], self.qT_d[h * 64:(h + 1) * 64, :], f"nq{hl}", r=[("kTq_d",)], w=[qtok])
            self.dma(ksTa[0:64, :], self.kT_d[g * 64:(g + 1) * 64, :], "nks", r=[("kTq_d",)], w=[kst])
            self.dma(kwTa[0:64, :], self.kT_d[256 + g * 64:256 + (g + 1) * 64, :], "nkw", r=[("kTq_d",)], w=[kwt])
            for (V_, Vt_, c0, nm) in ((Vs, Vst, g * 64, "nvs"), (Vw, Vwt, 256 + g * 64, "nvw")):
                for c in range(0, NT, 8):
                    n = min(8, NT - c)
                    self.dma(V_[:, c:c + n, 0:64],
                             self.v_d[c * 128:(c + n) * 128, c0:c0 + 64].rearrange("(t p) d -> p t d", p=128),
                             nm, r=[("v_d",)], **(dict(w=[Vt_]) if c == 0 else dict(pw=[Vt_])))

            def cmp_pairs(qt):
                out = []
                for kt in range(self.NCT):
                    dl = qt - 16 * kt
                    if dl < 0:
                        continue
                    out.append((kt, self.cmpm[:, dl, :] if dl <= 16 else None))
                return out

            def sel_A(qt):
                scq = imp[:, qt, :]
                self.E("dve", "max", out=m8a, in_=scq, r=[impt], w=[m8at])
                self.E("dve", "tensor_scalar", out=sc2, in0=scq, scalar1=m8a[:, 7:8], scalar2=-6.0e4, op0=ALU.is_ge, op1=ALU.mult,
                       r=[impt, m8at], w=[sc2t])
                self.E("dve", "tensor_tensor", out=sc2, in0=sc2, in1=scq, op=ALU.add, r=[sc2t, impt], w=[sc2t])
                self.E("dve", "max", out=m8b, in_=sc2, r=[sc2t], w=[m8bt])
                bs, bst = bs2[qt % 2]
                self.E("dve", "tensor_scalar", out=bs[:, 64:128], in0=scq, scalar1=m8b[:, 7:8], scalar2=1.0, op0=ALU.is_ge,
                       op1=ALU.subtract, r=[impt, m8bt, bst], w=[bst])

            def sel_B(qt):
                pbt = self.pb[1]
                bs, bst = bs2[qt % 2]
                self.tr(pbt[:, 0:128], bs, r=[bst], w=[("pb", 1)])
                for hl2 in range(4):
                    q2, _ = qTa[hl2]
                    self.E("dve", "tensor_copy", out=q2[64:128, qt * 128:(qt + 1) * 128], in_=pbt[64:128, 0:128],
                           r=[("pb", 1)], pw=[("qhi", hl2)])

            def sel_steps():
                sel_A(0)
                yield
                for qt in range(NT):
                    if qt + 1 < NT:
                        sel_A(qt + 1)
                    sel_B(qt)
                    yield

            for hl in range(4):
                h = g * 4 + hl
                q_ap, qtok = qTa[hl]

                def fin_c(qt, po, potok):
                    self.E("dve", "tensor_copy", out=stage[:, qt, :], in_=po[:, 0:128], r=[potok], pw=[staget])

                self.attention(q_ap, qtok, self.kcTa[:, g, :], "kcTa", self.vca[:, g], "vca", 128, cmp_pairs, fin_c, pts,
                               f"c{h}")
                zc = stage[:, :, 64]
                self.E("dve", "tensor_scalar", out=rzb, in0=zc, scalar1=1.0e-30, scalar2=None, op0=ALU.max, r=[staget], w=[rzbt])
                self.E("dve", "reciprocal", out=rzb, in_=rzb, r=[rzbt], w=[rzbt])
                self.E("dve", "tensor_tensor", out=cfb, in0=rzb, in1=self.g_sb[:, :, h * 3], op=ALU.mult, r=[rzbt, "g_sb"],
                       w=[cfbt])
                self.E("dve", "tensor_tensor", out=o_all[:, :, h * 64:(h + 1) * 64], in0=stage[:, :, 0:64],
                       in1=cfb.unsqueeze(2).to_broadcast([128, NT, 64]), op=ALU.mult, r=[staget, cfbt], pw=["big"])
                rzb3 = rzb.unsqueeze(2).to_broadcast([128, NT, 63])
                if hl == 0:
                    self.E("dve", "memset", imp[:, :, 63:64], 0.0, w=[impt])
                    self.E("dve", "tensor_tensor", out=imp[:, :, 0:63], in0=stage[:, :, 65:128], in1=rzb3, op=ALU.mult,
                           r=[staget, rzbt], pw=[impt])
                else:
                    self.E("dve", "tensor_tensor", out=stage[:, :, 65:128], in0=stage[:, :, 65:128], in1=rzb3, op=ALU.mult,
                           r=[staget, rzbt], pw=[staget])
                    self.E("dve", "tensor_tensor", out=imp[:, :, 0:63], in0=imp[:, :, 0:63], in1=stage[:, :, 65:128], op=ALU.add,
                           r=[staget, impt], pw=[impt])
            self.E("dve", "tensor_tensor", out=imp[:, :, :], in0=imp[:, :, :], in1=self.msel[:, :, :], op=ALU.add,
                   r=[impt, "msel"], w=[impt])

            def sel_pairs(qt):
                return [(kt, (self.tri[:] if kt == qt else None)) for kt in range(qt + 1)]

            def win_pairs(qt):
                out = []
                for kt in range(max(0, qt - 4), qt + 1):
                    b = self.tri[:] if kt == qt else (self.atri[:] if kt == qt - 4 else None)
                    out.append((kt, b))
                return out

            for hl in range(4):
                h = g * 4 + hl
                q_ap, qtok = qTa[hl]

                def fin_w(qt, po, potok, h=h):
                    self.E("dve", "reciprocal", out=rz, in_=po[:, 64:65], r=[potok], w=[rzt])
                    self.E("dve", "tensor_tensor", out=cf, in0=rz, in1=self.g_sb[:, qt, h * 3 + 2:h * 3 + 3], op=ALU.mult,
                           r=[rzt, "g_sb"], w=[cft])
                    self.E("dve", "scalar_tensor_tensor", out=oacc[:, qt, :], in0=po[:, 0:64], scalar=cf[:, 0:1],
                           in1=o_all[:, qt, h * 64:(h + 1) * 64], op0=ALU.mult, op1=ALU.add, r=[potok, cft, "big"],
                           pw=[oacct])

                def fin_s(qt, po, potok, h=h):
                    self.E("dve", "reciprocal", out=rz, in_=po[:, 64:65], r=[potok], w=[rzt])
                    self.E("dve", "tensor_tensor", out=cf, in0=rz, in1=self.g_sb[:, qt, h * 3 + 1:h * 3 + 2], op=ALU.mult,
                           r=[rzt, "g_sb"], w=[cft])
                    self.E("dve", "scalar_tensor_tensor", out=o_all[:, qt, h * 64:(h + 1) * 64], in0=po[:, 0:64],
                           scalar=cf[:, 0:1], in1=oacc[:, qt, :], op0=ALU.mult, op1=ALU.add, r=[potok, cft, oacct],
                           pw=["big"])

                side = sel_steps() if hl == 0 else None
                self.attention(q_ap, qtok, kwTa, kwt, Vw, Vwt, 65, win_pairs, fin_w, pts, f"w{h}", side=side)
                if side is not None:
                    for _ in side:
                        pass
                self.attention(q_ap, [qtok, ("qhi", hl)], ksTa, kst, Vs, Vst, 65, sel_pairs, fin_s, pts, f"s{h}")

    def phase_O(self, li, src):
        self.phase()
        NT = self.NT
        j = li // 2
        moba = (li % 2 == 0)
        o_all = self.big[:, 0:NT * D].rearrange("p (t f) -> p t f", f=D)
        Wo, Wot = self.ph("Wo", [128, 8, D], BF16)
        Wg, Wgt = self.ph("Wg", [128, 8, D], BF16)
        Wp, Wpt = self.ph("Wp", [128, 2, D], BF16)
        self.load_w(Wo, Wot, (self.moba_w_out if moba else self.nsa_w_out)[j], D, None, None)
        self.load_gcol(self.gcol2, "gcol2", self.ple_gate_gain[li])
        self.load_w(Wg, Wgt, self.ple_w_gate[li], D, self.gcol2, "gcol2")
        self.load_w(Wp, Wpt, self.ple_w_proj[li], D, None, None, nk=2)
        xt = [self.ph(f"oxt{i}", [128, D], F32) for i in range(2)]
        zt = [self.ph(f"ozt{i}", [128, D], BF16) for i in range(2)]
        pt_ = [self.ph(f"opt{i}", [128, PLE], F32) for i in range(2)]
        og = [self.ph(f"og{i}", [128, D], BF16) for i in range(2)]
        ogT = [self.ph(f"ogT{i}", [128, 8, 128], BF16) for i in range(2)]
        x1 = [self.ph(f"x1_{i}", [128, D], F32) for i in range(2)]
        sqs, sqst = self.ph("osq", [128, D], F32)
        rs = [self.ph(f"ors{i}", [128, 1], F32) for i in range(2)]
        xn, xnt = self.ph("xn", [128, D], BF16)
        xnT, xnTt = self.ph("xnT", [128, 8, 128], BF16)
        gate, gatet = self.ph("gatef", [128, D], F32)
        pbf, pbft = self.ph("pbf", [128, PLE], BF16)
        pT, pTt = self.ph("pT", [128, 2, 128], BF16)
        x2 = [self.ph(f"x2_{i}", [128, D], F32) for i in range(2)]

        def S1_T(t):
            x_ap, xtok = xt[t % 2]
            z_ap, ztok = zt[t % 2]
            p_ap, ptok = pt_[t % 2]
            og_ap, ogt = og[t % 2]
            ogT_ap, ogTt = ogT[t % 2]
            self.dma(x_ap, src[t * 128:(t + 1) * 128, :], f"oxl{t % 2}", r=[("y", t)], w=[xtok])
            self.dma(z_ap, self.zs_d[t * 128:(t + 1) * 128, :], f"ozl{t % 2}", r=[("zs_d",)], w=[ztok])
            self.dma(p_ap, self.p_in[li, t * 128:(t + 1) * 128, :], f"opl{t % 2}", w=[ptok])
            self.E("dve", "tensor_tensor", out=og_ap, in0=o_all[:, t, :], in1=z_ap, op=ALU.mult, r=["big", ztok], w=[ogt])
            self.transpose8(og_ap, ogt, ogT_ap, ogTt, 0)

        def S1_M(t):
            x_ap, xtok = xt[t % 2]
            ogT_ap, ogTt = ogT[t % 2]
            x1_ap, x1t = x1[t % 2]
            for c in range(2):
                ps = self.pf[c]
                for k in range(8):
                    self.mm(ps[:, :], ogT_ap[:, k, :], Wo[:, k, c * 512:(c + 1) * 512], k == 0, k == 7, r=[ogTt, Wot],
                            w=[("pf", c)] if k == 0 else ())
                last = self.S.q["pe"][-1]
                self.S.tw[("pf", c)] = {last.stream: last}
                self.E("dve", "tensor_tensor", out=x1_ap[:, c * 512:(c + 1) * 512], in0=x_ap[:, c * 512:(c + 1) * 512],
                       in1=ps[:, :], op=ALU.add, r=[xtok, ("pf", c)], **(dict(w=[x1t]) if c == 0 else dict(pw=[x1t])))

        def S2a(t):
            x1_ap, x1t = x1[t % 2]
            r_ap, rtok = rs[t % 2]
            self.rstd_of(x1_ap, x1t, D, sqs, sqst, r_ap, rtok)
            self.act(xn, x1_ap, AF.Copy, r=[x1t, rtok], w=[xnt], scale=r_ap[:, 0:1])

        def S2b_T(t):
            self.transpose8(xn, xnt, xnT, xnTt, 1)

        def S2b_M(t):
            x1_ap, x1t = x1[t % 2]
            p_ap, ptok = pt_[t % 2]
            for c in range(2):
                ps = self.pf[2 + c]
                for k in range(8):
                    self.mm(ps[:, :], xnT[:, k, :], Wg[:, k, c * 512:(c + 1) * 512], k == 0, k == 7, r=[xnTt, Wgt],
                            w=[("pf", 2 + c)] if k == 0 else ())
                last = self.S.q["pe"][-1]
                self.S.tw[("pf", 2 + c)] = {last.stream: last}
                self.act(gate[:, c * 512:(c + 1) * 512], ps[:, :], AF.Sigmoid, r=[("pf", 2 + c)],
                         **(dict(w=[gatet]) if c == 0 else dict(pw=[gatet])))
            self.E("dve", "tensor_copy", out=pbf, in_=p_ap, r=[ptok], w=[pbft])
            self.transpose8(pbf, pbft, pT, pTt, 1, nblk=2)
            x2_ap, x2tok = x2[t % 2]
            for c in range(2):
                ps = self.pf[4 + c]
                for k in range(2):
                    self.mm(ps[:, :], pT[:, k, :], Wp[:, k, c * 512:(c + 1) * 512], k == 0, k == 1, r=[pTt, Wpt],
                            w=[("pf", 4 + c)] if k == 0 else ())
                last = self.S.q["pe"][-1]
                self.S.tw[("pf", 4 + c)] = {last.stream: last}
                sl = slice(c * 512, (c + 1) * 512)
                self.E("dve", "tensor_tensor", out=x2_ap[:, sl], in0=gate[:, sl], in1=ps[:, :], op=ALU.mult,
                       r=[gatet, ("pf", 4 + c)], **(dict(w=[x2tok]) if c == 0 else dict(pw=[x2tok])))
                self.E("dve", "tensor_tensor", out=x2_ap[:, sl], in0=x2_ap[:, sl], in1=x1_ap[:, sl], op=ALU.add,
                       r=[x2tok, x1t], pw=[x2tok])
            self.dma(self.y[t * 128:(t + 1) * 128, :], x2_ap, f"oyst{t % 2}", r=[x2tok], w=[("y", t)], q="pool")

        S1_T(0)
        S1_M(0)
        for t in range(NT):
            S2a(t)
            if t + 1 < NT:
                S1_T(t + 1)
            S2b_T(t)
            if t + 1 < NT:
                S1_M(t + 1)
            S2b_M(t)


def rope_tables(T):
    NT = T // 128
    half = 32
    inv_freq = (10000.0 ** (-np.arange(half, dtype=np.float32) / half)).astype(np.float32)
    pos = (np.arange(NT)[None, :] * 128 + np.arange(128)[:, None]).astype(np.float32)
    ang = pos[:, :, None] * inv_freq[None, None, :]
    cs = np.stack([np.cos(ang), np.sin(ang)], 0).astype(np.float32)
    NCT = max(1, (T // 16) // 128)
    posc = (16.0 * (np.arange(NCT)[None, :] * 128 + np.arange(128)[:, None]) + 31.0).astype(np.float32)
    angc = posc[:, :, None] * inv_freq[None, None, :]
    csc = np.stack([np.cos(angc), np.sin(angc)], 0).astype(np.float32)
    return cs, csc


_CACHE = {}


def run(inputs, T, layers, n_cores):
    key = (T, tuple(layers))
    if key not in _CACHE:
        _CACHE[key] = Builder(T, list(layers)).build()
    nc = _CACHE[key]
    cs, csc = rope_tables(T)
    shared = {k: np.ascontiguousarray(v, dtype=np.float32) for k, v in inputs.items() if k not in ("x", "p")}
    shared["rope_cs"] = cs
    shared["rope_cs_c"] = csc
    in_maps = []
    for b in range(n_cores):
        m = dict(shared)
        m["x"] = np.ascontiguousarray(inputs["x"][b], dtype=np.float32)
        m["p"] = np.ascontiguousarray(inputs["p"][:, b], dtype=np.float32)
        in_maps.append(m)
    res = run_bass_kernel_spmd(nc, in_maps, core_ids=list(range(n_cores)))
    return np.stack([r["y"] for r in res.results], 0).astype(np.float32)


def kernel(**inputs):
    return run(inputs, 4096, [0, 1, 2, 3], 8)
```
